# Optimizing a Trainium2 kernel written in Bass

```python
import math
import jax
import jax.numpy as jnp
from jax import lax
import numpy as np


D_MODEL = 1024
BATCH = 4
SEQ = 8192
DEPTH = 4

N_MIXERS = 3
N_HYENA = (DEPTH + 2) // 3
N_ATTN = (DEPTH + 1) // 3
N_POOL = DEPTH // 3
EPS = 1e-6

HY_ORDER = 2
HY_SHORT = 3
HY_EMB = 33
HY_BANDS = (HY_EMB - 1) // 2
HY_FW = 64
HY_INNER = 2
HY_TARGET = 1e-2
HY_FAST_PCT = 0.3
HY_SLOW_PCT = 1.5
HY_MIN_DECAY = math.log(HY_TARGET) / HY_SLOW_PCT
HY_MAX_DECAY = math.log(HY_TARGET) / HY_FAST_PCT

HEAD_DIM = 64
N_HEADS = D_MODEL // HEAD_DIM
N_KV = N_HEADS // 4
GROUP = N_HEADS // N_KV
WINDOW = 128
BLOCK = 128
REL_BUCKETS = 32
REL_MAX_DIST = 128

POOL_WINDOWS = (2, 4, 8, 16)
POOL_GROUP = D_MODEL // len(POOL_WINDOWS)

D_FF = -(-8 * D_MODEL // (3 * 256)) * 256

kernel_name = 'hybrid_hyena_swa_pool_encoder'


def rmsnorm(x, g):
    xf = x.astype(jnp.float32)
    y = xf * lax.rsqrt(jnp.mean(xf * xf, axis=-1, keepdims=True) + EPS)
    return (y * g.astype(jnp.float32)).astype(x.dtype)


def hyena_filters(L, w1, b1, wi, bi, w3, freq):
    f32 = jnp.float32
    D = D_MODEL
    t = jnp.linspace(0.0, 1.0, L, dtype=f32)[:, None]
    pos = jnp.arange(L, dtype=f32)[:, None]
    f = jnp.linspace(1e-4, HY_BANDS - 1, HY_BANDS, dtype=f32)[None, :]
    ang = (2.0 * math.pi / L) * pos * f
    z = jnp.concatenate([t, jnp.cos(ang), -jnp.sin(ang)], axis=-1)
    fr = freq.astype(f32)
    h = jnp.sin(fr * (z @ w1.astype(f32) + b1.astype(f32)))
    for j in range(HY_INNER):
        h = jnp.sin(fr * (h @ wi[j].astype(f32) + bi[j].astype(f32)))
    h = h @ w3.astype(f32)
    deltas = jnp.abs(jnp.linspace(HY_MIN_DECAY, HY_MAX_DECAY, D, dtype=f32))
    decay = jnp.exp(-t * deltas[None, :])
    h = h.reshape(L, 2, HY_ORDER, D) * decay[:, None, None, :]
    fwd, bwd = h[:, 0], h[:, 1]
    k = jnp.concatenate([fwd, jnp.zeros((1, HY_ORDER, D), f32), bwd[1:][::-1]], axis=0)
    return k / jnp.sum(jnp.abs(k), axis=0, keepdims=True)


def long_conv(u, kf):
    L = u.shape[1]
    U = jnp.fft.rfft(u, n=2 * L, axis=1)
    return jnp.fft.irfft(U * kf[None], n=2 * L, axis=1)[:, :L]


def hyena_mixer(xn, w_in, b_in, conv_w, conv_b, fw1, fb1, fwi, fbi, fw3, freq, fbias, w_out, b_out):
    B, L, D = xn.shape
    f32 = jnp.float32
    p = xn @ w_in + b_in
    r = HY_SHORT // 2
    pp = jnp.pad(p, ((0, 0), (r, r), (0, 0)))
    p = conv_b + sum(pp[:, j:j + L] * conv_w[j] for j in range(HY_SHORT))
    x1, x2, v = jnp.split(p, 3, axis=-1)
    kf = jnp.fft.rfft(hyena_filters(L, fw1, fb1, fwi, fbi, fw3, freq), axis=0)
    z = v.astype(f32)
    for o, gate in enumerate((x1, x2)):
        z = gate.astype(f32) * (long_conv(z, kf[:, o]) + fbias[o].astype(f32) * z)
    return z.astype(xn.dtype) @ w_out + b_out


def t5_bucket(rel):
    half = REL_BUCKETS // 2
    exact = half // 2
    n = jnp.abs(rel)
    nf = jnp.maximum(n, 1).astype(jnp.float32)
    large = exact + (jnp.log(nf / exact) / math.log(REL_MAX_DIST / exact) * (half - exact)).astype(jnp.int32)
    large = jnp.minimum(large, half - 1)
    return jnp.where(rel > 0, half, 0) + jnp.where(n < exact, n, large)


def window_attention(xn, w_qkv, q_gain, k_gain, sink, w_o, rel_table):
    B, L, D = xn.shape
    f32 = jnp.float32
    nb = L // BLOCK
    qkv = xn @ w_qkv
    q, k, v = jnp.split(qkv, [N_HEADS * HEAD_DIM, (N_HEADS + N_KV) * HEAD_DIM], axis=-1)
    q = rmsnorm(q.reshape(B, L, N_HEADS, HEAD_DIM), q_gain)
    k = rmsnorm(k.reshape(B, L, N_KV, HEAD_DIM), k_gain)
    v = v.reshape(B, L, N_KV, HEAD_DIM)
    q = q.reshape(B, nb, BLOCK, N_KV, GROUP, HEAD_DIM)

    def windows(t):
        tp = jnp.pad(t, ((0, 0), (BLOCK, BLOCK), (0, 0), (0, 0))).reshape(B, nb + 2, BLOCK, N_KV, HEAD_DIM)
        return jnp.concatenate([tp[:, 0:nb], tp[:, 1:nb + 1], tp[:, 2:nb + 2]], axis=2)

    kw, vw = windows(k), windows(v)
    logits = jnp.einsum('bnqkgd,bnskd->bnkgqs', q, kw, preferred_element_type=f32) * (HEAD_DIM ** -0.5)
    a = jnp.arange(BLOCK)[:, None]
    j = jnp.arange(3 * BLOCK)[None, :]
    rel = j - BLOCK - a
    bias = rel_table[t5_bucket(rel)].astype(f32)
    bias = bias.transpose(2, 0, 1).reshape(N_KV, GROUP, BLOCK, 3 * BLOCK)
    kpos = (jnp.arange(nb)[:, None, None] - 1) * BLOCK + j[None]
    mask = (jnp.abs(rel)[None] <= WINDOW) & (kpos >= 0) & (kpos < L)
    logits = jnp.where(mask[None, :, None, None], logits + bias, -1e30)
    sink_l = sink.astype(f32).reshape(1, 1, N_KV, GROUP, 1, 1)
    m = jnp.maximum(jnp.max(logits, axis=-1, keepdims=True), sink_l)
    pr = jnp.exp(logits - m)
    pr = pr / (jnp.sum(pr, axis=-1, keepdims=True) + jnp.exp(sink_l - m))
    o = jnp.einsum('bnkgqs,bnskd->bnqkgd', pr.astype(vw.dtype), vw)
    return o.reshape(B, L, N_HEADS * HEAD_DIM) @ w_o


def pool_mixer(xn, w_grp, b, scale):
    B, L, D = xn.shape
    f32 = jnp.float32
    xf = xn.astype(f32)
    cs = jnp.pad(jnp.cumsum(xf, axis=1), ((0, 0), (1, 0), (0, 0)))
    t = jnp.arange(L)
    outs = []
    for g, w in enumerate(POOL_WINDOWS):
        r = w // 2
        lo = jnp.clip(t - r, 0, L)
        hi = jnp.clip(t + r + 1, 0, L)
        c = cs[..., g * POOL_GROUP:(g + 1) * POOL_GROUP]
        mean = (jnp.take(c, hi, axis=1) - jnp.take(c, lo, axis=1)) / (hi - lo).astype(f32)[:, None]
        outs.append(mean - xf[..., g * POOL_GROUP:(g + 1) * POOL_GROUP])
    d = jnp.stack(outs, axis=2).astype(xn.dtype)
    y = jnp.einsum('blgc,gcd->blgd', d, w_grp).reshape(B, L, D) + b
    return y * scale


def swiglu(xn, wg, wu, wd):
    return (jax.nn.silu(xn @ wg) * (xn @ wu)) @ wd


def _normal(key, shape, scale):
    return jax.random.normal(key, shape, jnp.float32) * scale


def setup_inputs(seed: int = 0) -> dict:
    key = jax.random.key(seed)
    k = jax.random.split(key, 28)
    D = D_MODEL
    QKV = (N_HEADS + 2 * N_KV) * HEAD_DIM
    return {
        'x': _normal(k[0], (BATCH, SEQ, D), 1.0),
        'norm_mix': 1.0 + _normal(k[1], (DEPTH, D), 0.02),
        'norm_ffn': 1.0 + _normal(k[2], (DEPTH, D), 0.02),
        'hy_w_in': _normal(k[3], (N_HYENA, D, 3 * D), D ** -0.5),
        'hy_b_in': _normal(k[4], (N_HYENA, 3 * D), 0.02),
        'hy_conv_w': _normal(k[5], (N_HYENA, HY_SHORT, 3 * D), HY_SHORT ** -0.5),
        'hy_conv_b': _normal(k[6], (N_HYENA, 3 * D), 0.02),
        'hy_f_w1': _normal(k[7], (N_HYENA, HY_EMB, HY_FW), HY_EMB ** -0.5),
        'hy_f_b1': _normal(k[8], (N_HYENA, HY_FW), 0.1),
        'hy_f_wi': _normal(k[9], (N_HYENA, HY_INNER, HY_FW, HY_FW), HY_FW ** -0.5),
        'hy_f_bi': _normal(k[10], (N_HYENA, HY_INNER, HY_FW), 0.1),
        'hy_f_w3': _normal(k[11], (N_HYENA, HY_FW, 2 * HY_ORDER * D), HY_FW ** -0.5),
        'hy_f_freq': 1.0 + _normal(k[12], (N_HYENA, HY_FW), 0.02),
        'hy_f_bias': _normal(k[13], (N_HYENA, HY_ORDER, D), 0.5),
        'hy_w_out': _normal(k[14], (N_HYENA, D, D), D ** -0.5),
        'hy_b_out': _normal(k[15], (N_HYENA, D), 0.02),
        'at_w_qkv': _normal(k[16], (N_ATTN, D, QKV), D ** -0.5),
        'at_q_gain': 1.0 + _normal(k[17], (N_ATTN, HEAD_DIM), 0.02),
        'at_k_gain': 1.0 + _normal(k[18], (N_ATTN, HEAD_DIM), 0.02),
        'at_sink': _normal(k[19], (N_ATTN, N_HEADS), 0.5),
        'at_w_o': _normal(k[20], (N_ATTN, N_HEADS * HEAD_DIM, D), (N_HEADS * HEAD_DIM) ** -0.5),
        'rel_table': _normal(k[21], (REL_BUCKETS, N_HEADS), 0.5),
        'pl_w': _normal(k[22], (N_POOL, len(POOL_WINDOWS), POOL_GROUP, POOL_GROUP), POOL_GROUP ** -0.5),
        'pl_b': _normal(k[23], (N_POOL, D), 0.02),
        'pl_scale': 1.0 + _normal(k[24], (N_POOL, D), 0.02),
        'ff_w_gate': _normal(k[25], (DEPTH, D, D_FF), D ** -0.5),
        'ff_w_up': _normal(k[26], (DEPTH, D, D_FF), D ** -0.5),
        'ff_w_down': _normal(k[27], (DEPTH, D_FF, D), D_FF ** -0.5),
    }


def reference(x, norm_mix, norm_ffn, hy_w_in, hy_b_in, hy_conv_w, hy_conv_b, hy_f_w1, hy_f_b1,
              hy_f_wi, hy_f_bi, hy_f_w3, hy_f_freq, hy_f_bias, hy_w_out, hy_b_out,
              at_w_qkv, at_q_gain, at_k_gain, at_sink, at_w_o, rel_table,
              pl_w, pl_b, pl_scale, ff_w_gate, ff_w_up, ff_w_down):
    for i in range(DEPTH):
        kind, s = i % N_MIXERS, i // N_MIXERS
        h = rmsnorm(x, norm_mix[i])
        if kind == 0:
            y = hyena_mixer(h, hy_w_in[s], hy_b_in[s], hy_conv_w[s], hy_conv_b[s], hy_f_w1[s], hy_f_b1[s],
                            hy_f_wi[s], hy_f_bi[s], hy_f_w3[s], hy_f_freq[s], hy_f_bias[s],
                            hy_w_out[s], hy_b_out[s])
        elif kind == 1:
            y = window_attention(h, at_w_qkv[s], at_q_gain[s], at_k_gain[s], at_sink[s], at_w_o[s], rel_table)
        else:
            y = pool_mixer(h, pl_w[s], pl_b[s], pl_scale[s])
        x = x + y
        x = x + swiglu(rmsnorm(x, norm_ffn[i]), ff_w_gate[i], ff_w_up[i], ff_w_down[i])
    return x
```

```python
import contextlib
import math

import numpy as np
import concourse.bass as bass
import concourse.mybir as mybir
from concourse.bass_utils import run_bass_kernel_spmd

F32 = mybir.dt.float32
F32R = mybir.dt.float32r
AF = mybir.ActivationFunctionType
ALU = mybir.AluOpType
AX = mybir.AxisListType

D = 1024
L = 8192
DFF = 2816
NF = DFF // 128
EPS = 1e-6
TT = 512
NBLK = L // 128
NSLOT = 20
SAME_ENGINE_SYNC = True


class T:
    __slots__ = ("w", "r")

    def __init__(self):
        self.w = []
        self.r = []


class Op:
    __slots__ = ("eng", "idx", "fn", "deps", "dma", "dj", "sig", "waited")

    def __init__(self, eng, idx, fn, dma):
        self.eng = eng
        self.idx = idx
        self.fn = fn
        self.deps = ()
        self.dma = dma
        self.dj = None
        self.sig = None
        self.waited = False


class MK:
    ENGS = ("pe", "act", "dve", "pool", "sp")

    def __init__(self, nc):
        self.nc = nc
        self.ops = {e: [] for e in self.ENGS}
        self.dma_ops = {e: [] for e in self.ENGS}
        self.last_c = {e: None for e in self.ENGS}
        self.bar = None
        self.bar_seen = {e: True for e in self.ENGS}

    def barrier(self):
        deps = set()
        for e in self.ENGS:
            if self.last_c[e] is not None:
                deps.add(self.last_c[e])
            for o in self.dma_ops[e][-NSLOT:]:
                deps.add(o)
        self.bar = deps
        self.bar_seen = {e: False for e in self.ENGS}

    def op(self, eng, fn, reads=(), writes=(), dma=False):
        lst = self.ops[eng]
        o = Op(eng, len(lst), fn, dma)
        lst.append(o)
        deps = set()
        if not self.bar_seen[eng]:
            self.bar_seen[eng] = True
            deps |= self.bar
        for t in reads:
            deps.update(t.w)
        for t in writes:
            deps.update(t.w)
            deps.update(t.r)
        if dma:
            dl = self.dma_ops[eng]
            o.dj = len(dl)
            dl.append(o)
            if o.dj >= NSLOT:
                deps.add(dl[o.dj - NSLOT])
        else:
            self.last_c[eng] = o
        deps.discard(o)
        o.deps = deps
        for t in reads:
            if dma:
                t.r.append(o)
            else:
                t.r = [x for x in t.r if x.dma or x.eng != eng] + [o]
        for t in writes:
            t.w = [o]
            t.r = []
        return o

    @staticmethod
    def _skip(d, o):
        return (not d.dma) and d.eng == o.eng and (not o.dma) and (d.eng == "pe" or not SAME_ENGINE_SYNC)

    def emit(self):
        nc = self.nc
        for e in self.ENGS:
            for o in self.ops[e]:
                for d in o.deps:
                    if d.dma or self._skip(d, o):
                        continue
                    d.waited = True
        for e in self.ENGS:
            c = 0
            for o in self.ops[e]:
                if not o.dma and o.waited:
                    c += 1
                    o.sig = c
        with contextlib.ExitStack() as st:
            csem = {e: st.enter_context(nc.semaphore("c_" + e)) for e in ("pe", "act", "dve", "pool")}
            dsem = {}
            for q in self.ENGS:
                n = len(self.dma_ops[q])
                if n:
                    dsem[q] = [st.enter_context(nc.semaphore("d_%s_%d" % (q, i))) for i in range(min(NSLOT, n))]
            block = st.enter_context(nc.Block())
            mk = self

            def run(ename):
                def body(e):
                    known_c = {}
                    known_d = set()
                    for o in mk.ops[ename]:
                        cw = {}
                        for d in o.deps:
                            if d.dma:
                                key = (d.eng, d.dj)
                                if key in known_d:
                                    continue
                                known_d.add(key)
                                e.wait_ge(dsem[d.eng][d.dj % NSLOT], 16 * (d.dj // NSLOT + 1))
                            else:
                                if mk._skip(d, o):
                                    continue
                                if known_c.get(d.eng, 0) >= d.sig:
                                    continue
                                cw[d.eng] = max(cw.get(d.eng, 0), d.sig)
                        for en, v in cw.items():
                            known_c[en] = v
                            e.wait_ge(csem[en], v)
                        ins = o.fn(e)
                        if o.dma:
                            ins.then_inc(dsem[ename][o.dj % NSLOT], 16)
                        elif o.sig is not None:
                            ins.then_inc(csem[ename], 1)
                    for o in mk.dma_ops[ename][-NSLOT:]:
                        if (ename, o.dj) not in known_d:
                            e.wait_ge(dsem[ename][o.dj % NSLOT], 16 * (o.dj // NSLOT + 1))
                return body

            if self.ops["sp"]:
                block.sync(run("sp"))
            if self.ops["pe"]:
                block.tensor(run("pe"))
            if self.ops["act"]:
                block.scalar(run("act"))
            if self.ops["dve"]:
                block.vector(run("dve"))
            if self.ops["pool"]:
                block.gpsimd(run("pool"))


ARENA = 51 * 1024


class Ctx:
    def __init__(self, nc, st):
        self.nc = nc
        self.mk = MK(nc)
        self.sb_base = 16512
        self.ntens = 0
        self.psum = [st.enter_context(nc.psum_tensor("psb%d" % i, [128, 512], F32)) for i in range(8)]
        self.off = 0
        self.base = 0
        self.dram = {}

    def din(self, name, shape, dt=F32):
        if name in self.dram:
            return self.dram[name].ap()
        t = self.nc.dram_tensor(name, list(shape), dt, kind="ExternalInput")
        self.dram[name] = t
        return t.ap()

    def dscratch(self, name, shape, dt=F32):
        t = self.nc.dram_tensor(name, list(shape), dt)
        return t.ap()

    def sb(self, cols, dt=F32, parts=128):
        self.off = (self.off + 7) // 8 * 8
        self.ntens += 1
        t = self.nc.alloc_sbuf_tensor_at("t%d" % self.ntens, [parts, cols], dt, offset=self.sb_base + 4 * self.off)
        self.off += cols
        assert self.off <= ARENA, ("arena overflow", self.off)
        return t[:, :]

    def sb_at(self, off, cols, dt=F32, parts=128):
        self.ntens += 1
        t = self.nc.alloc_sbuf_tensor_at("t%d" % self.ntens, [parts, cols], dt, offset=self.sb_base + 4 * off)
        return t[:, :]

    def new_phase(self):
        self.mk.barrier()
        self.off = self.base

    def op(self, *a, **k):
        return self.mk.op(*a, **k)


def load_consts(c):
    c.ident = c.sb(128)
    c.t_const = T()
    ident_d = c.din("ident", [128, 128])
    c.op("sp", lambda e: e.dma_start(out=c.ident, in_=ident_d), writes=[c.t_const], dma=True)
    c.epsc = c.sb(1)
    c.op("dve", lambda e: e.memset(c.epsc, EPS), writes=[c.t_const])
    c.base = c.off


class Front:
    def __init__(self, c, want_hT=True, xn_dt=F32):
        self.c = c
        self.xin = [c.sb(D) for _ in range(4)]
        self.t_xin = [T() for _ in range(4)]
        self.xn = [c.sb(D, xn_dt) for _ in range(4)]
        self.t_xn = [T() for _ in range(4)]
        self.ss = c.sb(4)
        self.t_ss = [T() for _ in range(4)]
        self.rstd = c.sb(4)
        self.t_rstd = [T() for _ in range(4)]
        if want_hT:
            self.hT = [c.sb(TT, F32R) for _ in range(8)]
            self.t_hT = [T() for _ in range(8)]
        self.tcount = 0

    def load_norm(self, src, r0, blocks=(0, 1, 2, 3)):
        c = self.c
        for tb in blocks:
            c.op("sp", lambda e, tb=tb, r0=r0: e.dma_start(out=self.xin[tb], in_=src[r0 + tb * 128:r0 + (tb + 1) * 128, :]),
                 writes=[self.t_xin[tb]], dma=True)
        for tb in blocks:
            c.op("dve", lambda e, tb=tb: e.memset(self.ss[:, tb:tb + 1], 0.0), writes=[self.t_ss[tb]])
            c.op("act", lambda e, tb=tb: e.activation(out=self.xn[tb], in_=self.xin[tb], func=AF.Square, accum_out=self.ss[:, tb:tb + 1]),
                 reads=[self.t_xin[tb]], writes=[self.t_xn[tb], self.t_ss[tb]])
            c.op("act", lambda e, tb=tb: e.activation(out=self.rstd[:, tb:tb + 1], in_=self.ss[:, tb:tb + 1], func=AF.Sqrt, scale=1.0 / D, bias=c.epsc),
                 reads=[self.t_ss[tb], c.t_const], writes=[self.t_rstd[tb]])
            c.op("dve", lambda e, tb=tb: e.reciprocal(out=self.rstd[:, tb:tb + 1], in_=self.rstd[:, tb:tb + 1]),
                 reads=[self.t_rstd[tb]], writes=[self.t_rstd[tb]])
            c.op("act", lambda e, tb=tb: e.activation(out=self.xn[tb], in_=self.xin[tb], func=AF.Copy, scale=self.rstd[:, tb:tb + 1]),
                 reads=[self.t_xin[tb], self.t_rstd[tb]], writes=[self.t_xn[tb]])

    def transpose(self, gcol, t_g, pbanks, t_pb):
        c = self.c
        for k in range(8):
            b = self.tcount % len(pbanks)
            self.tcount += 1
            for tb in range(4):
                c.op("pe", lambda e, tb=tb, k=k, b=b: e.transpose(out=pbanks[b][:, tb * 128:(tb + 1) * 128], in_=self.xn[tb][:, k * 128:(k + 1) * 128], identity=c.ident),
                     reads=[self.t_xn[tb], c.t_const], writes=[t_pb[b]])
            c.op("act", lambda e, k=k, b=b: e.activation(out=self.hT[k], in_=pbanks[b][:, :], func=AF.Copy, scale=gcol[:, k:k + 1]),
                 reads=[t_pb[b], t_g], writes=[self.t_hT[k]])


def ffn_phase(c, src, dst, li, ntok=L):
    c.new_phase()
    g_d = c.din("ffn_g%d" % li, [128, 8])
    wgu_d = c.din("ffn_wgu%d" % li, [NF, 128, 2048])
    wd_d = c.din("ffn_wd%d" % li, [NF, 128, D])
    gcol = c.sb(8)
    t_g = T()
    c.op("sp", lambda e: e.dma_start(out=gcol, in_=g_d), writes=[t_g], dma=True)
    fr = Front(c)
    aT = [c.sb(TT, F32R) for _ in range(NF)]
    t_aT = [T() for _ in range(NF)]
    wd = [c.sb(D, F32R) for _ in range(NF)]
    t_wd = [T() for _ in range(NF)]
    wgu = [c.sb(2048, F32R) for _ in range(2)]
    t_wgu = [T() for _ in range(2)]
    sg = [c.sb(TT) for _ in range(2)]
    t_sg = [T() for _ in range(2)]
    P = c.psum
    ps_t, ps_g, ps_u, ps_d = P[0:2], P[2:4], P[4:6], P[6:8]
    t_pst = [T(), T()]
    t_psg = [T(), T()]
    t_psu = [T(), T()]
    t_psd = [T(), T()]
    cnt = {"gu": 0, "d": 0}
    for it in range(ntok // TT):
        r0 = it * TT
        fr.load_norm(src, r0)
        for f in range(NF):
            c.op("pool", lambda e, f=f: e.dma_start(out=wd[f], in_=wd_d[f]), writes=[t_wd[f]], dma=True)
        fr.transpose(gcol, t_g, ps_t, t_pst)
        hT, t_hT = fr.hT, fr.t_hT
        for f in range(NF):
            wb = f % 2
            c.op("pool", lambda e, f=f, wb=wb: e.dma_start(out=wgu[wb], in_=wgu_d[f]), writes=[t_wgu[wb]], dma=True)
            b = cnt["gu"] % 2
            cnt["gu"] += 1
            for k in range(8):
                c.op("pe", lambda e, k=k, wb=wb, b=b: e.matmul(ps_g[b][:, :], wgu[wb][:, k * 256:k * 256 + 128], hT[k], start=(k == 0), stop=(k == 7)),
                     reads=[t_wgu[wb], t_hT[k]], writes=[t_psg[b]])
            for k in range(8):
                c.op("pe", lambda e, k=k, wb=wb, b=b: e.matmul(ps_u[b][:, :], wgu[wb][:, k * 256 + 128:k * 256 + 256], hT[k], start=(k == 0), stop=(k == 7)),
                     reads=[t_wgu[wb], t_hT[k]], writes=[t_psu[b]])
            c.op("act", lambda e, b=b: e.activation(out=sg[b], in_=ps_g[b][:, :], func=AF.Silu), reads=[t_psg[b]], writes=[t_sg[b]])
            c.op("dve", lambda e, b=b, f=f: e.tensor_tensor(out=aT[f], in0=sg[b], in1=ps_u[b][:, :], op=ALU.mult),
                 reads=[t_sg[b], t_psu[b]], writes=[t_aT[f]])
        for tb in range(4):
            for dh in range(2):
                b = cnt["d"] % 2
                cnt["d"] += 1
                for f in range(NF):
                    c.op("pe", lambda e, f=f, tb=tb, dh=dh, b=b: e.matmul(ps_d[b][:, :], aT[f][:, tb * 128:(tb + 1) * 128], wd[f][:, dh * 512:(dh + 1) * 512], start=(f == 0), stop=(f == NF - 1)),
                         reads=[t_aT[f], t_wd[f]], writes=[t_psd[b]])
                c.op("dve", lambda e, tb=tb, dh=dh, b=b: e.tensor_tensor(out=fr.xin[tb][:, dh * 512:(dh + 1) * 512], in0=ps_d[b][:, :], in1=fr.xin[tb][:, dh * 512:(dh + 1) * 512], op=ALU.add),
                     reads=[t_psd[b], fr.t_xin[tb]], writes=[fr.t_xin[tb]])
            c.op("sp", lambda e, tb=tb, r0=r0: e.dma_start(out=dst[r0 + tb * 128:r0 + (tb + 1) * 128, :], in_=fr.xin[tb]), reads=[fr.t_xin[tb]], dma=True)


def ffn_host(inputs, li):
    wg = np.asarray(inputs["ff_w_gate"][li], np.float32)
    wu = np.asarray(inputs["ff_w_up"][li], np.float32)
    wdn = np.asarray(inputs["ff_w_down"][li], np.float32)
    g = np.asarray(inputs["norm_ffn"][li], np.float32)
    wgu = np.stack([wg.reshape(8, 128, NF, 128), wu.reshape(8, 128, NF, 128)], axis=0)
    wgu = np.ascontiguousarray(wgu.transpose(3, 2, 1, 0, 4)).reshape(NF, 128, 2048)
    return {"ffn_g%d" % li: np.ascontiguousarray(g.reshape(8, 128).T),
            "ffn_wgu%d" % li: wgu,
            "ffn_wd%d" % li: np.ascontiguousarray(wdn.reshape(NF, 128, D))}


POOL_WINDOWS = (2, 4, 8, 16)


def pool_consts():
    mats = np.zeros((4, 7, 128, 128), np.float32)
    t = np.arange(L)
    for g, w in enumerate(POOL_WINDOWS):
        r = w // 2
        lo = np.clip(t - r, 0, L)
        hi = np.clip(t + r + 1, 0, L)
        inv = (1.0 / (hi - lo)).astype(np.float32)

        def blk(bi, bj):
            tp = np.arange(bi * 128, (bi + 1) * 128)[:, None]
            tt = np.arange(bj * 128, (bj + 1) * 128)[None, :]
            m = ((tp >= lo[tt]) & (tp < hi[tt])).astype(np.float32) * inv[tt]
            m = m - (tp == tt).astype(np.float32)
            return m
        mats[g, 0] = blk(4, 5)
        mats[g, 1] = blk(5, 5)
        mats[g, 2] = blk(6, 5)
        mats[g, 3] = blk(0, 0)
        mats[g, 4] = blk(1, 0)
        mats[g, 5] = blk(NBLK - 2, NBLK - 1)
        mats[g, 6] = blk(NBLK - 1, NBLK - 1)
    return np.ascontiguousarray(mats.transpose(2, 0, 1, 3)).reshape(128, 4 * 7 * 128)


def pool_phase(c, src, dst, inputs_key="pl"):
    c.new_phase()
    pm_d = c.din("pl_mats", [128, 28 * 128])
    g_d = c.din("pl_g", [128, 8])
    w_d = c.din("pl_wt", [128, 8, 256])
    b_d = c.din("pl_b", [1, D])
    s_d = c.din("pl_scale", [1, D])
    pm = c.sb(28 * 128, F32R)
    gcol = c.sb(8)
    wg = c.sb(8 * 256, F32R)
    brow = c.sb(D)
    srow = c.sb(D)
    t_k = T()
    c.op("pool", lambda e: e.dma_start(out=pm, in_=pm_d), writes=[t_k], dma=True)
    t_k2 = T()
    c.op("pool", lambda e: e.dma_start(out=wg, in_=w_d.rearrange("p a b -> p (a b)")), writes=[t_k2], dma=True)
    t_k3 = T()
    c.op("sp", lambda e: e.dma_start(out=gcol, in_=g_d), writes=[t_k3], dma=True)
    t_k4 = T()
    c.op("sp", lambda e: e.dma_start(out=brow, in_=b_d.broadcast_to([128, D])), writes=[t_k4], dma=True)
    t_k5 = T()
    c.op("sp", lambda e: e.dma_start(out=srow, in_=s_d.broadcast_to([128, D])), writes=[t_k5], dma=True)
    RING = 4
    xin = [c.sb(D) for _ in range(RING)]
    t_xin = [T() for _ in range(RING)]
    xn = [c.sb(D, F32R) for _ in range(RING)]
    t_xn = [T() for _ in range(RING)]
    junk = c.sb(D)
    t_junk = T()
    ss = c.sb(RING)
    rstd = c.sb(RING)
    t_ss = [T() for _ in range(RING)]
    t_rstd = [T() for _ in range(RING)]
    dT = [c.sb(128, F32R) for _ in range(8)]
    t_dT = [T() for _ in range(8)]
    yt = [c.sb(D) for _ in range(2)]
    t_yt = [T(), T()]
    P = c.psum
    ps_p = P[0:4]
    t_psp = [T() for _ in range(4)]
    ps_y = [P[4:6], P[6:8]]
    t_psy = [T(), T()]

    def prep(i):
        s = i % RING
        c.op("sp", lambda e, s=s, i=i: e.dma_start(out=xin[s], in_=src[i * 128:(i + 1) * 128, :]), writes=[t_xin[s]], dma=True)
        c.op("dve", lambda e, s=s: e.memset(ss[:, s:s + 1], 0.0), writes=[t_ss[s]])
        c.op("act", lambda e, s=s: e.activation(out=junk, in_=xin[s], func=AF.Square, accum_out=ss[:, s:s + 1]),
             reads=[t_xin[s]], writes=[t_junk, t_ss[s]])
        c.op("act", lambda e, s=s: e.activation(out=rstd[:, s:s + 1], in_=ss[:, s:s + 1], func=AF.Sqrt, scale=1.0 / D, bias=c.epsc),
             reads=[t_ss[s], c.t_const], writes=[t_rstd[s]])
        c.op("dve", lambda e, s=s: e.reciprocal(out=rstd[:, s:s + 1], in_=rstd[:, s:s + 1]), reads=[t_rstd[s]], writes=[t_rstd[s]])
        c.op("act", lambda e, s=s: e.activation(out=xn[s], in_=xin[s], func=AF.Copy, scale=rstd[:, s:s + 1]),
             reads=[t_xin[s], t_rstd[s]], writes=[t_xn[s]])

    prep(0)
    for i in range(NBLK):
        if i + 1 < NBLK:
            prep(i + 1)
        if i == 0:
            terms = [(0, 3), (1, 4)]
        elif i == NBLK - 1:
            terms = [(-1, 5), (0, 6)]
        else:
            terms = [(-1, 0), (0, 1), (1, 2)]
        for g in range(4):
            pb = g
            for j in range(2):
                cc = 2 * g + j
                for ti, (rel, mi) in enumerate(terms):
                    s = (i + rel) % RING
                    c.op("pe", lambda e, s=s, cc=cc, g=g, mi=mi, j=j, pb=pb, ti=ti, nt=len(terms):
                         e.matmul(ps_p[pb][:, j * 128:(j + 1) * 128], xn[s][:, cc * 128:(cc + 1) * 128], pm[:, (g * 7 + mi) * 128:(g * 7 + mi + 1) * 128], start=(ti == 0), stop=(ti == nt - 1)),
                         reads=[t_xn[s], t_k], writes=[t_psp[pb]])
            for j in range(2):
                cc = 2 * g + j
                c.op("act", lambda e, cc=cc, j=j, pb=pb: e.activation(out=dT[cc], in_=ps_p[pb][:, j * 128:(j + 1) * 128], func=AF.Copy, scale=gcol[:, cc:cc + 1]),
                     reads=[t_psp[pb], t_k3], writes=[t_dT[cc]])
        yb = i % 2
        for g in range(4):
            for j in range(2):
                cc = 2 * g + j
                c.op("pe", lambda e, cc=cc, g=g, j=j, yb=yb: e.matmul(ps_y[yb][g // 2][:, (g % 2) * 256:(g % 2 + 1) * 256], dT[cc], wg[:, cc * 256:(cc + 1) * 256], start=(j == 0), stop=(j == 1)),
                     reads=[t_dT[cc], t_k2], writes=[t_psy[yb]])
        s = i % RING
        for hh in range(2):
            sl = slice(hh * 512, (hh + 1) * 512)
            c.op("dve", lambda e, yb=yb, hh=hh, sl=sl: e.tensor_tensor(out=yt[yb][:, sl], in0=ps_y[yb][hh][:, :], in1=brow[:, sl], op=ALU.add),
                 reads=[t_psy[yb], t_k4], writes=[t_yt[yb]])
        c.op("pool", lambda e, yb=yb: e.tensor_tensor(out=yt[yb], in0=yt[yb], in1=srow, op=ALU.mult), reads=[t_yt[yb], t_k5], writes=[t_yt[yb]])
        c.op("pool", lambda e, yb=yb, s=s: e.tensor_tensor(out=yt[yb], in0=yt[yb], in1=xin[s], op=ALU.add), reads=[t_yt[yb], t_xin[s]], writes=[t_yt[yb]])
        c.op("sp", lambda e, yb=yb, i=i: e.dma_start(out=dst[i * 128:(i + 1) * 128, :], in_=yt[yb]), reads=[t_yt[yb]], dma=True)


def pool_host(inputs):
    w = np.asarray(inputs["pl_w"][0], np.float32)
    wt = w.reshape(4, 2, 128, 256).transpose(2, 0, 1, 3).reshape(128, 8, 256)
    g = np.asarray(inputs["norm_mix"][2], np.float32)
    return {"pl_mats": pool_consts(),
            "pl_g": np.ascontiguousarray(g.reshape(8, 128).T),
            "pl_wt": np.ascontiguousarray(wt),
            "pl_b": np.asarray(inputs["pl_b"][0], np.float32).reshape(1, D),
            "pl_scale": np.asarray(inputs["pl_scale"][0], np.float32).reshape(1, D)}


NH = 16
NKV = 4
HD = 64
NEG = -30000.0
_T5_THR = (8, 12, 16, 23, 32, 46, 64, 91)


def _t5_bucket(rel):
    n = abs(rel)
    if n < 8:
        b = n
    else:
        b = 7 + sum(1 for t in _T5_THR if n >= t)
    return (16 if rel > 0 else 0) + b


def attn_onehot():
    oh = np.zeros((33, 3, 128, 128), np.float32)
    for kb in range(3):
        for a in range(128):
            for j in range(128):
                rel = 128 * (kb - 1) + j - a
                if abs(rel) <= 128:
                    oh[_t5_bucket(rel), kb, a, j] = 1.0
                else:
                    oh[32, kb, a, j] = 1.0
    return oh.reshape(33, 3 * 128 * 128)


def const_r(c, cols, val, parts):
    tmp = c.sb(cols, parts=parts)
    out = c.sb(cols, F32R, parts=parts)
    t = T()
    c.op("dve", lambda e: e.memset(tmp, val), writes=[t])
    c.op("act", lambda e: e.activation(out=out, in_=tmp, func=AF.Copy), reads=[t], writes=[t])
    return out, t


def attn_qkv_phase(c, src, S):
    c.new_phase()
    g_d = c.din("at_g", [128, 8])
    w_d = c.din("at_wqkv", [8, 128, 1536])
    qg_d = c.din("at_qg", [64, 1])
    kg_d = c.din("at_kg", [64, 1])
    gcol = c.sb(8)
    qg = c.sb(1, parts=64)
    kg = c.sb(1, parts=64)
    t_g = T()
    c.op("sp", lambda e: e.dma_start(out=gcol, in_=g_d), writes=[t_g], dma=True)
    c.op("sp", lambda e: e.dma_start(out=qg, in_=qg_d), writes=[t_g], dma=True)
    c.op("sp", lambda e: e.dma_start(out=kg, in_=kg_d), writes=[t_g], dma=True)
    wq = [c.sb(1536, F32R) for _ in range(8)]
    t_wq = [T() for _ in range(8)]
    for k in range(8):
        c.op("pool", lambda e, k=k: e.dma_start(out=wq[k], in_=w_d[k]), writes=[t_wq[k]], dma=True)
    ones64, t_ones = const_r(c, 64, 1.0 / 64, 64)
    fr = Front(c)
    sq = [c.sb(TT, F32R, parts=64) for _ in range(2)]
    t_sq = [T(), T()]
    rs = [c.sb(TT, parts=64) for _ in range(2)]
    t_rs = [T(), T()]
    qn = [c.sb(TT, parts=64) for _ in range(2)]
    t_qn = [T(), T()]
    vt = [c.sb(256) for _ in range(2)]
    t_vt = [T(), T()]
    P = c.psum
    ps_t, ps_q, ps_m, ps_v = P[0:2], P[2:4], P[4:6], P[6:8]
    t_pst, t_psq, t_psm, t_psv = [T(), T()], [T(), T()], [T(), T()], [T(), T()]
    cnt = 0
    cv = 0
    for it in range(L // TT):
        r0 = it * TT
        fr.load_norm(src, r0)
        fr.transpose(gcol, t_g, ps_t, t_pst)
        hT, t_hT = fr.hT, fr.t_hT
        for h in range(NH + NKV):
            b = cnt % 2
            cnt += 1
            isq = h < NH
            col0 = h * 64 if isq else 1024 + (h - NH) * 64
            gain = qg if isq else kg
            dstT = S["qT"][h] if isq else S["kT"][h - NH]
            for k in range(8):
                c.op("pe", lambda e, k=k, b=b, col0=col0: e.matmul(ps_q[b][0:64, :], wq[k][:, col0:col0 + 64], hT[k], start=(k == 0), stop=(k == 7)),
                     reads=[t_wq[k], t_hT[k]], writes=[t_psq[b]])
            c.op("act", lambda e, b=b: e.activation(out=sq[b], in_=ps_q[b][0:64, :], func=AF.Square), reads=[t_psq[b]], writes=[t_sq[b]])
            c.op("pe", lambda e, b=b: e.matmul(ps_m[b][0:64, :], ones64, sq[b], start=True, stop=True), reads=[t_ones, t_sq[b]], writes=[t_psm[b]])
            c.op("act", lambda e, b=b: e.activation(out=rs[b], in_=ps_m[b][0:64, :], func=AF.Sqrt, bias=c.epsc[0:64, :]), reads=[t_psm[b], c.t_const], writes=[t_rs[b]])
            c.op("dve", lambda e, b=b: e.reciprocal(out=rs[b], in_=rs[b]), reads=[t_rs[b]], writes=[t_rs[b]])
            c.op("dve", lambda e, b=b, gain=gain: e.scalar_tensor_tensor(out=qn[b], in0=ps_q[b][0:64, :], scalar=gain, in1=rs[b], op0=ALU.mult, op1=ALU.mult),
                 reads=[t_psq[b], t_rs[b], t_g], writes=[t_qn[b]])
            c.op("sp", lambda e, b=b, dstT=dstT, r0=r0: e.dma_start(out=dstT[:, r0:r0 + TT], in_=qn[b]), reads=[t_qn[b]], dma=True)
        for tb in range(4):
            b = cv % 2
            cv += 1
            for k in range(8):
                c.op("pe", lambda e, k=k, b=b, tb=tb: e.matmul(ps_v[b][:, 0:256], hT[k][:, tb * 128:(tb + 1) * 128], wq[k][:, 1280:1536], start=(k == 0), stop=(k == 7)),
                     reads=[t_wq[k], t_hT[k]], writes=[t_psv[b]])
            c.op("act", lambda e, b=b: e.activation(out=vt[b], in_=ps_v[b][:, 0:256], func=AF.Copy), reads=[t_psv[b]], writes=[t_vt[b]])
            c.op("sp", lambda e, b=b, tb=tb, r0=r0: e.dma_start(out=S["v"][r0 + tb * 128:r0 + (tb + 1) * 128, :], in_=vt[b]), reads=[t_vt[b]], dma=True)


def attn_core_phase(c, src, dst, S):
    c.new_phase()
    oh_d = c.din("at_oh", [33, 3 * 128 * 128])
    rt_d = c.din("at_rel", [32, 16])
    sink_d = c.din("at_sink", [1, 16])
    wo_d = c.din("at_wo", [64, 16, D])
    bias = c.sb(3 * 16 * 128)
    t_bias = T()
    table = c.sb(16, parts=33)
    t_tab = T()
    c.op("dve", lambda e: e.memset(table[32:33, :], NEG), writes=[t_tab])
    c.op("sp", lambda e: e.dma_start(out=table[0:32, :], in_=rt_d), writes=[t_tab], dma=True)
    ohb = [c.sb(32 * 128, parts=33) for _ in range(2)]
    t_ohb = [T(), T()]
    P = c.psum
    ps_s, ps_o, ps_den, ps_y = P[0:2], P[2:4], P[4:6], P[6:8]
    t_pss, t_pso, t_psden, t_psy = [T(), T()], [T(), T()], [T(), T()], [T(), T()]
    nb = 0
    for kb in range(3):
        for q4 in range(4):
            ob = nb % 2
            pb = nb % 2
            nb += 1
            a0 = q4 * 32
            c.op("sp", lambda e, kb=kb, a0=a0, ob=ob: e.dma_start(out=ohb[ob], in_=oh_d[:, (kb * 128 + a0) * 128:(kb * 128 + a0 + 32) * 128]),
                 writes=[t_ohb[ob]], dma=True)
            for al in range(32):
                c.op("pe", lambda e, ob=ob, al=al, pb=pb: e.matmul(ps_s[pb][:, al * 16:(al + 1) * 16], ohb[ob][:, al * 128:(al + 1) * 128], table, start=True, stop=True),
                     reads=[t_ohb[ob], t_tab], writes=[t_pss[pb]])
            c.op("dve", lambda e, kb=kb, a0=a0, pb=pb: e.tensor_copy(
                out=bias[:, kb * 2048:(kb + 1) * 2048].rearrange("p (h a) -> p h a", h=16)[:, :, a0:a0 + 32],
                in_=ps_s[pb][:, :].rearrange("p (a h) -> p h a", h=16)),
                reads=[t_pss[pb]], writes=[t_bias])
    es16 = c.sb(16, parts=64)
    esink = c.sb(16 * 128, parts=64)
    t_es = T()
    c.op("sp", lambda e: e.dma_start(out=es16, in_=sink_d.broadcast_to([64, 16])), writes=[t_es], dma=True)
    c.op("act", lambda e: e.activation(out=es16, in_=es16, func=AF.Exp), reads=[t_es], writes=[t_es])
    c.op("dve", lambda e: e.tensor_copy(out=esink.rearrange("p (h a) -> p h a", h=16), in_=es16.unsqueeze(2).broadcast_to([64, 16, 128])), reads=[t_es], writes=[t_es])
    wo = c.sb(16 * D, F32R, parts=64)
    t_wo = T()
    c.op("pool", lambda e: e.dma_start(out=wo, in_=wo_d.rearrange("p h n -> p (h n)")), writes=[t_wo], dma=True)
    oneskv, t_ones = const_r(c, 64, 1.0, 128)
    q_sb = [c.sb(16 * 128, F32R, parts=64) for _ in range(2)]
    t_q = [T(), T()]
    RING = 4
    k_r = [c.sb(4 * 128, F32R, parts=64) for _ in range(RING)]
    t_kr = [T() for _ in range(RING)]
    v_r = [c.sb(256, F32R) for _ in range(RING)]
    t_vr = [T() for _ in range(RING)]
    xin = [c.sb(D) for _ in range(2)]
    t_xin = [T(), T()]
    tt = [c.sb(TT) for _ in range(2)]
    t_tt = [T(), T()]
    pT = [c.sb(TT, F32R) for _ in range(2)]
    t_pT = [T(), T()]
    den = [c.sb(TT, parts=64) for _ in range(2)]
    t_den = [T(), T()]
    oT = [[c.sb(TT, F32R, parts=64) for _ in range(4)] for _ in range(2)]
    t_oT = [[T() for _ in range(4)] for _ in range(2)]

    def prep_kv(i):
        s = i % RING
        c.op("pool", lambda e, s=s, i=i: e.dma_start(out=k_r[s].rearrange("p (g t) -> p g t", g=4), in_=S["kT3"][:, :, i * 128:(i + 1) * 128]), writes=[t_kr[s]], dma=True)
        c.op("pool", lambda e, s=s, i=i: e.dma_start(out=v_r[s], in_=S["v"][i * 128:(i + 1) * 128, :]), writes=[t_vr[s]], dma=True)

    prep_kv(0)
    cs = 0
    co = 0
    cy = 0
    for n in range(NBLK):
        if n + 1 < NBLK:
            prep_kv(n + 1)
        qb = n % 2
        c.op("pool", lambda e, qb=qb, n=n: e.dma_start(out=q_sb[qb].rearrange("p (h t) -> p h t", h=16), in_=S["qT3"][:, :, n * 128:(n + 1) * 128]), writes=[t_q[qb]], dma=True)
        c.op("sp", lambda e, qb=qb, n=n: e.dma_start(out=xin[qb], in_=src[n * 128:(n + 1) * 128, :]), writes=[t_xin[qb]], dma=True)
        kbs = [kb for kb in range(3) if 0 <= n + kb - 1 < NBLK]
        for g in range(4):
            ob = co % 2
            co += 1
            for ki, kb in enumerate(kbs):
                s = (n + kb - 1) % RING
                b = cs % 2
                cs += 1
                c.op("pe", lambda e, s=s, g=g, qb=qb, b=b: e.matmul(ps_s[b][:, :], k_r[s][:, g * 128:(g + 1) * 128], q_sb[qb][:, g * 512:(g + 1) * 512], start=True, stop=True),
                     reads=[t_kr[s], t_q[qb]], writes=[t_pss[b]])
                c.op("dve", lambda e, b=b, kb=kb, g=g: e.scalar_tensor_tensor(out=tt[b], in0=ps_s[b][:, :], scalar=HD ** -0.5, in1=bias[:, kb * 2048 + g * 512:kb * 2048 + (g + 1) * 512], op0=ALU.mult, op1=ALU.add),
                     reads=[t_pss[b], t_bias], writes=[t_tt[b]])
                c.op("act", lambda e, b=b: e.activation(out=pT[b], in_=tt[b], func=AF.Exp), reads=[t_tt[b]], writes=[t_pT[b]])
                c.op("pe", lambda e, s=s, g=g, b=b, ob=ob, ki=ki, nk=len(kbs): e.matmul(ps_o[ob][0:64, :], v_r[s][:, g * 64:(g + 1) * 64], pT[b], start=(ki == 0), stop=(ki == nk - 1)),
                     reads=[t_vr[s], t_pT[b]], writes=[t_pso[ob]])
                c.op("pe", lambda e, b=b, ob=ob, ki=ki, nk=len(kbs): e.matmul(ps_den[ob][0:64, :], oneskv, pT[b], start=(ki == 0), stop=(ki == nk - 1)),
                     reads=[t_ones, t_pT[b]], writes=[t_psden[ob]])
            c.op("dve", lambda e, ob=ob, g=g: e.tensor_tensor(out=den[ob], in0=ps_den[ob][0:64, :], in1=esink[:, g * 512:(g + 1) * 512], op=ALU.add),
                 reads=[t_psden[ob], t_es], writes=[t_den[ob]])
            c.op("dve", lambda e, ob=ob: e.reciprocal(out=den[ob], in_=den[ob]), reads=[t_den[ob]], writes=[t_den[ob]])
            c.op("dve", lambda e, ob=ob, g=g, qb=qb: e.tensor_tensor(out=oT[qb][g], in0=ps_o[ob][0:64, :], in1=den[ob], op=ALU.mult),
                 reads=[t_pso[ob], t_den[ob]], writes=[t_oT[qb][g]])
        for dh in range(2):
            yb = cy % 2
            cy += 1
            for h in range(NH):
                c.op("pe", lambda e, h=h, dh=dh, yb=yb, qb=qb: e.matmul(ps_y[yb][:, :], oT[qb][h // 4][:, (h % 4) * 128:(h % 4 + 1) * 128], wo[:, h * D + dh * 512:h * D + (dh + 1) * 512], start=(h == 0), stop=(h == NH - 1)),
                     reads=[t_oT[qb][h // 4], t_wo], writes=[t_psy[yb]])
            c.op("dve", lambda e, dh=dh, yb=yb, qb=qb: e.tensor_tensor(out=xin[qb][:, dh * 512:(dh + 1) * 512], in0=ps_y[yb][:, :], in1=xin[qb][:, dh * 512:(dh + 1) * 512], op=ALU.add),
                 reads=[t_psy[yb], t_xin[qb]], writes=[t_xin[qb]])
        c.op("sp", lambda e, qb=qb, n=n: e.dma_start(out=dst[n * 128:(n + 1) * 128, :], in_=xin[qb]), reads=[t_xin[qb]], dma=True)


def attn_scratch(c):
    qT = c.nc.dram_tensor("qT_s", [NH, 64, L], F32)
    kT = c.nc.dram_tensor("kT_s", [NKV, 64, L], F32)
    v = c.nc.dram_tensor("v_s", [L, 256], F32)
    return {"qT": [qT.ap()[h] for h in range(NH)], "kT": [kT.ap()[h] for h in range(NKV)], "v": v.ap(),
            "qT3": qT.ap().rearrange("h p t -> p h t"), "kT3": kT.ap().rearrange("h p t -> p h t")}


def attn_host(inputs):
    g = np.asarray(inputs["norm_mix"][1], np.float32)
    wo = np.asarray(inputs["at_w_o"][0], np.float32).reshape(16, 64, D).transpose(1, 0, 2)
    return {"at_g": np.ascontiguousarray(g.reshape(8, 128).T),
            "at_wqkv": np.ascontiguousarray(np.asarray(inputs["at_w_qkv"][0], np.float32).reshape(8, 128, 1536)),
            "at_qg": np.asarray(inputs["at_q_gain"][0], np.float32).reshape(64, 1),
            "at_kg": np.asarray(inputs["at_k_gain"][0], np.float32).reshape(64, 1),
            "at_oh": attn_onehot(),
            "at_rel": np.asarray(inputs["rel_table"], np.float32),
            "at_sink": np.asarray(inputs["at_sink"][0], np.float32).reshape(1, 16),
            "at_wo": np.ascontiguousarray(wo)}


NFFT = 2 * L
HY_MIN_DECAY = math.log(1e-2) / 1.5
HY_MAX_DECAY = math.log(1e-2) / 0.3
_FA, _FC, _FS, _FSN, _GCS, _GSNC, _HAC, _HASN, _NCR = 0, 256, 384, 512, 640, 896, 1152, 1216, 1280


def hy_consts():
    i = np.arange(128, dtype=np.float64)
    th = 2 * np.pi * np.outer(i, i) / 128.0
    cr = np.zeros((128, _NCR), np.float64)
    cr[:, _FA:_FA + 128] = np.cos(th)
    cr[:, _FA + 128:_FA + 256] = -np.sin(th)
    cr[:, _FC:_FC + 128] = np.cos(th)
    cr[:, _FS:_FS + 128] = np.sin(th)
    cr[:, _FSN:_FSN + 128] = -np.sin(th)
    cr[:, _GCS:_GCS + 128] = np.cos(th)
    cr[:, _GCS + 128:_GCS + 256] = np.sin(th)
    cr[:, _GSNC:_GSNC + 128] = -np.sin(th)
    cr[:, _GSNC + 128:_GSNC + 256] = np.cos(th)
    cr[:, _HAC:_HAC + 64] = np.cos(th[:, :64])
    cr[:, _HASN:_HASN + 64] = -np.sin(th[:, :64])
    tw = 2 * np.pi * np.outer(i, i) / NFFT
    cf = np.concatenate([np.tile(np.cos(tw), (1, 4)), np.tile(np.sin(tw), (1, 4))], axis=1)
    n = np.arange(NFFT)
    pos = np.where(n < L, n, L - (n - L)).astype(np.float64)
    pos[L] = 0.0
    t = pos / (L - 1)
    f = np.linspace(1e-4, 15.0, 16)
    ang = (2 * np.pi / L) * pos[None, :] * f[:, None]
    z = np.concatenate([t[None, :], np.cos(ang), -np.sin(ang)], axis=0)
    tdec = t.copy()
    tdec[L] = 1.0e4
    deltas = np.abs(np.linspace(HY_MIN_DECAY, HY_MAX_DECAY, D))
    return {"hy_cr": cr.astype(np.float32), "hy_cf": cf.astype(np.float32), "hy_z": z.astype(np.float32),
            "hy_tdec": tdec.astype(np.float32).reshape(1, NFFT),
            "hy_negdelta": np.ascontiguousarray((-deltas).astype(np.float32).reshape(8, 128).T)}


class HyFFT:
    def __init__(self, c, inverse):
        self.c = c
        self.inverse = inverse
        cr_d = c.din("hy_cr", [128, _NCR])
        cf_d = c.din("hy_cf", [128, 1024])
        self.cr = c.sb(_NCR, F32R)
        self.cf = c.sb(1024)
        self.t_k = T()
        c.op("pool", lambda e: e.dma_start(out=self.cr, in_=cr_d), writes=[self.t_k], dma=True)
        self.t_k2 = T()
        c.op("sp", lambda e: e.dma_start(out=self.cf, in_=cf_d), writes=[self.t_k2], dma=True)
        self.C2 = self.cf[:, 0:512]
        self.S2 = self.cf[:, 512:1024]
        P = c.psum
        mk2 = lambda dt=F32: [c.sb(512, dt) for _ in range(2)]
        self.t1, self.t2 = mk2(), mk2()
        self.t_t1, self.t_t2 = [T(), T()], [T(), T()]
        self.Bre, self.Bim = mk2(F32R), mk2(F32R)
        self.t_B = [[T(), T()], [T(), T()]]
        self.ps_a, self.t_psa = P[0:2], [T(), T()]
        self.ps_x, self.t_psx = P[2:4], [T(), T()]
        if inverse:
            self.u1, self.u2 = mk2(), mk2()
            self.t_u1, self.t_u2 = [T(), T()], [T(), T()]
            self.m = [c.sb(512) for _ in range(4)]
            self.t_m = [T() for _ in range(4)]
            self.Zre, self.Zim = mk2(F32R), mk2(F32R)
            self.t_Z = [[T(), T()], [T(), T()]]
            self.Vre, self.Vim = mk2(F32R), mk2(F32R)
            self.t_V = [[T(), T()], [T(), T()]]
            self.ps_v, self.t_psv = P[4:6], [T(), T()]
            self.ps_y, self.t_psy = P[6], T()

    def _tw_mul(self, ps, t_ps, a1, a2, t_a1, t_a2):
        c = self.c
        for b in range(2):
            c.op("dve", lambda e, b=b: e.tensor_tensor(out=a1[b], in0=ps[b][:, :], in1=self.C2, op=ALU.mult), reads=[t_ps[b], self.t_k2], writes=[t_a1[b]])
            c.op("dve", lambda e, b=b: e.tensor_tensor(out=a2[b], in0=ps[b][:, :], in1=self.S2, op=ALU.mult), reads=[t_ps[b], self.t_k2], writes=[t_a2[b]])

    def _tw_comb(self, a1, a2, t_a1, t_a2, outre, outim, t_out, forward):
        c = self.c
        for b in range(2):
            v1 = a1[b].rearrange("p (s c k) -> p s c k", s=2, c=2)
            v2 = a2[b].rearrange("p (s c k) -> p s c k", s=2, c=2)
            ore = outre[:, b * 256:(b + 1) * 256].rearrange("p (s k) -> p s k", s=2)
            oim = outim[:, b * 256:(b + 1) * 256].rearrange("p (s k) -> p s k", s=2)
            op_re, op_im = (ALU.add, ALU.subtract) if forward else (ALU.subtract, ALU.add)
            c.op("pool", lambda e, v1=v1, v2=v2, ore=ore, op_re=op_re: e.tensor_tensor(out=ore, in0=v1[:, :, 0, :], in1=v2[:, :, 1, :], op=op_re),
                 reads=[t_a1[b], t_a2[b]], writes=[t_out[0]])
            c.op("pool", lambda e, v1=v1, v2=v2, oim=oim, op_im=op_im: e.tensor_tensor(out=oim, in0=v1[:, :, 1, :], in1=v2[:, :, 0, :], op=op_im),
                 reads=[t_a1[b], t_a2[b]], writes=[t_out[1]])

    def stage(self, st, g, J):
        c = self.c
        cr = self.cr
        par = g % 2
        c0 = g * 4
        if st == 0:
            ut, ka = J["ut"], J["ka"]
            for s in range(4):
                b = s // 2
                c.op("pe", lambda e, s=s, b=b, ut=ut, ka=ka, c0=c0: e.matmul(self.ps_a[b][:, (s % 2) * 256:(s % 2 + 1) * 256], ut[0:ka, (c0 + s) * 128:(c0 + s + 1) * 128], cr[0:ka, _FA:_FA + 256], start=True, stop=True),
                     reads=[J["t_ut"][g], self.t_k], writes=[self.t_psa[b]])
        elif st == 1:
            self._tw_mul(self.ps_a, self.t_psa, self.t1, self.t2, self.t_t1, self.t_t2)
        elif st == 2:
            self._tw_comb(self.t1, self.t2, self.t_t1, self.t_t2, self.Bre[par], self.Bim[par], self.t_B[par], True)
        elif st == 3:
            Bre, Bim = self.Bre[par], self.Bim[par]
            rB = [self.t_B[par][0], self.t_B[par][1], self.t_k]
            c.op("pe", lambda e, Bre=Bre: e.matmul(self.ps_x[0][:, :], cr[:, _FC:_FC + 128], Bre, start=True, stop=False), reads=rB, writes=[self.t_psx[0]])
            c.op("pe", lambda e, Bim=Bim: e.matmul(self.ps_x[0][:, :], cr[:, _FS:_FS + 128], Bim, start=False, stop=True), reads=rB, writes=[self.t_psx[0]])
            c.op("pe", lambda e, Bim=Bim: e.matmul(self.ps_x[1][:, :], cr[:, _FC:_FC + 128], Bim, start=True, stop=False), reads=rB, writes=[self.t_psx[1]])
            c.op("pe", lambda e, Bre=Bre: e.matmul(self.ps_x[1][:, :], cr[:, _FSN:_FSN + 128], Bre, start=False, stop=True), reads=rB, writes=[self.t_psx[1]])
        elif not self.inverse:
            if st == 4:
                J["spec_out"](g, self.ps_x, self.t_psx)
        elif st == 4:
            kt, t_kt = J["kt"](g)
            m, t_m = self.m, self.t_m
            xr, xi = self.ps_x[0], self.ps_x[1]
            c.op("dve", lambda e, kt=kt: e.tensor_tensor(out=m[0], in0=xr[:, :], in1=kt[:, 0:512], op=ALU.mult), reads=[self.t_psx[0], t_kt], writes=[t_m[0]])
            c.op("dve", lambda e, kt=kt: e.tensor_tensor(out=m[1], in0=xi[:, :], in1=kt[:, 512:1024], op=ALU.mult), reads=[self.t_psx[1], t_kt], writes=[t_m[1]])
            c.op("dve", lambda e, kt=kt: e.tensor_tensor(out=m[2], in0=xr[:, :], in1=kt[:, 512:1024], op=ALU.mult), reads=[self.t_psx[0], t_kt], writes=[t_m[2]])
            c.op("dve", lambda e, kt=kt: e.tensor_tensor(out=m[3], in0=xi[:, :], in1=kt[:, 0:512], op=ALU.mult), reads=[self.t_psx[1], t_kt], writes=[t_m[3]])
        elif st == 5:
            m, t_m = self.m, self.t_m
            Zre, Zim = self.Zre[par], self.Zim[par]
            c.op("pool", lambda e, Zre=Zre: e.tensor_tensor(out=Zre, in0=m[0], in1=m[1], op=ALU.subtract), reads=[t_m[0], t_m[1]], writes=[self.t_Z[par][0]])
            c.op("pool", lambda e, Zim=Zim: e.tensor_tensor(out=Zim, in0=m[2], in1=m[3], op=ALU.add), reads=[t_m[2], t_m[3]], writes=[self.t_Z[par][1]])
        elif st == 6:
            Zre, Zim = self.Zre[par], self.Zim[par]
            for s in range(4):
                b = s // 2
                reg = self.ps_v[b][:, (s % 2) * 256:(s % 2 + 1) * 256]
                c.op("pe", lambda e, s=s, reg=reg, Zre=Zre: e.matmul(reg, Zre[:, s * 128:(s + 1) * 128], cr[:, _GCS:_GCS + 256], start=True, stop=False),
                     reads=[self.t_Z[par][0], self.t_k], writes=[self.t_psv[b]])
                c.op("pe", lambda e, s=s, reg=reg, Zim=Zim: e.matmul(reg, Zim[:, s * 128:(s + 1) * 128], cr[:, _GSNC:_GSNC + 256], start=False, stop=True),
                     reads=[self.t_Z[par][1], self.t_k], writes=[self.t_psv[b]])
        elif st == 7:
            self._tw_mul(self.ps_v, self.t_psv, self.u1, self.u2, self.t_u1, self.t_u2)
        elif st == 8:
            self._tw_comb(self.u1, self.u2, self.t_u1, self.t_u2, self.Vre[par], self.Vim[par], self.t_V[par], False)
        elif st == 9:
            Vre, Vim = self.Vre[par], self.Vim[par]
            rV = [self.t_V[par][0], self.t_V[par][1], self.t_k]
            c.op("pe", lambda e, Vre=Vre: e.matmul(self.ps_y[0:64, :], cr[:, _HAC:_HAC + 64], Vre, start=True, stop=False), reads=rV, writes=[self.t_psy])
            c.op("pe", lambda e, Vim=Vim: e.matmul(self.ps_y[0:64, :], cr[:, _HASN:_HASN + 64], Vim, start=False, stop=True), reads=rV, writes=[self.t_psy])
        elif st == 10:
            yt = J["yt"]
            c.op("act", lambda e, c0=c0, yt=yt: e.activation(out=yt[0:64, c0 * 128:(c0 + 4) * 128], in_=self.ps_y[0:64, :], func=AF.Copy), reads=[self.t_psy], writes=[J["t_ut"][g]])

    def run(self, J, ngroups=32):
        nst = 11 if self.inverse else 5
        for t in range(ngroups + nst - 1):
            if "pre" in J:
                J["pre"](t)
            for st in range(nst - 1, -1, -1):
                g = t - st
                if 0 <= g < ngroups:
                    self.stage(st, g, J)


def to_time_major(c, src_ct, t_src, ut, t_ut, na, ps, t_ps):
    v = src_ct.rearrange("p (a r) -> p r a", r=128)
    u3 = ut[0:na, :].rearrange("p (c r) -> p c r", r=128)
    for r0 in range(0, 128, 4):
        b = (r0 // 4) % 2
        for j in range(4):
            c.op("pe", lambda e, r0=r0, j=j, b=b: e.transpose(out=ps[b][0:na, j * 128:(j + 1) * 128], in_=v[:, r0 + j, :], identity=c.ident),
                 reads=[t_src, c.t_const], writes=[t_ps[b]])
        c.op("act", lambda e, r0=r0, b=b: e.activation(out=u3[:, :, r0:r0 + 4], in_=ps[b][0:na, :].rearrange("p (r c) -> p c r", r=4), func=AF.Copy),
             reads=[t_ps[b]], writes=(t_ut if isinstance(t_ut, list) else [t_ut]))


def to_feature_major(c, yt, t_yt, dst_ct, t_dst, ps, t_ps):
    y3 = yt[0:64, :].rearrange("p (c r) -> p r c", r=128)
    d3 = dst_ct.rearrange("p (a r) -> p r a", r=128)
    for r0 in range(0, 128, 8):
        b = (r0 // 8) % 2
        for j in range(8):
            c.op("pe", lambda e, r0=r0, j=j, b=b: e.transpose(out=ps[b][:, j * 64:(j + 1) * 64], in_=y3[:, r0 + j, :], identity=c.ident[0:64, 0:64]),
                 reads=(t_yt if isinstance(t_yt, list) else [t_yt]) + [c.t_const], writes=[t_ps[b]])
        c.op("dve", lambda e, r0=r0, b=b: e.tensor_copy(out=d3[:, r0:r0 + 8, :], in_=ps[b][:, :].rearrange("p (r a) -> p r a", r=8)),
             reads=[t_ps[b]], writes=[t_dst])


def hy_scratch(c, tag):
    nc = c.nc
    return {"p": nc.dram_tensor("hy_p" + tag, [24, 128, L], F32).ap(),
            "z": nc.dram_tensor("hy_zz" + tag, [8, 128, L], F32).ap(),
            "h3": nc.dram_tensor("hy_h3" + tag, [64, NFFT], F32).ap(),
            "ks": nc.dram_tensor("hy_ks" + tag, [2, 8, 32, 128, 1024], F32).ap()}


def hy_inproj_phase(c, src, S, s):
    c.new_phase()
    g_d = c.din("hy_g%d" % s, [128, 8])
    w_d = c.din("hy_win%d" % s, [8, 128, 3 * D])
    gcol = c.sb(8)
    t_g = T()
    c.op("sp", lambda e: e.dma_start(out=gcol, in_=g_d), writes=[t_g], dma=True)
    win = [c.sb(3 * D, F32R) for _ in range(8)]
    t_w = [T() for _ in range(8)]
    for k in range(8):
        c.op("pool", lambda e, k=k: e.dma_start(out=win[k], in_=w_d[k]), writes=[t_w[k]], dma=True)
    fr = Front(c)
    stage = [c.sb(TT) for _ in range(4)]
    t_st = [T() for _ in range(4)]
    P = c.psum
    ps_t, t_pst = P[0:2], [T(), T()]
    ps_o, t_pso = P[2:6], [T() for _ in range(4)]
    cnt = 0
    for it in range(L // TT):
        r0 = it * TT
        fr.load_norm(src, r0)
        fr.transpose(gcol, t_g, ps_t, t_pst)
        for oc in range(24):
            b = cnt % 4
            cnt += 1
            for k in range(8):
                c.op("pe", lambda e, k=k, oc=oc, b=b: e.matmul(ps_o[b][:, :], win[k][:, oc * 128:(oc + 1) * 128], fr.hT[k], start=(k == 0), stop=(k == 7)),
                     reads=[t_w[k], fr.t_hT[k]], writes=[t_pso[b]])
            if oc % 2 == 0:
                c.op("act", lambda e, b=b: e.activation(out=stage[b], in_=ps_o[b][:, :], func=AF.Copy), reads=[t_pso[b]], writes=[t_st[b]])
            else:
                c.op("dve", lambda e, b=b: e.tensor_copy(out=stage[b], in_=ps_o[b][:, :]), reads=[t_pso[b]], writes=[t_st[b]])
            c.op("sp", lambda e, b=b, oc=oc, r0=r0: e.dma_start(out=S["p"][oc][:, r0:r0 + TT], in_=stage[b]), reads=[t_st[b]], dma=True)


def hy_conv3_phase(c, S, s):
    c.new_phase()
    cols_d = c.din("hy_c3cols%d" % s, [128, 24 * 5])
    cols = c.sb(120)
    t_c = T()
    c.op("sp", lambda e: e.dma_start(out=cols, in_=cols_d), writes=[t_c], dma=True)
    raw = [c.sb(L + 2) for _ in range(2)]
    t_raw = [T(), T()]
    out = [c.sb(L) for _ in range(2)]
    t_out = [T(), T()]
    for oc in range(24):
        b = oc % 2
        eng = "dve"
        k0 = oc * 5
        c.op("sp", lambda e, b=b, oc=oc: e.dma_start(out=raw[b][:, 1:L + 1], in_=S["p"][oc]), writes=[t_raw[b]], dma=True)
        c.op("act", lambda e, b=b, k0=k0: e.activation(out=raw[b][:, 1:L + 1], in_=raw[b][:, 1:L + 1], func=AF.Identity, bias=cols[:, k0:k0 + 1]),
             reads=[t_raw[b], t_c], writes=[t_raw[b]])
        c.op(eng, lambda e, b=b: e.memset(raw[b][:, 0:1], 0.0), reads=[t_raw[b]], writes=[t_raw[b]])
        c.op(eng, lambda e, b=b: e.memset(raw[b][:, L + 1:L + 2], 0.0), reads=[t_raw[b]], writes=[t_raw[b]])
        c.op(eng, lambda e, b=b, k0=k0: e.tensor_scalar(out=out[b], in0=raw[b][:, 0:L], scalar1=cols[:, k0 + 1:k0 + 2], scalar2=cols[:, k0 + 4:k0 + 5], op0=ALU.mult, op1=ALU.add),
             reads=[t_raw[b], t_c], writes=[t_out[b]])
        c.op(eng, lambda e, b=b, k0=k0: e.scalar_tensor_tensor(out=out[b], in0=raw[b][:, 1:L + 1], scalar=cols[:, k0 + 2:k0 + 3], in1=out[b], op0=ALU.mult, op1=ALU.add),
             reads=[t_raw[b], t_c, t_out[b]], writes=[t_out[b]])
        c.op(eng, lambda e, b=b, k0=k0: e.scalar_tensor_tensor(out=out[b], in0=raw[b][:, 2:L + 2], scalar=cols[:, k0 + 3:k0 + 4], in1=out[b], op0=ALU.mult, op1=ALU.add),
             reads=[t_raw[b], t_c, t_out[b]], writes=[t_out[b]])
        c.op("sp", lambda e, b=b, oc=oc: e.dma_start(out=S["p"][oc], in_=out[b]), reads=[t_out[b]], dma=True)


def hy_mlp_phase(c, S, s):
    c.new_phase()
    z_d = c.din("hy_z", [33, NFFT])
    w1_d = c.din("hy_w1_%d" % s, [33, 64])
    wi_d = c.din("hy_wi_%d" % s, [64, 128])
    cols_d = c.din("hy_mlpcols%d" % s, [64, 4])
    w1 = c.sb(64, F32R, parts=33)
    wi = c.sb(128, F32R, parts=64)
    cols = c.sb(4, parts=64)
    negpi = c.sb(1, parts=64)
    t_k = T()
    c.op("pool", lambda e: e.dma_start(out=w1, in_=w1_d), writes=[t_k], dma=True)
    t_k1 = T()
    c.op("pool", lambda e: e.dma_start(out=wi, in_=wi_d), writes=[t_k1], dma=True)
    t_k2 = T()
    c.op("sp", lambda e: e.dma_start(out=cols, in_=cols_d), writes=[t_k2], dma=True)
    c.op("dve", lambda e: e.memset(negpi, -math.pi), writes=[t_k2])
    zt = [c.sb(TT, F32R, parts=33) for _ in range(2)]
    t_zt = [T(), T()]
    arg = [c.sb(TT, parts=64) for _ in range(2)]
    t_arg = [T(), T()]
    kk = [c.sb(TT, parts=64) for _ in range(2)]
    t_kk = [T(), T()]
    MAGIC = 12582912.0
    hh = [c.sb(TT, F32R, parts=64) for _ in range(2)]
    t_hh = [T(), T()]
    h3 = [c.sb(TT, parts=64) for _ in range(2)]
    t_h3 = [T(), T()]
    P = c.psum
    ps, t_ps = P[0:2], [T(), T()]
    cnt = 0
    for it in range(NFFT // TT):
        n0 = it * TT
        zb = it % 2
        c.op("pool", lambda e, zb=zb, n0=n0: e.dma_start(out=zt[zb], in_=z_d[:, n0:n0 + TT]), writes=[t_zt[zb]], dma=True)
        for layer in range(3):
            b = cnt % 2
            cnt += 1
            if layer == 0:
                c.op("pe", lambda e, zb=zb, b=b: e.matmul(ps[b][0:64, :], w1, zt[zb], start=True, stop=True), reads=[t_k, t_zt[zb]], writes=[t_ps[b]])
            else:
                pb = (cnt - 2) % 2
                c.op("pe", lambda e, b=b, pb=pb, layer=layer: e.matmul(ps[b][0:64, :], wi[:, (layer - 1) * 64:layer * 64], hh[pb], start=True, stop=True),
                     reads=[t_k1, t_hh[pb]], writes=[t_ps[b]])
            c.op("dve", lambda e, b=b, layer=layer: e.tensor_scalar(out=arg[b], in0=ps[b][0:64, :], scalar1=cols[:, layer:layer + 1], scalar2=cols[:, 3:4], op0=ALU.add, op1=ALU.mult),
                 reads=[t_ps[b], t_k2], writes=[t_arg[b]])
            c.op("dve", lambda e, b=b: e.tensor_scalar(out=kk[b], in0=arg[b], scalar1=1.0 / (2 * math.pi), scalar2=MAGIC, op0=ALU.mult, op1=ALU.add),
                 reads=[t_arg[b]], writes=[t_kk[b]])
            c.op("dve", lambda e, b=b: e.tensor_scalar(out=kk[b], in0=kk[b], scalar1=-MAGIC, scalar2=2 * math.pi, op0=ALU.add, op1=ALU.mult),
                 reads=[t_kk[b]], writes=[t_kk[b]])
            c.op("dve", lambda e, b=b: e.tensor_tensor(out=arg[b], in0=arg[b], in1=kk[b], op=ALU.subtract),
                 reads=[t_arg[b], t_kk[b]], writes=[t_arg[b]])
            if layer < 2:
                c.op("act", lambda e, b=b: e.activation(out=hh[b], in_=arg[b], func=AF.Sin), reads=[t_arg[b]], writes=[t_hh[b]])
            else:
                c.op("act", lambda e, b=b, zb=zb: e.activation(out=h3[zb], in_=arg[b], func=AF.Sin), reads=[t_arg[b]], writes=[t_h3[zb]])
                c.op("sp", lambda e, zb=zb, n0=n0: e.dma_start(out=S["h3"][:, n0:n0 + TT], in_=h3[zb]), reads=[t_h3[zb]], dma=True)


def hy_filter_phase(c, S, s):
    c.new_phase()
    w3_d = c.din("hy_w3_%d" % s, [64, 4 * D])
    nd_d = c.din("hy_negdelta", [128, 8])
    td_d = c.din("hy_tdec", [1, NFFT])
    ff = HyFFT(c, inverse=False)
    w3 = c.sb(4 * D, F32R, parts=64)
    t_w3 = T()
    c.op("pool", lambda e: e.dma_start(out=w3, in_=w3_d), writes=[t_w3], dma=True)
    nd = c.sb(8)
    t_nd = T()
    c.op("sp", lambda e: e.dma_start(out=nd, in_=nd_d), writes=[t_nd], dma=True)
    kT = c.sb(NFFT)
    t_kT = T()
    ut = c.sb(NFFT, F32R)
    t_ut = T()
    h3t = [c.sb(TT, F32R, parts=64) for _ in range(2)]
    t_h3t = [T(), T()]
    tdb = [c.sb(TT) for _ in range(2)]
    t_tdb = [T(), T()]
    dec = [c.sb(TT) for _ in range(2)]
    t_dec = [T(), T()]
    junk = c.sb(TT)
    t_junk = T()
    asum = c.sb(40)
    t_as = T()
    kst = [c.sb(1024) for _ in range(2)]
    t_kst = [T(), T()]
    P = c.psum
    ps_k, t_psk = P[4:6], [T(), T()]
    ps_tr, t_pstr = P[6:8], [T(), T()]
    cnt = 0
    for cc in range(8):
        for o in range(2):
            c.op("dve", lambda e: e.memset(asum[:, 0:40], 0.0), writes=[t_as])
            for it in range(NFFT // TT):
                n0 = it * TT
                b = cnt % 2
                cnt += 1
                dirn = 0 if n0 < L else 1
                col = (dirn * 2 + o) * D + cc * 128
                c.op("pool", lambda e, b=b, n0=n0: e.dma_start(out=h3t[b], in_=S["h3"][:, n0:n0 + TT]), writes=[t_h3t[b]], dma=True)
                c.op("sp", lambda e, b=b, n0=n0: e.dma_start(out=tdb[b], in_=td_d[:, n0:n0 + TT].broadcast_to([128, TT])), writes=[t_tdb[b]], dma=True)
                c.op("pe", lambda e, b=b, col=col: e.matmul(ps_k[b][:, :], w3[:, col:col + 128], h3t[b], start=True, stop=True), reads=[t_w3, t_h3t[b]], writes=[t_psk[b]])
                c.op("act", lambda e, b=b, cc=cc: e.activation(out=dec[b], in_=tdb[b], func=AF.Exp, scale=nd[:, cc:cc + 1]), reads=[t_tdb[b], t_nd], writes=[t_dec[b]])
                c.op("dve", lambda e, b=b, n0=n0: e.tensor_tensor(out=kT[:, n0:n0 + TT], in0=ps_k[b][:, :], in1=dec[b], op=ALU.mult), reads=[t_psk[b], t_dec[b]], writes=[t_kT])
                c.op("act", lambda e, n0=n0, it=it: e.activation(out=junk, in_=kT[:, n0:n0 + TT], func=AF.Abs, accum_out=asum[:, it:it + 1]), reads=[t_kT], writes=[t_junk, t_as])
            c.op("dve", lambda e: e.reduce_sum(out=asum[:, 32:33], in_=asum[:, 0:32], axis=AX.X), reads=[t_as], writes=[t_as])
            c.op("dve", lambda e: e.reciprocal(out=asum[:, 33:34], in_=asum[:, 32:33]), reads=[t_as], writes=[t_as])
            c.op("dve", lambda e: e.tensor_scalar(out=asum[:, 33:34], in0=asum[:, 33:34], scalar1=1.0 / NFFT, scalar2=0.0, op0=ALU.mult, op1=ALU.add), reads=[t_as], writes=[t_as])
            for q in range(4):
                eng = "dve" if q % 2 == 0 else "pool"
                c.op(eng, lambda e, q=q: e.tensor_scalar(out=kT[:, q * 4096:(q + 1) * 4096], in0=kT[:, q * 4096:(q + 1) * 4096], scalar1=asum[:, 33:34], scalar2=0.0, op0=ALU.mult, op1=ALU.add),
                     reads=[t_kT, t_as], writes=[t_kT])
            to_time_major(c, kT, t_kT, ut, t_ut, 128, ps_tr, t_pstr)
            def spec_out(g, ps_x, t_psx, o=o, cc=cc):
                kb = g % 2
                c.op("act", lambda e, kb=kb: e.activation(out=kst[kb][:, 0:512], in_=ps_x[0][:, :], func=AF.Copy), reads=[t_psx[0]], writes=[t_kst[kb]])
                c.op("act", lambda e, kb=kb: e.activation(out=kst[kb][:, 512:1024], in_=ps_x[1][:, :], func=AF.Copy), reads=[t_psx[1]], writes=[t_kst[kb]])
                c.op("sp", lambda e, kb=kb, g=g: e.dma_start(out=S["ks"][o, cc, g], in_=kst[kb]), reads=[t_kst[kb]], dma=True)
            ff.run({"ut": ut, "t_ut": [t_ut] * 32, "ka": 128, "spec_out": spec_out})


def hy_conv_phase(c, S, s):
    c.new_phase()
    fb_d = c.din("hy_fbias%d" % s, [128, 16])
    ff = HyFFT(c, inverse=True)
    fb = c.sb(16)
    t_fb = T()
    c.op("sp", lambda e: e.dma_start(out=fb, in_=fb_d), writes=[t_fb], dma=True)
    bufA = c.sb(L)
    bufB = c.sb(L)
    t_A, t_B = T(), T()
    off_ut = (c.off + 7) // 8 * 8
    ut = c.sb(64 * 256, F32R)
    yt = c.sb_at(off_ut, 64 * 256, F32)
    t_utg = [T() for _ in range(32)]
    PC = 512
    xp = [c.sb(PC) for _ in range(2)]
    t_xp = [T(), T()]
    kt = [c.sb(1024) for _ in range(3)]
    t_kt = [T(), T(), T()]
    P = c.psum
    ps_tr, t_pstr = [P[6], P[7]], [T(), T()]
    t_pstr[0] = ff.t_psy
    npc = 0
    nk = 0
    for cc in range(8):
        c.op("sp", lambda e, cc=cc: e.dma_start(out=bufA, in_=S["p"][16 + cc]), writes=[t_A], dma=True)
        src, t_src, dstb, t_dst = bufA, t_A, bufB, t_B
        for o in range(2):
            to_time_major(c, src, t_src, ut, t_utg, 64, ps_tr, t_pstr)
            ktmap = {}

            def pre(t, o=o, cc=cc, ktmap=ktmap):
                g = t - 2
                if 0 <= g < 32:
                    kb = g % 3
                    c.op("sp", lambda e, kb=kb, g=g: e.dma_start(out=kt[kb], in_=S["ks"][o, cc, g]), writes=[t_kt[kb]], dma=True)
                    ktmap[g] = (kt[kb], t_kt[kb])
            ff.run({"ut": ut, "t_ut": t_utg, "ka": 64, "yt": yt, "kt": lambda g, ktmap=ktmap: ktmap[g], "pre": pre})
            to_feature_major(c, yt, t_utg, dstb, t_dst, ps_tr, t_pstr)
            gate_chunk = (0 if o == 0 else 8) + cc
            for pc in range(L // PC):
                pb = npc % 2
                npc += 1
                sl = slice(pc * PC, (pc + 1) * PC)
                c.op("sp", lambda e, pb=pb, gate_chunk=gate_chunk, sl=sl: e.dma_start(out=xp[pb], in_=S["p"][gate_chunk][:, sl]), writes=[t_xp[pb]], dma=True)
                eng = "pool"
                c.op("dve", lambda e, sl=sl, o=o, cc=cc, src=src, dstb=dstb: e.scalar_tensor_tensor(out=dstb[:, sl], in0=src[:, sl], scalar=fb[:, o * 8 + cc:o * 8 + cc + 1], in1=dstb[:, sl], op0=ALU.mult, op1=ALU.add),
                     reads=[t_src, t_dst, t_fb], writes=[t_dst])
                c.op(eng, lambda e, sl=sl, pb=pb, dstb=dstb: e.tensor_tensor(out=dstb[:, sl], in0=dstb[:, sl], in1=xp[pb], op=ALU.mult),
                     reads=[t_dst, t_xp[pb]], writes=[t_dst])
            src, t_src, dstb, t_dst = dstb, t_dst, src, t_src
        c.op("sp", lambda e, cc=cc, src=src: e.dma_start(out=S["z"][cc], in_=src), reads=[t_src], dma=True)


def hy_outproj_phase(c, src, dst, S, s):
    c.new_phase()
    w_d = c.din("hy_wout%d" % s, [8, 128, D])
    b_d = c.din("hy_bout%d" % s, [1, D])
    wo = [c.sb(D, F32R) for _ in range(8)]
    t_wo = [T() for _ in range(8)]
    for k in range(8):
        c.op("pool", lambda e, k=k: e.dma_start(out=wo[k], in_=w_d[k]), writes=[t_wo[k]], dma=True)
    brow = c.sb(D)
    t_b = T()
    c.op("sp", lambda e: e.dma_start(out=brow, in_=b_d.broadcast_to([128, D])), writes=[t_b], dma=True)
    zT = [[c.sb(TT, F32R) for _ in range(8)] for _ in range(2)]
    t_zT = [[T() for _ in range(8)] for _ in range(2)]
    xin = [c.sb(D) for _ in range(4)]
    t_xin = [T() for _ in range(4)]
    P = c.psum
    ps, t_ps = P[0:4], [T() for _ in range(4)]
    cnt = 0
    for it in range(L // TT):
        r0 = it * TT
        zb = it % 2
        for k in range(8):
            c.op("pool", lambda e, k=k, zb=zb, r0=r0: e.dma_start(out=zT[zb][k], in_=S["z"][k][:, r0:r0 + TT]), writes=[t_zT[zb][k]], dma=True)
        for tb in range(4):
            xb = tb
            c.op("sp", lambda e, xb=xb, tb=tb, r0=r0: e.dma_start(out=xin[xb], in_=src[r0 + tb * 128:r0 + (tb + 1) * 128, :]), writes=[t_xin[xb]], dma=True)
            for dh in range(2):
                b = cnt % 4
                cnt += 1
                sl = slice(dh * 512, (dh + 1) * 512)
                for k in range(8):
                    c.op("pe", lambda e, k=k, zb=zb, tb=tb, sl=sl, b=b: e.matmul(ps[b][:, :], zT[zb][k][:, tb * 128:(tb + 1) * 128], wo[k][:, sl], start=(k == 0), stop=(k == 7)),
                         reads=[t_zT[zb][k], t_wo[k]], writes=[t_ps[b]])
                c.op("dve", lambda e, xb=xb, sl=sl, b=b: e.tensor_tensor(out=xin[xb][:, sl], in0=ps[b][:, :], in1=xin[xb][:, sl], op=ALU.add),
                     reads=[t_ps[b], t_xin[xb]], writes=[t_xin[xb]])
                c.op("pool", lambda e, xb=xb, sl=sl: e.tensor_tensor(out=xin[xb][:, sl], in0=xin[xb][:, sl], in1=brow[:, sl], op=ALU.add),
                     reads=[t_xin[xb], t_b], writes=[t_xin[xb]])
            c.op("sp", lambda e, xb=xb, tb=tb, r0=r0: e.dma_start(out=dst[r0 + tb * 128:r0 + (tb + 1) * 128, :], in_=xin[xb]), reads=[t_xin[xb]], dma=True)


def hyena_layer(c, src, dst, s, sub=None):
    if not hasattr(c, "hyS"):
        c.hyS = hy_scratch(c, "")
    S = c.hyS
    hy_inproj_phase(c, src, S, s)
    hy_conv3_phase(c, S, s)
    hy_mlp_phase(c, S, s)
    hy_filter_phase(c, S, s)
    hy_conv_phase(c, S, s)
    hy_outproj_phase(c, src, dst, S, s)
    return S


def hyena_host(inputs, s, li):
    g = np.asarray(inputs["norm_mix"][li], np.float32)
    b_in = np.asarray(inputs["hy_b_in"][s], np.float32)
    cw = np.asarray(inputs["hy_conv_w"][s], np.float32)
    cb = np.asarray(inputs["hy_conv_b"][s], np.float32)
    c3 = np.stack([b_in, cw[0], cw[1], cw[2], cb], axis=-1).reshape(24, 128, 5).transpose(1, 0, 2).reshape(128, 120)
    mlpc = np.stack([np.asarray(inputs["hy_f_b1"][s], np.float32), np.asarray(inputs["hy_f_bi"][s][0], np.float32),
                     np.asarray(inputs["hy_f_bi"][s][1], np.float32), np.asarray(inputs["hy_f_freq"][s], np.float32)], axis=-1)
    wi = np.asarray(inputs["hy_f_wi"][s], np.float32)
    fbias = np.asarray(inputs["hy_f_bias"][s], np.float32).reshape(2, 8, 128).transpose(2, 0, 1).reshape(128, 16)
    m = {"hy_g%d" % s: np.ascontiguousarray(g.reshape(8, 128).T),
         "hy_win%d" % s: np.ascontiguousarray(np.asarray(inputs["hy_w_in"][s], np.float32).reshape(8, 128, 3 * D)),
         "hy_c3cols%d" % s: np.ascontiguousarray(c3),
         "hy_w1_%d" % s: np.asarray(inputs["hy_f_w1"][s], np.float32),
         "hy_wi_%d" % s: np.ascontiguousarray(np.concatenate([wi[0], wi[1]], axis=1)),
         "hy_mlpcols%d" % s: np.ascontiguousarray(mlpc),
         "hy_w3_%d" % s: np.asarray(inputs["hy_f_w3"][s], np.float32),
         "hy_fbias%d" % s: np.ascontiguousarray(fbias),
         "hy_wout%d" % s: np.ascontiguousarray(np.asarray(inputs["hy_w_out"][s], np.float32).reshape(8, 128, D)),
         "hy_bout%d" % s: np.asarray(inputs["hy_b_out"][s], np.float32).reshape(1, D)}
    m.update(hy_consts())
    return m


def build_program(plan):
    nc = bass.Bass("TRN2", target_bir_lowering=False)
    st = contextlib.ExitStack()
    with st:
        c = Ctx(nc, st)
        x_in = c.din("x", [L, D])
        y_out = nc.dram_tensor("y", [L, D], F32, kind="ExternalOutput").ap()
        bufs = [c.dscratch("xa", [L, D]), c.dscratch("xb", [L, D])]
        load_consts(c)
        cur = x_in
        for pi, ph in enumerate(plan):
            last = pi == len(plan) - 1
            dst = y_out if last else bufs[pi % 2]
            if ph.startswith("ffn"):
                ffn_phase(c, cur, dst, int(ph[3:]))
            elif ph == "pool2":
                pool_phase(c, cur, dst)
            elif ph in ("hyena0", "hyena3"):
                hyena_layer(c, cur, dst, 0 if ph == "hyena0" else 1)
                if DEBUG_HY:
                    c.new_phase()
                    S = c.hyS
                    dp = nc.dram_tensor("dbg_p", [3, 128, L], F32, kind="ExternalOutput").ap()
                    dh = nc.dram_tensor("dbg_h3", [64, NFFT], F32, kind="ExternalOutput").ap()
                    dk = nc.dram_tensor("dbg_ks", [2, 128, 1024], F32, kind="ExternalOutput").ap()
                    dz = nc.dram_tensor("dbg_z", [128, L], F32, kind="ExternalOutput").ap()
                    for i, oc in enumerate((0, 8, 16)):
                        c.op("sp", lambda e, i=i, oc=oc: e.dma_start(out=dp[i], in_=S["p"][oc]), dma=True)
                    c.op("sp", lambda e: e.dma_start(out=dh, in_=S["h3"]), dma=True)
                    for o in range(2):
                        c.op("sp", lambda e, o=o: e.dma_start(out=dk[o], in_=S["ks"][o, 0, 0]), dma=True)
                    c.op("sp", lambda e: e.dma_start(out=dz, in_=S["z"][0]), dma=True)
            elif ph == "attn1":
                S = attn_scratch(c)
                attn_qkv_phase(c, cur, S)
                attn_core_phase(c, cur, dst, S)
            else:
                raise ValueError(ph)
            cur = dst
        c.mk.emit()
    return nc


def host_inputs(inputs, plan):
    m = {"ident": np.eye(128, dtype=np.float32)}
    for ph in plan:
        if ph.startswith("ffn"):
            m.update(ffn_host(inputs, int(ph[3:])))
        elif ph == "pool2":
            m.update(pool_host(inputs))
        elif ph == "attn1":
            m.update(attn_host(inputs))
        elif ph == "hyena0":
            m.update(hyena_host(inputs, 0, 0))
        elif ph == "hyena3":
            m.update(hyena_host(inputs, 1, 3))
    return m


DEBUG_HY = False
FULL_PLAN = ["hyena0", "ffn0", "attn1", "ffn1", "pool2", "ffn2", "hyena3", "ffn3"]


def run_plan(inputs, plan, x_override=None, trace=False):
    nc = build_program(plan)
    shared = host_inputs(inputs, plan)
    x = np.asarray(inputs["x"], np.float32) if x_override is None else x_override
    in_maps = []
    for core in range(8):
        m = dict(shared)
        m["x"] = np.ascontiguousarray(x[core % x.shape[0]])
        in_maps.append(m)
    res = run_bass_kernel_spmd(nc, in_maps, core_ids=list(range(8)), trace=trace)
    out = np.stack([res.results[b]["y"] for b in range(x.shape[0])], axis=0)
    return out, res


def kernel(**inputs):
    out, _ = run_plan(inputs, FULL_PLAN)
    return out.astype(np.float32)
```

```python
import contextlib
import math

import numpy as np
import concourse.bass as bass
import concourse.mybir as mybir
from concourse.bass_utils import run_bass_kernel_spmd

F32 = mybir.dt.float32
F32R = mybir.dt.float32r
AF = mybir.ActivationFunctionType
ALU = mybir.AluOpType
AX = mybir.AxisListType

D = 1024
L = 8192
DFF = 2816
NF = DFF // 128
EPS = 1e-6
TT = 512
NBLK = L // 128
NSLOT = 20
SAME_ENGINE_SYNC = True


class T:
    __slots__ = ("w", "r")

    def __init__(self):
        self.w = []
        self.r = []


class Op:
    __slots__ = ("eng", "idx", "fn", "deps", "dma", "dj", "sig", "waited", "q", "inc")

    def __init__(self, eng, idx, fn, dma):
        self.eng = eng
        self.idx = idx
        self.fn = fn
        self.deps = ()
        self.dma = dma
        self.q = eng
        self.inc = 16
        self.dj = None
        self.sig = None
        self.waited = False


class MK:
    ENGS = ("pe", "act", "dve", "pool", "sp")

    def __init__(self, nc):
        self.nc = nc
        self.ops = {e: [] for e in self.ENGS}
        self.dma_ops = {e: [] for e in self.ENGS + ("cc",)}
        self.last_c = {e: None for e in self.ENGS}
        self.bar = None
        self.bar_seen = {e: True for e in self.ENGS}

    def barrier(self):
        deps = set()
        for e in self.ENGS:
            if self.last_c[e] is not None:
                deps.add(self.last_c[e])
            for o in self.dma_ops[e][-NSLOT:]:
                deps.add(o)
        for o in self.dma_ops["cc"][-NSLOT:]:
            deps.add(o)
        self.bar = deps
        self.bar_seen = {e: False for e in self.ENGS}

    def op(self, eng, fn, reads=(), writes=(), dma=False, cc=False):
        lst = self.ops[eng]
        dma = dma or cc
        o = Op(eng, len(lst), fn, dma)
        if cc:
            o.q = "cc"
            o.inc = 1
        lst.append(o)
        deps = set()
        if not self.bar_seen[eng]:
            self.bar_seen[eng] = True
            deps |= self.bar
        for t in reads:
            deps.update(t.w)
        for t in writes:
            deps.update(t.w)
            deps.update(t.r)
        if dma:
            dl = self.dma_ops[o.q]
            o.dj = len(dl)
            dl.append(o)
            if o.dj >= NSLOT:
                deps.add(dl[o.dj - NSLOT])
        else:
            self.last_c[eng] = o
        deps.discard(o)
        o.deps = deps
        for t in reads:
            if dma:
                t.r.append(o)
            else:
                t.r = [x for x in t.r if x.dma or x.eng != eng] + [o]
        for t in writes:
            t.w = [o]
            t.r = []
        return o

    @staticmethod
    def _skip(d, o):
        return (not d.dma) and d.eng == o.eng and (not o.dma) and (d.eng == "pe" or not SAME_ENGINE_SYNC)

    def emit(self):
        nc = self.nc
        for e in self.ENGS:
            for o in self.ops[e]:
                for d in o.deps:
                    if d.dma or self._skip(d, o):
                        continue
                    d.waited = True
        for e in self.ENGS:
            c = 0
            for o in self.ops[e]:
                if not o.dma and o.waited:
                    c += 1
                    o.sig = c
        with contextlib.ExitStack() as st:
            csem = {e: st.enter_context(nc.semaphore("c_" + e)) for e in ("pe", "act", "dve", "pool")}
            dsem = {}
            for q in self.ENGS + ("cc",):
                n = len(self.dma_ops[q])
                if n:
                    dsem[q] = [st.enter_context(nc.semaphore("d_%s_%d" % (q, i))) for i in range(min(NSLOT, n))]
            block = st.enter_context(nc.Block())
            mk = self

            def run(ename):
                def body(e):
                    known_c = {}
                    known_d = set()
                    for o in mk.ops[ename]:
                        cw = {}
                        for d in o.deps:
                            if d.dma:
                                key = (d.q, d.dj)
                                if key in known_d:
                                    continue
                                known_d.add(key)
                                e.wait_ge(dsem[d.q][d.dj % NSLOT], d.inc * (d.dj // NSLOT + 1))
                            else:
                                if mk._skip(d, o):
                                    continue
                                if known_c.get(d.eng, 0) >= d.sig:
                                    continue
                                cw[d.eng] = max(cw.get(d.eng, 0), d.sig)
                        for en, v in cw.items():
                            known_c[en] = v
                            e.wait_ge(csem[en], v)
                        ins = o.fn(e)
                        if o.dma:
                            ins.then_inc(dsem[o.q][o.dj % NSLOT], o.inc)
                        elif o.sig is not None:
                            ins.then_inc(csem[ename], 1)
                    tail = list(mk.dma_ops[ename][-NSLOT:])
                    if ename == "pool":
                        tail += mk.dma_ops["cc"][-NSLOT:]
                    for o in tail:
                        if (o.q, o.dj) not in known_d:
                            e.wait_ge(dsem[o.q][o.dj % NSLOT], o.inc * (o.dj // NSLOT + 1))
                return body

            if self.ops["sp"]:
                block.sync(run("sp"))
            if self.ops["pe"]:
                block.tensor(run("pe"))
            if self.ops["act"]:
                block.scalar(run("act"))
            if self.ops["dve"]:
                block.vector(run("dve"))
            if self.ops["pool"]:
                block.gpsimd(run("pool"))


ARENA = 51 * 1024


class Ctx:
    def __init__(self, nc, st):
        self.nc = nc
        self.mk = MK(nc)
        self.sb_base = 16512
        self.ntens = 0
        self.psum = [st.enter_context(nc.psum_tensor("psb%d" % i, [128, 512], F32)) for i in range(8)]
        self.off = 0
        self.base = 0
        self.dram = {}

    def din(self, name, shape, dt=F32):
        if name in self.dram:
            return self.dram[name].ap()
        t = self.nc.dram_tensor(name, list(shape), dt, kind="ExternalInput")
        self.dram[name] = t
        return t.ap()

    def dscratch(self, name, shape, dt=F32):
        t = self.nc.dram_tensor(name, list(shape), dt)
        return t.ap()

    def sb(self, cols, dt=F32, parts=128):
        self.off = (self.off + 7) // 8 * 8
        self.ntens += 1
        t = self.nc.alloc_sbuf_tensor_at("t%d" % self.ntens, [parts, cols], dt, offset=self.sb_base + 4 * self.off)
        self.off += cols
        assert self.off <= ARENA, ("arena overflow", self.off)
        return t[:, :]

    def sb_at(self, off, cols, dt=F32, parts=128):
        self.ntens += 1
        t = self.nc.alloc_sbuf_tensor_at("t%d" % self.ntens, [parts, cols], dt, offset=self.sb_base + 4 * off)
        return t[:, :]

    def new_phase(self):
        self.mk.barrier()
        self.off = self.base

    def op(self, *a, **k):
        return self.mk.op(*a, **k)


def load_consts(c):
    c.ident = c.sb(128)
    c.t_const = T()
    ident_d = c.din("ident", [128, 128])
    c.op("sp", lambda e: e.dma_start(out=c.ident, in_=ident_d), writes=[c.t_const], dma=True)
    c.epsc = c.sb(1)
    c.op("dve", lambda e: e.memset(c.epsc, EPS), writes=[c.t_const])
    c.base = c.off


class Front:
    def __init__(self, c, want_hT=True, xn_dt=F32):
        self.c = c
        self.xin = [c.sb(D) for _ in range(4)]
        self.t_xin = [T() for _ in range(4)]
        self.xn = [c.sb(D, xn_dt) for _ in range(4)]
        self.t_xn = [T() for _ in range(4)]
        self.ss = c.sb(4)
        self.t_ss = [T() for _ in range(4)]
        self.rstd = c.sb(4)
        self.t_rstd = [T() for _ in range(4)]
        if want_hT:
            self.hT = [c.sb(TT, F32R) for _ in range(8)]
            self.t_hT = [T() for _ in range(8)]
        self.tcount = 0

    def load_norm(self, src, r0, blocks=(0, 1, 2, 3), rows=None):
        c = self.c
        if rows is None:
            rows = [src[r0 + tb * 128:r0 + (tb + 1) * 128, :] for tb in range(4)]
        for tb in blocks:
            c.op("sp", lambda e, tb=tb, ap=rows[tb]: e.dma_start(out=self.xin[tb], in_=ap),
                 writes=[self.t_xin[tb]], dma=True)
        for tb in blocks:
            c.op("dve", lambda e, tb=tb: e.scalar_tensor_tensor(out=self.xn[tb], in0=self.xin[tb], scalar=1.0, in1=self.xin[tb], op0=ALU.mult, op1=ALU.mult, accum_out=self.ss[:, tb:tb + 1]),
                 reads=[self.t_xin[tb]], writes=[self.t_xn[tb], self.t_ss[tb]])
        for tb in blocks:
            c.op("act", lambda e, tb=tb: e.activation(out=self.rstd[:, tb:tb + 1], in_=self.ss[:, tb:tb + 1], func=AF.Sqrt, scale=1.0 / D, bias=c.epsc),
                 reads=[self.t_ss[tb], c.t_const], writes=[self.t_rstd[tb]])
        for tb in blocks:
            c.op("dve", lambda e, tb=tb: e.reciprocal(out=self.rstd[:, tb:tb + 1], in_=self.rstd[:, tb:tb + 1]),
                 reads=[self.t_rstd[tb]], writes=[self.t_rstd[tb]])
            c.op("dve", lambda e, tb=tb: e.tensor_scalar(out=self.xn[tb], in0=self.xin[tb], scalar1=self.rstd[:, tb:tb + 1], scalar2=0.0, op0=ALU.mult, op1=ALU.add),
                 reads=[self.t_xin[tb], self.t_rstd[tb]], writes=[self.t_xn[tb]])

    def transpose(self, gcol, t_g, pbanks, t_pb):
        c = self.c
        for k in range(8):
            b = self.tcount % len(pbanks)
            self.tcount += 1
            for tb in range(4):
                c.op("pe", lambda e, tb=tb, k=k, b=b: e.transpose(out=pbanks[b][:, tb * 128:(tb + 1) * 128], in_=self.xn[tb][:, k * 128:(k + 1) * 128], identity=c.ident),
                     reads=[self.t_xn[tb], c.t_const], writes=[t_pb[b]])
            c.op("act", lambda e, k=k, b=b: e.activation(out=self.hT[k], in_=pbanks[b][:, :], func=AF.Copy, scale=gcol[:, k:k + 1]),
                 reads=[t_pb[b], t_g], writes=[self.t_hT[k]])


def ffn_phase(c, src, dst, li, ntok=L):
    c.new_phase()
    g_d = c.din("ffn_g%d" % li, [128, 8])
    wgu_d = c.din("ffn_wgu%d" % li, [NF, 128, 2048])
    wd_d = c.din("ffn_wd%d" % li, [NF, 128, D])
    gcol = c.sb(8)
    t_g = T()
    c.op("sp", lambda e: e.dma_start(out=gcol, in_=g_d), writes=[t_g], dma=True)
    fr = Front(c)
    aT = [c.sb(TT, F32R) for _ in range(NF)]
    t_aT = [T() for _ in range(NF)]
    wd = [c.sb(D, F32R) for _ in range(NF)]
    t_wd = [T() for _ in range(NF)]
    wgu = [c.sb(2048, F32R) for _ in range(2)]
    t_wgu = [T() for _ in range(2)]
    sg = [c.sb(TT) for _ in range(2)]
    t_sg = [T() for _ in range(2)]
    P = c.psum
    ps_t, ps_g, ps_u, ps_d = P[0:2], P[2:4], P[4:6], P[6:8]
    t_pst = [T(), T()]
    t_psg = [T(), T()]
    t_psu = [T(), T()]
    t_psd = [T(), T()]
    cnt = {"gu": 0, "d": 0}
    for it in range(ntok // TT):
        r0 = it * TT
        fr.load_norm(src, r0)
        fr.transpose(gcol, t_g, ps_t, t_pst)
        hT, t_hT = fr.hT, fr.t_hT
        for f in range(NF):
            wb = f % 2
            c.op("pool", lambda e, f=f, wb=wb: e.dma_start(out=wgu[wb], in_=wgu_d[f]), writes=[t_wgu[wb]], dma=True)
            c.op("pool", lambda e, f=f: e.dma_start(out=wd[f], in_=wd_d[f]), writes=[t_wd[f]], dma=True)
            b = cnt["gu"] % 2
            cnt["gu"] += 1
            for k in range(8):
                c.op("pe", lambda e, k=k, wb=wb, b=b: e.matmul(ps_g[b][:, :], wgu[wb][:, k * 256:k * 256 + 128], hT[k], start=(k == 0), stop=(k == 7)),
                     reads=[t_wgu[wb], t_hT[k]], writes=[t_psg[b]])
            for k in range(8):
                c.op("pe", lambda e, k=k, wb=wb, b=b: e.matmul(ps_u[b][:, :], wgu[wb][:, k * 256 + 128:k * 256 + 256], hT[k], start=(k == 0), stop=(k == 7)),
                     reads=[t_wgu[wb], t_hT[k]], writes=[t_psu[b]])
            c.op("act", lambda e, b=b: e.activation(out=sg[b], in_=ps_g[b][:, :], func=AF.Silu), reads=[t_psg[b]], writes=[t_sg[b]])
            c.op("dve", lambda e, b=b, f=f: e.tensor_tensor(out=aT[f], in0=sg[b], in1=ps_u[b][:, :], op=ALU.mult),
                 reads=[t_sg[b], t_psu[b]], writes=[t_aT[f]])
        for tb in range(4):
            for dh in range(2):
                b = cnt["d"] % 2
                cnt["d"] += 1
                for f in range(NF):
                    c.op("pe", lambda e, f=f, tb=tb, dh=dh, b=b: e.matmul(ps_d[b][:, :], aT[f][:, tb * 128:(tb + 1) * 128], wd[f][:, dh * 512:(dh + 1) * 512], start=(f == 0), stop=(f == NF - 1)),
                         reads=[t_aT[f], t_wd[f]], writes=[t_psd[b]])
                c.op("dve", lambda e, tb=tb, dh=dh, b=b: e.tensor_tensor(out=fr.xin[tb][:, dh * 512:(dh + 1) * 512], in0=ps_d[b][:, :], in1=fr.xin[tb][:, dh * 512:(dh + 1) * 512], op=ALU.add),
                     reads=[t_psd[b], fr.t_xin[tb]], writes=[fr.t_xin[tb]])
            c.op("sp", lambda e, tb=tb, r0=r0: e.dma_start(out=dst[r0 + tb * 128:r0 + (tb + 1) * 128, :], in_=fr.xin[tb]), reads=[fr.t_xin[tb]], dma=True)


def ffn_host(inputs, li):
    wg = np.asarray(inputs["ff_w_gate"][li], np.float32)
    wu = np.asarray(inputs["ff_w_up"][li], np.float32)
    wdn = np.asarray(inputs["ff_w_down"][li], np.float32)
    g = np.asarray(inputs["norm_ffn"][li], np.float32)
    wgu = np.stack([wg.reshape(8, 128, NF, 128), wu.reshape(8, 128, NF, 128)], axis=0)
    wgu = np.ascontiguousarray(wgu.transpose(3, 2, 1, 0, 4)).reshape(NF, 128, 2048)
    return {"ffn_g%d" % li: np.ascontiguousarray(g.reshape(8, 128).T),
            "ffn_wgu%d" % li: wgu,
            "ffn_wd%d" % li: np.ascontiguousarray(wdn.reshape(NF, 128, D))}


POOL_WINDOWS = (2, 4, 8, 16)


NTL = L // 2
NBL = NTL // 128


def pool_consts(h):
    mats = np.zeros((4, 9, 128, 128), np.float32)
    t = np.arange(L)
    for g, w in enumerate(POOL_WINDOWS):
        r = w // 2
        lo = np.clip(t - r, 0, L)
        hi = np.clip(t + r + 1, 0, L)
        inv = (1.0 / (hi - lo)).astype(np.float32)

        def blk(bi, bj):
            if bi < 0 or bi >= NBLK:
                return np.zeros((128, 128), np.float32)
            tp = np.arange(bi * 128, (bi + 1) * 128)[:, None]
            tt = np.arange(bj * 128, (bj + 1) * 128)[None, :]
            m = ((tp >= lo[tt]) & (tp < hi[tt])).astype(np.float32) * inv[tt]
            return m - (tp == tt).astype(np.float32)
        first = h * NBL
        last = h * NBL + NBL - 1
        for j, bj in enumerate((5, first, last)):
            for k in range(3):
                mats[g, j * 3 + k] = blk(bj - 1 + k, bj)
    return np.ascontiguousarray(mats.transpose(2, 0, 1, 3)).reshape(128, 4 * 9 * 128)


def halo_exchange(c, src):
    c.new_phase()
    nc = c.nc
    c.nhalo = getattr(c, "nhalo", 0) + 1
    hin = nc.dram_tensor("halo_in%d" % c.nhalo, [256, D], F32)
    hg = nc.dram_tensor("halo_g%d" % c.nhalo, [512, D], F32)
    t_h = T()
    c.op("sp", lambda e: e.dma_start(out=hin.ap()[0:128, :], in_=src[0:128, :]), writes=[t_h], dma=True)
    t_h2 = T()
    c.op("sp", lambda e: e.dma_start(out=hin.ap()[128:256, :], in_=src[NTL - 128:NTL, :]), writes=[t_h2], dma=True)
    c.op("pool", lambda e: e.collective_compute("AllGather", ALU.bypass, replica_groups=PAIRS, ins=[hin.ap()], outs=[hg.ap()]),
         reads=[t_h, t_h2], writes=[T()], cc=True)
    return hg.ap()


PAIRS = [[0, 1], [2, 3], [4, 5], [6, 7]]


def pool_phase(c, src, dst):
    hg = halo_exchange(c, src)
    c.new_phase()
    pm_d = c.din("pl_mats", [128, 36 * 128])
    g_d = c.din("pl_g", [128, 8])
    w_d = c.din("pl_wt", [128, 8, 256])
    b_d = c.din("pl_b", [1, D])
    s_d = c.din("pl_scale", [1, D])
    pm = c.sb(36 * 128, F32R)
    gcol = c.sb(8)
    wg = c.sb(8 * 256, F32R)
    brow = c.sb(D)
    srow = c.sb(D)
    t_k = T()
    c.op("pool", lambda e: e.dma_start(out=pm, in_=pm_d), writes=[t_k], dma=True)
    t_k2 = T()
    c.op("pool", lambda e: e.dma_start(out=wg, in_=w_d.rearrange("p a b -> p (a b)")), writes=[t_k2], dma=True)
    t_k3 = T()
    c.op("sp", lambda e: e.dma_start(out=gcol, in_=g_d), writes=[t_k3], dma=True)
    t_k4 = T()
    c.op("sp", lambda e: e.dma_start(out=brow, in_=b_d.broadcast_to([128, D])), writes=[t_k4], dma=True)
    t_k5 = T()
    c.op("sp", lambda e: e.dma_start(out=srow, in_=s_d.broadcast_to([128, D])), writes=[t_k5], dma=True)
    RING = 4
    xin = [c.sb(D) for _ in range(RING)]
    t_xin = [T() for _ in range(RING)]
    xn = [c.sb(D, F32R) for _ in range(RING)]
    t_xn = [T() for _ in range(RING)]
    junk = c.sb(D)
    t_junk = T()
    ss = c.sb(RING)
    rstd = c.sb(RING)
    t_ss = [T() for _ in range(RING)]
    t_rstd = [T() for _ in range(RING)]
    dT = [c.sb(128, F32R) for _ in range(8)]
    t_dT = [T() for _ in range(8)]
    yt = [c.sb(D) for _ in range(2)]
    t_yt = [T(), T()]
    P = c.psum
    ps_p = P[0:4]
    t_psp = [T() for _ in range(4)]
    ps_y = [P[4:6], P[6:8]]
    t_psy = [T(), T()]

    def rows(i):
        if i == 0:
            return hg[128:256, :]
        if i == NBL + 1:
            return hg[256:384, :]
        return src[(i - 1) * 128:i * 128, :]

    def prep(i):
        s = i % RING
        c.op("sp", lambda e, s=s, ap=rows(i): e.dma_start(out=xin[s], in_=ap), writes=[t_xin[s]], dma=True)
        c.op("act", lambda e, s=s: e.activation(out=junk, in_=xin[s], func=AF.Square, accum_out=ss[:, s:s + 1]),
             reads=[t_xin[s]], writes=[t_junk, t_ss[s]])
        c.op("act", lambda e, s=s: e.activation(out=rstd[:, s:s + 1], in_=ss[:, s:s + 1], func=AF.Sqrt, scale=1.0 / D, bias=c.epsc),
             reads=[t_ss[s], c.t_const], writes=[t_rstd[s]])
        c.op("dve", lambda e, s=s: e.reciprocal(out=rstd[:, s:s + 1], in_=rstd[:, s:s + 1]), reads=[t_rstd[s]], writes=[t_rstd[s]])
        c.op("act", lambda e, s=s: e.activation(out=xn[s], in_=xin[s], func=AF.Copy, scale=rstd[:, s:s + 1]),
             reads=[t_xin[s], t_rstd[s]], writes=[t_xn[s]])

    prep(0)
    prep(1)
    for i in range(1, NBL + 1):
        prep(i + 1)
        mbase = 3 if i == 1 else (6 if i == NBL else 0)
        terms = [(-1, mbase), (0, mbase + 1), (1, mbase + 2)]
        for g in range(4):
            pb = g
            for j in range(2):
                cc = 2 * g + j
                for ti, (rel, mi) in enumerate(terms):
                    s = (i + rel) % RING
                    c.op("pe", lambda e, s=s, cc=cc, g=g, mi=mi, j=j, pb=pb, ti=ti, nt=len(terms):
                         e.matmul(ps_p[pb][:, j * 128:(j + 1) * 128], xn[s][:, cc * 128:(cc + 1) * 128], pm[:, (g * 9 + mi) * 128:(g * 9 + mi + 1) * 128], start=(ti == 0), stop=(ti == nt - 1)),
                         reads=[t_xn[s], t_k], writes=[t_psp[pb]])
            for j in range(2):
                cc = 2 * g + j
                c.op("act", lambda e, cc=cc, j=j, pb=pb: e.activation(out=dT[cc], in_=ps_p[pb][:, j * 128:(j + 1) * 128], func=AF.Copy, scale=gcol[:, cc:cc + 1]),
                     reads=[t_psp[pb], t_k3], writes=[t_dT[cc]])
        yb = i % 2
        for g in range(4):
            for j in range(2):
                cc = 2 * g + j
                c.op("pe", lambda e, cc=cc, g=g, j=j, yb=yb: e.matmul(ps_y[yb][g // 2][:, (g % 2) * 256:(g % 2 + 1) * 256], dT[cc], wg[:, cc * 256:(cc + 1) * 256], start=(j == 0), stop=(j == 1)),
                     reads=[t_dT[cc], t_k2], writes=[t_psy[yb]])
        s = i % RING
        for hh in range(2):
            sl = slice(hh * 512, (hh + 1) * 512)
            c.op("dve", lambda e, yb=yb, hh=hh, sl=sl: e.tensor_tensor(out=yt[yb][:, sl], in0=ps_y[yb][hh][:, :], in1=brow[:, sl], op=ALU.add),
                 reads=[t_psy[yb], t_k4], writes=[t_yt[yb]])
        c.op("pool", lambda e, yb=yb: e.tensor_tensor(out=yt[yb], in0=yt[yb], in1=srow, op=ALU.mult), reads=[t_yt[yb], t_k5], writes=[t_yt[yb]])
        c.op("pool", lambda e, yb=yb, s=s: e.tensor_tensor(out=yt[yb], in0=yt[yb], in1=xin[s], op=ALU.add), reads=[t_yt[yb], t_xin[s]], writes=[t_yt[yb]])
        c.op("sp", lambda e, yb=yb, i=i: e.dma_start(out=dst[(i - 1) * 128:i * 128, :], in_=yt[yb]), reads=[t_yt[yb]], dma=True)


def pool_host(inputs, h):
    w = np.asarray(inputs["pl_w"][0], np.float32)
    wt = w.reshape(4, 2, 128, 256).transpose(2, 0, 1, 3).reshape(128, 8, 256)
    g = np.asarray(inputs["norm_mix"][2], np.float32)
    return {"pl_mats": pool_consts(h),
            "pl_g": np.ascontiguousarray(g.reshape(8, 128).T),
            "pl_wt": np.ascontiguousarray(wt),
            "pl_b": np.asarray(inputs["pl_b"][0], np.float32).reshape(1, D),
            "pl_scale": np.asarray(inputs["pl_scale"][0], np.float32).reshape(1, D)}


NH = 16
NKV = 4
HD = 64
NEG = -30000.0
_T5_THR = (8, 12, 16, 23, 32, 46, 64, 91)


def _t5_bucket(rel):
    n = abs(rel)
    if n < 8:
        b = n
    else:
        b = 7 + sum(1 for t in _T5_THR if n >= t)
    return (16 if rel > 0 else 0) + b


def attn_onehot():
    oh = np.zeros((33, 3, 128, 128), np.float32)
    for kb in range(3):
        for a in range(128):
            for j in range(128):
                rel = 128 * (kb - 1) + j - a
                if abs(rel) <= 128:
                    oh[_t5_bucket(rel), kb, a, j] = 1.0
                else:
                    oh[32, kb, a, j] = 1.0
    return oh.reshape(33, 3 * 128 * 128)


def const_r(c, cols, val, parts):
    tmp = c.sb(cols, parts=parts)
    out = c.sb(cols, F32R, parts=parts)
    t = T()
    c.op("dve", lambda e: e.memset(tmp, val), writes=[t])
    c.op("act", lambda e: e.activation(out=out, in_=tmp, func=AF.Copy), reads=[t], writes=[t])
    return out, t


def attn_qkv_phase(c, src, S, hg):
    c.new_phase()
    g_d = c.din("at_g", [128, 8])
    w_d = c.din("at_wqkv", [8, 128, 1536])
    qg_d = c.din("at_qg", [64, 1])
    kg_d = c.din("at_kg", [64, 1])
    gcol = c.sb(8)
    qg = c.sb(1, parts=64)
    kg = c.sb(1, parts=64)
    t_g = T()
    c.op("sp", lambda e: e.dma_start(out=gcol, in_=g_d), writes=[t_g], dma=True)
    c.op("sp", lambda e: e.dma_start(out=qg, in_=qg_d), writes=[t_g], dma=True)
    c.op("sp", lambda e: e.dma_start(out=kg, in_=kg_d), writes=[t_g], dma=True)
    wq = [c.sb(1536, F32R) for _ in range(8)]
    t_wq = [T() for _ in range(8)]
    for k in range(8):
        c.op("pool", lambda e, k=k: e.dma_start(out=wq[k], in_=w_d[k]), writes=[t_wq[k]], dma=True)
    ones64, t_ones = const_r(c, 64, 1.0 / 64, 64)
    fr = Front(c)
    sq = [c.sb(TT, F32R, parts=64) for _ in range(2)]
    t_sq = [T(), T()]
    rs = [c.sb(TT, parts=64) for _ in range(2)]
    t_rs = [T(), T()]
    qn = [c.sb(TT, parts=64) for _ in range(2)]
    t_qn = [T(), T()]
    vt = [c.sb(256) for _ in range(2)]
    t_vt = [T(), T()]
    P = c.psum
    ps_t, ps_q, ps_m, ps_v = P[0:2], P[2:4], P[4:6], P[6:8]
    t_pst, t_psq, t_psm, t_psv = [T(), T()], [T(), T()], [T(), T()], [T(), T()]
    cnt = 0
    cv = 0
    for it in range(NTL // TT + 1):
        if it < NTL // TT:
            r0 = it * TT
            fr.load_norm(src, r0)
            stores = [((1 + 4 * it) * 128, 0, TT)]
            store_q = True
        else:
            fr.load_norm(None, 0, rows=[hg[128:256, :], hg[256:384, :], hg[128:256, :], hg[256:384, :]])
            stores = [(0, 0, 128), ((NBL + 1) * 128, 128, 128)]
            store_q = False
        fr.transpose(gcol, t_g, ps_t, t_pst)
        hT, t_hT = fr.hT, fr.t_hT
        for h in range(NH + NKV):
            isq = h < NH
            if isq and not store_q:
                continue
            b = cnt % 2
            cnt += 1
            col0 = h * 64 if isq else 1024 + (h - NH) * 64
            gain = qg if isq else kg
            dstT = S["qT"][h] if isq else S["kT"][h - NH]
            for k in range(8):
                c.op("pe", lambda e, k=k, b=b, col0=col0: e.matmul(ps_q[b][0:64, :], wq[k][:, col0:col0 + 64], hT[k], start=(k == 0), stop=(k == 7)),
                     reads=[t_wq[k], t_hT[k]], writes=[t_psq[b]])
            c.op("act", lambda e, b=b: e.activation(out=sq[b], in_=ps_q[b][0:64, :], func=AF.Square), reads=[t_psq[b]], writes=[t_sq[b]])
            c.op("pe", lambda e, b=b: e.matmul(ps_m[b][0:64, :], ones64, sq[b], start=True, stop=True), reads=[t_ones, t_sq[b]], writes=[t_psm[b]])
            c.op("act", lambda e, b=b: e.activation(out=rs[b], in_=ps_m[b][0:64, :], func=AF.Sqrt, bias=c.epsc[0:64, :]), reads=[t_psm[b], c.t_const], writes=[t_rs[b]])
            c.op("dve", lambda e, b=b: e.reciprocal(out=rs[b], in_=rs[b]), reads=[t_rs[b]], writes=[t_rs[b]])
            c.op("dve", lambda e, b=b, gain=gain: e.scalar_tensor_tensor(out=qn[b], in0=ps_q[b][0:64, :], scalar=gain, in1=rs[b], op0=ALU.mult, op1=ALU.mult),
                 reads=[t_psq[b], t_rs[b], t_g], writes=[t_qn[b]])
            for (dc, sc, wd_) in stores:
                c.op("sp", lambda e, b=b, dstT=dstT, dc=dc, sc=sc, wd_=wd_: e.dma_start(out=dstT[:, dc:dc + wd_], in_=qn[b][:, sc:sc + wd_]), reads=[t_qn[b]], dma=True)
        for tb in range(4 if store_q else 2):
            b = cv % 2
            cv += 1
            for k in range(8):
                c.op("pe", lambda e, k=k, b=b, tb=tb: e.matmul(ps_v[b][:, 0:256], hT[k][:, tb * 128:(tb + 1) * 128], wq[k][:, 1280:1536], start=(k == 0), stop=(k == 7)),
                     reads=[t_wq[k], t_hT[k]], writes=[t_psv[b]])
            c.op("act", lambda e, b=b: e.activation(out=vt[b], in_=ps_v[b][:, 0:256], func=AF.Copy), reads=[t_psv[b]], writes=[t_vt[b]])
            if store_q:
                vrow = (1 + 4 * it + tb) * 128
            else:
                vrow = 0 if tb == 0 else (NBL + 1) * 128
            c.op("sp", lambda e, b=b, vrow=vrow: e.dma_start(out=S["v"][vrow:vrow + 128, :], in_=vt[b]), reads=[t_vt[b]], dma=True)


def attn_core_phase(c, src, dst, S):
    c.new_phase()
    oh_d = c.din("at_oh", [33, 3 * 128 * 128])
    rt_d = c.din("at_rel", [32, 16])
    sink_d = c.din("at_sink", [1, 16])
    edge_d = c.din("at_edge", [128, 2])
    edge = c.sb(2)
    t_edge = T()
    c.op("sp", lambda e: e.dma_start(out=edge, in_=edge_d), writes=[t_edge], dma=True)
    wo_d = c.din("at_wo", [64, 16, D])
    bias = c.sb(3 * 16 * 128)
    t_bias = T()
    table = c.sb(16, parts=33)
    t_tab = T()
    c.op("dve", lambda e: e.memset(table[32:33, :], NEG), writes=[t_tab])
    c.op("sp", lambda e: e.dma_start(out=table[0:32, :], in_=rt_d), writes=[t_tab], dma=True)
    ohb = [c.sb(32 * 128, parts=33) for _ in range(2)]
    t_ohb = [T(), T()]
    P = c.psum
    ps_s, ps_o, ps_den, ps_y = P[0:2], P[2:4], P[4:6], P[6:8]
    t_pss, t_pso, t_psden, t_psy = [T(), T()], [T(), T()], [T(), T()], [T(), T()]
    nb = 0
    for kb in range(3):
        for q4 in range(4):
            ob = nb % 2
            pb = nb % 2
            nb += 1
            a0 = q4 * 32
            c.op("sp", lambda e, kb=kb, a0=a0, ob=ob: e.dma_start(out=ohb[ob], in_=oh_d[:, (kb * 128 + a0) * 128:(kb * 128 + a0 + 32) * 128]),
                 writes=[t_ohb[ob]], dma=True)
            for al in range(32):
                c.op("pe", lambda e, ob=ob, al=al, pb=pb: e.matmul(ps_s[pb][:, al * 16:(al + 1) * 16], ohb[ob][:, al * 128:(al + 1) * 128], table, start=True, stop=True),
                     reads=[t_ohb[ob], t_tab], writes=[t_pss[pb]])
            c.op("dve", lambda e, kb=kb, a0=a0, pb=pb: e.tensor_copy(
                out=bias[:, kb * 2048:(kb + 1) * 2048].rearrange("p (h a) -> p h a", h=16)[:, :, a0:a0 + 32],
                in_=ps_s[pb][:, :].rearrange("p (a h) -> p h a", h=16)),
                reads=[t_pss[pb]], writes=[t_bias])
    es16 = c.sb(16, parts=64)
    esink = c.sb(16 * 128, parts=64)
    t_es = T()
    c.op("sp", lambda e: e.dma_start(out=es16, in_=sink_d.broadcast_to([64, 16])), writes=[t_es], dma=True)
    c.op("act", lambda e: e.activation(out=es16, in_=es16, func=AF.Exp), reads=[t_es], writes=[t_es])
    c.op("dve", lambda e: e.tensor_copy(out=esink.rearrange("p (h a) -> p h a", h=16), in_=es16.unsqueeze(2).broadcast_to([64, 16, 128])), reads=[t_es], writes=[t_es])
    wo = c.sb(16 * D, F32R, parts=64)
    t_wo = T()
    c.op("pool", lambda e: e.dma_start(out=wo, in_=wo_d.rearrange("p h n -> p (h n)")), writes=[t_wo], dma=True)
    oneskv, t_ones = const_r(c, 64, 1.0, 128)
    q_sb = [c.sb(16 * 128, F32R, parts=64) for _ in range(2)]
    t_q = [T(), T()]
    RING = 4
    k_r = [c.sb(4 * 128, F32R, parts=64) for _ in range(RING)]
    t_kr = [T() for _ in range(RING)]
    v_r = [c.sb(256, F32R) for _ in range(RING)]
    t_vr = [T() for _ in range(RING)]
    xin = [c.sb(D) for _ in range(2)]
    t_xin = [T(), T()]
    tt = [c.sb(TT) for _ in range(2)]
    t_tt = [T(), T()]
    pT = [c.sb(TT, F32R) for _ in range(2)]
    t_pT = [T(), T()]
    den = [c.sb(TT, parts=64) for _ in range(2)]
    t_den = [T(), T()]
    oT = [[c.sb(TT, F32R, parts=64) for _ in range(4)] for _ in range(2)]
    t_oT = [[T() for _ in range(4)] for _ in range(2)]

    def prep_kv(i):
        s = i % RING
        c.op("pool", lambda e, s=s, i=i: e.dma_start(out=k_r[s].rearrange("p (g t) -> p g t", g=4), in_=S["kT3"][:, :, i * 128:(i + 1) * 128]), writes=[t_kr[s]], dma=True)
        c.op("pool", lambda e, s=s, i=i: e.dma_start(out=v_r[s], in_=S["v"][i * 128:(i + 1) * 128, :]), writes=[t_vr[s]], dma=True)

    prep_kv(0)
    prep_kv(1)
    steps = [(n, g, kb) for n in range(1, NBL + 1) for g in range(4) for kb in range(3)]
    deferred = []

    def block_start(n):
        prep_kv(n + 1)
        qb = n % 2
        c.op("pool", lambda e, qb=qb, n=n: e.dma_start(out=q_sb[qb].rearrange("p (h t) -> p h t", h=16), in_=S["qT3"][:, :, n * 128:(n + 1) * 128]), writes=[t_q[qb]], dma=True)
        c.op("sp", lambda e, qb=qb, n=n: e.dma_start(out=xin[qb], in_=src[(n - 1) * 128:n * 128, :]), writes=[t_xin[qb]], dma=True)

    def emit_s(i):
        n, g, kb = steps[i]
        if g == 0 and kb == 0:
            block_start(n)
        qb = n % 2
        s_ = (n + kb - 1) % RING
        b = i % 2
        c.op("pe", lambda e, s_=s_, g=g, qb=qb, b=b: e.matmul(ps_s[b][:, :], k_r[s_][:, g * 128:(g + 1) * 128], q_sb[qb][:, g * 512:(g + 1) * 512], start=True, stop=True),
             reads=[t_kr[s_], t_q[qb]], writes=[t_pss[b]])

    def emit_rest(i):
        n, g, kb = steps[i]
        qb = n % 2
        s_ = (n + kb - 1) % RING
        b = i % 2
        ob = (i // 3) % 2
        c.op("dve", lambda e, b=b, kb=kb, g=g: e.scalar_tensor_tensor(out=tt[b], in0=ps_s[b][:, :], scalar=HD ** -0.5, in1=bias[:, kb * 2048 + g * 512:kb * 2048 + (g + 1) * 512], op0=ALU.mult, op1=ALU.add),
             reads=[t_pss[b], t_bias], writes=[t_tt[b]])
        if (n == 1 and kb == 0) or (n == NBL and kb == 2):
            ecol = 0 if kb == 0 else 1
            c.op("dve", lambda e, b=b, ecol=ecol: e.tensor_scalar(out=tt[b], in0=tt[b], scalar1=edge[:, ecol:ecol + 1], scalar2=0.0, op0=ALU.add, op1=ALU.add),
                 reads=[t_tt[b], t_edge], writes=[t_tt[b]])
        c.op("act", lambda e, b=b: e.activation(out=pT[b], in_=tt[b], func=AF.Exp), reads=[t_tt[b]], writes=[t_pT[b]])
        c.op("pe", lambda e, s_=s_, g=g, b=b, ob=ob, kb=kb: e.matmul(ps_o[ob][0:64, :], v_r[s_][:, g * 64:(g + 1) * 64], pT[b], start=(kb == 0), stop=(kb == 2)),
             reads=[t_vr[s_], t_pT[b]], writes=[t_pso[ob]])
        c.op("pe", lambda e, b=b, ob=ob, kb=kb: e.matmul(ps_den[ob][0:64, :], oneskv, pT[b], start=(kb == 0), stop=(kb == 2)),
             reads=[t_ones, t_pT[b]], writes=[t_psden[ob]])
        if kb == 2:
            c.op("dve", lambda e, ob=ob, g=g: e.tensor_tensor(out=den[ob], in0=ps_den[ob][0:64, :], in1=esink[:, g * 512:(g + 1) * 512], op=ALU.add),
                 reads=[t_psden[ob], t_es], writes=[t_den[ob]])
            c.op("dve", lambda e, ob=ob: e.reciprocal(out=den[ob], in_=den[ob]), reads=[t_den[ob]], writes=[t_den[ob]])
            c.op("dve", lambda e, ob=ob, g=g, qb=qb: e.tensor_tensor(out=oT[qb][g], in0=ps_o[ob][0:64, :], in1=den[ob], op=ALU.mult),
                 reads=[t_pso[ob], t_den[ob]], writes=[t_oT[qb][g]])
            if g == 3:
                deferred.append((i + 3, lambda n=n: block_end(n)))

    def block_end(n):
        qb = n % 2
        for dh in range(2):
            yb = dh
            for h in range(NH):
                c.op("pe", lambda e, h=h, dh=dh, yb=yb, qb=qb: e.matmul(ps_y[yb][:, :], oT[qb][h // 4][:, (h % 4) * 128:(h % 4 + 1) * 128], wo[:, h * D + dh * 512:h * D + (dh + 1) * 512], start=(h == 0), stop=(h == NH - 1)),
                     reads=[t_oT[qb][h // 4], t_wo], writes=[t_psy[yb]])
            c.op("dve", lambda e, dh=dh, yb=yb, qb=qb: e.tensor_tensor(out=xin[qb][:, dh * 512:(dh + 1) * 512], in0=ps_y[yb][:, :], in1=xin[qb][:, dh * 512:(dh + 1) * 512], op=ALU.add),
                 reads=[t_psy[yb], t_xin[qb]], writes=[t_xin[qb]])
        c.op("sp", lambda e, qb=qb, n=n: e.dma_start(out=dst[(n - 1) * 128:n * 128, :], in_=xin[qb]), reads=[t_xin[qb]], dma=True)

    ns = len(steps)
    for i in range(ns + 4):
        if i < ns:
            emit_s(i)
        if 1 <= i <= ns:
            emit_rest(i - 1)
        for (at, fn) in [d for d in deferred if d[0] <= i]:
            fn()
        deferred[:] = [d for d in deferred if d[0] > i]
    assert not deferred


def attn_scratch(c):
    qT = c.nc.dram_tensor("qT_s", [NH, 64, (NBL + 2) * 128], F32)
    kT = c.nc.dram_tensor("kT_s", [NKV, 64, (NBL + 2) * 128], F32)
    v = c.nc.dram_tensor("v_s", [(NBL + 2) * 128, 256], F32)
    return {"qT": [qT.ap()[h] for h in range(NH)], "kT": [kT.ap()[h] for h in range(NKV)], "v": v.ap(),
            "qT3": qT.ap().rearrange("h p t -> p h t"), "kT3": kT.ap().rearrange("h p t -> p h t")}


def attn_host(inputs, h):
    g = np.asarray(inputs["norm_mix"][1], np.float32)
    edge = np.zeros((128, 2), np.float32)
    edge[:, h] = NEG
    wo = np.asarray(inputs["at_w_o"][0], np.float32).reshape(16, 64, D).transpose(1, 0, 2)
    return {"at_g": np.ascontiguousarray(g.reshape(8, 128).T),
            "at_wqkv": np.ascontiguousarray(np.asarray(inputs["at_w_qkv"][0], np.float32).reshape(8, 128, 1536)),
            "at_qg": np.asarray(inputs["at_q_gain"][0], np.float32).reshape(64, 1),
            "at_kg": np.asarray(inputs["at_k_gain"][0], np.float32).reshape(64, 1),
            "at_oh": attn_onehot(),
            "at_rel": np.asarray(inputs["rel_table"], np.float32),
            "at_sink": np.asarray(inputs["at_sink"][0], np.float32).reshape(1, 16),
            "at_edge": edge,
            "at_wo": np.ascontiguousarray(wo)}


NFFT = 2 * L
HY_MIN_DECAY = math.log(1e-2) / 1.5
HY_MAX_DECAY = math.log(1e-2) / 0.3
_FA, _FC, _FS, _FSN, _GCS, _GSNC, _HAC, _HASN, _NCR = 0, 256, 384, 512, 640, 896, 1152, 1216, 1280


def hy_consts():
    i = np.arange(128, dtype=np.float64)
    th = 2 * np.pi * np.outer(i, i) / 128.0
    cr = np.zeros((128, _NCR), np.float64)
    cr[:, _FA:_FA + 128] = np.cos(th)
    cr[:, _FA + 128:_FA + 256] = -np.sin(th)
    cr[:, _FC:_FC + 128] = np.cos(th)
    cr[:, _FS:_FS + 128] = np.sin(th)
    cr[:, _FSN:_FSN + 128] = -np.sin(th)
    cr[:, _GCS:_GCS + 128] = np.cos(th)
    cr[:, _GCS + 128:_GCS + 256] = np.sin(th)
    cr[:, _GSNC:_GSNC + 128] = -np.sin(th)
    cr[:, _GSNC + 128:_GSNC + 256] = np.cos(th)
    cr[:, _HAC:_HAC + 64] = np.cos(th[:, :64])
    cr[:, _HASN:_HASN + 64] = -np.sin(th[:, :64])
    tw = 2 * np.pi * np.outer(i, i) / NFFT
    cf = np.concatenate([np.tile(np.cos(tw), (1, 4)), np.tile(np.sin(tw), (1, 4))], axis=1)
    n = np.arange(NFFT)
    pos = np.where(n < L, n, L - (n - L)).astype(np.float64)
    pos[L] = 0.0
    t = pos / (L - 1)
    f = np.linspace(1e-4, 15.0, 16)
    ang = (2 * np.pi / L) * pos[None, :] * f[:, None]
    z = np.concatenate([t[None, :], np.cos(ang), -np.sin(ang)], axis=0)
    tdec = t.copy()
    tdec[L] = 1.0e4
    deltas = np.abs(np.linspace(HY_MIN_DECAY, HY_MAX_DECAY, D))
    return {"hy_cr": cr.astype(np.float32), "hy_cf": cf.astype(np.float32), "hy_z": z.astype(np.float32),
            "hy_tdec": tdec.astype(np.float32).reshape(1, NFFT),
            "hy_negdelta_full": (-deltas).astype(np.float32)}


class HyFFT:
    def __init__(self, c, inverse):
        self.c = c
        self.inverse = inverse
        cr_d = c.din("hy_cr", [128, _NCR])
        cf_d = c.din("hy_cf", [128, 1024])
        self.cr = c.sb(_NCR, F32R)
        self.cf = c.sb(1024)
        self.t_k = T()
        c.op("pool", lambda e: e.dma_start(out=self.cr, in_=cr_d), writes=[self.t_k], dma=True)
        self.t_k2 = T()
        c.op("sp", lambda e: e.dma_start(out=self.cf, in_=cf_d), writes=[self.t_k2], dma=True)
        self.C2 = self.cf[:, 0:512]
        self.S2 = self.cf[:, 512:1024]
        P = c.psum
        mk2 = lambda dt=F32: [c.sb(512, dt) for _ in range(2)]
        self.t1, self.t2 = mk2(), mk2()
        self.t_t1, self.t_t2 = [T(), T()], [T(), T()]
        self.Bre, self.Bim = mk2(F32R), mk2(F32R)
        self.t_B = [[T(), T()], [T(), T()]]
        self.ps_a, self.t_psa = P[0:2], [T(), T()]
        self.ps_x, self.t_psx = P[2:4], [T(), T()]
        if inverse:
            self.u1, self.u2 = mk2(), mk2()
            self.t_u1, self.t_u2 = [T(), T()], [T(), T()]
            self.m = [c.sb(512) for _ in range(4)]
            self.t_m = [T() for _ in range(4)]
            self.Zre, self.Zim = mk2(F32R), mk2(F32R)
            self.t_Z = [[T(), T()], [T(), T()]]
            self.Vre, self.Vim = mk2(F32R), mk2(F32R)
            self.t_V = [[T(), T()], [T(), T()]]
            self.ps_v, self.t_psv = P[4:6], [T(), T()]
            self.ps_y, self.t_psy = P[6], T()

    def _tw_mul(self, ps, t_ps, a1, a2, t_a1, t_a2):
        c = self.c
        for b in range(2):
            c.op("dve", lambda e, b=b: e.tensor_tensor(out=a1[b], in0=ps[b][:, :], in1=self.C2, op=ALU.mult), reads=[t_ps[b], self.t_k2], writes=[t_a1[b]])
            c.op("dve", lambda e, b=b: e.tensor_tensor(out=a2[b], in0=ps[b][:, :], in1=self.S2, op=ALU.mult), reads=[t_ps[b], self.t_k2], writes=[t_a2[b]])

    def _tw_comb(self, a1, a2, t_a1, t_a2, outre, outim, t_out, forward):
        c = self.c
        for b in range(2):
            v1 = a1[b].rearrange("p (s c k) -> p s c k", s=2, c=2)
            v2 = a2[b].rearrange("p (s c k) -> p s c k", s=2, c=2)
            ore = outre[:, b * 256:(b + 1) * 256].rearrange("p (s k) -> p s k", s=2)
            oim = outim[:, b * 256:(b + 1) * 256].rearrange("p (s k) -> p s k", s=2)
            op_re, op_im = (ALU.add, ALU.subtract) if forward else (ALU.subtract, ALU.add)
            c.op("pool", lambda e, v1=v1, v2=v2, ore=ore, op_re=op_re: e.tensor_tensor(out=ore, in0=v1[:, :, 0, :], in1=v2[:, :, 1, :], op=op_re),
                 reads=[t_a1[b], t_a2[b]], writes=[t_out[0]])
            c.op("pool", lambda e, v1=v1, v2=v2, oim=oim, op_im=op_im: e.tensor_tensor(out=oim, in0=v1[:, :, 1, :], in1=v2[:, :, 0, :], op=op_im),
                 reads=[t_a1[b], t_a2[b]], writes=[t_out[1]])

    def stage(self, st, g, J):
        c = self.c
        cr = self.cr
        par = g % 2
        c0 = g * 4
        if st == 0:
            ut, ka = J["ut"], J["ka"]
            for s in range(4):
                b = s // 2
                c.op("pe", lambda e, s=s, b=b, ut=ut, ka=ka, c0=c0: e.matmul(self.ps_a[b][:, (s % 2) * 256:(s % 2 + 1) * 256], ut[0:ka, (c0 + s) * 128:(c0 + s + 1) * 128], cr[0:ka, _FA:_FA + 256], start=True, stop=True),
                     reads=[J["t_ut"][g], self.t_k], writes=[self.t_psa[b]])
        elif st == 1:
            self._tw_mul(self.ps_a, self.t_psa, self.t1, self.t2, self.t_t1, self.t_t2)
        elif st == 2:
            self._tw_comb(self.t1, self.t2, self.t_t1, self.t_t2, self.Bre[par], self.Bim[par], self.t_B[par], True)
        elif st == 3:
            Bre, Bim = self.Bre[par], self.Bim[par]
            rB = [self.t_B[par][0], self.t_B[par][1], self.t_k]
            c.op("pe", lambda e, Bre=Bre: e.matmul(self.ps_x[0][:, :], cr[:, _FC:_FC + 128], Bre, start=True, stop=False), reads=rB, writes=[self.t_psx[0]])
            c.op("pe", lambda e, Bim=Bim: e.matmul(self.ps_x[0][:, :], cr[:, _FS:_FS + 128], Bim, start=False, stop=True), reads=rB, writes=[self.t_psx[0]])
            c.op("pe", lambda e, Bim=Bim: e.matmul(self.ps_x[1][:, :], cr[:, _FC:_FC + 128], Bim, start=True, stop=False), reads=rB, writes=[self.t_psx[1]])
            c.op("pe", lambda e, Bre=Bre: e.matmul(self.ps_x[1][:, :], cr[:, _FSN:_FSN + 128], Bre, start=False, stop=True), reads=rB, writes=[self.t_psx[1]])
        elif not self.inverse:
            if st == 4:
                J["spec_out"](g, self.ps_x, self.t_psx)
        elif st == 4:
            kt, t_kt = J["kt"](g)
            m, t_m = self.m, self.t_m
            xr, xi = self.ps_x[0], self.ps_x[1]
            c.op("dve", lambda e, kt=kt: e.tensor_tensor(out=m[0], in0=xr[:, :], in1=kt[:, 0:512], op=ALU.mult), reads=[self.t_psx[0], t_kt], writes=[t_m[0]])
            c.op("dve", lambda e, kt=kt: e.tensor_tensor(out=m[1], in0=xi[:, :], in1=kt[:, 512:1024], op=ALU.mult), reads=[self.t_psx[1], t_kt], writes=[t_m[1]])
            c.op("dve", lambda e, kt=kt: e.tensor_tensor(out=m[2], in0=xr[:, :], in1=kt[:, 512:1024], op=ALU.mult), reads=[self.t_psx[0], t_kt], writes=[t_m[2]])
            c.op("dve", lambda e, kt=kt: e.tensor_tensor(out=m[3], in0=xi[:, :], in1=kt[:, 0:512], op=ALU.mult), reads=[self.t_psx[1], t_kt], writes=[t_m[3]])
        elif st == 5:
            m, t_m = self.m, self.t_m
            Zre, Zim = self.Zre[par], self.Zim[par]
            c.op("pool", lambda e, Zre=Zre: e.tensor_tensor(out=Zre, in0=m[0], in1=m[1], op=ALU.subtract), reads=[t_m[0], t_m[1]], writes=[self.t_Z[par][0]])
            c.op("pool", lambda e, Zim=Zim: e.tensor_tensor(out=Zim, in0=m[2], in1=m[3], op=ALU.add), reads=[t_m[2], t_m[3]], writes=[self.t_Z[par][1]])
        elif st == 6:
            Zre, Zim = self.Zre[par], self.Zim[par]
            for s in range(4):
                b = s // 2
                reg = self.ps_v[b][:, (s % 2) * 256:(s % 2 + 1) * 256]
                c.op("pe", lambda e, s=s, reg=reg, Zre=Zre: e.matmul(reg, Zre[:, s * 128:(s + 1) * 128], cr[:, _GCS:_GCS + 256], start=True, stop=False),
                     reads=[self.t_Z[par][0], self.t_k], writes=[self.t_psv[b]])
                c.op("pe", lambda e, s=s, reg=reg, Zim=Zim: e.matmul(reg, Zim[:, s * 128:(s + 1) * 128], cr[:, _GSNC:_GSNC + 256], start=False, stop=True),
                     reads=[self.t_Z[par][1], self.t_k], writes=[self.t_psv[b]])
        elif st == 7:
            self._tw_mul(self.ps_v, self.t_psv, self.u1, self.u2, self.t_u1, self.t_u2)
        elif st == 8:
            self._tw_comb(self.u1, self.u2, self.t_u1, self.t_u2, self.Vre[par], self.Vim[par], self.t_V[par], False)
        elif st == 9:
            Vre, Vim = self.Vre[par], self.Vim[par]
            rV = [self.t_V[par][0], self.t_V[par][1], self.t_k]
            c.op("pe", lambda e, Vre=Vre: e.matmul(self.ps_y[0:64, :], cr[:, _HAC:_HAC + 64], Vre, start=True, stop=False), reads=rV, writes=[self.t_psy])
            c.op("pe", lambda e, Vim=Vim: e.matmul(self.ps_y[0:64, :], cr[:, _HASN:_HASN + 64], Vim, start=False, stop=True), reads=rV, writes=[self.t_psy])
        elif st == 10:
            yt = J["yt"]
            c.op("act", lambda e, c0=c0, yt=yt: e.activation(out=yt[0:64, c0 * 128:(c0 + 4) * 128], in_=self.ps_y[0:64, :], func=AF.Copy), reads=[self.t_psy], writes=[J["t_ut"][g]])

    def run(self, J, ngroups=32):
        nst = 11 if self.inverse else 5
        for t in range(ngroups + nst - 1):
            if "pre" in J:
                J["pre"](t)
            for st in range(nst - 1, -1, -1):
                g = t - st
                if 0 <= g < ngroups:
                    self.stage(st, g, J)


def to_time_major(c, src_ct, t_src, ut, t_ut, na, ps, t_ps):
    v = src_ct.rearrange("p (a r) -> p r a", r=128)
    u3 = ut[0:na, :].rearrange("p (c r) -> p c r", r=128)
    for r0 in range(0, 128, 4):
        b = (r0 // 4) % 2
        for j in range(4):
            c.op("pe", lambda e, r0=r0, j=j, b=b: e.transpose(out=ps[b][0:na, j * 128:(j + 1) * 128], in_=v[:, r0 + j, :], identity=c.ident),
                 reads=[t_src, c.t_const], writes=[t_ps[b]])
        c.op("act", lambda e, r0=r0, b=b: e.activation(out=u3[:, :, r0:r0 + 4], in_=ps[b][0:na, :].rearrange("p (r c) -> p c r", r=4), func=AF.Copy),
             reads=[t_ps[b]], writes=(t_ut if isinstance(t_ut, list) else [t_ut]))


def to_feature_major(c, yt, t_yt, dst_ct, t_dst, ps, t_ps):
    y3 = yt[0:64, :].rearrange("p (c r) -> p r c", r=128)
    d3 = dst_ct.rearrange("p (a r) -> p r a", r=128)
    for r0 in range(0, 128, 8):
        b = (r0 // 8) % 2
        for j in range(8):
            c.op("pe", lambda e, r0=r0, j=j, b=b: e.transpose(out=ps[b][:, j * 64:(j + 1) * 64], in_=y3[:, r0 + j, :], identity=c.ident[0:64, 0:64]),
                 reads=(t_yt if isinstance(t_yt, list) else [t_yt]) + [c.t_const], writes=[t_ps[b]])
        c.op("dve", lambda e, r0=r0, b=b: e.tensor_copy(out=d3[:, r0:r0 + 8, :], in_=ps[b][:, :].rearrange("p (r a) -> p r a", r=8)),
             reads=[t_ps[b]], writes=[t_dst])


NCC = 4


def hy_scratch(c, tag):
    nc = c.nc
    zin = [[nc.dram_tensor("hy_zin%d_%d" % (hf, cc), [128, NTL], F32) for cc in range(NCC)] for hf in range(2)]
    zg = [[nc.dram_tensor("hy_zg%d_%d" % (hf, cc), [256, NTL], F32) for cc in range(NCC)] for hf in range(2)]
    return {"p": nc.dram_tensor("hy_p" + tag, [3 * NCC, 128, L], F32).ap(),
            "zin": zin, "zg": zg,
            "h3": nc.dram_tensor("hy_h3" + tag, [64, NFFT], F32).ap(),
            "ks": nc.dram_tensor("hy_ks" + tag, [2, NCC, 32, 128, 1024], F32).ap()}


def hy_inproj_phase(c, rows_fn, S, s):
    c.new_phase()
    NO = 3 * NCC * 128
    g_d = c.din("hy_g%d" % s, [128, 8])
    w_d = c.din("hy_win%d" % s, [8, 128, NO])
    gcol = c.sb(8)
    t_g = T()
    c.op("sp", lambda e: e.dma_start(out=gcol, in_=g_d), writes=[t_g], dma=True)
    win = [c.sb(NO, F32R) for _ in range(8)]
    t_w = [T() for _ in range(8)]
    for k in range(8):
        c.op("pool", lambda e, k=k: e.dma_start(out=win[k], in_=w_d[k]), writes=[t_w[k]], dma=True)
    fr = Front(c)
    stage = [c.sb(TT) for _ in range(4)]
    t_st = [T() for _ in range(4)]
    P = c.psum
    ps_t, t_pst = P[0:2], [T(), T()]
    ps_o, t_pso = P[2:6], [T() for _ in range(4)]
    cnt = 0
    for it in range(L // TT):
        r0 = it * TT
        fr.load_norm(None, r0, rows=[rows_fn(r0 + tb * 128) for tb in range(4)])
        fr.transpose(gcol, t_g, ps_t, t_pst)
        for oc in range(3 * NCC):
            b = cnt % 4
            cnt += 1
            for k in range(8):
                c.op("pe", lambda e, k=k, oc=oc, b=b: e.matmul(ps_o[b][:, :], win[k][:, oc * 128:(oc + 1) * 128], fr.hT[k], start=(k == 0), stop=(k == 7)),
                     reads=[t_w[k], fr.t_hT[k]], writes=[t_pso[b]])
            if oc % 2 == 0:
                c.op("act", lambda e, b=b: e.activation(out=stage[b], in_=ps_o[b][:, :], func=AF.Copy), reads=[t_pso[b]], writes=[t_st[b]])
            else:
                c.op("dve", lambda e, b=b: e.tensor_copy(out=stage[b], in_=ps_o[b][:, :]), reads=[t_pso[b]], writes=[t_st[b]])
            c.op("sp", lambda e, b=b, oc=oc, r0=r0: e.dma_start(out=S["p"][oc][:, r0:r0 + TT], in_=stage[b]), reads=[t_st[b]], dma=True)


def hy_conv3_phase(c, S, s):
    c.new_phase()
    cols_d = c.din("hy_c3cols%d" % s, [128, 3 * NCC * 5])
    cols = c.sb(3 * NCC * 5)
    t_c = T()
    c.op("sp", lambda e: e.dma_start(out=cols, in_=cols_d), writes=[t_c], dma=True)
    raw = [c.sb(L + 2) for _ in range(2)]
    t_raw = [T(), T()]
    out = [c.sb(L) for _ in range(2)]
    t_out = [T(), T()]
    for oc in range(3 * NCC):
        b = oc % 2
        eng = "dve"
        k0 = oc * 5
        c.op("sp", lambda e, b=b, oc=oc: e.dma_start(out=raw[b][:, 1:L + 1], in_=S["p"][oc]), writes=[t_raw[b]], dma=True)
        c.op("act", lambda e, b=b, k0=k0: e.activation(out=raw[b][:, 1:L + 1], in_=raw[b][:, 1:L + 1], func=AF.Identity, bias=cols[:, k0:k0 + 1]),
             reads=[t_raw[b], t_c], writes=[t_raw[b]])
        c.op(eng, lambda e, b=b: e.memset(raw[b][:, 0:1], 0.0), reads=[t_raw[b]], writes=[t_raw[b]])
        c.op(eng, lambda e, b=b: e.memset(raw[b][:, L + 1:L + 2], 0.0), reads=[t_raw[b]], writes=[t_raw[b]])
        c.op("act", lambda e, b=b, k0=k0: e.activation(out=out[b], in_=raw[b][:, 0:L], func=AF.Identity, scale=cols[:, k0 + 1:k0 + 2], bias=cols[:, k0 + 4:k0 + 5]),
             reads=[t_raw[b], t_c], writes=[t_out[b]])
        c.op(eng, lambda e, b=b, k0=k0: e.scalar_tensor_tensor(out=out[b], in0=raw[b][:, 1:L + 1], scalar=cols[:, k0 + 2:k0 + 3], in1=out[b], op0=ALU.mult, op1=ALU.add),
             reads=[t_raw[b], t_c, t_out[b]], writes=[t_out[b]])
        c.op(eng, lambda e, b=b, k0=k0: e.scalar_tensor_tensor(out=out[b], in0=raw[b][:, 2:L + 2], scalar=cols[:, k0 + 3:k0 + 4], in1=out[b], op0=ALU.mult, op1=ALU.add),
             reads=[t_raw[b], t_c, t_out[b]], writes=[t_out[b]])
        c.op("sp", lambda e, b=b, oc=oc: e.dma_start(out=S["p"][oc], in_=out[b]), reads=[t_out[b]], dma=True)


def hy_mlp_phase(c, S, s):
    c.new_phase()
    z_d = c.din("hy_z", [33, NFFT])
    w1_d = c.din("hy_w1_%d" % s, [33, 64])
    wi_d = c.din("hy_wi_%d" % s, [64, 128])
    cols_d = c.din("hy_mlpcols%d" % s, [64, 4])
    w1 = c.sb(64, F32R, parts=33)
    wi = c.sb(128, F32R, parts=64)
    cols = c.sb(4, parts=64)
    negpi = c.sb(1, parts=64)
    t_k = T()
    c.op("pool", lambda e: e.dma_start(out=w1, in_=w1_d), writes=[t_k], dma=True)
    t_k1 = T()
    c.op("pool", lambda e: e.dma_start(out=wi, in_=wi_d), writes=[t_k1], dma=True)
    t_k2 = T()
    c.op("sp", lambda e: e.dma_start(out=cols, in_=cols_d), writes=[t_k2], dma=True)
    c.op("dve", lambda e: e.memset(negpi, -math.pi), writes=[t_k2])
    zt = [c.sb(TT, F32R, parts=33) for _ in range(2)]
    t_zt = [T(), T()]
    arg = [c.sb(TT, parts=64) for _ in range(2)]
    t_arg = [T(), T()]
    kk = [c.sb(TT, parts=64) for _ in range(2)]
    t_kk = [T(), T()]
    MAGIC = 12582912.0
    hh = [c.sb(TT, F32R, parts=64) for _ in range(2)]
    t_hh = [T(), T()]
    h3 = [c.sb(TT, parts=64) for _ in range(2)]
    t_h3 = [T(), T()]
    P = c.psum
    ps, t_ps = P[0:2], [T(), T()]
    cnt = 0
    for it in range(NFFT // TT):
        n0 = it * TT
        zb = it % 2
        c.op("pool", lambda e, zb=zb, n0=n0: e.dma_start(out=zt[zb], in_=z_d[:, n0:n0 + TT]), writes=[t_zt[zb]], dma=True)
        for layer in range(3):
            b = cnt % 2
            cnt += 1
            if layer == 0:
                c.op("pe", lambda e, zb=zb, b=b: e.matmul(ps[b][0:64, :], w1, zt[zb], start=True, stop=True), reads=[t_k, t_zt[zb]], writes=[t_ps[b]])
            else:
                pb = (cnt - 2) % 2
                c.op("pe", lambda e, b=b, pb=pb, layer=layer: e.matmul(ps[b][0:64, :], wi[:, (layer - 1) * 64:layer * 64], hh[pb], start=True, stop=True),
                     reads=[t_k1, t_hh[pb]], writes=[t_ps[b]])
            c.op("dve", lambda e, b=b, layer=layer: e.tensor_scalar(out=arg[b], in0=ps[b][0:64, :], scalar1=cols[:, layer:layer + 1], scalar2=cols[:, 3:4], op0=ALU.add, op1=ALU.mult),
                 reads=[t_ps[b], t_k2], writes=[t_arg[b]])
            c.op("dve", lambda e, b=b: e.tensor_scalar(out=kk[b], in0=arg[b], scalar1=1.0 / (2 * math.pi), scalar2=MAGIC, op0=ALU.mult, op1=ALU.add),
                 reads=[t_arg[b]], writes=[t_kk[b]])
            c.op("dve", lambda e, b=b: e.tensor_scalar(out=kk[b], in0=kk[b], scalar1=-MAGIC, scalar2=2 * math.pi, op0=ALU.add, op1=ALU.mult),
                 reads=[t_kk[b]], writes=[t_kk[b]])
            c.op("dve", lambda e, b=b: e.tensor_tensor(out=arg[b], in0=arg[b], in1=kk[b], op=ALU.subtract),
                 reads=[t_arg[b], t_kk[b]], writes=[t_arg[b]])
            if layer < 2:
                c.op("act", lambda e, b=b: e.activation(out=hh[b], in_=arg[b], func=AF.Sin), reads=[t_arg[b]], writes=[t_hh[b]])
            else:
                c.op("act", lambda e, b=b, zb=zb: e.activation(out=h3[zb], in_=arg[b], func=AF.Sin), reads=[t_arg[b]], writes=[t_h3[zb]])
                c.op("sp", lambda e, zb=zb, n0=n0: e.dma_start(out=S["h3"][:, n0:n0 + TT], in_=h3[zb]), reads=[t_h3[zb]], dma=True)


def hy_filter_phase(c, S, s):
    c.new_phase()
    w3_d = c.din("hy_w3_%d" % s, [64, 4 * NCC * 128])
    nd_d = c.din("hy_negdelta", [128, NCC])
    td_d = c.din("hy_tdec", [1, NFFT])
    ff = HyFFT(c, inverse=False)
    w3 = c.sb(4 * NCC * 128, F32R, parts=64)
    t_w3 = T()
    c.op("pool", lambda e: e.dma_start(out=w3, in_=w3_d), writes=[t_w3], dma=True)
    nd = c.sb(NCC)
    t_nd = T()
    c.op("sp", lambda e: e.dma_start(out=nd, in_=nd_d), writes=[t_nd], dma=True)
    kT = c.sb(NFFT)
    t_kT = T()
    ut = c.sb(NFFT, F32R)
    t_ut = T()
    h3t = [c.sb(TT, F32R, parts=64) for _ in range(2)]
    t_h3t = [T(), T()]
    tdb = [c.sb(TT) for _ in range(2)]
    t_tdb = [T(), T()]
    dec = [c.sb(TT) for _ in range(2)]
    t_dec = [T(), T()]
    junk = c.sb(TT)
    t_junk = T()
    asum = c.sb(40)
    t_as = T()
    kst = [c.sb(1024) for _ in range(2)]
    t_kst = [T(), T()]
    P = c.psum
    ps_k, t_psk = P[4:6], [T(), T()]
    ps_tr, t_pstr = P[6:8], [T(), T()]
    cnt = 0
    for cc in range(NCC):
        for o in range(2):
            c.op("dve", lambda e: e.memset(asum[:, 0:40], 0.0), writes=[t_as])
            for it in range(NFFT // TT):
                n0 = it * TT
                b = cnt % 2
                cnt += 1
                dirn = 0 if n0 < L else 1
                col = (dirn * 2 + o) * NCC * 128 + cc * 128
                c.op("pool", lambda e, b=b, n0=n0: e.dma_start(out=h3t[b], in_=S["h3"][:, n0:n0 + TT]), writes=[t_h3t[b]], dma=True)
                c.op("sp", lambda e, b=b, n0=n0: e.dma_start(out=tdb[b], in_=td_d[:, n0:n0 + TT].broadcast_to([128, TT])), writes=[t_tdb[b]], dma=True)
                c.op("pe", lambda e, b=b, col=col: e.matmul(ps_k[b][:, :], w3[:, col:col + 128], h3t[b], start=True, stop=True), reads=[t_w3, t_h3t[b]], writes=[t_psk[b]])
                c.op("act", lambda e, b=b, cc=cc: e.activation(out=dec[b], in_=tdb[b], func=AF.Exp, scale=nd[:, cc:cc + 1]), reads=[t_tdb[b], t_nd], writes=[t_dec[b]])
                c.op("dve", lambda e, b=b, n0=n0: e.tensor_tensor(out=kT[:, n0:n0 + TT], in0=ps_k[b][:, :], in1=dec[b], op=ALU.mult), reads=[t_psk[b], t_dec[b]], writes=[t_kT])
                c.op("act", lambda e, n0=n0, it=it: e.activation(out=junk, in_=kT[:, n0:n0 + TT], func=AF.Abs, accum_out=asum[:, it:it + 1]), reads=[t_kT], writes=[t_junk, t_as])
            c.op("dve", lambda e: e.reduce_sum(out=asum[:, 32:33], in_=asum[:, 0:32], axis=AX.X), reads=[t_as], writes=[t_as])
            c.op("dve", lambda e: e.reciprocal(out=asum[:, 33:34], in_=asum[:, 32:33]), reads=[t_as], writes=[t_as])
            c.op("dve", lambda e: e.tensor_scalar(out=asum[:, 33:34], in0=asum[:, 33:34], scalar1=1.0 / NFFT, scalar2=0.0, op0=ALU.mult, op1=ALU.add), reads=[t_as], writes=[t_as])
            for q in range(4):
                eng = "dve" if q % 2 == 0 else "pool"
                c.op(eng, lambda e, q=q: e.tensor_scalar(out=kT[:, q * 4096:(q + 1) * 4096], in0=kT[:, q * 4096:(q + 1) * 4096], scalar1=asum[:, 33:34], scalar2=0.0, op0=ALU.mult, op1=ALU.add),
                     reads=[t_kT, t_as], writes=[t_kT])
            to_time_major(c, kT, t_kT, ut, t_ut, 128, ps_tr, t_pstr)
            def spec_out(g, ps_x, t_psx, o=o, cc=cc):
                kb = g % 2
                c.op("act", lambda e, kb=kb: e.activation(out=kst[kb][:, 0:512], in_=ps_x[0][:, :], func=AF.Copy), reads=[t_psx[0]], writes=[t_kst[kb]])
                c.op("act", lambda e, kb=kb: e.activation(out=kst[kb][:, 512:1024], in_=ps_x[1][:, :], func=AF.Copy), reads=[t_psx[1]], writes=[t_kst[kb]])
                c.op("sp", lambda e, kb=kb, g=g: e.dma_start(out=S["ks"][o, cc, g], in_=kst[kb]), reads=[t_kst[kb]], dma=True)
            ff.run({"ut": ut, "t_ut": [t_ut] * 32, "ka": 128, "spec_out": spec_out})


def hy_conv_phase(c, S, s):
    c.new_phase()
    fb_d = c.din("hy_fbias%d" % s, [128, 2 * NCC])
    ff = HyFFT(c, inverse=True)
    fb = c.sb(2 * NCC)
    t_fb = T()
    c.op("sp", lambda e: e.dma_start(out=fb, in_=fb_d), writes=[t_fb], dma=True)
    bufA = c.sb(L)
    bufB = c.sb(L)
    t_A, t_B = T(), T()
    off_ut = (c.off + 7) // 8 * 8
    ut = c.sb(64 * 256, F32R)
    yt = c.sb_at(off_ut, 64 * 256, F32)
    t_utg = [T() for _ in range(32)]
    PC = 512
    xp = [c.sb(PC) for _ in range(2)]
    t_xp = [T(), T()]
    kt = [c.sb(1024) for _ in range(3)]
    t_kt = [T(), T(), T()]
    P = c.psum
    ps_tr, t_pstr = [P[6], P[7]], [T(), T()]
    t_pstr[0] = ff.t_psy
    npc = 0
    nk = 0
    for cc in range(NCC):
        c.op("sp", lambda e, cc=cc: e.dma_start(out=bufA, in_=S["p"][2 * NCC + cc]), writes=[t_A], dma=True)
        src, t_src, dstb, t_dst = bufA, t_A, bufB, t_B
        for o in range(2):
            to_time_major(c, src, t_src, ut, t_utg, 64, ps_tr, t_pstr)
            ktmap = {}

            def pre(t, o=o, cc=cc, ktmap=ktmap):
                g = t - 2
                if 0 <= g < 32:
                    kb = g % 3
                    c.op("sp", lambda e, kb=kb, g=g: e.dma_start(out=kt[kb], in_=S["ks"][o, cc, g]), writes=[t_kt[kb]], dma=True)
                    ktmap[g] = (kt[kb], t_kt[kb])
            ff.run({"ut": ut, "t_ut": t_utg, "ka": 64, "yt": yt, "kt": lambda g, ktmap=ktmap: ktmap[g], "pre": pre})
            to_feature_major(c, yt, t_utg, dstb, t_dst, ps_tr, t_pstr)
            gate_chunk = (0 if o == 0 else NCC) + cc
            for pc in range(L // PC):
                pb = npc % 2
                npc += 1
                sl = slice(pc * PC, (pc + 1) * PC)
                c.op("sp", lambda e, pb=pb, gate_chunk=gate_chunk, sl=sl: e.dma_start(out=xp[pb], in_=S["p"][gate_chunk][:, sl]), writes=[t_xp[pb]], dma=True)
                eng = "pool"
                c.op("dve", lambda e, sl=sl, o=o, cc=cc, src=src, dstb=dstb: e.scalar_tensor_tensor(out=dstb[:, sl], in0=src[:, sl], scalar=fb[:, o * NCC + cc:o * NCC + cc + 1], in1=dstb[:, sl], op0=ALU.mult, op1=ALU.add),
                     reads=[t_src, t_dst, t_fb], writes=[t_dst])
                c.op(eng, lambda e, sl=sl, pb=pb, dstb=dstb: e.tensor_tensor(out=dstb[:, sl], in0=dstb[:, sl], in1=xp[pb], op=ALU.mult),
                     reads=[t_dst, t_xp[pb]], writes=[t_dst])
            src, t_src, dstb, t_dst = dstb, t_dst, src, t_src
        for hf in range(2):
            t_z = T()
            c.op("sp", lambda e, cc=cc, src=src, hf=hf: e.dma_start(out=S["zin"][hf][cc].ap(), in_=src[:, hf * NTL:(hf + 1) * NTL]), reads=[t_src], writes=[t_z], dma=True)
            c.op("pool", lambda e, cc=cc, hf=hf: e.collective_compute("AllGather", ALU.bypass, replica_groups=PAIRS, ins=[S["zin"][hf][cc].ap()], outs=[S["zg"][hf][cc].ap()]),
                 reads=[t_z], writes=[T()], cc=True)


def hy_outproj_phase(c, src, dst, S, s):
    c.new_phase()
    w_d = c.din("hy_wout%d" % s, [8, 128, D])
    b_d = c.din("hy_bout%d" % s, [1, D])
    sel_d = c.din("hy_sel", [128, 2])
    wo = [c.sb(D, F32R) for _ in range(8)]
    t_wo = [T() for _ in range(8)]
    for k in range(8):
        c.op("pool", lambda e, k=k: e.dma_start(out=wo[k], in_=w_d[k]), writes=[t_wo[k]], dma=True)
    brow = c.sb(D)
    t_b = T()
    c.op("sp", lambda e: e.dma_start(out=brow, in_=b_d.broadcast_to([128, D])), writes=[t_b], dma=True)
    sel = c.sb(2)
    t_sel = T()
    c.op("sp", lambda e: e.dma_start(out=sel, in_=sel_d), writes=[t_sel], dma=True)
    zT = [[c.sb(TT, F32R) for _ in range(8)] for _ in range(2)]
    t_zT = [[T() for _ in range(8)] for _ in range(2)]
    zA = [c.sb(TT) for _ in range(2)]
    zB = [c.sb(TT) for _ in range(2)]
    t_zA = [T(), T()]
    t_zB = [T(), T()]
    xin = [c.sb(D) for _ in range(4)]
    t_xin = [T() for _ in range(4)]
    P = c.psum
    ps, t_ps = P[0:4], [T() for _ in range(4)]
    cnt = 0
    nz = 0
    for it in range(NTL // TT):
        r0 = it * TT
        zb = it % 2
        for k in range(8):
            rank, cc = k // NCC, k % NCC
            b = nz % 2
            nz += 1
            c.op("sp", lambda e, b=b, rank=rank, cc=cc, r0=r0: e.dma_start(out=zA[b], in_=S["zg"][0][cc].ap()[rank * 128:(rank + 1) * 128, r0:r0 + TT]), writes=[t_zA[b]], dma=True)
            c.op("sp", lambda e, b=b, rank=rank, cc=cc, r0=r0: e.dma_start(out=zB[b], in_=S["zg"][1][cc].ap()[rank * 128:(rank + 1) * 128, r0:r0 + TT]), writes=[t_zB[b]], dma=True)
            c.op("dve", lambda e, b=b: e.tensor_scalar(out=zA[b], in0=zA[b], scalar1=sel[:, 0:1], scalar2=0.0, op0=ALU.mult, op1=ALU.add), reads=[t_zA[b], t_sel], writes=[t_zA[b]])
            c.op("dve", lambda e, b=b, k=k, zb=zb: e.scalar_tensor_tensor(out=zT[zb][k], in0=zB[b], scalar=sel[:, 1:2], in1=zA[b], op0=ALU.mult, op1=ALU.add),
                 reads=[t_zA[b], t_zB[b], t_sel], writes=[t_zT[zb][k]])
        for tb in range(4):
            xb = tb
            c.op("sp", lambda e, xb=xb, tb=tb, r0=r0: e.dma_start(out=xin[xb], in_=src[r0 + tb * 128:r0 + (tb + 1) * 128, :]), writes=[t_xin[xb]], dma=True)
            for dh in range(2):
                b = cnt % 4
                cnt += 1
                sl = slice(dh * 512, (dh + 1) * 512)
                for k in range(8):
                    c.op("pe", lambda e, k=k, zb=zb, tb=tb, sl=sl, b=b: e.matmul(ps[b][:, :], zT[zb][k][:, tb * 128:(tb + 1) * 128], wo[k][:, sl], start=(k == 0), stop=(k == 7)),
                         reads=[t_zT[zb][k], t_wo[k]], writes=[t_ps[b]])
                c.op("dve", lambda e, xb=xb, sl=sl, b=b: e.tensor_tensor(out=xin[xb][:, sl], in0=ps[b][:, :], in1=xin[xb][:, sl], op=ALU.add),
                     reads=[t_ps[b], t_xin[xb]], writes=[t_xin[xb]])
                c.op("pool", lambda e, xb=xb, sl=sl: e.tensor_tensor(out=xin[xb][:, sl], in0=xin[xb][:, sl], in1=brow[:, sl], op=ALU.add),
                     reads=[t_xin[xb], t_b], writes=[t_xin[xb]])
            c.op("sp", lambda e, xb=xb, tb=tb, r0=r0: e.dma_start(out=dst[r0 + tb * 128:r0 + (tb + 1) * 128, :], in_=xin[xb]), reads=[t_xin[xb]], dma=True)


def gather_x(c, src):
    c.new_phase()
    nc = c.nc
    outs = []
    for j in range(NTL // 512):
        cin = nc.dram_tensor("xg_in%d" % j, [512, D], F32)
        cg = nc.dram_tensor("xg_out%d" % j, [1024, D], F32)
        t_c = T()
        c.op("sp", lambda e, j=j, cin=cin: e.dma_start(out=cin.ap(), in_=src[j * 512:(j + 1) * 512, :]), writes=[t_c], dma=True)
        c.op("pool", lambda e, cin=cin, cg=cg: e.collective_compute("AllGather", ALU.bypass, replica_groups=PAIRS, ins=[cin.ap()], outs=[cg.ap()]),
             reads=[t_c], writes=[T()], cc=True)
        outs.append(cg)

    def rows_fn(R):
        rank, rr = R // NTL, R % NTL
        j, i = rr // 512, rr % 512
        return outs[j].ap()[rank * 512 + i:rank * 512 + i + 128, :]
    return rows_fn


def hyena_layer(c, rows_fn, src, dst, s):
    if not hasattr(c, "hyS"):
        c.hyS = hy_scratch(c, "")
    S = c.hyS
    hy_inproj_phase(c, rows_fn, S, s)
    hy_conv3_phase(c, S, s)
    hy_mlp_phase(c, S, s)
    hy_filter_phase(c, S, s)
    hy_conv_phase(c, S, s)
    hy_outproj_phase(c, src, dst, S, s)
    return S


def hyena_host(inputs, s, li, h):
    g = np.asarray(inputs["norm_mix"][li], np.float32)
    CH = NCC * 128
    csl = np.concatenate([np.arange(t * D + h * CH, t * D + (h + 1) * CH) for t in range(3)])
    b_in = np.asarray(inputs["hy_b_in"][s], np.float32)[csl]
    cw = np.asarray(inputs["hy_conv_w"][s], np.float32)[:, csl]
    cb = np.asarray(inputs["hy_conv_b"][s], np.float32)[csl]
    c3 = np.stack([b_in, cw[0], cw[1], cw[2], cb], axis=-1).reshape(3 * NCC, 128, 5).transpose(1, 0, 2).reshape(128, 3 * NCC * 5)
    mlpc = np.stack([np.asarray(inputs["hy_f_b1"][s], np.float32), np.asarray(inputs["hy_f_bi"][s][0], np.float32),
                     np.asarray(inputs["hy_f_bi"][s][1], np.float32), np.asarray(inputs["hy_f_freq"][s], np.float32)], axis=-1)
    wi = np.asarray(inputs["hy_f_wi"][s], np.float32)
    fbias = np.asarray(inputs["hy_f_bias"][s], np.float32)[:, h * CH:(h + 1) * CH].reshape(2, NCC, 128).transpose(2, 0, 1).reshape(128, 2 * NCC)
    w3 = np.asarray(inputs["hy_f_w3"][s], np.float32).reshape(64, 4, D)[:, :, h * CH:(h + 1) * CH].reshape(64, 4 * CH)
    sel = np.zeros((128, 2), np.float32)
    sel[:, h] = 1.0
    m = {"hy_g%d" % s: np.ascontiguousarray(g.reshape(8, 128).T),
         "hy_win%d" % s: np.ascontiguousarray(np.asarray(inputs["hy_w_in"][s], np.float32)[:, csl].reshape(8, 128, 3 * CH)),
         "hy_c3cols%d" % s: np.ascontiguousarray(c3),
         "hy_w1_%d" % s: np.asarray(inputs["hy_f_w1"][s], np.float32),
         "hy_wi_%d" % s: np.ascontiguousarray(np.concatenate([wi[0], wi[1]], axis=1)),
         "hy_mlpcols%d" % s: np.ascontiguousarray(mlpc),
         "hy_w3_%d" % s: np.ascontiguousarray(w3),
         "hy_fbias%d" % s: np.ascontiguousarray(fbias),
         "hy_wout%d" % s: np.ascontiguousarray(np.asarray(inputs["hy_w_out"][s], np.float32).reshape(8, 128, D)),
         "hy_bout%d" % s: np.asarray(inputs["hy_b_out"][s], np.float32).reshape(1, D),
         "hy_sel": sel}
    hc = hy_consts()
    hc["hy_negdelta"] = np.ascontiguousarray(hc["hy_negdelta_full"][h * CH:(h + 1) * CH].reshape(NCC, 128).T)
    del hc["hy_negdelta_full"]
    m.update(hc)
    return m


def build_program(plan):
    nc = bass.Bass("TRN2", target_bir_lowering=False)
    st = contextlib.ExitStack()
    with st:
        c = Ctx(nc, st)
        x_full = c.din("x", [L, D])
        x_loc = c.din("xloc", [NTL, D])
        y_out = nc.dram_tensor("y", [NTL, D], F32, kind="ExternalOutput").ap()
        bufs = [c.dscratch("xa", [NTL, D]), c.dscratch("xb", [NTL, D])]
        load_consts(c)
        cur = x_loc
        for pi, ph in enumerate(plan):
            last = pi == len(plan) - 1
            dst = y_out if last else bufs[pi % 2]
            if ph.startswith("ffn"):
                ffn_phase(c, cur, dst, int(ph[3:]), ntok=NTL)
            elif ph == "pool2":
                pool_phase(c, cur, dst)
            elif ph == "hyena0":
                hyena_layer(c, (lambda R: x_full[R:R + 128, :]) if pi == 0 else gather_x(c, cur), cur, dst, 0)
            elif ph == "hyena3":
                hyena_layer(c, gather_x(c, cur), cur, dst, 1)
            elif ph == "attn1":
                S = attn_scratch(c)
                hg = halo_exchange(c, cur)
                attn_qkv_phase(c, cur, S, hg)
                attn_core_phase(c, cur, dst, S)
            else:
                raise ValueError(ph)
            cur = dst
        c.mk.emit()
    return nc


def host_inputs(inputs, plan, h):
    m = {"ident": np.eye(128, dtype=np.float32)}
    for ph in plan:
        if ph.startswith("ffn"):
            m.update(ffn_host(inputs, int(ph[3:])))
        elif ph == "pool2":
            m.update(pool_host(inputs, h))
        elif ph == "attn1":
            m.update(attn_host(inputs, h))
        elif ph == "hyena0":
            m.update(hyena_host(inputs, 0, 0, h))
        elif ph == "hyena3":
            m.update(hyena_host(inputs, 1, 3, h))
    return m


DEBUG_HY = False
FULL_PLAN = ["hyena0", "ffn0", "attn1", "ffn1", "pool2", "ffn2", "hyena3", "ffn3"]


def run_plan(inputs, plan, x_override=None, trace=False):
    nc = build_program(plan)
    shared = [host_inputs(inputs, plan, h) for h in range(2)]
    x = np.asarray(inputs["x"], np.float32) if x_override is None else x_override
    nb = x.shape[0]
    in_maps = []
    for core in range(8):
        b, h = (core // 2) % nb, core % 2
        m = dict(shared[h])
        m["x"] = np.ascontiguousarray(x[b])
        m["xloc"] = np.ascontiguousarray(x[b, h * NTL:(h + 1) * NTL])
        in_maps.append(m)
    res = run_bass_kernel_spmd(nc, in_maps, core_ids=list(range(8)), trace=trace)
    out = np.stack([np.concatenate([res.results[2 * b]["y"], res.results[2 * b + 1]["y"]], axis=0) for b in range(nb)], axis=0)
    return out, res


def kernel(**inputs):
    out, _ = run_plan(inputs, FULL_PLAN)
    return out.astype(np.float32)
```

```python
import contextlib
import math

import numpy as np
import concourse.bass as bass
import concourse.mybir as mybir
from concourse.bass_utils import run_bass_kernel_spmd

F32 = mybir.dt.float32
F32R = mybir.dt.float32r
AF = mybir.ActivationFunctionType
ALU = mybir.AluOpType
AX = mybir.AxisListType

D = 1024
L = 8192
DFF = 2816
NF = DFF // 128
EPS = 1e-6
TT = 512
NBLK = L // 128
NSLOT = 20
SAME_ENGINE_SYNC = True


class T:
    __slots__ = ("w", "r")

    def __init__(self):
        self.w = []
        self.r = []


class Op:
    __slots__ = ("eng", "idx", "fn", "deps", "dma", "dj", "sig", "waited", "q", "inc")

    def __init__(self, eng, idx, fn, dma):
        self.eng = eng
        self.idx = idx
        self.fn = fn
        self.deps = ()
        self.dma = dma
        self.q = eng
        self.inc = 16
        self.dj = None
        self.sig = None
        self.waited = False


class MK:
    ENGS = ("pe", "act", "dve", "pool", "sp")

    def __init__(self, nc):
        self.nc = nc
        self.ops = {e: [] for e in self.ENGS}
        self.dma_ops = {e: [] for e in self.ENGS + ("cc",)}
        self.last_c = {e: None for e in self.ENGS}
        self.bar = None
        self.bar_seen = {e: True for e in self.ENGS}

    def barrier(self):
        deps = set()
        for e in self.ENGS:
            if self.last_c[e] is not None:
                deps.add(self.last_c[e])
            for o in self.dma_ops[e][-NSLOT:]:
                deps.add(o)
        for o in self.dma_ops["cc"][-NSLOT:]:
            deps.add(o)
        self.bar = deps
        self.bar_seen = {e: False for e in self.ENGS}

    def op(self, eng, fn, reads=(), writes=(), dma=False, cc=False):
        lst = self.ops[eng]
        dma = dma or cc
        o = Op(eng, len(lst), fn, dma)
        if cc:
            o.q = "cc"
            o.inc = 1
        lst.append(o)
        deps = set()
        if not self.bar_seen[eng]:
            self.bar_seen[eng] = True
            deps |= self.bar
        for t in reads:
            deps.update(t.w)
        for t in writes:
            deps.update(t.w)
            deps.update(t.r)
        if dma:
            dl = self.dma_ops[o.q]
            o.dj = len(dl)
            dl.append(o)
            if o.dj >= NSLOT:
                deps.add(dl[o.dj - NSLOT])
        else:
            self.last_c[eng] = o
        deps.discard(o)
        o.deps = deps
        for t in reads:
            if dma:
                t.r.append(o)
            else:
                t.r = [x for x in t.r if x.dma or x.eng != eng] + [o]
        for t in writes:
            t.w = [o]
            t.r = []
        return o

    @staticmethod
    def _skip(d, o):
        return (not d.dma) and d.eng == o.eng and (not o.dma) and (d.eng == "pe" or not SAME_ENGINE_SYNC)

    def emit(self):
        nc = self.nc
        for e in self.ENGS:
            for o in self.ops[e]:
                for d in o.deps:
                    if d.dma or self._skip(d, o):
                        continue
                    d.waited = True
        for e in self.ENGS:
            c = 0
            for o in self.ops[e]:
                if not o.dma and o.waited:
                    c += 1
                    o.sig = c
        with contextlib.ExitStack() as st:
            csem = {e: st.enter_context(nc.semaphore("c_" + e)) for e in ("pe", "act", "dve", "pool")}
            dsem = {}
            for q in self.ENGS + ("cc",):
                n = len(self.dma_ops[q])
                if n:
                    dsem[q] = [st.enter_context(nc.semaphore("d_%s_%d" % (q, i))) for i in range(min(NSLOT, n))]
            block = st.enter_context(nc.Block())
            mk = self

            def run(ename):
                def body(e):
                    known_c = {}
                    known_d = set()
                    for o in mk.ops[ename]:
                        cw = {}
                        for d in o.deps:
                            if d.dma:
                                key = (d.q, d.dj)
                                if key in known_d:
                                    continue
                                known_d.add(key)
                                e.wait_ge(dsem[d.q][d.dj % NSLOT], d.inc * (d.dj // NSLOT + 1))
                            else:
                                if mk._skip(d, o):
                                    continue
                                if known_c.get(d.eng, 0) >= d.sig:
                                    continue
                                cw[d.eng] = max(cw.get(d.eng, 0), d.sig)
                        for en, v in cw.items():
                            known_c[en] = v
                            e.wait_ge(csem[en], v)
                        ins = o.fn(e)
                        if o.dma:
                            ins.then_inc(dsem[o.q][o.dj % NSLOT], o.inc)
                        elif o.sig is not None:
                            ins.then_inc(csem[ename], 1)
                    tail = list(mk.dma_ops[ename][-NSLOT:])
                    if ename == "pool":
                        tail += mk.dma_ops["cc"][-NSLOT:]
                    for o in tail:
                        if (o.q, o.dj) not in known_d:
                            e.wait_ge(dsem[o.q][o.dj % NSLOT], o.inc * (o.dj // NSLOT + 1))
                return body

            if self.ops["sp"]:
                block.sync(run("sp"))
            if self.ops["pe"]:
                block.tensor(run("pe"))
            if self.ops["act"]:
                block.scalar(run("act"))
            if self.ops["dve"]:
                block.vector(run("dve"))
            if self.ops["pool"]:
                block.gpsimd(run("pool"))


ARENA = 51 * 1024


class Ctx:
    def __init__(self, nc, st):
        self.nc = nc
        self.mk = MK(nc)
        self.sb_base = 16512
        self.ntens = 0
        self.psum = [st.enter_context(nc.psum_tensor("psb%d" % i, [128, 512], F32)) for i in range(8)]
        self.off = 0
        self.base = 0
        self.dram = {}

    def din(self, name, shape, dt=F32):
        if name in self.dram:
            return self.dram[name].ap()
        t = self.nc.dram_tensor(name, list(shape), dt, kind="ExternalInput")
        self.dram[name] = t
        return t.ap()

    def dscratch(self, name, shape, dt=F32):
        t = self.nc.dram_tensor(name, list(shape), dt)
        return t.ap()

    def sb(self, cols, dt=F32, parts=128):
        self.off = (self.off + 7) // 8 * 8
        self.ntens += 1
        t = self.nc.alloc_sbuf_tensor_at("t%d" % self.ntens, [parts, cols], dt, offset=self.sb_base + 4 * self.off)
        self.off += cols
        assert self.off <= ARENA, ("arena overflow", self.off)
        return t[:, :]

    def sb_at(self, off, cols, dt=F32, parts=128):
        self.ntens += 1
        t = self.nc.alloc_sbuf_tensor_at("t%d" % self.ntens, [parts, cols], dt, offset=self.sb_base + 4 * off)
        return t[:, :]

    def new_phase(self):
        self.mk.barrier()
        self.off = self.base

    def op(self, *a, **k):
        return self.mk.op(*a, **k)


def load_consts(c):
    c.ident = c.sb(128)
    c.t_const = T()
    ident_d = c.din("ident", [128, 128])
    c.op("sp", lambda e: e.dma_start(out=c.ident, in_=ident_d), writes=[c.t_const], dma=True)
    c.epsc = c.sb(1)
    c.op("dve", lambda e: e.memset(c.epsc, EPS), writes=[c.t_const])
    c.base = c.off


class Front:
    def __init__(self, c, want_hT=True, xn_dt=F32):
        self.c = c
        self.xin = [c.sb(D) for _ in range(4)]
        self.t_xin = [T() for _ in range(4)]
        self.xn = [c.sb(D, xn_dt) for _ in range(4)]
        self.t_xn = [T() for _ in range(4)]
        self.ss = c.sb(4)
        self.t_ss = [T() for _ in range(4)]
        self.rstd = c.sb(4)
        self.t_rstd = [T() for _ in range(4)]
        if want_hT:
            self.hT = [c.sb(TT, F32R) for _ in range(8)]
            self.t_hT = [T() for _ in range(8)]
        self.tcount = 0

    def load_norm(self, src, r0, blocks=(0, 1, 2, 3), rows=None):
        c = self.c
        if rows is None:
            rows = [src[r0 + tb * 128:r0 + (tb + 1) * 128, :] for tb in range(4)]
        for tb in blocks:
            c.op("sp", lambda e, tb=tb, ap=rows[tb]: e.dma_start(out=self.xin[tb], in_=ap),
                 writes=[self.t_xin[tb]], dma=True)
        for tb in blocks:
            c.op("dve", lambda e, tb=tb: e.scalar_tensor_tensor(out=self.xn[tb], in0=self.xin[tb], scalar=1.0, in1=self.xin[tb], op0=ALU.mult, op1=ALU.mult, accum_out=self.ss[:, tb:tb + 1]),
                 reads=[self.t_xin[tb]], writes=[self.t_xn[tb], self.t_ss[tb]])
        for tb in blocks:
            c.op("act", lambda e, tb=tb: e.activation(out=self.rstd[:, tb:tb + 1], in_=self.ss[:, tb:tb + 1], func=AF.Sqrt, scale=1.0 / D, bias=c.epsc),
                 reads=[self.t_ss[tb], c.t_const], writes=[self.t_rstd[tb]])
        for tb in blocks:
            c.op("dve", lambda e, tb=tb: e.reciprocal(out=self.rstd[:, tb:tb + 1], in_=self.rstd[:, tb:tb + 1]),
                 reads=[self.t_rstd[tb]], writes=[self.t_rstd[tb]])
            if getattr(self, "scale_on_act", False):
                c.op("act", lambda e, tb=tb: e.activation(out=self.xn[tb], in_=self.xin[tb], func=AF.Copy, scale=self.rstd[:, tb:tb + 1]),
                     reads=[self.t_xin[tb], self.t_rstd[tb]], writes=[self.t_xn[tb]])
            else:
                c.op("dve", lambda e, tb=tb: e.tensor_scalar(out=self.xn[tb], in0=self.xin[tb], scalar1=self.rstd[:, tb:tb + 1], scalar2=0.0, op0=ALU.mult, op1=ALU.add),
                     reads=[self.t_xin[tb], self.t_rstd[tb]], writes=[self.t_xn[tb]])

    def transpose(self, gcol, t_g, pbanks, t_pb):
        c = self.c
        for k in range(8):
            b = self.tcount % len(pbanks)
            self.tcount += 1
            for tb in range(4):
                c.op("pe", lambda e, tb=tb, k=k, b=b: e.transpose(out=pbanks[b][:, tb * 128:(tb + 1) * 128], in_=self.xn[tb][:, k * 128:(k + 1) * 128], identity=c.ident),
                     reads=[self.t_xn[tb], c.t_const], writes=[t_pb[b]])
            c.op("act", lambda e, k=k, b=b: e.activation(out=self.hT[k], in_=pbanks[b][:, :], func=AF.Copy, scale=gcol[:, k:k + 1]),
                 reads=[t_pb[b], t_g], writes=[self.t_hT[k]])


def ffn_phase(c, src, dst, li, ntok=L):
    c.new_phase()
    g_d = c.din("ffn_g%d" % li, [128, 8])
    wgu_d = c.din("ffn_wgu%d" % li, [NF, 128, 2048])
    wd_d = c.din("ffn_wd%d" % li, [NF, 128, D])
    gcol = c.sb(8)
    t_g = T()
    c.op("sp", lambda e: e.dma_start(out=gcol, in_=g_d), writes=[t_g], dma=True)
    fr = Front(c)
    aT = [c.sb(TT, F32R) for _ in range(NF)]
    t_aT = [T() for _ in range(NF)]
    wd = [c.sb(D, F32R) for _ in range(NF)]
    t_wd = [T() for _ in range(NF)]
    wgu = [c.sb(2048, F32R) for _ in range(2)]
    t_wgu = [T() for _ in range(2)]
    sg = [c.sb(TT) for _ in range(2)]
    t_sg = [T() for _ in range(2)]
    P = c.psum
    ps_t, ps_g, ps_u, ps_d = P[0:2], P[2:4], P[4:6], P[6:8]
    t_pst = [T(), T()]
    t_psg = [T(), T()]
    t_psu = [T(), T()]
    t_psd = [T(), T()]
    cnt = {"gu": 0, "d": 0}
    for it in range(ntok // TT):
        r0 = it * TT
        fr.load_norm(src, r0)
        fr.transpose(gcol, t_g, ps_t, t_pst)
        hT, t_hT = fr.hT, fr.t_hT
        for f in range(NF):
            wb = f % 2
            c.op("pool", lambda e, f=f, wb=wb: e.dma_start(out=wgu[wb], in_=wgu_d[f]), writes=[t_wgu[wb]], dma=True)
            c.op("pool", lambda e, f=f: e.dma_start(out=wd[f], in_=wd_d[f]), writes=[t_wd[f]], dma=True)
            b = cnt["gu"] % 2
            cnt["gu"] += 1
            for k in range(8):
                c.op("pe", lambda e, k=k, wb=wb, b=b: e.matmul(ps_g[b][:, :], wgu[wb][:, k * 256:k * 256 + 128], hT[k], start=(k == 0), stop=(k == 7)),
                     reads=[t_wgu[wb], t_hT[k]], writes=[t_psg[b]])
            for k in range(8):
                c.op("pe", lambda e, k=k, wb=wb, b=b: e.matmul(ps_u[b][:, :], wgu[wb][:, k * 256 + 128:k * 256 + 256], hT[k], start=(k == 0), stop=(k == 7)),
                     reads=[t_wgu[wb], t_hT[k]], writes=[t_psu[b]])
            c.op("act", lambda e, b=b: e.activation(out=sg[b], in_=ps_g[b][:, :], func=AF.Silu), reads=[t_psg[b]], writes=[t_sg[b]])
            c.op("dve", lambda e, b=b, f=f: e.tensor_tensor(out=aT[f], in0=sg[b], in1=ps_u[b][:, :], op=ALU.mult),
                 reads=[t_sg[b], t_psu[b]], writes=[t_aT[f]])
        for tb in range(4):
            for dh in range(2):
                b = cnt["d"] % 2
                cnt["d"] += 1
                for f in range(NF):
                    c.op("pe", lambda e, f=f, tb=tb, dh=dh, b=b: e.matmul(ps_d[b][:, :], aT[f][:, tb * 128:(tb + 1) * 128], wd[f][:, dh * 512:(dh + 1) * 512], start=(f == 0), stop=(f == NF - 1)),
                         reads=[t_aT[f], t_wd[f]], writes=[t_psd[b]])
                c.op("dve", lambda e, tb=tb, dh=dh, b=b: e.tensor_tensor(out=fr.xin[tb][:, dh * 512:(dh + 1) * 512], in0=ps_d[b][:, :], in1=fr.xin[tb][:, dh * 512:(dh + 1) * 512], op=ALU.add),
                     reads=[t_psd[b], fr.t_xin[tb]], writes=[fr.t_xin[tb]])
            c.op("sp", lambda e, tb=tb, r0=r0: e.dma_start(out=dst[r0 + tb * 128:r0 + (tb + 1) * 128, :], in_=fr.xin[tb]), reads=[fr.t_xin[tb]], dma=True)


def ffn_host(inputs, li):
    wg = np.asarray(inputs["ff_w_gate"][li], np.float32)
    wu = np.asarray(inputs["ff_w_up"][li], np.float32)
    wdn = np.asarray(inputs["ff_w_down"][li], np.float32)
    g = np.asarray(inputs["norm_ffn"][li], np.float32)
    wgu = np.stack([wg.reshape(8, 128, NF, 128), wu.reshape(8, 128, NF, 128)], axis=0)
    wgu = np.ascontiguousarray(wgu.transpose(3, 2, 1, 0, 4)).reshape(NF, 128, 2048)
    return {"ffn_g%d" % li: np.ascontiguousarray(g.reshape(8, 128).T),
            "ffn_wgu%d" % li: wgu,
            "ffn_wd%d" % li: np.ascontiguousarray(wdn.reshape(NF, 128, D))}


POOL_WINDOWS = (2, 4, 8, 16)


NTL = L // 2
NBL = NTL // 128


def pool_consts(h):
    mats = np.zeros((4, 9, 128, 128), np.float32)
    t = np.arange(L)
    for g, w in enumerate(POOL_WINDOWS):
        r = w // 2
        lo = np.clip(t - r, 0, L)
        hi = np.clip(t + r + 1, 0, L)
        inv = (1.0 / (hi - lo)).astype(np.float32)

        def blk(bi, bj):
            if bi < 0 or bi >= NBLK:
                return np.zeros((128, 128), np.float32)
            tp = np.arange(bi * 128, (bi + 1) * 128)[:, None]
            tt = np.arange(bj * 128, (bj + 1) * 128)[None, :]
            m = ((tp >= lo[tt]) & (tp < hi[tt])).astype(np.float32) * inv[tt]
            return m - (tp == tt).astype(np.float32)
        first = h * NBL
        last = h * NBL + NBL - 1
        for j, bj in enumerate((5, first, last)):
            for k in range(3):
                mats[g, j * 3 + k] = blk(bj - 1 + k, bj)
    return np.ascontiguousarray(mats.transpose(2, 0, 1, 3)).reshape(128, 4 * 9 * 128)


def halo_exchange(c, src):
    c.new_phase()
    nc = c.nc
    c.nhalo = getattr(c, "nhalo", 0) + 1
    hin = nc.dram_tensor("halo_in%d" % c.nhalo, [256, D], F32)
    hg = nc.dram_tensor("halo_g%d" % c.nhalo, [512, D], F32)
    t_h = T()
    c.op("sp", lambda e: e.dma_start(out=hin.ap()[0:128, :], in_=src[0:128, :]), writes=[t_h], dma=True)
    t_h2 = T()
    c.op("sp", lambda e: e.dma_start(out=hin.ap()[128:256, :], in_=src[NTL - 128:NTL, :]), writes=[t_h2], dma=True)
    c.op("pool", lambda e: e.collective_compute("AllGather", ALU.bypass, replica_groups=PAIRS, ins=[hin.ap()], outs=[hg.ap()]),
         reads=[t_h, t_h2], writes=[T()], cc=True)
    return hg.ap()


PAIRS = [[0, 1], [2, 3], [4, 5], [6, 7]]


def pool_phase(c, src, dst):
    hg = halo_exchange(c, src)
    c.new_phase()
    pm_d = c.din("pl_mats", [128, 36 * 128])
    g_d = c.din("pl_g", [128, 8])
    w_d = c.din("pl_wt", [128, 8, 256])
    b_d = c.din("pl_b", [1, D])
    s_d = c.din("pl_scale", [1, D])
    pm = c.sb(36 * 128, F32R)
    gcol = c.sb(8)
    wg = c.sb(8 * 256, F32R)
    brow = c.sb(D)
    srow = c.sb(D)
    t_k = T()
    c.op("pool", lambda e: e.dma_start(out=pm, in_=pm_d), writes=[t_k], dma=True)
    t_k2 = T()
    c.op("pool", lambda e: e.dma_start(out=wg, in_=w_d.rearrange("p a b -> p (a b)")), writes=[t_k2], dma=True)
    t_k3 = T()
    c.op("sp", lambda e: e.dma_start(out=gcol, in_=g_d), writes=[t_k3], dma=True)
    t_k4 = T()
    c.op("sp", lambda e: e.dma_start(out=brow, in_=b_d.broadcast_to([128, D])), writes=[t_k4], dma=True)
    t_k5 = T()
    c.op("sp", lambda e: e.dma_start(out=srow, in_=s_d.broadcast_to([128, D])), writes=[t_k5], dma=True)
    RING = 4
    xin = [c.sb(D) for _ in range(RING)]
    t_xin = [T() for _ in range(RING)]
    xn = [c.sb(D, F32R) for _ in range(RING)]
    t_xn = [T() for _ in range(RING)]
    junk = c.sb(D)
    t_junk = T()
    ss = c.sb(RING)
    rstd = c.sb(RING)
    t_ss = [T() for _ in range(RING)]
    t_rstd = [T() for _ in range(RING)]
    dT = [c.sb(128, F32R) for _ in range(8)]
    t_dT = [T() for _ in range(8)]
    yt = [c.sb(D) for _ in range(2)]
    t_yt = [T(), T()]
    P = c.psum
    ps_p = P[0:4]
    t_psp = [T() for _ in range(4)]
    ps_y = [P[4:6], P[6:8]]
    t_psy = [T(), T()]

    def rows(i):
        if i == 0:
            return hg[128:256, :]
        if i == NBL + 1:
            return hg[256:384, :]
        return src[(i - 1) * 128:i * 128, :]

    def prep(i):
        s = i % RING
        c.op("sp", lambda e, s=s, ap=rows(i): e.dma_start(out=xin[s], in_=ap), writes=[t_xin[s]], dma=True)
        c.op("act", lambda e, s=s: e.activation(out=junk, in_=xin[s], func=AF.Square, accum_out=ss[:, s:s + 1]),
             reads=[t_xin[s]], writes=[t_junk, t_ss[s]])
        c.op("act", lambda e, s=s: e.activation(out=rstd[:, s:s + 1], in_=ss[:, s:s + 1], func=AF.Sqrt, scale=1.0 / D, bias=c.epsc),
             reads=[t_ss[s], c.t_const], writes=[t_rstd[s]])
        c.op("dve", lambda e, s=s: e.reciprocal(out=rstd[:, s:s + 1], in_=rstd[:, s:s + 1]), reads=[t_rstd[s]], writes=[t_rstd[s]])
        c.op("act", lambda e, s=s: e.activation(out=xn[s], in_=xin[s], func=AF.Copy, scale=rstd[:, s:s + 1]),
             reads=[t_xin[s], t_rstd[s]], writes=[t_xn[s]])

    prep(0)
    prep(1)
    for i in range(1, NBL + 1):
        prep(i + 1)
        mbase = 3 if i == 1 else (6 if i == NBL else 0)
        terms = [(-1, mbase), (0, mbase + 1), (1, mbase + 2)]
        for g in range(4):
            pb = g
            for j in range(2):
                cc = 2 * g + j
                for ti, (rel, mi) in enumerate(terms):
                    s = (i + rel) % RING
                    c.op("pe", lambda e, s=s, cc=cc, g=g, mi=mi, j=j, pb=pb, ti=ti, nt=len(terms):
                         e.matmul(ps_p[pb][:, j * 128:(j + 1) * 128], xn[s][:, cc * 128:(cc + 1) * 128], pm[:, (g * 9 + mi) * 128:(g * 9 + mi + 1) * 128], start=(ti == 0), stop=(ti == nt - 1)),
                         reads=[t_xn[s], t_k], writes=[t_psp[pb]])
            for j in range(2):
                cc = 2 * g + j
                c.op("act", lambda e, cc=cc, j=j, pb=pb: e.activation(out=dT[cc], in_=ps_p[pb][:, j * 128:(j + 1) * 128], func=AF.Copy, scale=gcol[:, cc:cc + 1]),
                     reads=[t_psp[pb], t_k3], writes=[t_dT[cc]])
        yb = i % 2
        for g in range(4):
            for j in range(2):
                cc = 2 * g + j
                c.op("pe", lambda e, cc=cc, g=g, j=j, yb=yb: e.matmul(ps_y[yb][g // 2][:, (g % 2) * 256:(g % 2 + 1) * 256], dT[cc], wg[:, cc * 256:(cc + 1) * 256], start=(j == 0), stop=(j == 1)),
                     reads=[t_dT[cc], t_k2], writes=[t_psy[yb]])
        s = i % RING
        for hh in range(2):
            sl = slice(hh * 512, (hh + 1) * 512)
            c.op("dve", lambda e, yb=yb, hh=hh, sl=sl: e.tensor_tensor(out=yt[yb][:, sl], in0=ps_y[yb][hh][:, :], in1=brow[:, sl], op=ALU.add),
                 reads=[t_psy[yb], t_k4], writes=[t_yt[yb]])
        c.op("pool", lambda e, yb=yb: e.tensor_tensor(out=yt[yb], in0=yt[yb], in1=srow, op=ALU.mult), reads=[t_yt[yb], t_k5], writes=[t_yt[yb]])
        c.op("pool", lambda e, yb=yb, s=s: e.tensor_tensor(out=yt[yb], in0=yt[yb], in1=xin[s], op=ALU.add), reads=[t_yt[yb], t_xin[s]], writes=[t_yt[yb]])
        c.op("sp", lambda e, yb=yb, i=i: e.dma_start(out=dst[(i - 1) * 128:i * 128, :], in_=yt[yb]), reads=[t_yt[yb]], dma=True)


def pool_host(inputs, h):
    w = np.asarray(inputs["pl_w"][0], np.float32)
    wt = w.reshape(4, 2, 128, 256).transpose(2, 0, 1, 3).reshape(128, 8, 256)
    g = np.asarray(inputs["norm_mix"][2], np.float32)
    return {"pl_mats": pool_consts(h),
            "pl_g": np.ascontiguousarray(g.reshape(8, 128).T),
            "pl_wt": np.ascontiguousarray(wt),
            "pl_b": np.asarray(inputs["pl_b"][0], np.float32).reshape(1, D),
            "pl_scale": np.asarray(inputs["pl_scale"][0], np.float32).reshape(1, D)}


NH = 16
NKV = 4
HD = 64
NEG = -30000.0
_T5_THR = (8, 12, 16, 23, 32, 46, 64, 91)


def _t5_bucket(rel):
    n = abs(rel)
    if n < 8:
        b = n
    else:
        b = 7 + sum(1 for t in _T5_THR if n >= t)
    return (16 if rel > 0 else 0) + b


def attn_onehot():
    oh = np.zeros((33, 3, 128, 128), np.float32)
    for kb in range(3):
        for a in range(128):
            for j in range(128):
                rel = 128 * (kb - 1) + j - a
                if abs(rel) <= 128:
                    oh[_t5_bucket(rel), kb, a, j] = 1.0
                else:
                    oh[32, kb, a, j] = 1.0
    return oh.reshape(33, 3 * 128 * 128)


def const_r(c, cols, val, parts):
    tmp = c.sb(cols, parts=parts)
    out = c.sb(cols, F32R, parts=parts)
    t = T()
    c.op("dve", lambda e: e.memset(tmp, val), writes=[t])
    c.op("act", lambda e: e.activation(out=out, in_=tmp, func=AF.Copy), reads=[t], writes=[t])
    return out, t


def attn_qkv_phase(c, src, S, hg):
    c.new_phase()
    g_d = c.din("at_g", [128, 8])
    w_d = c.din("at_wqkv", [8, 128, 1536])
    qg_d = c.din("at_qg", [64, 1])
    kg_d = c.din("at_kg", [64, 1])
    gcol = c.sb(8)
    qg = c.sb(1, parts=64)
    kg = c.sb(1, parts=64)
    t_g = T()
    c.op("sp", lambda e: e.dma_start(out=gcol, in_=g_d), writes=[t_g], dma=True)
    c.op("sp", lambda e: e.dma_start(out=qg, in_=qg_d), writes=[t_g], dma=True)
    c.op("sp", lambda e: e.dma_start(out=kg, in_=kg_d), writes=[t_g], dma=True)
    wq = [c.sb(1536, F32R) for _ in range(8)]
    t_wq = [T() for _ in range(8)]
    for k in range(8):
        c.op("pool", lambda e, k=k: e.dma_start(out=wq[k], in_=w_d[k]), writes=[t_wq[k]], dma=True)
    ones64, t_ones = const_r(c, 64, 1.0 / 64, 64)
    fr = Front(c)
    fr.scale_on_act = True
    sq = [c.sb(TT, F32R, parts=64) for _ in range(2)]
    t_sq = [T(), T()]
    rs = [c.sb(TT, parts=64) for _ in range(2)]
    t_rs = [T(), T()]
    qn = [c.sb(TT, parts=64) for _ in range(2)]
    t_qn = [T(), T()]
    vt = [c.sb(256) for _ in range(2)]
    t_vt = [T(), T()]
    P = c.psum
    ps_t, ps_q, ps_m, ps_v = P[0:2], P[2:4], P[4:6], P[6:8]
    t_pst, t_psq, t_psm, t_psv = [T(), T()], [T(), T()], [T(), T()], [T(), T()]
    cnt = 0
    cv = 0
    for it in range(NTL // TT + 1):
        if it < NTL // TT:
            r0 = it * TT
            fr.load_norm(src, r0)
            stores = [((1 + 4 * it) * 128, 0, TT)]
            store_q = True
        else:
            fr.load_norm(None, 0, rows=[hg[128:256, :], hg[256:384, :], hg[128:256, :], hg[256:384, :]])
            stores = [(0, 0, 128), ((NBL + 1) * 128, 128, 128)]
            store_q = False
        fr.transpose(gcol, t_g, ps_t, t_pst)
        hT, t_hT = fr.hT, fr.t_hT
        for h in range(NH + NKV):
            isq = h < NH
            if isq and not store_q:
                continue
            b = cnt % 2
            cnt += 1
            col0 = h * 64 if isq else 1024 + (h - NH) * 64
            gain = qg if isq else kg
            dstT = S["qT"][h] if isq else S["kT"][h - NH]
            for k in range(8):
                c.op("pe", lambda e, k=k, b=b, col0=col0: e.matmul(ps_q[b][0:64, :], wq[k][:, col0:col0 + 64], hT[k], start=(k == 0), stop=(k == 7)),
                     reads=[t_wq[k], t_hT[k]], writes=[t_psq[b]])
            c.op("act", lambda e, b=b: e.activation(out=sq[b], in_=ps_q[b][0:64, :], func=AF.Square), reads=[t_psq[b]], writes=[t_sq[b]])
            c.op("pe", lambda e, b=b: e.matmul(ps_m[b][0:64, :], ones64, sq[b], start=True, stop=True), reads=[t_ones, t_sq[b]], writes=[t_psm[b]])
            c.op("act", lambda e, b=b: e.activation(out=rs[b], in_=ps_m[b][0:64, :], func=AF.Sqrt, bias=c.epsc[0:64, :]), reads=[t_psm[b], c.t_const], writes=[t_rs[b]])
            c.op("dve", lambda e, b=b: e.reciprocal(out=rs[b], in_=rs[b]), reads=[t_rs[b]], writes=[t_rs[b]])
            c.op("dve", lambda e, b=b, gain=gain: e.scalar_tensor_tensor(out=qn[b], in0=ps_q[b][0:64, :], scalar=gain, in1=rs[b], op0=ALU.mult, op1=ALU.mult),
                 reads=[t_psq[b], t_rs[b], t_g], writes=[t_qn[b]])
            for (dc, sc, wd_) in stores:
                c.op("sp", lambda e, b=b, dstT=dstT, dc=dc, sc=sc, wd_=wd_: e.dma_start(out=dstT[:, dc:dc + wd_], in_=qn[b][:, sc:sc + wd_]), reads=[t_qn[b]], dma=True)
        for tb in range(4 if store_q else 2):
            b = cv % 2
            cv += 1
            for k in range(8):
                c.op("pe", lambda e, k=k, b=b, tb=tb: e.matmul(ps_v[b][:, 0:256], hT[k][:, tb * 128:(tb + 1) * 128], wq[k][:, 1280:1536], start=(k == 0), stop=(k == 7)),
                     reads=[t_wq[k], t_hT[k]], writes=[t_psv[b]])
            c.op("act", lambda e, b=b: e.activation(out=vt[b], in_=ps_v[b][:, 0:256], func=AF.Copy), reads=[t_psv[b]], writes=[t_vt[b]])
            if store_q:
                vrow = (1 + 4 * it + tb) * 128
            else:
                vrow = 0 if tb == 0 else (NBL + 1) * 128
            c.op("sp", lambda e, b=b, vrow=vrow: e.dma_start(out=S["v"][vrow:vrow + 128, :], in_=vt[b]), reads=[t_vt[b]], dma=True)


def attn_core_phase(c, src, dst, S):
    c.new_phase()
    oh_d = c.din("at_oh", [33, 3 * 128 * 128])
    rt_d = c.din("at_rel", [32, 16])
    sink_d = c.din("at_sink", [1, 16])
    edge_d = c.din("at_edge", [128, 2])
    edge = c.sb(2)
    t_edge = T()
    c.op("sp", lambda e: e.dma_start(out=edge, in_=edge_d), writes=[t_edge], dma=True)
    wo_d = c.din("at_wo", [64, 16, D])
    bias = c.sb(3 * 16 * 128)
    t_bias = T()
    table = c.sb(16, parts=33)
    t_tab = T()
    c.op("dve", lambda e: e.memset(table[32:33, :], NEG), writes=[t_tab])
    c.op("sp", lambda e: e.dma_start(out=table[0:32, :], in_=rt_d), writes=[t_tab], dma=True)
    ohb = [c.sb(32 * 128, parts=33) for _ in range(2)]
    t_ohb = [T(), T()]
    P = c.psum
    ps_s, ps_o, ps_den, ps_y = P[0:2], P[2:4], P[4:6], P[6:8]
    t_pss, t_pso, t_psden, t_psy = [T(), T()], [T(), T()], [T(), T()], [T(), T()]
    nb = 0
    for kb in range(3):
        for q4 in range(4):
            ob = nb % 2
            pb = nb % 2
            nb += 1
            a0 = q4 * 32
            c.op("sp", lambda e, kb=kb, a0=a0, ob=ob: e.dma_start(out=ohb[ob], in_=oh_d[:, (kb * 128 + a0) * 128:(kb * 128 + a0 + 32) * 128]),
                 writes=[t_ohb[ob]], dma=True)
            for al in range(32):
                c.op("pe", lambda e, ob=ob, al=al, pb=pb: e.matmul(ps_s[pb][:, al * 16:(al + 1) * 16], ohb[ob][:, al * 128:(al + 1) * 128], table, start=True, stop=True),
                     reads=[t_ohb[ob], t_tab], writes=[t_pss[pb]])
            c.op("dve", lambda e, kb=kb, a0=a0, pb=pb: e.tensor_copy(
                out=bias[:, kb * 2048:(kb + 1) * 2048].rearrange("p (h a) -> p h a", h=16)[:, :, a0:a0 + 32],
                in_=ps_s[pb][:, :].rearrange("p (a h) -> p h a", h=16)),
                reads=[t_pss[pb]], writes=[t_bias])
    es16 = c.sb(16, parts=64)
    esink = c.sb(16 * 128, parts=64)
    t_es = T()
    c.op("sp", lambda e: e.dma_start(out=es16, in_=sink_d.broadcast_to([64, 16])), writes=[t_es], dma=True)
    c.op("act", lambda e: e.activation(out=es16, in_=es16, func=AF.Exp), reads=[t_es], writes=[t_es])
    c.op("dve", lambda e: e.tensor_copy(out=esink.rearrange("p (h a) -> p h a", h=16), in_=es16.unsqueeze(2).broadcast_to([64, 16, 128])), reads=[t_es], writes=[t_es])
    wo = c.sb(16 * D, F32R, parts=64)
    t_wo = T()
    c.op("pool", lambda e: e.dma_start(out=wo, in_=wo_d.rearrange("p h n -> p (h n)")), writes=[t_wo], dma=True)
    oneskv, t_ones = const_r(c, 64, 1.0, 128)
    q_sb = [c.sb(16 * 128, F32R, parts=64) for _ in range(2)]
    t_q = [T(), T()]
    RING = 4
    k_r = [c.sb(4 * 128, F32R, parts=64) for _ in range(RING)]
    t_kr = [T() for _ in range(RING)]
    v_r = [c.sb(256, F32R) for _ in range(RING)]
    t_vr = [T() for _ in range(RING)]
    xin = [c.sb(D) for _ in range(2)]
    t_xin = [T(), T()]
    tt = [c.sb(TT) for _ in range(2)]
    t_tt = [T(), T()]
    pT = [c.sb(TT, F32R) for _ in range(2)]
    t_pT = [T(), T()]
    den = [c.sb(TT, parts=64) for _ in range(2)]
    t_den = [T(), T()]
    oT = [[c.sb(TT, F32R, parts=64) for _ in range(4)] for _ in range(2)]
    t_oT = [[T() for _ in range(4)] for _ in range(2)]

    def prep_kv(i):
        s = i % RING
        c.op("pool", lambda e, s=s, i=i: e.dma_start(out=k_r[s].rearrange("p (g t) -> p g t", g=4), in_=S["kT3"][:, :, i * 128:(i + 1) * 128]), writes=[t_kr[s]], dma=True)
        c.op("pool", lambda e, s=s, i=i: e.dma_start(out=v_r[s], in_=S["v"][i * 128:(i + 1) * 128, :]), writes=[t_vr[s]], dma=True)

    prep_kv(0)
    prep_kv(1)
    steps = [(n, g, kb) for n in range(1, NBL + 1) for g in range(4) for kb in range(3)]
    deferred = []

    def block_start(n):
        prep_kv(n + 1)
        qb = n % 2
        c.op("pool", lambda e, qb=qb, n=n: e.dma_start(out=q_sb[qb].rearrange("p (h t) -> p h t", h=16), in_=S["qT3"][:, :, n * 128:(n + 1) * 128]), writes=[t_q[qb]], dma=True)
        c.op("sp", lambda e, qb=qb, n=n: e.dma_start(out=xin[qb], in_=src[(n - 1) * 128:n * 128, :]), writes=[t_xin[qb]], dma=True)

    def emit_s(i):
        n, g, kb = steps[i]
        if g == 0 and kb == 0:
            block_start(n)
        qb = n % 2
        s_ = (n + kb - 1) % RING
        b = i % 2
        c.op("pe", lambda e, s_=s_, g=g, qb=qb, b=b: e.matmul(ps_s[b][:, :], k_r[s_][:, g * 128:(g + 1) * 128], q_sb[qb][:, g * 512:(g + 1) * 512], start=True, stop=True),
             reads=[t_kr[s_], t_q[qb]], writes=[t_pss[b]])

    def emit_rest(i):
        n, g, kb = steps[i]
        qb = n % 2
        s_ = (n + kb - 1) % RING
        b = i % 2
        ob = (i // 3) % 2
        c.op("dve", lambda e, b=b, kb=kb, g=g: e.scalar_tensor_tensor(out=tt[b], in0=ps_s[b][:, :], scalar=HD ** -0.5, in1=bias[:, kb * 2048 + g * 512:kb * 2048 + (g + 1) * 512], op0=ALU.mult, op1=ALU.add),
             reads=[t_pss[b], t_bias], writes=[t_tt[b]])
        if (n == 1 and kb == 0) or (n == NBL and kb == 2):
            ecol = 0 if kb == 0 else 1
            c.op("dve", lambda e, b=b, ecol=ecol: e.tensor_scalar(out=tt[b], in0=tt[b], scalar1=edge[:, ecol:ecol + 1], scalar2=0.0, op0=ALU.add, op1=ALU.add),
                 reads=[t_tt[b], t_edge], writes=[t_tt[b]])
        c.op("act", lambda e, b=b: e.activation(out=pT[b], in_=tt[b], func=AF.Exp), reads=[t_tt[b]], writes=[t_pT[b]])
        c.op("pe", lambda e, s_=s_, g=g, b=b, ob=ob, kb=kb: e.matmul(ps_o[ob][0:64, :], v_r[s_][:, g * 64:(g + 1) * 64], pT[b], start=(kb == 0), stop=(kb == 2)),
             reads=[t_vr[s_], t_pT[b]], writes=[t_pso[ob]])
        c.op("pe", lambda e, b=b, ob=ob, kb=kb: e.matmul(ps_den[ob][0:64, :], oneskv, pT[b], start=(kb == 0), stop=(kb == 2)),
             reads=[t_ones, t_pT[b]], writes=[t_psden[ob]])
        if kb == 2:
            c.op("dve", lambda e, ob=ob, g=g: e.tensor_tensor(out=den[ob], in0=ps_den[ob][0:64, :], in1=esink[:, g * 512:(g + 1) * 512], op=ALU.add),
                 reads=[t_psden[ob], t_es], writes=[t_den[ob]])
            c.op("dve", lambda e, ob=ob: e.reciprocal(out=den[ob], in_=den[ob]), reads=[t_den[ob]], writes=[t_den[ob]])
            c.op("dve", lambda e, ob=ob, g=g, qb=qb: e.tensor_tensor(out=oT[qb][g], in0=ps_o[ob][0:64, :], in1=den[ob], op=ALU.mult),
                 reads=[t_pso[ob], t_den[ob]], writes=[t_oT[qb][g]])
            if g == 3:
                deferred.append((i + 3, lambda n=n: block_end(n)))

    def block_end(n):
        qb = n % 2
        for dh in range(2):
            yb = dh
            for h in range(NH):
                c.op("pe", lambda e, h=h, dh=dh, yb=yb, qb=qb: e.matmul(ps_y[yb][:, :], oT[qb][h // 4][:, (h % 4) * 128:(h % 4 + 1) * 128], wo[:, h * D + dh * 512:h * D + (dh + 1) * 512], start=(h == 0), stop=(h == NH - 1)),
                     reads=[t_oT[qb][h // 4], t_wo], writes=[t_psy[yb]])
            c.op("dve", lambda e, dh=dh, yb=yb, qb=qb: e.tensor_tensor(out=xin[qb][:, dh * 512:(dh + 1) * 512], in0=ps_y[yb][:, :], in1=xin[qb][:, dh * 512:(dh + 1) * 512], op=ALU.add),
                 reads=[t_psy[yb], t_xin[qb]], writes=[t_xin[qb]])
        c.op("sp", lambda e, qb=qb, n=n: e.dma_start(out=dst[(n - 1) * 128:n * 128, :], in_=xin[qb]), reads=[t_xin[qb]], dma=True)

    ns = len(steps)
    for i in range(ns + 4):
        if i < ns:
            emit_s(i)
        if 1 <= i <= ns:
            emit_rest(i - 1)
        for (at, fn) in [d for d in deferred if d[0] <= i]:
            fn()
        deferred[:] = [d for d in deferred if d[0] > i]
    assert not deferred


def attn_scratch(c):
    qT = c.nc.dram_tensor("qT_s", [NH, 64, (NBL + 2) * 128], F32)
    kT = c.nc.dram_tensor("kT_s", [NKV, 64, (NBL + 2) * 128], F32)
    v = c.nc.dram_tensor("v_s", [(NBL + 2) * 128, 256], F32)
    return {"qT": [qT.ap()[h] for h in range(NH)], "kT": [kT.ap()[h] for h in range(NKV)], "v": v.ap(),
            "qT3": qT.ap().rearrange("h p t -> p h t"), "kT3": kT.ap().rearrange("h p t -> p h t")}


def attn_host(inputs, h):
    g = np.asarray(inputs["norm_mix"][1], np.float32)
    edge = np.zeros((128, 2), np.float32)
    edge[:, h] = NEG
    wo = np.asarray(inputs["at_w_o"][0], np.float32).reshape(16, 64, D).transpose(1, 0, 2)
    return {"at_g": np.ascontiguousarray(g.reshape(8, 128).T),
            "at_wqkv": np.ascontiguousarray(np.asarray(inputs["at_w_qkv"][0], np.float32).reshape(8, 128, 1536)),
            "at_qg": np.asarray(inputs["at_q_gain"][0], np.float32).reshape(64, 1),
            "at_kg": np.asarray(inputs["at_k_gain"][0], np.float32).reshape(64, 1),
            "at_oh": attn_onehot(),
            "at_rel": np.asarray(inputs["rel_table"], np.float32),
            "at_sink": np.asarray(inputs["at_sink"][0], np.float32).reshape(1, 16),
            "at_edge": edge,
            "at_wo": np.ascontiguousarray(wo)}


NFFT = 2 * L
HY_MIN_DECAY = math.log(1e-2) / 1.5
HY_MAX_DECAY = math.log(1e-2) / 0.3
_FA, _FC, _FS, _FSN, _GCS, _GSNC, _HAC, _HASN, _NCR = 0, 256, 384, 512, 640, 896, 1152, 1216, 1280


def hy_consts():
    i = np.arange(128, dtype=np.float64)
    th = 2 * np.pi * np.outer(i, i) / 128.0
    cr = np.zeros((128, _NCR), np.float64)
    cr[:, _FA:_FA + 128] = np.cos(th)
    cr[:, _FA + 128:_FA + 256] = -np.sin(th)
    cr[:, _FC:_FC + 128] = np.cos(th)
    cr[:, _FS:_FS + 128] = np.sin(th)
    cr[:, _FSN:_FSN + 128] = -np.sin(th)
    cr[:, _GCS:_GCS + 128] = np.cos(th)
    cr[:, _GCS + 128:_GCS + 256] = np.sin(th)
    cr[:, _GSNC:_GSNC + 128] = -np.sin(th)
    cr[:, _GSNC + 128:_GSNC + 256] = np.cos(th)
    cr[:, _HAC:_HAC + 64] = np.cos(th[:, :64])
    cr[:, _HASN:_HASN + 64] = -np.sin(th[:, :64])
    tw = 2 * np.pi * np.outer(i, i) / NFFT
    cf = np.concatenate([np.tile(np.cos(tw), (1, 4)), np.tile(np.sin(tw), (1, 4))], axis=1)
    n = np.arange(NFFT)
    pos = np.where(n < L, n, L - (n - L)).astype(np.float64)
    pos[L] = 0.0
    t = pos / (L - 1)
    f = np.linspace(1e-4, 15.0, 16)
    ang = (2 * np.pi / L) * pos[None, :] * f[:, None]
    z = np.concatenate([t[None, :], np.cos(ang), -np.sin(ang)], axis=0)
    tdec = t.copy()
    tdec[L] = 1.0e4
    deltas = np.abs(np.linspace(HY_MIN_DECAY, HY_MAX_DECAY, D))
    return {"hy_cr": cr.astype(np.float32), "hy_cf": cf.astype(np.float32), "hy_z": z.astype(np.float32),
            "hy_tdec": tdec.astype(np.float32).reshape(1, NFFT),
            "hy_negdelta_full": (-deltas).astype(np.float32)}


class HyFFT:
    def __init__(self, c, inverse):
        self.c = c
        self.inverse = inverse
        cr_d = c.din("hy_cr", [128, _NCR])
        cf_d = c.din("hy_cf", [128, 1024])
        self.cr = c.sb(_NCR, F32R)
        self.cf = c.sb(1024)
        self.t_k = T()
        c.op("pool", lambda e: e.dma_start(out=self.cr, in_=cr_d), writes=[self.t_k], dma=True)
        self.t_k2 = T()
        c.op("sp", lambda e: e.dma_start(out=self.cf, in_=cf_d), writes=[self.t_k2], dma=True)
        self.C2 = self.cf[:, 0:512]
        self.S2 = self.cf[:, 512:1024]
        P = c.psum
        mk2 = lambda dt=F32: [c.sb(512, dt) for _ in range(2)]
        self.t1, self.t2 = mk2(), mk2()
        self.t_t1, self.t_t2 = [T(), T()], [T(), T()]
        self.Bre, self.Bim = mk2(F32R), mk2(F32R)
        self.t_B = [[T(), T()], [T(), T()]]
        self.ps_a, self.t_psa = P[0:2], [T(), T()]
        self.ps_x, self.t_psx = P[2:4], [T(), T()]
        if inverse:
            self.u1, self.u2 = mk2(), mk2()
            self.t_u1, self.t_u2 = [T(), T()], [T(), T()]
            self.m = [c.sb(512) for _ in range(4)]
            self.t_m = [T() for _ in range(4)]
            self.Zre, self.Zim = mk2(F32R), mk2(F32R)
            self.t_Z = [[T(), T()], [T(), T()]]
            self.Vre, self.Vim = mk2(F32R), mk2(F32R)
            self.t_V = [[T(), T()], [T(), T()]]
            self.ps_v, self.t_psv = P[4:6], [T(), T()]
            self.ps_y, self.t_psy = P[6], T()

    def _tw_mul(self, ps, t_ps, a1, a2, t_a1, t_a2):
        c = self.c
        for b in range(2):
            c.op("dve", lambda e, b=b: e.tensor_tensor(out=a1[b], in0=ps[b][:, :], in1=self.C2, op=ALU.mult), reads=[t_ps[b], self.t_k2], writes=[t_a1[b]])
            c.op("dve", lambda e, b=b: e.tensor_tensor(out=a2[b], in0=ps[b][:, :], in1=self.S2, op=ALU.mult), reads=[t_ps[b], self.t_k2], writes=[t_a2[b]])

    def _tw_comb(self, a1, a2, t_a1, t_a2, outre, outim, t_out, forward):
        c = self.c
        for b in range(2):
            v1 = a1[b].rearrange("p (s c k) -> p s c k", s=2, c=2)
            v2 = a2[b].rearrange("p (s c k) -> p s c k", s=2, c=2)
            ore = outre[:, b * 256:(b + 1) * 256].rearrange("p (s k) -> p s k", s=2)
            oim = outim[:, b * 256:(b + 1) * 256].rearrange("p (s k) -> p s k", s=2)
            op_re, op_im = (ALU.add, ALU.subtract) if forward else (ALU.subtract, ALU.add)
            c.op("pool", lambda e, v1=v1, v2=v2, ore=ore, op_re=op_re: e.tensor_tensor(out=ore, in0=v1[:, :, 0, :], in1=v2[:, :, 1, :], op=op_re),
                 reads=[t_a1[b], t_a2[b]], writes=[t_out[0]])
            c.op("pool", lambda e, v1=v1, v2=v2, oim=oim, op_im=op_im: e.tensor_tensor(out=oim, in0=v1[:, :, 1, :], in1=v2[:, :, 0, :], op=op_im),
                 reads=[t_a1[b], t_a2[b]], writes=[t_out[1]])

    def stage(self, st, g, J):
        c = self.c
        cr = self.cr
        par = g % 2
        c0 = g * 4
        if st == 0:
            ut, ka = J["ut"], J["ka"]
            for s in range(4):
                b = s // 2
                c.op("pe", lambda e, s=s, b=b, ut=ut, ka=ka, c0=c0: e.matmul(self.ps_a[b][:, (s % 2) * 256:(s % 2 + 1) * 256], ut[0:ka, (c0 + s) * 128:(c0 + s + 1) * 128], cr[0:ka, _FA:_FA + 256], start=True, stop=True),
                     reads=[J["t_ut"][g], self.t_k], writes=[self.t_psa[b]])
        elif st == 1:
            self._tw_mul(self.ps_a, self.t_psa, self.t1, self.t2, self.t_t1, self.t_t2)
        elif st == 2:
            self._tw_comb(self.t1, self.t2, self.t_t1, self.t_t2, self.Bre[par], self.Bim[par], self.t_B[par], True)
        elif st == 3:
            Bre, Bim = self.Bre[par], self.Bim[par]
            rB = [self.t_B[par][0], self.t_B[par][1], self.t_k]
            c.op("pe", lambda e, Bre=Bre: e.matmul(self.ps_x[0][:, :], cr[:, _FC:_FC + 128], Bre, start=True, stop=False), reads=rB, writes=[self.t_psx[0]])
            c.op("pe", lambda e, Bim=Bim: e.matmul(self.ps_x[0][:, :], cr[:, _FS:_FS + 128], Bim, start=False, stop=True), reads=rB, writes=[self.t_psx[0]])
            c.op("pe", lambda e, Bim=Bim: e.matmul(self.ps_x[1][:, :], cr[:, _FC:_FC + 128], Bim, start=True, stop=False), reads=rB, writes=[self.t_psx[1]])
            c.op("pe", lambda e, Bre=Bre: e.matmul(self.ps_x[1][:, :], cr[:, _FSN:_FSN + 128], Bre, start=False, stop=True), reads=rB, writes=[self.t_psx[1]])
        elif not self.inverse:
            if st == 4:
                J["spec_out"](g, self.ps_x, self.t_psx)
        elif st == 4:
            kt, t_kt = J["kt"](g)
            m, t_m = self.m, self.t_m
            xr, xi = self.ps_x[0], self.ps_x[1]
            c.op("dve", lambda e, kt=kt: e.tensor_tensor(out=m[0], in0=xr[:, :], in1=kt[:, 0:512], op=ALU.mult), reads=[self.t_psx[0], t_kt], writes=[t_m[0]])
            c.op("dve", lambda e, kt=kt: e.tensor_tensor(out=m[1], in0=xi[:, :], in1=kt[:, 512:1024], op=ALU.mult), reads=[self.t_psx[1], t_kt], writes=[t_m[1]])
            c.op("dve", lambda e, kt=kt: e.tensor_tensor(out=m[2], in0=xr[:, :], in1=kt[:, 512:1024], op=ALU.mult), reads=[self.t_psx[0], t_kt], writes=[t_m[2]])
            c.op("dve", lambda e, kt=kt: e.tensor_tensor(out=m[3], in0=xi[:, :], in1=kt[:, 0:512], op=ALU.mult), reads=[self.t_psx[1], t_kt], writes=[t_m[3]])
        elif st == 5:
            m, t_m = self.m, self.t_m
            Zre, Zim = self.Zre[par], self.Zim[par]
            c.op("pool", lambda e, Zre=Zre: e.tensor_tensor(out=Zre, in0=m[0], in1=m[1], op=ALU.subtract), reads=[t_m[0], t_m[1]], writes=[self.t_Z[par][0]])
            c.op("pool", lambda e, Zim=Zim: e.tensor_tensor(out=Zim, in0=m[2], in1=m[3], op=ALU.add), reads=[t_m[2], t_m[3]], writes=[self.t_Z[par][1]])
        elif st == 6:
            Zre, Zim = self.Zre[par], self.Zim[par]
            for s in range(4):
                b = s // 2
                reg = self.ps_v[b][:, (s % 2) * 256:(s % 2 + 1) * 256]
                c.op("pe", lambda e, s=s, reg=reg, Zre=Zre: e.matmul(reg, Zre[:, s * 128:(s + 1) * 128], cr[:, _GCS:_GCS + 256], start=True, stop=False),
                     reads=[self.t_Z[par][0], self.t_k], writes=[self.t_psv[b]])
                c.op("pe", lambda e, s=s, reg=reg, Zim=Zim: e.matmul(reg, Zim[:, s * 128:(s + 1) * 128], cr[:, _GSNC:_GSNC + 256], start=False, stop=True),
                     reads=[self.t_Z[par][1], self.t_k], writes=[self.t_psv[b]])
        elif st == 7:
            self._tw_mul(self.ps_v, self.t_psv, self.u1, self.u2, self.t_u1, self.t_u2)
        elif st == 8:
            self._tw_comb(self.u1, self.u2, self.t_u1, self.t_u2, self.Vre[par], self.Vim[par], self.t_V[par], False)
        elif st == 9:
            Vre, Vim = self.Vre[par], self.Vim[par]
            rV = [self.t_V[par][0], self.t_V[par][1], self.t_k]
            c.op("pe", lambda e, Vre=Vre: e.matmul(self.ps_y[0:64, :], cr[:, _HAC:_HAC + 64], Vre, start=True, stop=False), reads=rV, writes=[self.t_psy])
            c.op("pe", lambda e, Vim=Vim: e.matmul(self.ps_y[0:64, :], cr[:, _HASN:_HASN + 64], Vim, start=False, stop=True), reads=rV, writes=[self.t_psy])
        elif st == 10:
            yt = J["yt"]
            c.op("act", lambda e, c0=c0, yt=yt: e.activation(out=yt[0:64, c0 * 128:(c0 + 4) * 128], in_=self.ps_y[0:64, :], func=AF.Copy), reads=[self.t_psy], writes=[J["t_ut"][g]])

    def run(self, J, ngroups=32):
        nst = 11 if self.inverse else 5
        for t in range(ngroups + nst - 1):
            if "pre" in J:
                J["pre"](t)
            for st in range(nst - 1, -1, -1):
                g = t - st
                if 0 <= g < ngroups:
                    self.stage(st, g, J)


def to_time_major(c, src_ct, t_src, ut, t_ut, na, ps, t_ps):
    v = src_ct.rearrange("p (a r) -> p r a", r=128)
    u3 = ut[0:na, :].rearrange("p (c r) -> p c r", r=128)
    for r0 in range(0, 128, 4):
        b = (r0 // 4) % 2
        for j in range(4):
            c.op("pe", lambda e, r0=r0, j=j, b=b: e.transpose(out=ps[b][0:na, j * 128:(j + 1) * 128], in_=v[:, r0 + j, :], identity=c.ident),
                 reads=[t_src, c.t_const], writes=[t_ps[b]])
        c.op("act", lambda e, r0=r0, b=b: e.activation(out=u3[:, :, r0:r0 + 4], in_=ps[b][0:na, :].rearrange("p (r c) -> p c r", r=4), func=AF.Copy),
             reads=[t_ps[b]], writes=(t_ut if isinstance(t_ut, list) else [t_ut]))


def to_feature_major(c, yt, t_yt, dst_ct, t_dst, ps, t_ps):
    y3 = yt[0:64, :].rearrange("p (c r) -> p r c", r=128)
    d3 = dst_ct.rearrange("p (a r) -> p r a", r=128)
    for r0 in range(0, 128, 8):
        b = (r0 // 8) % 2
        for j in range(8):
            c.op("pe", lambda e, r0=r0, j=j, b=b: e.transpose(out=ps[b][:, j * 64:(j + 1) * 64], in_=y3[:, r0 + j, :], identity=c.ident[0:64, 0:64]),
                 reads=(t_yt if isinstance(t_yt, list) else [t_yt]) + [c.t_const], writes=[t_ps[b]])
        c.op("dve", lambda e, r0=r0, b=b: e.tensor_copy(out=d3[:, r0:r0 + 8, :], in_=ps[b][:, :].rearrange("p (r a) -> p r a", r=8)),
             reads=[t_ps[b]], writes=[t_dst])


NCC = 4


def hy_scratch(c, tag):
    nc = c.nc
    zin = [[nc.dram_tensor("hy_zin%d_%d" % (hf, cc), [128, NTL], F32) for cc in range(NCC)] for hf in range(2)]
    zg = [[nc.dram_tensor("hy_zg%d_%d" % (hf, cc), [256, NTL], F32) for cc in range(NCC)] for hf in range(2)]
    return {"p": nc.dram_tensor("hy_p" + tag, [3 * NCC, 128, L], F32).ap(),
            "zin": zin, "zg": zg,
            "h3": nc.dram_tensor("hy_h3" + tag, [64, NFFT], F32).ap(),
            "ks": nc.dram_tensor("hy_ks" + tag, [2, NCC, 32, 128, 1024], F32).ap()}


def hy_inproj_phase(c, rows_fn, S, s):
    c.new_phase()
    NO = 3 * NCC * 128
    g_d = c.din("hy_g%d" % s, [128, 8])
    w_d = c.din("hy_win%d" % s, [8, 128, NO])
    gcol = c.sb(8)
    t_g = T()
    c.op("sp", lambda e: e.dma_start(out=gcol, in_=g_d), writes=[t_g], dma=True)
    win = [c.sb(NO, F32R) for _ in range(8)]
    t_w = [T() for _ in range(8)]
    for k in range(8):
        c.op("pool", lambda e, k=k: e.dma_start(out=win[k], in_=w_d[k]), writes=[t_w[k]], dma=True)
    frs = [Front(c), Front(c)]
    stage = [c.sb(TT) for _ in range(4)]
    t_st = [T() for _ in range(4)]
    P = c.psum
    ps_t, t_pst = P[0:2], [T(), T()]
    ps_o, t_pso = P[2:6], [T() for _ in range(4)]
    cnt = 0
    ntile = L // TT

    def prep(it):
        fr = frs[it % 2]
        r0 = it * TT
        fr.load_norm(None, r0, rows=[rows_fn(r0 + tb * 128) for tb in range(4)])
        fr.transpose(gcol, t_g, ps_t, t_pst)

    prep(0)
    for it in range(ntile):
        r0 = it * TT
        fr = frs[it % 2]
        if it + 1 < ntile:
            prep(it + 1)
        for oc in range(3 * NCC):
            b = cnt % 4
            cnt += 1
            for k in range(8):
                c.op("pe", lambda e, k=k, oc=oc, b=b, fr=fr: e.matmul(ps_o[b][:, :], win[k][:, oc * 128:(oc + 1) * 128], fr.hT[k], start=(k == 0), stop=(k == 7)),
                     reads=[t_w[k], fr.t_hT[k]], writes=[t_pso[b]])
            if oc % 2 == 0:
                c.op("act", lambda e, b=b: e.activation(out=stage[b], in_=ps_o[b][:, :], func=AF.Copy), reads=[t_pso[b]], writes=[t_st[b]])
            else:
                c.op("dve", lambda e, b=b: e.tensor_copy(out=stage[b], in_=ps_o[b][:, :]), reads=[t_pso[b]], writes=[t_st[b]])
            c.op("sp", lambda e, b=b, oc=oc, r0=r0: e.dma_start(out=S["p"][oc][:, r0:r0 + TT], in_=stage[b]), reads=[t_st[b]], dma=True)


def hy_conv3_phase(c, S, s):
    c.new_phase()
    cols_d = c.din("hy_c3cols%d" % s, [128, 3 * NCC * 5])
    cols = c.sb(3 * NCC * 5)
    t_c = T()
    c.op("sp", lambda e: e.dma_start(out=cols, in_=cols_d), writes=[t_c], dma=True)
    raw = [c.sb(L + 2) for _ in range(2)]
    t_raw = [T(), T()]
    out = [c.sb(L) for _ in range(2)]
    t_out = [T(), T()]
    for oc in range(3 * NCC):
        b = oc % 2
        eng = "dve"
        k0 = oc * 5
        c.op("sp", lambda e, b=b, oc=oc: e.dma_start(out=raw[b][:, 1:L + 1], in_=S["p"][oc]), writes=[t_raw[b]], dma=True)
        c.op("act", lambda e, b=b, k0=k0: e.activation(out=raw[b][:, 1:L + 1], in_=raw[b][:, 1:L + 1], func=AF.Identity, bias=cols[:, k0:k0 + 1]),
             reads=[t_raw[b], t_c], writes=[t_raw[b]])
        c.op(eng, lambda e, b=b: e.memset(raw[b][:, 0:1], 0.0), reads=[t_raw[b]], writes=[t_raw[b]])
        c.op(eng, lambda e, b=b: e.memset(raw[b][:, L + 1:L + 2], 0.0), reads=[t_raw[b]], writes=[t_raw[b]])
        c.op("act", lambda e, b=b, k0=k0: e.activation(out=out[b], in_=raw[b][:, 0:L], func=AF.Identity, scale=cols[:, k0 + 1:k0 + 2], bias=cols[:, k0 + 4:k0 + 5]),
             reads=[t_raw[b], t_c], writes=[t_out[b]])
        c.op(eng, lambda e, b=b, k0=k0: e.scalar_tensor_tensor(out=out[b], in0=raw[b][:, 1:L + 1], scalar=cols[:, k0 + 2:k0 + 3], in1=out[b], op0=ALU.mult, op1=ALU.add),
             reads=[t_raw[b], t_c, t_out[b]], writes=[t_out[b]])
        c.op(eng, lambda e, b=b, k0=k0: e.scalar_tensor_tensor(out=out[b], in0=raw[b][:, 2:L + 2], scalar=cols[:, k0 + 3:k0 + 4], in1=out[b], op0=ALU.mult, op1=ALU.add),
             reads=[t_raw[b], t_c, t_out[b]], writes=[t_out[b]])
        c.op("sp", lambda e, b=b, oc=oc: e.dma_start(out=S["p"][oc], in_=out[b]), reads=[t_out[b]], dma=True)


def hy_mlp_phase(c, S, s):
    c.new_phase()
    z_d = c.din("hy_z", [33, NFFT])
    w1_d = c.din("hy_w1_%d" % s, [33, 64])
    wi_d = c.din("hy_wi_%d" % s, [64, 128])
    cols_d = c.din("hy_mlpcols%d" % s, [64, 4])
    w1 = c.sb(64, F32R, parts=33)
    wi = c.sb(128, F32R, parts=64)
    cols = c.sb(4, parts=64)
    negpi = c.sb(1, parts=64)
    t_k = T()
    c.op("pool", lambda e: e.dma_start(out=w1, in_=w1_d), writes=[t_k], dma=True)
    t_k1 = T()
    c.op("pool", lambda e: e.dma_start(out=wi, in_=wi_d), writes=[t_k1], dma=True)
    t_k2 = T()
    c.op("sp", lambda e: e.dma_start(out=cols, in_=cols_d), writes=[t_k2], dma=True)
    c.op("dve", lambda e: e.memset(negpi, -math.pi), writes=[t_k2])
    NI = 4
    zt = [c.sb(TT, F32R, parts=33) for _ in range(NI)]
    t_zt = [T() for _ in range(NI)]
    arg = [c.sb(TT, parts=64) for _ in range(NI)]
    t_arg = [T() for _ in range(NI)]
    kk = [c.sb(TT, parts=64) for _ in range(NI)]
    t_kk = [T() for _ in range(NI)]
    MAGIC = 12582912.0
    hh = [[c.sb(TT, F32R, parts=64) for _ in range(2)] for _ in range(NI)]
    t_hh = [[T(), T()] for _ in range(NI)]
    h3 = [c.sb(TT, parts=64) for _ in range(NI)]
    t_h3 = [T() for _ in range(NI)]
    P = c.psum
    ps, t_ps = P[0:NI], [T() for _ in range(NI)]
    for it0 in range(0, NFFT // TT, NI):
        for p in range(NI):
            n0 = (it0 + p) * TT
            c.op("pool", lambda e, p=p, n0=n0: e.dma_start(out=zt[p], in_=z_d[:, n0:n0 + TT]), writes=[t_zt[p]], dma=True)
        for layer in range(3):
            for p in range(NI):
                n0 = (it0 + p) * TT
                if layer == 0:
                    c.op("pe", lambda e, p=p: e.matmul(ps[p][0:64, :], w1, zt[p], start=True, stop=True), reads=[t_k, t_zt[p]], writes=[t_ps[p]])
                else:
                    c.op("pe", lambda e, p=p, layer=layer: e.matmul(ps[p][0:64, :], wi[:, (layer - 1) * 64:layer * 64], hh[p][(layer - 1) % 2], start=True, stop=True),
                         reads=[t_k1, t_hh[p][(layer - 1) % 2]], writes=[t_ps[p]])
            for p in range(NI):
                c.op("dve", lambda e, p=p, layer=layer: e.tensor_scalar(out=arg[p], in0=ps[p][0:64, :], scalar1=cols[:, layer:layer + 1], scalar2=cols[:, 3:4], op0=ALU.add, op1=ALU.mult),
                     reads=[t_ps[p], t_k2], writes=[t_arg[p]])
            for p in range(NI):
                c.op("dve", lambda e, p=p: e.tensor_scalar(out=kk[p], in0=arg[p], scalar1=1.0 / (2 * math.pi), scalar2=MAGIC, op0=ALU.mult, op1=ALU.add),
                     reads=[t_arg[p]], writes=[t_kk[p]])
            for p in range(NI):
                c.op("dve", lambda e, p=p: e.tensor_scalar(out=kk[p], in0=kk[p], scalar1=-MAGIC, scalar2=2 * math.pi, op0=ALU.add, op1=ALU.mult),
                     reads=[t_kk[p]], writes=[t_kk[p]])
            for p in range(NI):
                c.op("dve", lambda e, p=p: e.tensor_tensor(out=arg[p], in0=arg[p], in1=kk[p], op=ALU.subtract),
                     reads=[t_arg[p], t_kk[p]], writes=[t_arg[p]])
            for p in range(NI):
                n0 = (it0 + p) * TT
                if layer < 2:
                    c.op("act", lambda e, p=p, layer=layer: e.activation(out=hh[p][layer % 2], in_=arg[p], func=AF.Sin), reads=[t_arg[p]], writes=[t_hh[p][layer % 2]])
                else:
                    c.op("act", lambda e, p=p: e.activation(out=h3[p], in_=arg[p], func=AF.Sin), reads=[t_arg[p]], writes=[t_h3[p]])
                    c.op("sp", lambda e, p=p, n0=n0: e.dma_start(out=S["h3"][:, n0:n0 + TT], in_=h3[p]), reads=[t_h3[p]], dma=True)


def hy_filter_phase(c, S, s):
    c.new_phase()
    w3_d = c.din("hy_w3_%d" % s, [64, 4 * NCC * 128])
    nd_d = c.din("hy_negdelta", [128, NCC])
    td_d = c.din("hy_tdec", [1, NFFT])
    ff = HyFFT(c, inverse=False)
    w3 = c.sb(4 * NCC * 128, F32R, parts=64)
    t_w3 = T()
    c.op("pool", lambda e: e.dma_start(out=w3, in_=w3_d), writes=[t_w3], dma=True)
    nd = c.sb(NCC)
    t_nd = T()
    c.op("sp", lambda e: e.dma_start(out=nd, in_=nd_d), writes=[t_nd], dma=True)
    kT = c.sb(NFFT)
    t_kT = T()
    ut = c.sb(NFFT, F32R)
    t_ut = T()
    h3t = [c.sb(TT, F32R, parts=64) for _ in range(2)]
    t_h3t = [T(), T()]
    tdb = [c.sb(TT) for _ in range(2)]
    t_tdb = [T(), T()]
    dec = [c.sb(TT) for _ in range(2)]
    t_dec = [T(), T()]
    junk = c.sb(TT)
    t_junk = T()
    asum = c.sb(40)
    t_as = T()
    kst = [c.sb(1024) for _ in range(2)]
    t_kst = [T(), T()]
    P = c.psum
    ps_k, t_psk = P[4:6], [T(), T()]
    ps_tr, t_pstr = P[6:8], [T(), T()]
    cnt = 0
    for cc in range(NCC):
        for o in range(2):
            c.op("dve", lambda e: e.memset(asum[:, 0:40], 0.0), writes=[t_as])
            for it in range(NFFT // TT):
                n0 = it * TT
                b = cnt % 2
                cnt += 1
                dirn = 0 if n0 < L else 1
                col = (dirn * 2 + o) * NCC * 128 + cc * 128
                c.op("pool", lambda e, b=b, n0=n0: e.dma_start(out=h3t[b], in_=S["h3"][:, n0:n0 + TT]), writes=[t_h3t[b]], dma=True)
                c.op("sp", lambda e, b=b, n0=n0: e.dma_start(out=tdb[b], in_=td_d[:, n0:n0 + TT].broadcast_to([128, TT])), writes=[t_tdb[b]], dma=True)
                c.op("pe", lambda e, b=b, col=col: e.matmul(ps_k[b][:, :], w3[:, col:col + 128], h3t[b], start=True, stop=True), reads=[t_w3, t_h3t[b]], writes=[t_psk[b]])
                c.op("act", lambda e, b=b, cc=cc: e.activation(out=dec[b], in_=tdb[b], func=AF.Exp, scale=nd[:, cc:cc + 1]), reads=[t_tdb[b], t_nd], writes=[t_dec[b]])
                c.op("dve", lambda e, b=b, n0=n0: e.tensor_tensor(out=kT[:, n0:n0 + TT], in0=ps_k[b][:, :], in1=dec[b], op=ALU.mult), reads=[t_psk[b], t_dec[b]], writes=[t_kT])
                c.op("act", lambda e, n0=n0, it=it: e.activation(out=junk, in_=kT[:, n0:n0 + TT], func=AF.Abs, accum_out=asum[:, it:it + 1]), reads=[t_kT], writes=[t_junk, t_as])
            c.op("dve", lambda e: e.reduce_sum(out=asum[:, 32:33], in_=asum[:, 0:32], axis=AX.X), reads=[t_as], writes=[t_as])
            c.op("dve", lambda e: e.reciprocal(out=asum[:, 33:34], in_=asum[:, 32:33]), reads=[t_as], writes=[t_as])
            c.op("dve", lambda e: e.tensor_scalar(out=asum[:, 33:34], in0=asum[:, 33:34], scalar1=1.0 / NFFT, scalar2=0.0, op0=ALU.mult, op1=ALU.add), reads=[t_as], writes=[t_as])
            for q in range(4):
                eng = "dve" if q % 2 == 0 else "pool"
                c.op(eng, lambda e, q=q: e.tensor_scalar(out=kT[:, q * 4096:(q + 1) * 4096], in0=kT[:, q * 4096:(q + 1) * 4096], scalar1=asum[:, 33:34], scalar2=0.0, op0=ALU.mult, op1=ALU.add),
                     reads=[t_kT, t_as], writes=[t_kT])
            to_time_major(c, kT, t_kT, ut, t_ut, 128, ps_tr, t_pstr)
            def spec_out(g, ps_x, t_psx, o=o, cc=cc):
                kb = g % 2
                c.op("act", lambda e, kb=kb: e.activation(out=kst[kb][:, 0:512], in_=ps_x[0][:, :], func=AF.Copy), reads=[t_psx[0]], writes=[t_kst[kb]])
                c.op("act", lambda e, kb=kb: e.activation(out=kst[kb][:, 512:1024], in_=ps_x[1][:, :], func=AF.Copy), reads=[t_psx[1]], writes=[t_kst[kb]])
                c.op("sp", lambda e, kb=kb, g=g: e.dma_start(out=S["ks"][o, cc, g], in_=kst[kb]), reads=[t_kst[kb]], dma=True)
            ff.run({"ut": ut, "t_ut": [t_ut] * 32, "ka": 128, "spec_out": spec_out})


def hy_conv_phase(c, S, s):
    c.new_phase()
    fb_d = c.din("hy_fbias%d" % s, [128, 2 * NCC])
    ff = HyFFT(c, inverse=True)
    fb = c.sb(2 * NCC)
    t_fb = T()
    c.op("sp", lambda e: e.dma_start(out=fb, in_=fb_d), writes=[t_fb], dma=True)
    bufA = c.sb(L)
    bufB = c.sb(L)
    t_A, t_B = T(), T()
    off_ut = (c.off + 7) // 8 * 8
    ut = c.sb(64 * 256, F32R)
    yt = c.sb_at(off_ut, 64 * 256, F32)
    t_utg = [T() for _ in range(32)]
    PC = 512
    xp = [c.sb(PC) for _ in range(2)]
    t_xp = [T(), T()]
    kt = [c.sb(1024) for _ in range(3)]
    t_kt = [T(), T(), T()]
    P = c.psum
    ps_tr, t_pstr = [P[6], P[7]], [T(), T()]
    t_pstr[0] = ff.t_psy
    npc = 0
    nk = 0
    for cc in range(NCC):
        c.op("sp", lambda e, cc=cc: e.dma_start(out=bufA, in_=S["p"][2 * NCC + cc]), writes=[t_A], dma=True)
        src, t_src, dstb, t_dst = bufA, t_A, bufB, t_B
        for o in range(2):
            to_time_major(c, src, t_src, ut, t_utg, 64, ps_tr, t_pstr)
            ktmap = {}

            def pre(t, o=o, cc=cc, ktmap=ktmap):
                g = t - 2
                if 0 <= g < 32:
                    kb = g % 3
                    c.op("sp", lambda e, kb=kb, g=g: e.dma_start(out=kt[kb], in_=S["ks"][o, cc, g]), writes=[t_kt[kb]], dma=True)
                    ktmap[g] = (kt[kb], t_kt[kb])
            ff.run({"ut": ut, "t_ut": t_utg, "ka": 64, "yt": yt, "kt": lambda g, ktmap=ktmap: ktmap[g], "pre": pre})
            to_feature_major(c, yt, t_utg, dstb, t_dst, ps_tr, t_pstr)
            gate_chunk = (0 if o == 0 else NCC) + cc
            for pc in range(L // PC):
                pb = npc % 2
                npc += 1
                sl = slice(pc * PC, (pc + 1) * PC)
                c.op("sp", lambda e, pb=pb, gate_chunk=gate_chunk, sl=sl: e.dma_start(out=xp[pb], in_=S["p"][gate_chunk][:, sl]), writes=[t_xp[pb]], dma=True)
                eng = "pool"
                c.op("dve", lambda e, sl=sl, o=o, cc=cc, src=src, dstb=dstb: e.scalar_tensor_tensor(out=dstb[:, sl], in0=src[:, sl], scalar=fb[:, o * NCC + cc:o * NCC + cc + 1], in1=dstb[:, sl], op0=ALU.mult, op1=ALU.add),
                     reads=[t_src, t_dst, t_fb], writes=[t_dst])
                c.op(eng, lambda e, sl=sl, pb=pb, dstb=dstb: e.tensor_tensor(out=dstb[:, sl], in0=dstb[:, sl], in1=xp[pb], op=ALU.mult),
                     reads=[t_dst, t_xp[pb]], writes=[t_dst])
            src, t_src, dstb, t_dst = dstb, t_dst, src, t_src
        for hf in range(2):
            t_z = T()
            c.op("sp", lambda e, cc=cc, src=src, hf=hf: e.dma_start(out=S["zin"][hf][cc].ap(), in_=src[:, hf * NTL:(hf + 1) * NTL]), reads=[t_src], writes=[t_z], dma=True)
            c.op("pool", lambda e, cc=cc, hf=hf: e.collective_compute("AllGather", ALU.bypass, replica_groups=PAIRS, ins=[S["zin"][hf][cc].ap()], outs=[S["zg"][hf][cc].ap()]),
                 reads=[t_z], writes=[T()], cc=True)


def hy_outproj_phase(c, src, dst, S, s):
    c.new_phase()
    w_d = c.din("hy_wout%d" % s, [8, 128, D])
    b_d = c.din("hy_bout%d" % s, [1, D])
    sel_d = c.din("hy_sel", [128, 2])
    wo = [c.sb(D, F32R) for _ in range(8)]
    t_wo = [T() for _ in range(8)]
    for k in range(8):
        c.op("pool", lambda e, k=k: e.dma_start(out=wo[k], in_=w_d[k]), writes=[t_wo[k]], dma=True)
    brow = c.sb(D)
    t_b = T()
    c.op("sp", lambda e: e.dma_start(out=brow, in_=b_d.broadcast_to([128, D])), writes=[t_b], dma=True)
    sel = c.sb(2)
    t_sel = T()
    c.op("sp", lambda e: e.dma_start(out=sel, in_=sel_d), writes=[t_sel], dma=True)
    zT = [[c.sb(TT, F32R) for _ in range(8)] for _ in range(2)]
    t_zT = [[T() for _ in range(8)] for _ in range(2)]
    zA = [c.sb(TT) for _ in range(2)]
    zB = [c.sb(TT) for _ in range(2)]
    t_zA = [T(), T()]
    t_zB = [T(), T()]
    xin = [c.sb(D) for _ in range(4)]
    t_xin = [T() for _ in range(4)]
    P = c.psum
    ps, t_ps = P[0:4], [T() for _ in range(4)]
    cnt = 0
    nz = 0
    for it in range(NTL // TT):
        r0 = it * TT
        zb = it % 2
        for k in range(8):
            rank, cc = k // NCC, k % NCC
            b = nz % 2
            nz += 1
            c.op("sp", lambda e, b=b, rank=rank, cc=cc, r0=r0: e.dma_start(out=zA[b], in_=S["zg"][0][cc].ap()[rank * 128:(rank + 1) * 128, r0:r0 + TT]), writes=[t_zA[b]], dma=True)
            c.op("sp", lambda e, b=b, rank=rank, cc=cc, r0=r0: e.dma_start(out=zB[b], in_=S["zg"][1][cc].ap()[rank * 128:(rank + 1) * 128, r0:r0 + TT]), writes=[t_zB[b]], dma=True)
            c.op("dve", lambda e, b=b: e.tensor_scalar(out=zA[b], in0=zA[b], scalar1=sel[:, 0:1], scalar2=0.0, op0=ALU.mult, op1=ALU.add), reads=[t_zA[b], t_sel], writes=[t_zA[b]])
            c.op("dve", lambda e, b=b, k=k, zb=zb: e.scalar_tensor_tensor(out=zT[zb][k], in0=zB[b], scalar=sel[:, 1:2], in1=zA[b], op0=ALU.mult, op1=ALU.add),
                 reads=[t_zA[b], t_zB[b], t_sel], writes=[t_zT[zb][k]])
        for tb in range(4):
            xb = tb
            c.op("sp", lambda e, xb=xb, tb=tb, r0=r0: e.dma_start(out=xin[xb], in_=src[r0 + tb * 128:r0 + (tb + 1) * 128, :]), writes=[t_xin[xb]], dma=True)
            for dh in range(2):
                b = cnt % 4
                cnt += 1
                sl = slice(dh * 512, (dh + 1) * 512)
                for k in range(8):
                    c.op("pe", lambda e, k=k, zb=zb, tb=tb, sl=sl, b=b: e.matmul(ps[b][:, :], zT[zb][k][:, tb * 128:(tb + 1) * 128], wo[k][:, sl], start=(k == 0), stop=(k == 7)),
                         reads=[t_zT[zb][k], t_wo[k]], writes=[t_ps[b]])
                c.op("dve", lambda e, xb=xb, sl=sl, b=b: e.tensor_tensor(out=xin[xb][:, sl], in0=ps[b][:, :], in1=xin[xb][:, sl], op=ALU.add),
                     reads=[t_ps[b], t_xin[xb]], writes=[t_xin[xb]])
                c.op("pool", lambda e, xb=xb, sl=sl: e.tensor_tensor(out=xin[xb][:, sl], in0=xin[xb][:, sl], in1=brow[:, sl], op=ALU.add),
                     reads=[t_xin[xb], t_b], writes=[t_xin[xb]])
            c.op("sp", lambda e, xb=xb, tb=tb, r0=r0: e.dma_start(out=dst[r0 + tb * 128:r0 + (tb + 1) * 128, :], in_=xin[xb]), reads=[t_xin[xb]], dma=True)


def gather_x(c, src):
    c.new_phase()
    nc = c.nc
    outs = []
    for j in range(NTL // 512):
        cin = nc.dram_tensor("xg_in%d" % j, [512, D], F32)
        cg = nc.dram_tensor("xg_out%d" % j, [1024, D], F32)
        t_c = T()
        c.op("sp", lambda e, j=j, cin=cin: e.dma_start(out=cin.ap(), in_=src[j * 512:(j + 1) * 512, :]), writes=[t_c], dma=True)
        c.op("pool", lambda e, cin=cin, cg=cg: e.collective_compute("AllGather", ALU.bypass, replica_groups=PAIRS, ins=[cin.ap()], outs=[cg.ap()]),
             reads=[t_c], writes=[T()], cc=True)
        outs.append(cg)

    def rows_fn(R):
        rank, rr = R // NTL, R % NTL
        j, i = rr // 512, rr % 512
        return outs[j].ap()[rank * 512 + i:rank * 512 + i + 128, :]
    return rows_fn


def hyena_layer(c, rows_fn, src, dst, s):
    if not hasattr(c, "hyS"):
        c.hyS = hy_scratch(c, "")
    S = c.hyS
    hy_inproj_phase(c, rows_fn, S, s)
    hy_conv3_phase(c, S, s)
    hy_mlp_phase(c, S, s)
    hy_filter_phase(c, S, s)
    hy_conv_phase(c, S, s)
    hy_outproj_phase(c, src, dst, S, s)
    return S


def hyena_host(inputs, s, li, h):
    g = np.asarray(inputs["norm_mix"][li], np.float32)
    CH = NCC * 128
    csl = np.concatenate([np.arange(t * D + h * CH, t * D + (h + 1) * CH) for t in range(3)])
    b_in = np.asarray(inputs["hy_b_in"][s], np.float32)[csl]
    cw = np.asarray(inputs["hy_conv_w"][s], np.float32)[:, csl]
    cb = np.asarray(inputs["hy_conv_b"][s], np.float32)[csl]
    c3 = np.stack([b_in, cw[0], cw[1], cw[2], cb], axis=-1).reshape(3 * NCC, 128, 5).transpose(1, 0, 2).reshape(128, 3 * NCC * 5)
    mlpc = np.stack([np.asarray(inputs["hy_f_b1"][s], np.float32), np.asarray(inputs["hy_f_bi"][s][0], np.float32),
                     np.asarray(inputs["hy_f_bi"][s][1], np.float32), np.asarray(inputs["hy_f_freq"][s], np.float32)], axis=-1)
    wi = np.asarray(inputs["hy_f_wi"][s], np.float32)
    fbias = np.asarray(inputs["hy_f_bias"][s], np.float32)[:, h * CH:(h + 1) * CH].reshape(2, NCC, 128).transpose(2, 0, 1).reshape(128, 2 * NCC)
    w3 = np.asarray(inputs["hy_f_w3"][s], np.float32).reshape(64, 4, D)[:, :, h * CH:(h + 1) * CH].reshape(64, 4 * CH)
    sel = np.zeros((128, 2), np.float32)
    sel[:, h] = 1.0
    m = {"hy_g%d" % s: np.ascontiguousarray(g.reshape(8, 128).T),
         "hy_win%d" % s: np.ascontiguousarray(np.asarray(inputs["hy_w_in"][s], np.float32)[:, csl].reshape(8, 128, 3 * CH)),
         "hy_c3cols%d" % s: np.ascontiguousarray(c3),
         "hy_w1_%d" % s: np.asarray(inputs["hy_f_w1"][s], np.float32),
         "hy_wi_%d" % s: np.ascontiguousarray(np.concatenate([wi[0], wi[1]], axis=1)),
         "hy_mlpcols%d" % s: np.ascontiguousarray(mlpc),
         "hy_w3_%d" % s: np.ascontiguousarray(w3),
         "hy_fbias%d" % s: np.ascontiguousarray(fbias),
         "hy_wout%d" % s: np.ascontiguousarray(np.asarray(inputs["hy_w_out"][s], np.float32).reshape(8, 128, D)),
         "hy_bout%d" % s: np.asarray(inputs["hy_b_out"][s], np.float32).reshape(1, D),
         "hy_sel": sel}
    hc = hy_consts()
    hc["hy_negdelta"] = np.ascontiguousarray(hc["hy_negdelta_full"][h * CH:(h + 1) * CH].reshape(NCC, 128).T)
    del hc["hy_negdelta_full"]
    m.update(hc)
    return m


def build_program(plan):
    nc = bass.Bass("TRN2", target_bir_lowering=False)
    st = contextlib.ExitStack()
    with st:
        c = Ctx(nc, st)
        x_full = c.din("x", [L, D])
        x_loc = c.din("xloc", [NTL, D])
        y_out = nc.dram_tensor("y", [NTL, D], F32, kind="ExternalOutput").ap()
        bufs = [c.dscratch("xa", [NTL, D]), c.dscratch("xb", [NTL, D])]
        load_consts(c)
        cur = x_loc
        for pi, ph in enumerate(plan):
            last = pi == len(plan) - 1
            dst = y_out if last else bufs[pi % 2]
            if ph.startswith("ffn"):
                ffn_phase(c, cur, dst, int(ph[3:]), ntok=NTL)
            elif ph == "pool2":
                pool_phase(c, cur, dst)
            elif ph == "hyena0":
                hyena_layer(c, (lambda R: x_full[R:R + 128, :]) if pi == 0 else gather_x(c, cur), cur, dst, 0)
            elif ph == "hyena3":
                hyena_layer(c, gather_x(c, cur), cur, dst, 1)
            elif ph == "attn1":
                S = attn_scratch(c)
                hg = halo_exchange(c, cur)
                attn_qkv_phase(c, cur, S, hg)
                attn_core_phase(c, cur, dst, S)
            else:
                raise ValueError(ph)
            cur = dst
        c.mk.emit()
    return nc


def host_inputs(inputs, plan, h):
    m = {"ident": np.eye(128, dtype=np.float32)}
    for ph in plan:
        if ph.startswith("ffn"):
            m.update(ffn_host(inputs, int(ph[3:])))
        elif ph == "pool2":
            m.update(pool_host(inputs, h))
        elif ph == "attn1":
            m.update(attn_host(inputs, h))
        elif ph == "hyena0":
            m.update(hyena_host(inputs, 0, 0, h))
        elif ph == "hyena3":
            m.update(hyena_host(inputs, 1, 3, h))
    return m


DEBUG_HY = False
FULL_PLAN = ["hyena0", "ffn0", "attn1", "ffn1", "pool2", "ffn2", "hyena3", "ffn3"]


def run_plan(inputs, plan, x_override=None, trace=False):
    nc = build_program(plan)
    shared = [host_inputs(inputs, plan, h) for h in range(2)]
    x = np.asarray(inputs["x"], np.float32) if x_override is None else x_override
    nb = x.shape[0]
    in_maps = []
    for core in range(8):
        b, h = (core // 2) % nb, core % 2
        m = dict(shared[h])
        m["x"] = np.ascontiguousarray(x[b])
        m["xloc"] = np.ascontiguousarray(x[b, h * NTL:(h + 1) * NTL])
        in_maps.append(m)
    res = run_bass_kernel_spmd(nc, in_maps, core_ids=list(range(8)), trace=trace)
    out = np.stack([np.concatenate([res.results[2 * b]["y"], res.results[2 * b + 1]["y"]], axis=0) for b in range(nb)], axis=0)
    return out, res


def kernel(**inputs):
    out, _ = run_plan(inputs, FULL_PLAN)
    return out.astype(np.float32)
```

```python
import contextlib
import math

import numpy as np
import concourse.bass as bass
import concourse.mybir as mybir
from concourse.bass_utils import run_bass_kernel_spmd

F32 = mybir.dt.float32
F32R = mybir.dt.float32r
AF = mybir.ActivationFunctionType
ALU = mybir.AluOpType
AX = mybir.AxisListType

D = 1024
L = 8192
DFF = 2816
NF = DFF // 128
EPS = 1e-6
TT = 512
NBLK = L // 128
NSLOT = 20
SAME_ENGINE_SYNC = True


class T:
    __slots__ = ("w", "r")

    def __init__(self):
        self.w = []
        self.r = []


class Op:
    __slots__ = ("eng", "idx", "fn", "deps", "dma", "dj", "sig", "waited", "q", "inc")

    def __init__(self, eng, idx, fn, dma):
        self.eng = eng
        self.idx = idx
        self.fn = fn
        self.deps = ()
        self.dma = dma
        self.q = eng
        self.inc = 16
        self.dj = None
        self.sig = None
        self.waited = False


class MK:
    ENGS = ("pe", "act", "dve", "pool", "sp")

    def __init__(self, nc):
        self.nc = nc
        self.ops = {e: [] for e in self.ENGS}
        self.dma_ops = {e: [] for e in self.ENGS + ("cc",)}
        self.last_c = {e: None for e in self.ENGS}
        self.bar = None
        self.bar_seen = {e: True for e in self.ENGS}

    def barrier(self):
        deps = set()
        for e in self.ENGS:
            if self.last_c[e] is not None:
                deps.add(self.last_c[e])
            for o in self.dma_ops[e][-NSLOT:]:
                deps.add(o)
        for o in self.dma_ops["cc"][-NSLOT:]:
            deps.add(o)
        self.bar = deps
        self.bar_seen = {e: False for e in self.ENGS}

    def op(self, eng, fn, reads=(), writes=(), dma=False, cc=False):
        lst = self.ops[eng]
        dma = dma or cc
        o = Op(eng, len(lst), fn, dma)
        if cc:
            o.q = "cc"
            o.inc = 1
        lst.append(o)
        deps = set()
        if not self.bar_seen[eng]:
            self.bar_seen[eng] = True
            deps |= self.bar
        for t in reads:
            deps.update(t.w)
        for t in writes:
            deps.update(t.w)
            deps.update(t.r)
        if dma:
            dl = self.dma_ops[o.q]
            o.dj = len(dl)
            dl.append(o)
            if o.dj >= NSLOT:
                deps.add(dl[o.dj - NSLOT])
        else:
            self.last_c[eng] = o
        deps.discard(o)
        o.deps = deps
        for t in reads:
            if dma:
                t.r.append(o)
            else:
                t.r = [x for x in t.r if x.dma or x.eng != eng] + [o]
        for t in writes:
            t.w = [o]
            t.r = []
        return o

    @staticmethod
    def _skip(d, o):
        return (not d.dma) and d.eng == o.eng and (not o.dma) and (d.eng == "pe" or not SAME_ENGINE_SYNC)

    def emit(self):
        nc = self.nc
        for e in self.ENGS:
            for o in self.ops[e]:
                for d in o.deps:
                    if d.dma or self._skip(d, o):
                        continue
                    d.waited = True
        for e in self.ENGS:
            c = 0
            for o in self.ops[e]:
                if not o.dma and o.waited:
                    c += 1
                    o.sig = c
        with contextlib.ExitStack() as st:
            csem = {e: st.enter_context(nc.semaphore("c_" + e)) for e in ("pe", "act", "dve", "pool")}
            dsem = {}
            for q in self.ENGS + ("cc",):
                n = len(self.dma_ops[q])
                if n:
                    dsem[q] = [st.enter_context(nc.semaphore("d_%s_%d" % (q, i))) for i in range(min(NSLOT, n))]
            block = st.enter_context(nc.Block())
            mk = self

            def run(ename):
                def body(e):
                    known_c = {}
                    known_d = set()
                    for o in mk.ops[ename]:
                        cw = {}
                        for d in o.deps:
                            if d.dma:
                                key = (d.q, d.dj)
                                if key in known_d:
                                    continue
                                known_d.add(key)
                                e.wait_ge(dsem[d.q][d.dj % NSLOT], d.inc * (d.dj // NSLOT + 1))
                            else:
                                if mk._skip(d, o):
                                    continue
                                if known_c.get(d.eng, 0) >= d.sig:
                                    continue
                                cw[d.eng] = max(cw.get(d.eng, 0), d.sig)
                        for en, v in cw.items():
                            known_c[en] = v
                            e.wait_ge(csem[en], v)
                        ins = o.fn(e)
                        if o.dma:
                            ins.then_inc(dsem[o.q][o.dj % NSLOT], o.inc)
                        elif o.sig is not None:
                            ins.then_inc(csem[ename], 1)
                    tail = list(mk.dma_ops[ename][-NSLOT:])
                    if ename == "pool":
                        tail += mk.dma_ops["cc"][-NSLOT:]
                    for o in tail:
                        if (o.q, o.dj) not in known_d:
                            e.wait_ge(dsem[o.q][o.dj % NSLOT], o.inc * (o.dj // NSLOT + 1))
                return body

            if self.ops["sp"]:
                block.sync(run("sp"))
            if self.ops["pe"]:
                block.tensor(run("pe"))
            if self.ops["act"]:
                block.scalar(run("act"))
            if self.ops["dve"]:
                block.vector(run("dve"))
            if self.ops["pool"]:
                block.gpsimd(run("pool"))


ARENA = 51 * 1024


class Ctx:
    def __init__(self, nc, st):
        self.nc = nc
        self.mk = MK(nc)
        self.sb_base = 16512
        self.ntens = 0
        self.psum = [st.enter_context(nc.psum_tensor("psb%d" % i, [128, 512], F32)) for i in range(8)]
        self.off = 0
        self.base = 0
        self.dram = {}

    def din(self, name, shape, dt=F32):
        if name in self.dram:
            return self.dram[name].ap()
        t = self.nc.dram_tensor(name, list(shape), dt, kind="ExternalInput")
        self.dram[name] = t
        return t.ap()

    def dscratch(self, name, shape, dt=F32):
        t = self.nc.dram_tensor(name, list(shape), dt)
        return t.ap()

    def sb(self, cols, dt=F32, parts=128):
        self.off = (self.off + 7) // 8 * 8
        self.last_off = self.off
        self.ntens += 1
        t = self.nc.alloc_sbuf_tensor_at("t%d" % self.ntens, [parts, cols], dt, offset=self.sb_base + 4 * self.off)
        self.off += cols
        assert self.off <= ARENA, ("arena overflow", self.off)
        return t[:, :]

    def sb_at(self, off, cols, dt=F32, parts=128):
        self.ntens += 1
        t = self.nc.alloc_sbuf_tensor_at("t%d" % self.ntens, [parts, cols], dt, offset=self.sb_base + 4 * off)
        return t[:, :]

    def new_phase(self):
        self.mk.barrier()
        self.off = self.base

    def op(self, *a, **k):
        return self.mk.op(*a, **k)


def load_consts(c):
    c.ident = c.sb(128)
    c.t_const = T()
    ident_d = c.din("ident", [128, 128])
    c.op("sp", lambda e: e.dma_start(out=c.ident, in_=ident_d), writes=[c.t_const], dma=True)
    c.epsc = c.sb(1)
    c.op("dve", lambda e: e.memset(c.epsc, EPS), writes=[c.t_const])
    c.base = c.off


class Front:
    def __init__(self, c, want_hT=True, xn_dt=F32):
        self.c = c
        self.xin = [c.sb(D) for _ in range(4)]
        self.t_xin = [T() for _ in range(4)]
        self.xn = []
        for i in range(4):
            self.xn.append(c.sb(D, xn_dt))
            if i == 0:
                self.xn_off = c.last_off
        self.t_xn = [T() for _ in range(4)]
        self.ss = c.sb(4)
        self.t_ss = [T() for _ in range(4)]
        self.rstd = c.sb(4)
        self.t_rstd = [T() for _ in range(4)]
        if want_hT:
            self.hT = [c.sb(TT, F32R) for _ in range(8)]
            self.t_hT = [T() for _ in range(8)]
        self.tcount = 0

    def load_norm(self, src, r0, blocks=(0, 1, 2, 3), rows=None):
        c = self.c
        if rows is None:
            rows = [src[r0 + tb * 128:r0 + (tb + 1) * 128, :] for tb in range(4)]
        for tb in blocks:
            c.op("sp", lambda e, tb=tb, ap=rows[tb]: e.dma_start(out=self.xin[tb], in_=ap),
                 writes=[self.t_xin[tb]], dma=True)
        for tb in blocks:
            c.op("dve", lambda e, tb=tb: e.scalar_tensor_tensor(out=self.xn[tb], in0=self.xin[tb], scalar=1.0, in1=self.xin[tb], op0=ALU.mult, op1=ALU.mult, accum_out=self.ss[:, tb:tb + 1]),
                 reads=[self.t_xin[tb]], writes=[self.t_xn[tb], self.t_ss[tb]])
        for tb in blocks:
            c.op("act", lambda e, tb=tb: e.activation(out=self.rstd[:, tb:tb + 1], in_=self.ss[:, tb:tb + 1], func=AF.Sqrt, scale=1.0 / D, bias=c.epsc),
                 reads=[self.t_ss[tb], c.t_const], writes=[self.t_rstd[tb]])
        for tb in blocks:
            c.op("dve", lambda e, tb=tb: e.reciprocal(out=self.rstd[:, tb:tb + 1], in_=self.rstd[:, tb:tb + 1]),
                 reads=[self.t_rstd[tb]], writes=[self.t_rstd[tb]])
            if getattr(self, "scale_on_act", False):
                c.op("act", lambda e, tb=tb: e.activation(out=self.xn[tb], in_=self.xin[tb], func=AF.Copy, scale=self.rstd[:, tb:tb + 1]),
                     reads=[self.t_xin[tb], self.t_rstd[tb]], writes=[self.t_xn[tb]])
            else:
                c.op("dve", lambda e, tb=tb: e.tensor_scalar(out=self.xn[tb], in0=self.xin[tb], scalar1=self.rstd[:, tb:tb + 1], scalar2=0.0, op0=ALU.mult, op1=ALU.add),
                     reads=[self.t_xin[tb], self.t_rstd[tb]], writes=[self.t_xn[tb]])

    def transpose(self, gcol, t_g, pbanks, t_pb):
        c = self.c
        for k in range(8):
            b = self.tcount % len(pbanks)
            self.tcount += 1
            for tb in range(4):
                c.op("pe", lambda e, tb=tb, k=k, b=b: e.transpose(out=pbanks[b][:, tb * 128:(tb + 1) * 128], in_=self.xn[tb][:, k * 128:(k + 1) * 128], identity=c.ident),
                     reads=[self.t_xn[tb], c.t_const], writes=[t_pb[b]])
            c.op("act", lambda e, k=k, b=b: e.activation(out=self.hT[k], in_=pbanks[b][:, :], func=AF.Copy, scale=gcol[:, k:k + 1]),
                 reads=[t_pb[b], t_g], writes=[self.t_hT[k]])


def ffn_phase(c, src, dst, li, ntok=L):
    c.new_phase()
    g_d = c.din("ffn_g%d" % li, [128, 8])
    wgu_d = c.din("ffn_wgu%d" % li, [NF, 128, 2048])
    wd_d = c.din("ffn_wd%d" % li, [NF, 128, D])
    gcol = c.sb(8)
    t_g = T()
    c.op("sp", lambda e: e.dma_start(out=gcol, in_=g_d), writes=[t_g], dma=True)
    fr = Front(c)
    aT = [c.sb(TT, F32R) for _ in range(NF)]
    t_aT = [T() for _ in range(NF)]
    wd = [c.sb(D, F32R) for _ in range(NF)]
    t_wd = [T() for _ in range(NF)]
    wgu = [c.sb(2048, F32R) for _ in range(2)] + [c.sb_at(fr.xn_off, 2048, F32R), c.sb_at(fr.xn_off + 2048, 2048, F32R)]
    t_wgu = [T() for _ in range(4)]
    al = {0: [], 1: [], 2: [fr.t_xn[0], fr.t_xn[1]], 3: [fr.t_xn[2], fr.t_xn[3]]}
    NWB = 4
    sg = [c.sb(TT) for _ in range(2)]
    t_sg = [T() for _ in range(2)]
    P = c.psum
    ps_t, ps_g, ps_u, ps_d = P[0:2], P[2:4], P[4:6], P[6:8]
    t_pst = [T(), T()]
    t_psg = [T(), T()]
    t_psu = [T(), T()]
    t_psd = [T(), T()]
    cnt = {"gu": 0, "d": 0}
    for it in range(ntok // TT):
        r0 = it * TT
        fr.load_norm(src, r0)
        fr.transpose(gcol, t_g, ps_t, t_pst)
        hT, t_hT = fr.hT, fr.t_hT
        for f in range(NF):
            wb = f % NWB
            c.op("pool", lambda e, f=f, wb=wb: e.dma_start(out=wgu[wb], in_=wgu_d[f]), writes=[t_wgu[wb]] + al[wb], dma=True)
            c.op("pool", lambda e, f=f: e.dma_start(out=wd[f], in_=wd_d[f]), writes=[t_wd[f]], dma=True)
            b = cnt["gu"] % 2
            cnt["gu"] += 1
            for k in range(8):
                c.op("pe", lambda e, k=k, wb=wb, b=b: e.matmul(ps_g[b][:, :], wgu[wb][:, k * 256:k * 256 + 128], hT[k], start=(k == 0), stop=(k == 7)),
                     reads=[t_wgu[wb], t_hT[k]] + al[wb], writes=[t_psg[b]])
            for k in range(8):
                c.op("pe", lambda e, k=k, wb=wb, b=b: e.matmul(ps_u[b][:, :], wgu[wb][:, k * 256 + 128:k * 256 + 256], hT[k], start=(k == 0), stop=(k == 7)),
                     reads=[t_wgu[wb], t_hT[k]] + al[wb], writes=[t_psu[b]])
            c.op("act", lambda e, b=b: e.activation(out=sg[b], in_=ps_g[b][:, :], func=AF.Silu), reads=[t_psg[b]], writes=[t_sg[b]])
            c.op("dve", lambda e, b=b, f=f: e.tensor_tensor(out=aT[f], in0=sg[b], in1=ps_u[b][:, :], op=ALU.mult),
                 reads=[t_sg[b], t_psu[b]], writes=[t_aT[f]])
        for tb in range(4):
            for dh in range(2):
                b = cnt["d"] % 2
                cnt["d"] += 1
                for f in range(NF):
                    c.op("pe", lambda e, f=f, tb=tb, dh=dh, b=b: e.matmul(ps_d[b][:, :], aT[f][:, tb * 128:(tb + 1) * 128], wd[f][:, dh * 512:(dh + 1) * 512], start=(f == 0), stop=(f == NF - 1)),
                         reads=[t_aT[f], t_wd[f]], writes=[t_psd[b]])
                c.op("dve", lambda e, tb=tb, dh=dh, b=b: e.tensor_tensor(out=fr.xin[tb][:, dh * 512:(dh + 1) * 512], in0=ps_d[b][:, :], in1=fr.xin[tb][:, dh * 512:(dh + 1) * 512], op=ALU.add),
                     reads=[t_psd[b], fr.t_xin[tb]], writes=[fr.t_xin[tb]])
            c.op("sp", lambda e, tb=tb, r0=r0: e.dma_start(out=dst[r0 + tb * 128:r0 + (tb + 1) * 128, :], in_=fr.xin[tb]), reads=[fr.t_xin[tb]], dma=True)


def ffn_host(inputs, li):
    wg = np.asarray(inputs["ff_w_gate"][li], np.float32)
    wu = np.asarray(inputs["ff_w_up"][li], np.float32)
    wdn = np.asarray(inputs["ff_w_down"][li], np.float32)
    g = np.asarray(inputs["norm_ffn"][li], np.float32)
    wgu = np.stack([wg.reshape(8, 128, NF, 128), wu.reshape(8, 128, NF, 128)], axis=0)
    wgu = np.ascontiguousarray(wgu.transpose(3, 2, 1, 0, 4)).reshape(NF, 128, 2048)
    return {"ffn_g%d" % li: np.ascontiguousarray(g.reshape(8, 128).T),
            "ffn_wgu%d" % li: wgu,
            "ffn_wd%d" % li: np.ascontiguousarray(wdn.reshape(NF, 128, D))}


POOL_WINDOWS = (2, 4, 8, 16)


NTL = L // 2
NBL = NTL // 128


def pool_consts(h):
    mats = np.zeros((4, 9, 128, 128), np.float32)
    t = np.arange(L)
    for g, w in enumerate(POOL_WINDOWS):
        r = w // 2
        lo = np.clip(t - r, 0, L)
        hi = np.clip(t + r + 1, 0, L)
        inv = (1.0 / (hi - lo)).astype(np.float32)

        def blk(bi, bj):
            if bi < 0 or bi >= NBLK:
                return np.zeros((128, 128), np.float32)
            tp = np.arange(bi * 128, (bi + 1) * 128)[:, None]
            tt = np.arange(bj * 128, (bj + 1) * 128)[None, :]
            m = ((tp >= lo[tt]) & (tp < hi[tt])).astype(np.float32) * inv[tt]
            return m - (tp == tt).astype(np.float32)
        first = h * NBL
        last = h * NBL + NBL - 1
        for j, bj in enumerate((5, first, last)):
            for k in range(3):
                mats[g, j * 3 + k] = blk(bj - 1 + k, bj)
    return np.ascontiguousarray(mats.transpose(2, 0, 1, 3)).reshape(128, 4 * 9 * 128)


def halo_exchange(c, src):
    c.new_phase()
    nc = c.nc
    c.nhalo = getattr(c, "nhalo", 0) + 1
    hin = nc.dram_tensor("halo_in%d" % c.nhalo, [256, D], F32)
    hg = nc.dram_tensor("halo_g%d" % c.nhalo, [512, D], F32)
    t_h = T()
    c.op("sp", lambda e: e.dma_start(out=hin.ap()[0:128, :], in_=src[0:128, :]), writes=[t_h], dma=True)
    t_h2 = T()
    c.op("sp", lambda e: e.dma_start(out=hin.ap()[128:256, :], in_=src[NTL - 128:NTL, :]), writes=[t_h2], dma=True)
    c.op("pool", lambda e: e.collective_compute("AllGather", ALU.bypass, replica_groups=PAIRS, ins=[hin.ap()], outs=[hg.ap()]),
         reads=[t_h, t_h2], writes=[T()], cc=True)
    return hg.ap()


PAIRS = [[0, 1], [2, 3], [4, 5], [6, 7]]


def pool_phase(c, src, dst):
    hg = halo_exchange(c, src)
    c.new_phase()
    pm_d = c.din("pl_mats", [128, 36 * 128])
    g_d = c.din("pl_g", [128, 8])
    w_d = c.din("pl_wt", [128, 8, 256])
    b_d = c.din("pl_b", [1, D])
    s_d = c.din("pl_scale", [1, D])
    pm = c.sb(36 * 128, F32R)
    gcol = c.sb(8)
    wg = c.sb(8 * 256, F32R)
    brow = c.sb(D)
    srow = c.sb(D)
    t_k = T()
    c.op("pool", lambda e: e.dma_start(out=pm, in_=pm_d), writes=[t_k], dma=True)
    t_k2 = T()
    c.op("pool", lambda e: e.dma_start(out=wg, in_=w_d.rearrange("p a b -> p (a b)")), writes=[t_k2], dma=True)
    t_k3 = T()
    c.op("sp", lambda e: e.dma_start(out=gcol, in_=g_d), writes=[t_k3], dma=True)
    t_k4 = T()
    c.op("sp", lambda e: e.dma_start(out=brow, in_=b_d.broadcast_to([128, D])), writes=[t_k4], dma=True)
    t_k5 = T()
    c.op("sp", lambda e: e.dma_start(out=srow, in_=s_d.broadcast_to([128, D])), writes=[t_k5], dma=True)
    RING = 4
    xin = [c.sb(D) for _ in range(RING)]
    t_xin = [T() for _ in range(RING)]
    xn = [c.sb(D, F32R) for _ in range(RING)]
    t_xn = [T() for _ in range(RING)]
    junk = c.sb(D)
    t_junk = T()
    ss = c.sb(RING)
    rstd = c.sb(RING)
    t_ss = [T() for _ in range(RING)]
    t_rstd = [T() for _ in range(RING)]
    dT = [c.sb(128, F32R) for _ in range(8)]
    t_dT = [T() for _ in range(8)]
    yt = [c.sb(D) for _ in range(2)]
    t_yt = [T(), T()]
    P = c.psum
    ps_p = P[0:4]
    t_psp = [T() for _ in range(4)]
    ps_y = [P[4:6], P[6:8]]
    t_psy = [T(), T()]

    def rows(i):
        if i == 0:
            return hg[128:256, :]
        if i == NBL + 1:
            return hg[256:384, :]
        return src[(i - 1) * 128:i * 128, :]

    def prep(i):
        s = i % RING
        c.op("sp", lambda e, s=s, ap=rows(i): e.dma_start(out=xin[s], in_=ap), writes=[t_xin[s]], dma=True)
        c.op("act", lambda e, s=s: e.activation(out=junk, in_=xin[s], func=AF.Square, accum_out=ss[:, s:s + 1]),
             reads=[t_xin[s]], writes=[t_junk, t_ss[s]])
        c.op("act", lambda e, s=s: e.activation(out=rstd[:, s:s + 1], in_=ss[:, s:s + 1], func=AF.Sqrt, scale=1.0 / D, bias=c.epsc),
             reads=[t_ss[s], c.t_const], writes=[t_rstd[s]])
        c.op("dve", lambda e, s=s: e.reciprocal(out=rstd[:, s:s + 1], in_=rstd[:, s:s + 1]), reads=[t_rstd[s]], writes=[t_rstd[s]])
        c.op("act", lambda e, s=s: e.activation(out=xn[s], in_=xin[s], func=AF.Copy, scale=rstd[:, s:s + 1]),
             reads=[t_xin[s], t_rstd[s]], writes=[t_xn[s]])

    prep(0)
    prep(1)
    for i in range(1, NBL + 1):
        prep(i + 1)
        mbase = 3 if i == 1 else (6 if i == NBL else 0)
        terms = [(-1, mbase), (0, mbase + 1), (1, mbase + 2)]
        for g in range(4):
            pb = g
            for j in range(2):
                cc = 2 * g + j
                for ti, (rel, mi) in enumerate(terms):
                    s = (i + rel) % RING
                    c.op("pe", lambda e, s=s, cc=cc, g=g, mi=mi, j=j, pb=pb, ti=ti, nt=len(terms):
                         e.matmul(ps_p[pb][:, j * 128:(j + 1) * 128], xn[s][:, cc * 128:(cc + 1) * 128], pm[:, (g * 9 + mi) * 128:(g * 9 + mi + 1) * 128], start=(ti == 0), stop=(ti == nt - 1)),
                         reads=[t_xn[s], t_k], writes=[t_psp[pb]])
            for j in range(2):
                cc = 2 * g + j
                c.op("act", lambda e, cc=cc, j=j, pb=pb: e.activation(out=dT[cc], in_=ps_p[pb][:, j * 128:(j + 1) * 128], func=AF.Copy, scale=gcol[:, cc:cc + 1]),
                     reads=[t_psp[pb], t_k3], writes=[t_dT[cc]])
        yb = i % 2
        for g in range(4):
            for j in range(2):
                cc = 2 * g + j
                c.op("pe", lambda e, cc=cc, g=g, j=j, yb=yb: e.matmul(ps_y[yb][g // 2][:, (g % 2) * 256:(g % 2 + 1) * 256], dT[cc], wg[:, cc * 256:(cc + 1) * 256], start=(j == 0), stop=(j == 1)),
                     reads=[t_dT[cc], t_k2], writes=[t_psy[yb]])
        s = i % RING
        for hh in range(2):
            sl = slice(hh * 512, (hh + 1) * 512)
            c.op("dve", lambda e, yb=yb, hh=hh, sl=sl: e.tensor_tensor(out=yt[yb][:, sl], in0=ps_y[yb][hh][:, :], in1=brow[:, sl], op=ALU.add),
                 reads=[t_psy[yb], t_k4], writes=[t_yt[yb]])
        c.op("pool", lambda e, yb=yb: e.tensor_tensor(out=yt[yb], in0=yt[yb], in1=srow, op=ALU.mult), reads=[t_yt[yb], t_k5], writes=[t_yt[yb]])
        c.op("pool", lambda e, yb=yb, s=s: e.tensor_tensor(out=yt[yb], in0=yt[yb], in1=xin[s], op=ALU.add), reads=[t_yt[yb], t_xin[s]], writes=[t_yt[yb]])
        c.op("sp", lambda e, yb=yb, i=i: e.dma_start(out=dst[(i - 1) * 128:i * 128, :], in_=yt[yb]), reads=[t_yt[yb]], dma=True)


def pool_host(inputs, h):
    w = np.asarray(inputs["pl_w"][0], np.float32)
    wt = w.reshape(4, 2, 128, 256).transpose(2, 0, 1, 3).reshape(128, 8, 256)
    g = np.asarray(inputs["norm_mix"][2], np.float32)
    return {"pl_mats": pool_consts(h),
            "pl_g": np.ascontiguousarray(g.reshape(8, 128).T),
            "pl_wt": np.ascontiguousarray(wt),
            "pl_b": np.asarray(inputs["pl_b"][0], np.float32).reshape(1, D),
            "pl_scale": np.asarray(inputs["pl_scale"][0], np.float32).reshape(1, D)}


NH = 16
NKV = 4
HD = 64
NEG = -30000.0
_T5_THR = (8, 12, 16, 23, 32, 46, 64, 91)


def _t5_bucket(rel):
    n = abs(rel)
    if n < 8:
        b = n
    else:
        b = 7 + sum(1 for t in _T5_THR if n >= t)
    return (16 if rel > 0 else 0) + b


def attn_onehot():
    oh = np.zeros((33, 3, 128, 128), np.float32)
    for kb in range(3):
        for a in range(128):
            for j in range(128):
                rel = 128 * (kb - 1) + j - a
                if abs(rel) <= 128:
                    oh[_t5_bucket(rel), kb, a, j] = 1.0
                else:
                    oh[32, kb, a, j] = 1.0
    return oh.reshape(33, 3 * 128 * 128)


def const_r(c, cols, val, parts):
    tmp = c.sb(cols, parts=parts)
    out = c.sb(cols, F32R, parts=parts)
    t = T()
    c.op("dve", lambda e: e.memset(tmp, val), writes=[t])
    c.op("act", lambda e: e.activation(out=out, in_=tmp, func=AF.Copy), reads=[t], writes=[t])
    return out, t


def attn_qkv_phase(c, src, S, hg):
    c.new_phase()
    g_d = c.din("at_g", [128, 8])
    w_d = c.din("at_wqkv", [8, 128, 1536])
    qg_d = c.din("at_qg", [64, 1])
    kg_d = c.din("at_kg", [64, 1])
    gcol = c.sb(8)
    qg = c.sb(1, parts=64)
    kg = c.sb(1, parts=64)
    t_g = T()
    c.op("sp", lambda e: e.dma_start(out=gcol, in_=g_d), writes=[t_g], dma=True)
    c.op("sp", lambda e: e.dma_start(out=qg, in_=qg_d), writes=[t_g], dma=True)
    c.op("sp", lambda e: e.dma_start(out=kg, in_=kg_d), writes=[t_g], dma=True)
    wq = [c.sb(1536, F32R) for _ in range(8)]
    t_wq = [T() for _ in range(8)]
    for k in range(8):
        c.op("pool", lambda e, k=k: e.dma_start(out=wq[k], in_=w_d[k]), writes=[t_wq[k]], dma=True)
    ones64, t_ones = const_r(c, 64, 1.0 / 64, 64)
    fr = Front(c)
    fr.scale_on_act = True
    sq = [c.sb(TT, F32R, parts=64) for _ in range(2)]
    t_sq = [T(), T()]
    rs = [c.sb(TT, parts=64) for _ in range(2)]
    t_rs = [T(), T()]
    qn = [c.sb(TT, parts=64) for _ in range(2)]
    t_qn = [T(), T()]
    vt = [c.sb(256) for _ in range(2)]
    t_vt = [T(), T()]
    P = c.psum
    ps_t, ps_q, ps_m, ps_v = P[0:2], P[2:4], P[4:6], P[6:8]
    t_pst, t_psq, t_psm, t_psv = [T(), T()], [T(), T()], [T(), T()], [T(), T()]
    cnt = 0
    cv = 0
    for it in range(NTL // TT + 1):
        if it < NTL // TT:
            r0 = it * TT
            fr.load_norm(src, r0)
            stores = [((1 + 4 * it) * 128, 0, TT)]
            store_q = True
        else:
            fr.load_norm(None, 0, rows=[hg[128:256, :], hg[256:384, :], hg[128:256, :], hg[256:384, :]])
            stores = [(0, 0, 128), ((NBL + 1) * 128, 128, 128)]
            store_q = False
        fr.transpose(gcol, t_g, ps_t, t_pst)
        hT, t_hT = fr.hT, fr.t_hT
        for h in range(NH + NKV):
            isq = h < NH
            if isq and not store_q:
                continue
            b = cnt % 2
            cnt += 1
            col0 = h * 64 if isq else 1024 + (h - NH) * 64
            gain = qg if isq else kg
            dstT = S["qT"][h] if isq else S["kT"][h - NH]
            for k in range(8):
                c.op("pe", lambda e, k=k, b=b, col0=col0: e.matmul(ps_q[b][0:64, :], wq[k][:, col0:col0 + 64], hT[k], start=(k == 0), stop=(k == 7)),
                     reads=[t_wq[k], t_hT[k]], writes=[t_psq[b]])
            c.op("act", lambda e, b=b: e.activation(out=sq[b], in_=ps_q[b][0:64, :], func=AF.Square), reads=[t_psq[b]], writes=[t_sq[b]])
            c.op("pe", lambda e, b=b: e.matmul(ps_m[b][0:64, :], ones64, sq[b], start=True, stop=True), reads=[t_ones, t_sq[b]], writes=[t_psm[b]])
            c.op("act", lambda e, b=b: e.activation(out=rs[b], in_=ps_m[b][0:64, :], func=AF.Sqrt, bias=c.epsc[0:64, :]), reads=[t_psm[b], c.t_const], writes=[t_rs[b]])
            c.op("dve", lambda e, b=b: e.reciprocal(out=rs[b], in_=rs[b]), reads=[t_rs[b]], writes=[t_rs[b]])
            c.op("dve", lambda e, b=b, gain=gain: e.scalar_tensor_tensor(out=qn[b], in0=ps_q[b][0:64, :], scalar=gain, in1=rs[b], op0=ALU.mult, op1=ALU.mult),
                 reads=[t_psq[b], t_rs[b], t_g], writes=[t_qn[b]])
            for (dc, sc, wd_) in stores:
                c.op("sp", lambda e, b=b, dstT=dstT, dc=dc, sc=sc, wd_=wd_: e.dma_start(out=dstT[:, dc:dc + wd_], in_=qn[b][:, sc:sc + wd_]), reads=[t_qn[b]], dma=True)
        for tb in range(4 if store_q else 2):
            b = cv % 2
            cv += 1
            for k in range(8):
                c.op("pe", lambda e, k=k, b=b, tb=tb: e.matmul(ps_v[b][:, 0:256], hT[k][:, tb * 128:(tb + 1) * 128], wq[k][:, 1280:1536], start=(k == 0), stop=(k == 7)),
                     reads=[t_wq[k], t_hT[k]], writes=[t_psv[b]])
            c.op("act", lambda e, b=b: e.activation(out=vt[b], in_=ps_v[b][:, 0:256], func=AF.Copy), reads=[t_psv[b]], writes=[t_vt[b]])
            if store_q:
                vrow = (1 + 4 * it + tb) * 128
            else:
                vrow = 0 if tb == 0 else (NBL + 1) * 128
            c.op("sp", lambda e, b=b, vrow=vrow: e.dma_start(out=S["v"][vrow:vrow + 128, :], in_=vt[b]), reads=[t_vt[b]], dma=True)


def attn_core_phase(c, src, dst, S):
    c.new_phase()
    oh_d = c.din("at_oh", [33, 3 * 128 * 128])
    rt_d = c.din("at_rel", [32, 16])
    sink_d = c.din("at_sink", [1, 16])
    edge_d = c.din("at_edge", [128, 2])
    edge = c.sb(2)
    t_edge = T()
    c.op("sp", lambda e: e.dma_start(out=edge, in_=edge_d), writes=[t_edge], dma=True)
    wo_d = c.din("at_wo", [64, 16, D])
    bias = c.sb(3 * 16 * 128)
    t_bias = T()
    table = c.sb(16, parts=33)
    t_tab = T()
    c.op("dve", lambda e: e.memset(table[32:33, :], NEG), writes=[t_tab])
    c.op("sp", lambda e: e.dma_start(out=table[0:32, :], in_=rt_d), writes=[t_tab], dma=True)
    ohb = [c.sb(32 * 128, parts=33) for _ in range(2)]
    t_ohb = [T(), T()]
    P = c.psum
    ps_s, ps_o, ps_den, ps_y = P[0:2], P[2:4], P[4:6], P[6:8]
    t_pss, t_pso, t_psden, t_psy = [T(), T()], [T(), T()], [T(), T()], [T(), T()]
    nb = 0
    for kb in range(3):
        for q4 in range(4):
            ob = nb % 2
            pb = nb % 2
            nb += 1
            a0 = q4 * 32
            c.op("sp", lambda e, kb=kb, a0=a0, ob=ob: e.dma_start(out=ohb[ob], in_=oh_d[:, (kb * 128 + a0) * 128:(kb * 128 + a0 + 32) * 128]),
                 writes=[t_ohb[ob]], dma=True)
            for al in range(32):
                c.op("pe", lambda e, ob=ob, al=al, pb=pb: e.matmul(ps_s[pb][:, al * 16:(al + 1) * 16], ohb[ob][:, al * 128:(al + 1) * 128], table, start=True, stop=True),
                     reads=[t_ohb[ob], t_tab], writes=[t_pss[pb]])
            c.op("dve", lambda e, kb=kb, a0=a0, pb=pb: e.tensor_copy(
                out=bias[:, kb * 2048:(kb + 1) * 2048].rearrange("p (h a) -> p h a", h=16)[:, :, a0:a0 + 32],
                in_=ps_s[pb][:, :].rearrange("p (a h) -> p h a", h=16)),
                reads=[t_pss[pb]], writes=[t_bias])
    es16 = c.sb(16, parts=64)
    esink = c.sb(16 * 128, parts=64)
    t_es = T()
    c.op("sp", lambda e: e.dma_start(out=es16, in_=sink_d.broadcast_to([64, 16])), writes=[t_es], dma=True)
    c.op("act", lambda e: e.activation(out=es16, in_=es16, func=AF.Exp), reads=[t_es], writes=[t_es])
    c.op("dve", lambda e: e.tensor_copy(out=esink.rearrange("p (h a) -> p h a", h=16), in_=es16.unsqueeze(2).broadcast_to([64, 16, 128])), reads=[t_es], writes=[t_es])
    wo = c.sb(16 * D, F32R, parts=64)
    t_wo = T()
    c.op("pool", lambda e: e.dma_start(out=wo, in_=wo_d.rearrange("p h n -> p (h n)")), writes=[t_wo], dma=True)
    oneskv, t_ones = const_r(c, 64, 1.0, 128)
    q_sb = [c.sb(16 * 128, F32R, parts=64) for _ in range(2)]
    t_q = [T(), T()]
    RING = 4
    k_r = [c.sb(4 * 128, F32R, parts=64) for _ in range(RING)]
    t_kr = [T() for _ in range(RING)]
    v_r = [c.sb(256, F32R) for _ in range(RING)]
    t_vr = [T() for _ in range(RING)]
    xin = [c.sb(D) for _ in range(2)]
    t_xin = [T(), T()]
    tt = [c.sb(TT) for _ in range(2)]
    t_tt = [T(), T()]
    pT = [c.sb(TT, F32R) for _ in range(2)]
    t_pT = [T(), T()]
    den = [c.sb(TT, parts=64) for _ in range(2)]
    t_den = [T(), T()]
    oT = [[c.sb(TT, F32R, parts=64) for _ in range(4)] for _ in range(2)]
    t_oT = [[T() for _ in range(4)] for _ in range(2)]

    def prep_kv(i):
        s = i % RING
        c.op("pool", lambda e, s=s, i=i: e.dma_start(out=k_r[s].rearrange("p (g t) -> p g t", g=4), in_=S["kT3"][:, :, i * 128:(i + 1) * 128]), writes=[t_kr[s]], dma=True)
        c.op("pool", lambda e, s=s, i=i: e.dma_start(out=v_r[s], in_=S["v"][i * 128:(i + 1) * 128, :]), writes=[t_vr[s]], dma=True)

    prep_kv(0)
    prep_kv(1)
    steps = [(n, g, kb) for n in range(1, NBL + 1) for g in range(4) for kb in range(3)]
    deferred = []

    def block_start(n):
        prep_kv(n + 1)
        qb = n % 2
        c.op("pool", lambda e, qb=qb, n=n: e.dma_start(out=q_sb[qb].rearrange("p (h t) -> p h t", h=16), in_=S["qT3"][:, :, n * 128:(n + 1) * 128]), writes=[t_q[qb]], dma=True)
        c.op("sp", lambda e, qb=qb, n=n: e.dma_start(out=xin[qb], in_=src[(n - 1) * 128:n * 128, :]), writes=[t_xin[qb]], dma=True)

    def emit_s(i):
        n, g, kb = steps[i]
        if g == 0 and kb == 0:
            block_start(n)
        qb = n % 2
        s_ = (n + kb - 1) % RING
        b = i % 2
        c.op("pe", lambda e, s_=s_, g=g, qb=qb, b=b: e.matmul(ps_s[b][:, :], k_r[s_][:, g * 128:(g + 1) * 128], q_sb[qb][:, g * 512:(g + 1) * 512], start=True, stop=True),
             reads=[t_kr[s_], t_q[qb]], writes=[t_pss[b]])

    def emit_rest(i):
        n, g, kb = steps[i]
        qb = n % 2
        s_ = (n + kb - 1) % RING
        b = i % 2
        ob = (i // 3) % 2
        c.op("dve", lambda e, b=b, kb=kb, g=g: e.scalar_tensor_tensor(out=tt[b], in0=ps_s[b][:, :], scalar=HD ** -0.5, in1=bias[:, kb * 2048 + g * 512:kb * 2048 + (g + 1) * 512], op0=ALU.mult, op1=ALU.add),
             reads=[t_pss[b], t_bias], writes=[t_tt[b]])
        if (n == 1 and kb == 0) or (n == NBL and kb == 2):
            ecol = 0 if kb == 0 else 1
            c.op("dve", lambda e, b=b, ecol=ecol: e.tensor_scalar(out=tt[b], in0=tt[b], scalar1=edge[:, ecol:ecol + 1], scalar2=0.0, op0=ALU.add, op1=ALU.add),
                 reads=[t_tt[b], t_edge], writes=[t_tt[b]])
        c.op("act", lambda e, b=b: e.activation(out=pT[b], in_=tt[b], func=AF.Exp), reads=[t_tt[b]], writes=[t_pT[b]])
        c.op("pe", lambda e, s_=s_, g=g, b=b, ob=ob, kb=kb: e.matmul(ps_o[ob][0:64, :], v_r[s_][:, g * 64:(g + 1) * 64], pT[b], start=(kb == 0), stop=(kb == 2)),
             reads=[t_vr[s_], t_pT[b]], writes=[t_pso[ob]])
        c.op("pe", lambda e, b=b, ob=ob, kb=kb: e.matmul(ps_den[ob][0:64, :], oneskv, pT[b], start=(kb == 0), stop=(kb == 2)),
             reads=[t_ones, t_pT[b]], writes=[t_psden[ob]])
        if kb == 2:
            c.op("dve", lambda e, ob=ob, g=g: e.tensor_tensor(out=den[ob], in0=ps_den[ob][0:64, :], in1=esink[:, g * 512:(g + 1) * 512], op=ALU.add),
                 reads=[t_psden[ob], t_es], writes=[t_den[ob]])
            c.op("dve", lambda e, ob=ob: e.reciprocal(out=den[ob], in_=den[ob]), reads=[t_den[ob]], writes=[t_den[ob]])
            c.op("dve", lambda e, ob=ob, g=g, qb=qb: e.tensor_tensor(out=oT[qb][g], in0=ps_o[ob][0:64, :], in1=den[ob], op=ALU.mult),
                 reads=[t_pso[ob], t_den[ob]], writes=[t_oT[qb][g]])
            if g == 3:
                deferred.append((i + 3, lambda n=n: block_end(n)))

    def block_end(n):
        qb = n % 2
        for dh in range(2):
            yb = dh
            for h in range(NH):
                c.op("pe", lambda e, h=h, dh=dh, yb=yb, qb=qb: e.matmul(ps_y[yb][:, :], oT[qb][h // 4][:, (h % 4) * 128:(h % 4 + 1) * 128], wo[:, h * D + dh * 512:h * D + (dh + 1) * 512], start=(h == 0), stop=(h == NH - 1)),
                     reads=[t_oT[qb][h // 4], t_wo], writes=[t_psy[yb]])
            c.op("dve", lambda e, dh=dh, yb=yb, qb=qb: e.tensor_tensor(out=xin[qb][:, dh * 512:(dh + 1) * 512], in0=ps_y[yb][:, :], in1=xin[qb][:, dh * 512:(dh + 1) * 512], op=ALU.add),
                 reads=[t_psy[yb], t_xin[qb]], writes=[t_xin[qb]])
        c.op("sp", lambda e, qb=qb, n=n: e.dma_start(out=dst[(n - 1) * 128:n * 128, :], in_=xin[qb]), reads=[t_xin[qb]], dma=True)

    ns = len(steps)
    for i in range(ns + 4):
        if i < ns:
            emit_s(i)
        if 1 <= i <= ns:
            emit_rest(i - 1)
        for (at, fn) in [d for d in deferred if d[0] <= i]:
            fn()
        deferred[:] = [d for d in deferred if d[0] > i]
    assert not deferred


def attn_scratch(c):
    qT = c.nc.dram_tensor("qT_s", [NH, 64, (NBL + 2) * 128], F32)
    kT = c.nc.dram_tensor("kT_s", [NKV, 64, (NBL + 2) * 128], F32)
    v = c.nc.dram_tensor("v_s", [(NBL + 2) * 128, 256], F32)
    return {"qT": [qT.ap()[h] for h in range(NH)], "kT": [kT.ap()[h] for h in range(NKV)], "v": v.ap(),
            "qT3": qT.ap().rearrange("h p t -> p h t"), "kT3": kT.ap().rearrange("h p t -> p h t")}


def attn_host(inputs, h):
    g = np.asarray(inputs["norm_mix"][1], np.float32)
    edge = np.zeros((128, 2), np.float32)
    edge[:, h] = NEG
    wo = np.asarray(inputs["at_w_o"][0], np.float32).reshape(16, 64, D).transpose(1, 0, 2)
    return {"at_g": np.ascontiguousarray(g.reshape(8, 128).T),
            "at_wqkv": np.ascontiguousarray(np.asarray(inputs["at_w_qkv"][0], np.float32).reshape(8, 128, 1536)),
            "at_qg": np.asarray(inputs["at_q_gain"][0], np.float32).reshape(64, 1),
            "at_kg": np.asarray(inputs["at_k_gain"][0], np.float32).reshape(64, 1),
            "at_oh": attn_onehot(),
            "at_rel": np.asarray(inputs["rel_table"], np.float32),
            "at_sink": np.asarray(inputs["at_sink"][0], np.float32).reshape(1, 16),
            "at_edge": edge,
            "at_wo": np.ascontiguousarray(wo)}


NFFT = 2 * L
HY_MIN_DECAY = math.log(1e-2) / 1.5
HY_MAX_DECAY = math.log(1e-2) / 0.3
_FA, _FC, _FS, _FSN, _GCS, _GSNC, _HAC, _HASN, _NCR = 0, 256, 384, 512, 640, 896, 1152, 1216, 1280


def hy_consts():
    i = np.arange(128, dtype=np.float64)
    th = 2 * np.pi * np.outer(i, i) / 128.0
    cr = np.zeros((128, _NCR), np.float64)
    cr[:, _FA:_FA + 128] = np.cos(th)
    cr[:, _FA + 128:_FA + 256] = -np.sin(th)
    cr[:, _FC:_FC + 128] = np.cos(th)
    cr[:, _FS:_FS + 128] = np.sin(th)
    cr[:, _FSN:_FSN + 128] = -np.sin(th)
    cr[:, _GCS:_GCS + 128] = np.cos(th)
    cr[:, _GCS + 128:_GCS + 256] = np.sin(th)
    cr[:, _GSNC:_GSNC + 128] = -np.sin(th)
    cr[:, _GSNC + 128:_GSNC + 256] = np.cos(th)
    cr[:, _HAC:_HAC + 64] = np.cos(th[:, :64])
    cr[:, _HASN:_HASN + 64] = -np.sin(th[:, :64])
    tw = 2 * np.pi * np.outer(i, i) / NFFT
    cf = np.concatenate([np.tile(np.cos(tw), (1, 4)), np.tile(np.sin(tw), (1, 4))], axis=1)
    n = np.arange(NFFT)
    pos = np.where(n < L, n, L - (n - L)).astype(np.float64)
    pos[L] = 0.0
    t = pos / (L - 1)
    f = np.linspace(1e-4, 15.0, 16)
    ang = (2 * np.pi / L) * pos[None, :] * f[:, None]
    z = np.concatenate([t[None, :], np.cos(ang), -np.sin(ang)], axis=0)
    tdec = t.copy()
    tdec[L] = 1.0e4
    deltas = np.abs(np.linspace(HY_MIN_DECAY, HY_MAX_DECAY, D))
    return {"hy_cr": cr.astype(np.float32), "hy_cf": cf.astype(np.float32), "hy_z": z.astype(np.float32),
            "hy_tdec": tdec.astype(np.float32).reshape(1, NFFT),
            "hy_negdelta_full": (-deltas).astype(np.float32)}


class HyFFT:
    def __init__(self, c, inverse):
        self.c = c
        self.inverse = inverse
        cr_d = c.din("hy_cr", [128, _NCR])
        cf_d = c.din("hy_cf", [128, 1024])
        self.cr = c.sb(_NCR, F32R)
        self.cf = c.sb(1024)
        self.t_k = T()
        c.op("pool", lambda e: e.dma_start(out=self.cr, in_=cr_d), writes=[self.t_k], dma=True)
        self.t_k2 = T()
        c.op("sp", lambda e: e.dma_start(out=self.cf, in_=cf_d), writes=[self.t_k2], dma=True)
        self.C2 = self.cf[:, 0:512]
        self.S2 = self.cf[:, 512:1024]
        P = c.psum
        mk2 = lambda dt=F32: [c.sb(512, dt) for _ in range(2)]
        self.t1, self.t2 = mk2(), mk2()
        self.t_t1, self.t_t2 = [T(), T()], [T(), T()]
        self.Bre, self.Bim = mk2(F32R), mk2(F32R)
        self.t_B = [[T(), T()], [T(), T()]]
        self.ps_a, self.t_psa = P[0:2], [T(), T()]
        self.ps_x, self.t_psx = P[2:4], [T(), T()]
        if inverse:
            self.u1, self.u2 = mk2(), mk2()
            self.t_u1, self.t_u2 = [T(), T()], [T(), T()]
            self.m = [c.sb(512) for _ in range(4)]
            self.t_m = [T() for _ in range(4)]
            self.Zre, self.Zim = mk2(F32R), mk2(F32R)
            self.t_Z = [[T(), T()], [T(), T()]]
            self.Vre, self.Vim = mk2(F32R), mk2(F32R)
            self.t_V = [[T(), T()], [T(), T()]]
            self.ps_v, self.t_psv = P[4:6], [T(), T()]
            self.ps_y, self.t_psy = P[6], T()

    def _tw_mul(self, ps, t_ps, a1, a2, t_a1, t_a2):
        c = self.c
        for b in range(2):
            c.op("dve", lambda e, b=b: e.tensor_tensor(out=a1[b], in0=ps[b][:, :], in1=self.C2, op=ALU.mult), reads=[t_ps[b], self.t_k2], writes=[t_a1[b]])
            c.op("dve", lambda e, b=b: e.tensor_tensor(out=a2[b], in0=ps[b][:, :], in1=self.S2, op=ALU.mult), reads=[t_ps[b], self.t_k2], writes=[t_a2[b]])

    def _tw_comb(self, a1, a2, t_a1, t_a2, outre, outim, t_out, forward):
        c = self.c
        for b in range(2):
            v1 = a1[b].rearrange("p (s c k) -> p s c k", s=2, c=2)
            v2 = a2[b].rearrange("p (s c k) -> p s c k", s=2, c=2)
            ore = outre[:, b * 256:(b + 1) * 256].rearrange("p (s k) -> p s k", s=2)
            oim = outim[:, b * 256:(b + 1) * 256].rearrange("p (s k) -> p s k", s=2)
            op_re, op_im = (ALU.add, ALU.subtract) if forward else (ALU.subtract, ALU.add)
            c.op("pool", lambda e, v1=v1, v2=v2, ore=ore, op_re=op_re: e.tensor_tensor(out=ore, in0=v1[:, :, 0, :], in1=v2[:, :, 1, :], op=op_re),
                 reads=[t_a1[b], t_a2[b]], writes=[t_out[0]])
            c.op("pool", lambda e, v1=v1, v2=v2, oim=oim, op_im=op_im: e.tensor_tensor(out=oim, in0=v1[:, :, 1, :], in1=v2[:, :, 0, :], op=op_im),
                 reads=[t_a1[b], t_a2[b]], writes=[t_out[1]])

    def stage(self, st, g, J):
        c = self.c
        cr = self.cr
        par = g % 2
        c0 = g * 4
        if st == 0:
            ut, ka = J["ut"], J["ka"]
            for s in range(4):
                b = s // 2
                c.op("pe", lambda e, s=s, b=b, ut=ut, ka=ka, c0=c0: e.matmul(self.ps_a[b][:, (s % 2) * 256:(s % 2 + 1) * 256], ut[0:ka, (c0 + s) * 128:(c0 + s + 1) * 128], cr[0:ka, _FA:_FA + 256], start=True, stop=True),
                     reads=[J["t_ut"][g], self.t_k], writes=[self.t_psa[b]])
        elif st == 1:
            self._tw_mul(self.ps_a, self.t_psa, self.t1, self.t2, self.t_t1, self.t_t2)
        elif st == 2:
            self._tw_comb(self.t1, self.t2, self.t_t1, self.t_t2, self.Bre[par], self.Bim[par], self.t_B[par], True)
        elif st == 3:
            Bre, Bim = self.Bre[par], self.Bim[par]
            rB = [self.t_B[par][0], self.t_B[par][1], self.t_k]
            c.op("pe", lambda e, Bre=Bre: e.matmul(self.ps_x[0][:, :], cr[:, _FC:_FC + 128], Bre, start=True, stop=False), reads=rB, writes=[self.t_psx[0]])
            c.op("pe", lambda e, Bim=Bim: e.matmul(self.ps_x[0][:, :], cr[:, _FS:_FS + 128], Bim, start=False, stop=True), reads=rB, writes=[self.t_psx[0]])
            c.op("pe", lambda e, Bim=Bim: e.matmul(self.ps_x[1][:, :], cr[:, _FC:_FC + 128], Bim, start=True, stop=False), reads=rB, writes=[self.t_psx[1]])
            c.op("pe", lambda e, Bre=Bre: e.matmul(self.ps_x[1][:, :], cr[:, _FSN:_FSN + 128], Bre, start=False, stop=True), reads=rB, writes=[self.t_psx[1]])
        elif not self.inverse:
            if st == 4:
                J["spec_out"](g, self.ps_x, self.t_psx)
        elif st == 4:
            kt, t_kt = J["kt"](g)
            m, t_m = self.m, self.t_m
            xr, xi = self.ps_x[0], self.ps_x[1]
            c.op("dve", lambda e, kt=kt: e.tensor_tensor(out=m[0], in0=xr[:, :], in1=kt[:, 0:512], op=ALU.mult), reads=[self.t_psx[0], t_kt], writes=[t_m[0]])
            c.op("dve", lambda e, kt=kt: e.tensor_tensor(out=m[1], in0=xi[:, :], in1=kt[:, 512:1024], op=ALU.mult), reads=[self.t_psx[1], t_kt], writes=[t_m[1]])
            c.op("dve", lambda e, kt=kt: e.tensor_tensor(out=m[2], in0=xr[:, :], in1=kt[:, 512:1024], op=ALU.mult), reads=[self.t_psx[0], t_kt], writes=[t_m[2]])
            c.op("dve", lambda e, kt=kt: e.tensor_tensor(out=m[3], in0=xi[:, :], in1=kt[:, 0:512], op=ALU.mult), reads=[self.t_psx[1], t_kt], writes=[t_m[3]])
        elif st == 5:
            m, t_m = self.m, self.t_m
            Zre, Zim = self.Zre[par], self.Zim[par]
            c.op("pool", lambda e, Zre=Zre: e.tensor_tensor(out=Zre, in0=m[0], in1=m[1], op=ALU.subtract), reads=[t_m[0], t_m[1]], writes=[self.t_Z[par][0]])
            c.op("pool", lambda e, Zim=Zim: e.tensor_tensor(out=Zim, in0=m[2], in1=m[3], op=ALU.add), reads=[t_m[2], t_m[3]], writes=[self.t_Z[par][1]])
        elif st == 6:
            Zre, Zim = self.Zre[par], self.Zim[par]
            for s in range(4):
                b = s // 2
                reg = self.ps_v[b][:, (s % 2) * 256:(s % 2 + 1) * 256]
                c.op("pe", lambda e, s=s, reg=reg, Zre=Zre: e.matmul(reg, Zre[:, s * 128:(s + 1) * 128], cr[:, _GCS:_GCS + 256], start=True, stop=False),
                     reads=[self.t_Z[par][0], self.t_k], writes=[self.t_psv[b]])
                c.op("pe", lambda e, s=s, reg=reg, Zim=Zim: e.matmul(reg, Zim[:, s * 128:(s + 1) * 128], cr[:, _GSNC:_GSNC + 256], start=False, stop=True),
                     reads=[self.t_Z[par][1], self.t_k], writes=[self.t_psv[b]])
        elif st == 7:
            self._tw_mul(self.ps_v, self.t_psv, self.u1, self.u2, self.t_u1, self.t_u2)
        elif st == 8:
            self._tw_comb(self.u1, self.u2, self.t_u1, self.t_u2, self.Vre[par], self.Vim[par], self.t_V[par], False)
        elif st == 9:
            Vre, Vim = self.Vre[par], self.Vim[par]
            rV = [self.t_V[par][0], self.t_V[par][1], self.t_k]
            c.op("pe", lambda e, Vre=Vre: e.matmul(self.ps_y[0:64, :], cr[:, _HAC:_HAC + 64], Vre, start=True, stop=False), reads=rV, writes=[self.t_psy])
            c.op("pe", lambda e, Vim=Vim: e.matmul(self.ps_y[0:64, :], cr[:, _HASN:_HASN + 64], Vim, start=False, stop=True), reads=rV, writes=[self.t_psy])
        elif st == 10:
            yt = J["yt"]
            c.op("act", lambda e, c0=c0, yt=yt: e.activation(out=yt[0:64, c0 * 128:(c0 + 4) * 128], in_=self.ps_y[0:64, :], func=AF.Copy), reads=[self.t_psy], writes=[J["t_ut"][g]])

    def run(self, J, ngroups=32):
        nst = 11 if self.inverse else 5
        for t in range(ngroups + nst - 1):
            if "pre" in J:
                J["pre"](t)
            for st in range(nst - 1, -1, -1):
                g = t - st
                if 0 <= g < ngroups:
                    self.stage(st, g, J)


def to_time_major(c, src_ct, t_src, ut, t_ut, na, ps, t_ps):
    v = src_ct.rearrange("p (a r) -> p r a", r=128)
    u3 = ut[0:na, :].rearrange("p (c r) -> p c r", r=128)
    for r0 in range(0, 128, 4):
        b = (r0 // 4) % 2
        for j in range(4):
            c.op("pe", lambda e, r0=r0, j=j, b=b: e.transpose(out=ps[b][0:na, j * 128:(j + 1) * 128], in_=v[:, r0 + j, :], identity=c.ident),
                 reads=[t_src, c.t_const], writes=[t_ps[b]])
        c.op("act", lambda e, r0=r0, b=b: e.activation(out=u3[:, :, r0:r0 + 4], in_=ps[b][0:na, :].rearrange("p (r c) -> p c r", r=4), func=AF.Copy),
             reads=[t_ps[b]], writes=(t_ut if isinstance(t_ut, list) else [t_ut]))


def to_feature_major(c, yt, t_yt, dst_ct, t_dst, ps, t_ps):
    y3 = yt[0:64, :].rearrange("p (c r) -> p r c", r=128)
    d3 = dst_ct.rearrange("p (a r) -> p r a", r=128)
    for r0 in range(0, 128, 8):
        b = (r0 // 8) % 2
        for j in range(8):
            c.op("pe", lambda e, r0=r0, j=j, b=b: e.transpose(out=ps[b][:, j * 64:(j + 1) * 64], in_=y3[:, r0 + j, :], identity=c.ident[0:64, 0:64]),
                 reads=(t_yt if isinstance(t_yt, list) else [t_yt]) + [c.t_const], writes=[t_ps[b]])
        c.op("dve", lambda e, r0=r0, b=b: e.tensor_copy(out=d3[:, r0:r0 + 8, :], in_=ps[b][:, :].rearrange("p (r a) -> p r a", r=8)),
             reads=[t_ps[b]], writes=[t_dst])


NCC = 4


def hy_scratch(c, tag):
    nc = c.nc
    zin = [[nc.dram_tensor("hy_zin%d_%d" % (hf, cc), [128, NTL], F32) for cc in range(NCC)] for hf in range(2)]
    zg = [[nc.dram_tensor("hy_zg%d_%d" % (hf, cc), [256, NTL], F32) for cc in range(NCC)] for hf in range(2)]
    return {"p": nc.dram_tensor("hy_p" + tag, [3 * NCC, 128, L], F32).ap(),
            "zin": zin, "zg": zg,
            "h3": nc.dram_tensor("hy_h3" + tag, [64, NFFT], F32).ap(),
            "ks": nc.dram_tensor("hy_ks" + tag, [2, NCC, 32, 128, 1024], F32).ap()}


def hy_inproj_phase(c, rows_fn, S, s):
    c.new_phase()
    NO = 3 * NCC * 128
    g_d = c.din("hy_g%d" % s, [128, 8])
    w_d = c.din("hy_win%d" % s, [8, 128, NO])
    gcol = c.sb(8)
    t_g = T()
    c.op("sp", lambda e: e.dma_start(out=gcol, in_=g_d), writes=[t_g], dma=True)
    win = [c.sb(NO, F32R) for _ in range(8)]
    t_w = [T() for _ in range(8)]
    for k in range(8):
        c.op("pool", lambda e, k=k: e.dma_start(out=win[k], in_=w_d[k]), writes=[t_w[k]], dma=True)
    frs = [Front(c), Front(c)]
    stage = [c.sb(TT) for _ in range(4)]
    t_st = [T() for _ in range(4)]
    P = c.psum
    ps_t, t_pst = P[0:2], [T(), T()]
    ps_o, t_pso = P[2:6], [T() for _ in range(4)]
    cnt = 0
    ntile = L // TT

    def prep(it):
        fr = frs[it % 2]
        r0 = it * TT
        fr.load_norm(None, r0, rows=[rows_fn(r0 + tb * 128) for tb in range(4)])
        fr.transpose(gcol, t_g, ps_t, t_pst)

    prep(0)
    for it in range(ntile):
        r0 = it * TT
        fr = frs[it % 2]
        if it + 1 < ntile:
            prep(it + 1)
        for oc in range(3 * NCC):
            b = cnt % 4
            cnt += 1
            for k in range(8):
                c.op("pe", lambda e, k=k, oc=oc, b=b, fr=fr: e.matmul(ps_o[b][:, :], win[k][:, oc * 128:(oc + 1) * 128], fr.hT[k], start=(k == 0), stop=(k == 7)),
                     reads=[t_w[k], fr.t_hT[k]], writes=[t_pso[b]])
            if oc % 2 == 0:
                c.op("act", lambda e, b=b: e.activation(out=stage[b], in_=ps_o[b][:, :], func=AF.Copy), reads=[t_pso[b]], writes=[t_st[b]])
            else:
                c.op("dve", lambda e, b=b: e.tensor_copy(out=stage[b], in_=ps_o[b][:, :]), reads=[t_pso[b]], writes=[t_st[b]])
            c.op("sp", lambda e, b=b, oc=oc, r0=r0: e.dma_start(out=S["p"][oc][:, r0:r0 + TT], in_=stage[b]), reads=[t_st[b]], dma=True)


def hy_conv3_phase(c, S, s):
    c.new_phase()
    cols_d = c.din("hy_c3cols%d" % s, [128, 3 * NCC * 5])
    cols = c.sb(3 * NCC * 5)
    t_c = T()
    c.op("sp", lambda e: e.dma_start(out=cols, in_=cols_d), writes=[t_c], dma=True)
    raw = [c.sb(L + 2) for _ in range(2)]
    t_raw = [T(), T()]
    out = [c.sb(L) for _ in range(2)]
    t_out = [T(), T()]
    for oc in range(3 * NCC):
        b = oc % 2
        eng = "dve"
        k0 = oc * 5
        c.op("sp", lambda e, b=b, oc=oc: e.dma_start(out=raw[b][:, 1:L + 1], in_=S["p"][oc]), writes=[t_raw[b]], dma=True)
        c.op("act", lambda e, b=b, k0=k0: e.activation(out=raw[b][:, 1:L + 1], in_=raw[b][:, 1:L + 1], func=AF.Identity, bias=cols[:, k0:k0 + 1]),
             reads=[t_raw[b], t_c], writes=[t_raw[b]])
        c.op(eng, lambda e, b=b: e.memset(raw[b][:, 0:1], 0.0), reads=[t_raw[b]], writes=[t_raw[b]])
        c.op(eng, lambda e, b=b: e.memset(raw[b][:, L + 1:L + 2], 0.0), reads=[t_raw[b]], writes=[t_raw[b]])
        c.op("act", lambda e, b=b, k0=k0: e.activation(out=out[b], in_=raw[b][:, 0:L], func=AF.Identity, scale=cols[:, k0 + 1:k0 + 2], bias=cols[:, k0 + 4:k0 + 5]),
             reads=[t_raw[b], t_c], writes=[t_out[b]])
        c.op(eng, lambda e, b=b, k0=k0: e.scalar_tensor_tensor(out=out[b], in0=raw[b][:, 1:L + 1], scalar=cols[:, k0 + 2:k0 + 3], in1=out[b], op0=ALU.mult, op1=ALU.add),
             reads=[t_raw[b], t_c, t_out[b]], writes=[t_out[b]])
        c.op(eng, lambda e, b=b, k0=k0: e.scalar_tensor_tensor(out=out[b], in0=raw[b][:, 2:L + 2], scalar=cols[:, k0 + 3:k0 + 4], in1=out[b], op0=ALU.mult, op1=ALU.add),
             reads=[t_raw[b], t_c, t_out[b]], writes=[t_out[b]])
        c.op("sp", lambda e, b=b, oc=oc: e.dma_start(out=S["p"][oc], in_=out[b]), reads=[t_out[b]], dma=True)


def hy_mlp_phase(c, S, s):
    c.new_phase()
    z_d = c.din("hy_z", [33, NFFT])
    w1_d = c.din("hy_w1_%d" % s, [33, 64])
    wi_d = c.din("hy_wi_%d" % s, [64, 128])
    cols_d = c.din("hy_mlpcols%d" % s, [64, 4])
    w1 = c.sb(64, F32R, parts=33)
    wi = c.sb(128, F32R, parts=64)
    cols = c.sb(4, parts=64)
    negpi = c.sb(1, parts=64)
    t_k = T()
    c.op("pool", lambda e: e.dma_start(out=w1, in_=w1_d), writes=[t_k], dma=True)
    t_k1 = T()
    c.op("pool", lambda e: e.dma_start(out=wi, in_=wi_d), writes=[t_k1], dma=True)
    t_k2 = T()
    c.op("sp", lambda e: e.dma_start(out=cols, in_=cols_d), writes=[t_k2], dma=True)
    c.op("dve", lambda e: e.memset(negpi, -math.pi), writes=[t_k2])
    NI = 4
    zt = [c.sb(TT, F32R, parts=33) for _ in range(NI)]
    t_zt = [T() for _ in range(NI)]
    arg = [c.sb(TT, parts=64) for _ in range(NI)]
    t_arg = [T() for _ in range(NI)]
    kk = [c.sb(TT, parts=64) for _ in range(NI)]
    t_kk = [T() for _ in range(NI)]
    MAGIC = 12582912.0
    hh = [[c.sb(TT, F32R, parts=64) for _ in range(2)] for _ in range(NI)]
    t_hh = [[T(), T()] for _ in range(NI)]
    h3 = [c.sb(TT, parts=64) for _ in range(NI)]
    t_h3 = [T() for _ in range(NI)]
    P = c.psum
    ps, t_ps = P[0:NI], [T() for _ in range(NI)]
    for it0 in range(0, NFFT // TT, NI):
        for p in range(NI):
            n0 = (it0 + p) * TT
            c.op("pool", lambda e, p=p, n0=n0: e.dma_start(out=zt[p], in_=z_d[:, n0:n0 + TT]), writes=[t_zt[p]], dma=True)
        for layer in range(3):
            for p in range(NI):
                n0 = (it0 + p) * TT
                if layer == 0:
                    c.op("pe", lambda e, p=p: e.matmul(ps[p][0:64, :], w1, zt[p], start=True, stop=True), reads=[t_k, t_zt[p]], writes=[t_ps[p]])
                else:
                    c.op("pe", lambda e, p=p, layer=layer: e.matmul(ps[p][0:64, :], wi[:, (layer - 1) * 64:layer * 64], hh[p][(layer - 1) % 2], start=True, stop=True),
                         reads=[t_k1, t_hh[p][(layer - 1) % 2]], writes=[t_ps[p]])
            for p in range(NI):
                c.op("dve", lambda e, p=p, layer=layer: e.tensor_scalar(out=arg[p], in0=ps[p][0:64, :], scalar1=cols[:, layer:layer + 1], scalar2=cols[:, 3:4], op0=ALU.add, op1=ALU.mult),
                     reads=[t_ps[p], t_k2], writes=[t_arg[p]])
            for p in range(NI):
                c.op("dve", lambda e, p=p: e.tensor_scalar(out=kk[p], in0=arg[p], scalar1=1.0 / (2 * math.pi), scalar2=MAGIC, op0=ALU.mult, op1=ALU.add),
                     reads=[t_arg[p]], writes=[t_kk[p]])
            for p in range(NI):
                c.op("dve", lambda e, p=p: e.tensor_scalar(out=kk[p], in0=kk[p], scalar1=-MAGIC, scalar2=2 * math.pi, op0=ALU.add, op1=ALU.mult),
                     reads=[t_kk[p]], writes=[t_kk[p]])
            for p in range(NI):
                c.op("dve", lambda e, p=p: e.tensor_tensor(out=arg[p], in0=arg[p], in1=kk[p], op=ALU.subtract),
                     reads=[t_arg[p], t_kk[p]], writes=[t_arg[p]])
            for p in range(NI):
                n0 = (it0 + p) * TT
                if layer < 2:
                    c.op("act", lambda e, p=p, layer=layer: e.activation(out=hh[p][layer % 2], in_=arg[p], func=AF.Sin), reads=[t_arg[p]], writes=[t_hh[p][layer % 2]])
                else:
                    c.op("act", lambda e, p=p: e.activation(out=h3[p], in_=arg[p], func=AF.Sin), reads=[t_arg[p]], writes=[t_h3[p]])
                    c.op("sp", lambda e, p=p, n0=n0: e.dma_start(out=S["h3"][:, n0:n0 + TT], in_=h3[p]), reads=[t_h3[p]], dma=True)


def hy_filter_phase(c, S, s):
    c.new_phase()
    w3_d = c.din("hy_w3_%d" % s, [64, 4 * NCC * 128])
    nd_d = c.din("hy_negdelta", [128, NCC])
    td_d = c.din("hy_tdec", [1, NFFT])
    ff = HyFFT(c, inverse=False)
    w3 = c.sb(4 * NCC * 128, F32R, parts=64)
    t_w3 = T()
    c.op("pool", lambda e: e.dma_start(out=w3, in_=w3_d), writes=[t_w3], dma=True)
    nd = c.sb(NCC)
    t_nd = T()
    c.op("sp", lambda e: e.dma_start(out=nd, in_=nd_d), writes=[t_nd], dma=True)
    kT = c.sb(NFFT)
    t_kT = T()
    ut = c.sb(NFFT, F32R)
    t_ut = T()
    h3t = [c.sb(TT, F32R, parts=64) for _ in range(2)]
    t_h3t = [T(), T()]
    tdb = [c.sb(TT) for _ in range(2)]
    t_tdb = [T(), T()]
    dec = [c.sb(TT) for _ in range(2)]
    t_dec = [T(), T()]
    junk = c.sb(TT)
    t_junk = T()
    asum = c.sb(40)
    t_as = T()
    kst = [c.sb(1024) for _ in range(2)]
    t_kst = [T(), T()]
    P = c.psum
    ps_k, t_psk = P[4:6], [T(), T()]
    ps_tr, t_pstr = P[6:8], [T(), T()]
    cnt = 0
    for cc in range(NCC):
        for o in range(2):
            c.op("dve", lambda e: e.memset(asum[:, 0:40], 0.0), writes=[t_as])
            for it in range(NFFT // TT):
                n0 = it * TT
                b = cnt % 2
                cnt += 1
                dirn = 0 if n0 < L else 1
                col = (dirn * 2 + o) * NCC * 128 + cc * 128
                c.op("pool", lambda e, b=b, n0=n0: e.dma_start(out=h3t[b], in_=S["h3"][:, n0:n0 + TT]), writes=[t_h3t[b]], dma=True)
                c.op("sp", lambda e, b=b, n0=n0: e.dma_start(out=tdb[b], in_=td_d[:, n0:n0 + TT].broadcast_to([128, TT])), writes=[t_tdb[b]], dma=True)
                c.op("pe", lambda e, b=b, col=col: e.matmul(ps_k[b][:, :], w3[:, col:col + 128], h3t[b], start=True, stop=True), reads=[t_w3, t_h3t[b]], writes=[t_psk[b]])
                c.op("act", lambda e, b=b, cc=cc: e.activation(out=dec[b], in_=tdb[b], func=AF.Exp, scale=nd[:, cc:cc + 1]), reads=[t_tdb[b], t_nd], writes=[t_dec[b]])
                c.op("dve", lambda e, b=b, n0=n0: e.tensor_tensor(out=kT[:, n0:n0 + TT], in0=ps_k[b][:, :], in1=dec[b], op=ALU.mult), reads=[t_psk[b], t_dec[b]], writes=[t_kT])
                c.op("act", lambda e, n0=n0, it=it: e.activation(out=junk, in_=kT[:, n0:n0 + TT], func=AF.Abs, accum_out=asum[:, it:it + 1]), reads=[t_kT], writes=[t_junk, t_as])
            c.op("dve", lambda e: e.reduce_sum(out=asum[:, 32:33], in_=asum[:, 0:32], axis=AX.X), reads=[t_as], writes=[t_as])
            c.op("dve", lambda e: e.reciprocal(out=asum[:, 33:34], in_=asum[:, 32:33]), reads=[t_as], writes=[t_as])
            c.op("dve", lambda e: e.tensor_scalar(out=asum[:, 33:34], in0=asum[:, 33:34], scalar1=1.0 / NFFT, scalar2=0.0, op0=ALU.mult, op1=ALU.add), reads=[t_as], writes=[t_as])
            for q in range(4):
                eng = "dve" if q % 2 == 0 else "pool"
                c.op(eng, lambda e, q=q: e.tensor_scalar(out=kT[:, q * 4096:(q + 1) * 4096], in0=kT[:, q * 4096:(q + 1) * 4096], scalar1=asum[:, 33:34], scalar2=0.0, op0=ALU.mult, op1=ALU.add),
                     reads=[t_kT, t_as], writes=[t_kT])
            to_time_major(c, kT, t_kT, ut, t_ut, 128, ps_tr, t_pstr)
            def spec_out(g, ps_x, t_psx, o=o, cc=cc):
                kb = g % 2
                c.op("act", lambda e, kb=kb: e.activation(out=kst[kb][:, 0:512], in_=ps_x[0][:, :], func=AF.Copy), reads=[t_psx[0]], writes=[t_kst[kb]])
                c.op("act", lambda e, kb=kb: e.activation(out=kst[kb][:, 512:1024], in_=ps_x[1][:, :], func=AF.Copy), reads=[t_psx[1]], writes=[t_kst[kb]])
                c.op("sp", lambda e, kb=kb, g=g: e.dma_start(out=S["ks"][o, cc, g], in_=kst[kb]), reads=[t_kst[kb]], dma=True)
            ff.run({"ut": ut, "t_ut": [t_ut] * 32, "ka": 128, "spec_out": spec_out})


def hy_conv_phase(c, S, s):
    c.new_phase()
    fb_d = c.din("hy_fbias%d" % s, [128, 2 * NCC])
    ff = HyFFT(c, inverse=True)
    fb = c.sb(2 * NCC)
    t_fb = T()
    c.op("sp", lambda e: e.dma_start(out=fb, in_=fb_d), writes=[t_fb], dma=True)
    bufA = c.sb(L)
    bufB = c.sb(L)
    t_A, t_B = T(), T()
    off_ut = (c.off + 7) // 8 * 8
    ut = c.sb(64 * 256, F32R)
    yt = c.sb_at(off_ut, 64 * 256, F32)
    t_utg = [T() for _ in range(32)]
    PC = 512
    xp = [c.sb(PC) for _ in range(2)]
    t_xp = [T(), T()]
    kt = [c.sb(1024) for _ in range(3)]
    t_kt = [T(), T(), T()]
    P = c.psum
    ps_tr, t_pstr = [P[6], P[7]], [T(), T()]
    t_pstr[0] = ff.t_psy
    npc = 0
    nk = 0
    for cc in range(NCC):
        c.op("sp", lambda e, cc=cc: e.dma_start(out=bufA, in_=S["p"][2 * NCC + cc]), writes=[t_A], dma=True)
        src, t_src, dstb, t_dst = bufA, t_A, bufB, t_B
        for o in range(2):
            to_time_major(c, src, t_src, ut, t_utg, 64, ps_tr, t_pstr)
            ktmap = {}

            def pre(t, o=o, cc=cc, ktmap=ktmap):
                g = t - 2
                if 0 <= g < 32:
                    kb = g % 3
                    c.op("sp", lambda e, kb=kb, g=g: e.dma_start(out=kt[kb], in_=S["ks"][o, cc, g]), writes=[t_kt[kb]], dma=True)
                    ktmap[g] = (kt[kb], t_kt[kb])
            ff.run({"ut": ut, "t_ut": t_utg, "ka": 64, "yt": yt, "kt": lambda g, ktmap=ktmap: ktmap[g], "pre": pre})
            to_feature_major(c, yt, t_utg, dstb, t_dst, ps_tr, t_pstr)
            gate_chunk = (0 if o == 0 else NCC) + cc
            for pc in range(L // PC):
                pb = npc % 2
                npc += 1
                sl = slice(pc * PC, (pc + 1) * PC)
                c.op("sp", lambda e, pb=pb, gate_chunk=gate_chunk, sl=sl: e.dma_start(out=xp[pb], in_=S["p"][gate_chunk][:, sl]), writes=[t_xp[pb]], dma=True)
                eng = "pool"
                c.op("dve", lambda e, sl=sl, o=o, cc=cc, src=src, dstb=dstb: e.scalar_tensor_tensor(out=dstb[:, sl], in0=src[:, sl], scalar=fb[:, o * NCC + cc:o * NCC + cc + 1], in1=dstb[:, sl], op0=ALU.mult, op1=ALU.add),
                     reads=[t_src, t_dst, t_fb], writes=[t_dst])
                c.op(eng, lambda e, sl=sl, pb=pb, dstb=dstb: e.tensor_tensor(out=dstb[:, sl], in0=dstb[:, sl], in1=xp[pb], op=ALU.mult),
                     reads=[t_dst, t_xp[pb]], writes=[t_dst])
            src, t_src, dstb, t_dst = dstb, t_dst, src, t_src
        for hf in range(2):
            t_z = T()
            c.op("sp", lambda e, cc=cc, src=src, hf=hf: e.dma_start(out=S["zin"][hf][cc].ap(), in_=src[:, hf * NTL:(hf + 1) * NTL]), reads=[t_src], writes=[t_z], dma=True)
            c.op("pool", lambda e, cc=cc, hf=hf: e.collective_compute("AllGather", ALU.bypass, replica_groups=PAIRS, ins=[S["zin"][hf][cc].ap()], outs=[S["zg"][hf][cc].ap()]),
                 reads=[t_z], writes=[T()], cc=True)


def hy_outproj_phase(c, src, dst, S, s):
    c.new_phase()
    w_d = c.din("hy_wout%d" % s, [8, 128, D])
    b_d = c.din("hy_bout%d" % s, [1, D])
    sel_d = c.din("hy_sel", [128, 2])
    wo = [c.sb(D, F32R) for _ in range(8)]
    t_wo = [T() for _ in range(8)]
    for k in range(8):
        c.op("pool", lambda e, k=k: e.dma_start(out=wo[k], in_=w_d[k]), writes=[t_wo[k]], dma=True)
    brow = c.sb(D)
    t_b = T()
    c.op("sp", lambda e: e.dma_start(out=brow, in_=b_d.broadcast_to([128, D])), writes=[t_b], dma=True)
    sel = c.sb(2)
    t_sel = T()
    c.op("sp", lambda e: e.dma_start(out=sel, in_=sel_d), writes=[t_sel], dma=True)
    zT = [[c.sb(TT, F32R) for _ in range(8)] for _ in range(2)]
    t_zT = [[T() for _ in range(8)] for _ in range(2)]
    zA = [c.sb(TT) for _ in range(2)]
    zB = [c.sb(TT) for _ in range(2)]
    t_zA = [T(), T()]
    t_zB = [T(), T()]
    xin = [c.sb(D) for _ in range(4)]
    t_xin = [T() for _ in range(4)]
    P = c.psum
    ps, t_ps = P[0:4], [T() for _ in range(4)]
    cnt = 0
    nz = 0
    for it in range(NTL // TT):
        r0 = it * TT
        zb = it % 2
        for k in range(8):
            rank, cc = k // NCC, k % NCC
            b = nz % 2
            nz += 1
            c.op("sp", lambda e, b=b, rank=rank, cc=cc, r0=r0: e.dma_start(out=zA[b], in_=S["zg"][0][cc].ap()[rank * 128:(rank + 1) * 128, r0:r0 + TT]), writes=[t_zA[b]], dma=True)
            c.op("sp", lambda e, b=b, rank=rank, cc=cc, r0=r0: e.dma_start(out=zB[b], in_=S["zg"][1][cc].ap()[rank * 128:(rank + 1) * 128, r0:r0 + TT]), writes=[t_zB[b]], dma=True)
            c.op("dve", lambda e, b=b: e.tensor_scalar(out=zA[b], in0=zA[b], scalar1=sel[:, 0:1], scalar2=0.0, op0=ALU.mult, op1=ALU.add), reads=[t_zA[b], t_sel], writes=[t_zA[b]])
            c.op("dve", lambda e, b=b, k=k, zb=zb: e.scalar_tensor_tensor(out=zT[zb][k], in0=zB[b], scalar=sel[:, 1:2], in1=zA[b], op0=ALU.mult, op1=ALU.add),
                 reads=[t_zA[b], t_zB[b], t_sel], writes=[t_zT[zb][k]])
        for tb in range(4):
            xb = tb
            c.op("sp", lambda e, xb=xb, tb=tb, r0=r0: e.dma_start(out=xin[xb], in_=src[r0 + tb * 128:r0 + (tb + 1) * 128, :]), writes=[t_xin[xb]], dma=True)
            for dh in range(2):
                b = cnt % 4
                cnt += 1
                sl = slice(dh * 512, (dh + 1) * 512)
                for k in range(8):
                    c.op("pe", lambda e, k=k, zb=zb, tb=tb, sl=sl, b=b: e.matmul(ps[b][:, :], zT[zb][k][:, tb * 128:(tb + 1) * 128], wo[k][:, sl], start=(k == 0), stop=(k == 7)),
                         reads=[t_zT[zb][k], t_wo[k]], writes=[t_ps[b]])
                c.op("dve", lambda e, xb=xb, sl=sl, b=b: e.tensor_tensor(out=xin[xb][:, sl], in0=ps[b][:, :], in1=xin[xb][:, sl], op=ALU.add),
                     reads=[t_ps[b], t_xin[xb]], writes=[t_xin[xb]])
                c.op("pool", lambda e, xb=xb, sl=sl: e.tensor_tensor(out=xin[xb][:, sl], in0=xin[xb][:, sl], in1=brow[:, sl], op=ALU.add),
                     reads=[t_xin[xb], t_b], writes=[t_xin[xb]])
            c.op("sp", lambda e, xb=xb, tb=tb, r0=r0: e.dma_start(out=dst[r0 + tb * 128:r0 + (tb + 1) * 128, :], in_=xin[xb]), reads=[t_xin[xb]], dma=True)


def gather_x(c, src):
    c.new_phase()
    nc = c.nc
    outs = []
    for j in range(NTL // 512):
        cin = nc.dram_tensor("xg_in%d" % j, [512, D], F32)
        cg = nc.dram_tensor("xg_out%d" % j, [1024, D], F32)
        t_c = T()
        c.op("sp", lambda e, j=j, cin=cin: e.dma_start(out=cin.ap(), in_=src[j * 512:(j + 1) * 512, :]), writes=[t_c], dma=True)
        c.op("pool", lambda e, cin=cin, cg=cg: e.collective_compute("AllGather", ALU.bypass, replica_groups=PAIRS, ins=[cin.ap()], outs=[cg.ap()]),
             reads=[t_c], writes=[T()], cc=True)
        outs.append(cg)

    def rows_fn(R):
        rank, rr = R // NTL, R % NTL
        j, i = rr // 512, rr % 512
        return outs[j].ap()[rank * 512 + i:rank * 512 + i + 128, :]
    return rows_fn


def hyena_layer(c, rows_fn, src, dst, s):
    if not hasattr(c, "hyS"):
        c.hyS = hy_scratch(c, "")
    S = c.hyS
    hy_inproj_phase(c, rows_fn, S, s)
    hy_conv3_phase(c, S, s)
    hy_mlp_phase(c, S, s)
    hy_filter_phase(c, S, s)
    hy_conv_phase(c, S, s)
    hy_outproj_phase(c, src, dst, S, s)
    return S


def hyena_host(inputs, s, li, h):
    g = np.asarray(inputs["norm_mix"][li], np.float32)
    CH = NCC * 128
    csl = np.concatenate([np.arange(t * D + h * CH, t * D + (h + 1) * CH) for t in range(3)])
    b_in = np.asarray(inputs["hy_b_in"][s], np.float32)[csl]
    cw = np.asarray(inputs["hy_conv_w"][s], np.float32)[:, csl]
    cb = np.asarray(inputs["hy_conv_b"][s], np.float32)[csl]
    c3 = np.stack([b_in, cw[0], cw[1], cw[2], cb], axis=-1).reshape(3 * NCC, 128, 5).transpose(1, 0, 2).reshape(128, 3 * NCC * 5)
    mlpc = np.stack([np.asarray(inputs["hy_f_b1"][s], np.float32), np.asarray(inputs["hy_f_bi"][s][0], np.float32),
                     np.asarray(inputs["hy_f_bi"][s][1], np.float32), np.asarray(inputs["hy_f_freq"][s], np.float32)], axis=-1)
    wi = np.asarray(inputs["hy_f_wi"][s], np.float32)
    fbias = np.asarray(inputs["hy_f_bias"][s], np.float32)[:, h * CH:(h + 1) * CH].reshape(2, NCC, 128).transpose(2, 0, 1).reshape(128, 2 * NCC)
    w3 = np.asarray(inputs["hy_f_w3"][s], np.float32).reshape(64, 4, D)[:, :, h * CH:(h + 1) * CH].reshape(64, 4 * CH)
    sel = np.zeros((128, 2), np.float32)
    sel[:, h] = 1.0
    m = {"hy_g%d" % s: np.ascontiguousarray(g.reshape(8, 128).T),
         "hy_win%d" % s: np.ascontiguousarray(np.asarray(inputs["hy_w_in"][s], np.float32)[:, csl].reshape(8, 128, 3 * CH)),
         "hy_c3cols%d" % s: np.ascontiguousarray(c3),
         "hy_w1_%d" % s: np.asarray(inputs["hy_f_w1"][s], np.float32),
         "hy_wi_%d" % s: np.ascontiguousarray(np.concatenate([wi[0], wi[1]], axis=1)),
         "hy_mlpcols%d" % s: np.ascontiguousarray(mlpc),
         "hy_w3_%d" % s: np.ascontiguousarray(w3),
         "hy_fbias%d" % s: np.ascontiguousarray(fbias),
         "hy_wout%d" % s: np.ascontiguousarray(np.asarray(inputs["hy_w_out"][s], np.float32).reshape(8, 128, D)),
         "hy_bout%d" % s: np.asarray(inputs["hy_b_out"][s], np.float32).reshape(1, D),
         "hy_sel": sel}
    hc = hy_consts()
    hc["hy_negdelta"] = np.ascontiguousarray(hc["hy_negdelta_full"][h * CH:(h + 1) * CH].reshape(NCC, 128).T)
    del hc["hy_negdelta_full"]
    m.update(hc)
    return m


def build_program(plan):
    nc = bass.Bass("TRN2", target_bir_lowering=False)
    st = contextlib.ExitStack()
    with st:
        c = Ctx(nc, st)
        x_full = c.din("x", [L, D])
        x_loc = c.din("xloc", [NTL, D])
        y_out = nc.dram_tensor("y", [NTL, D], F32, kind="ExternalOutput").ap()
        bufs = [c.dscratch("xa", [NTL, D]), c.dscratch("xb", [NTL, D])]
        load_consts(c)
        cur = x_loc
        for pi, ph in enumerate(plan):
            last = pi == len(plan) - 1
            dst = y_out if last else bufs[pi % 2]
            if ph.startswith("ffn"):
                ffn_phase(c, cur, dst, int(ph[3:]), ntok=NTL)
            elif ph == "pool2":
                pool_phase(c, cur, dst)
            elif ph == "hyena0":
                hyena_layer(c, (lambda R: x_full[R:R + 128, :]) if pi == 0 else gather_x(c, cur), cur, dst, 0)
            elif ph == "hyena3":
                hyena_layer(c, gather_x(c, cur), cur, dst, 1)
            elif ph == "attn1":
                S = attn_scratch(c)
                hg = halo_exchange(c, cur)
                attn_qkv_phase(c, cur, S, hg)
                attn_core_phase(c, cur, dst, S)
            else:
                raise ValueError(ph)
            cur = dst
        c.mk.emit()
    return nc


def host_inputs(inputs, plan, h):
    m = {"ident": np.eye(128, dtype=np.float32)}
    for ph in plan:
        if ph.startswith("ffn"):
            m.update(ffn_host(inputs, int(ph[3:])))
        elif ph == "pool2":
            m.update(pool_host(inputs, h))
        elif ph == "attn1":
            m.update(attn_host(inputs, h))
        elif ph == "hyena0":
            m.update(hyena_host(inputs, 0, 0, h))
        elif ph == "hyena3":
            m.update(hyena_host(inputs, 1, 3, h))
    return m


DEBUG_HY = False
FULL_PLAN = ["hyena0", "ffn0", "attn1", "ffn1", "pool2", "ffn2", "hyena3", "ffn3"]


def run_plan(inputs, plan, x_override=None, trace=False):
    nc = build_program(plan)
    shared = [host_inputs(inputs, plan, h) for h in range(2)]
    x = np.asarray(inputs["x"], np.float32) if x_override is None else x_override
    nb = x.shape[0]
    in_maps = []
    for core in range(8):
        b, h = (core // 2) % nb, core % 2
        m = dict(shared[h])
        m["x"] = np.ascontiguousarray(x[b])
        m["xloc"] = np.ascontiguousarray(x[b, h * NTL:(h + 1) * NTL])
        in_maps.append(m)
    res = run_bass_kernel_spmd(nc, in_maps, core_ids=list(range(8)), trace=trace)
    out = np.stack([np.concatenate([res.results[2 * b]["y"], res.results[2 * b + 1]["y"]], axis=0) for b in range(nb)], axis=0)
    return out, res


def kernel(**inputs):
    out, _ = run_plan(inputs, FULL_PLAN)
    return out.astype(np.float32)
```

```python
import contextlib
import math

import numpy as np
import concourse.bass as bass
import concourse.mybir as mybir
from concourse.bass_utils import run_bass_kernel_spmd

F32 = mybir.dt.float32
F32R = mybir.dt.float32r
AF = mybir.ActivationFunctionType
ALU = mybir.AluOpType
AX = mybir.AxisListType

D = 1024
L = 8192
DFF = 2816
NF = DFF // 128
EPS = 1e-6
TT = 512
NBLK = L // 128
NSLOT = 20
SAME_ENGINE_SYNC = True


class T:
    __slots__ = ("w", "r")

    def __init__(self):
        self.w = []
        self.r = []


class Op:
    __slots__ = ("eng", "idx", "fn", "deps", "dma", "dj", "sig", "waited", "q", "inc")

    def __init__(self, eng, idx, fn, dma):
        self.eng = eng
        self.idx = idx
        self.fn = fn
        self.deps = ()
        self.dma = dma
        self.q = eng
        self.inc = 16
        self.dj = None
        self.sig = None
        self.waited = False


class MK:
    ENGS = ("pe", "act", "dve", "pool", "sp")

    def __init__(self, nc):
        self.nc = nc
        self.ops = {e: [] for e in self.ENGS}
        self.dma_ops = {e: [] for e in self.ENGS + ("cc",)}
        self.last_c = {e: None for e in self.ENGS}
        self.bar = None
        self.bar_seen = {e: True for e in self.ENGS}

    def barrier(self):
        deps = set()
        for e in self.ENGS:
            if self.last_c[e] is not None:
                deps.add(self.last_c[e])
            for o in self.dma_ops[e][-NSLOT:]:
                deps.add(o)
        for o in self.dma_ops["cc"][-NSLOT:]:
            deps.add(o)
        self.bar = deps
        self.bar_seen = {e: False for e in self.ENGS}

    def op(self, eng, fn, reads=(), writes=(), dma=False, cc=False):
        lst = self.ops[eng]
        dma = dma or cc
        o = Op(eng, len(lst), fn, dma)
        if cc:
            o.q = "cc"
            o.inc = 1
        lst.append(o)
        deps = set()
        if not self.bar_seen[eng]:
            self.bar_seen[eng] = True
            deps |= self.bar
        for t in reads:
            deps.update(t.w)
        for t in writes:
            deps.update(t.w)
            deps.update(t.r)
        if dma:
            dl = self.dma_ops[o.q]
            o.dj = len(dl)
            dl.append(o)
            if o.dj >= NSLOT:
                deps.add(dl[o.dj - NSLOT])
        else:
            self.last_c[eng] = o
        deps.discard(o)
        o.deps = deps
        for t in reads:
            if dma:
                t.r.append(o)
            else:
                t.r = [x for x in t.r if x.dma or x.eng != eng] + [o]
        for t in writes:
            t.w = [o]
            t.r = []
        return o

    @staticmethod
    def _skip(d, o):
        return (not d.dma) and d.eng == o.eng and (not o.dma) and (d.eng == "pe" or not SAME_ENGINE_SYNC)

    def emit(self):
        nc = self.nc
        for e in self.ENGS:
            for o in self.ops[e]:
                for d in o.deps:
                    if d.dma or self._skip(d, o):
                        continue
                    d.waited = True
        for e in self.ENGS:
            c = 0
            for o in self.ops[e]:
                if not o.dma and o.waited:
                    c += 1
                    o.sig = c
        with contextlib.ExitStack() as st:
            csem = {e: st.enter_context(nc.semaphore("c_" + e)) for e in ("pe", "act", "dve", "pool")}
            dsem = {}
            for q in self.ENGS + ("cc",):
                n = len(self.dma_ops[q])
                if n:
                    dsem[q] = [st.enter_context(nc.semaphore("d_%s_%d" % (q, i))) for i in range(min(NSLOT, n))]
            block = st.enter_context(nc.Block())
            mk = self

            def run(ename):
                def body(e):
                    known_c = {}
                    known_d = set()
                    for o in mk.ops[ename]:
                        cw = {}
                        for d in o.deps:
                            if d.dma:
                                key = (d.q, d.dj)
                                if key in known_d:
                                    continue
                                known_d.add(key)
                                e.wait_ge(dsem[d.q][d.dj % NSLOT], d.inc * (d.dj // NSLOT + 1))
                            else:
                                if mk._skip(d, o):
                                    continue
                                if known_c.get(d.eng, 0) >= d.sig:
                                    continue
                                cw[d.eng] = max(cw.get(d.eng, 0), d.sig)
                        for en, v in cw.items():
                            known_c[en] = v
                            e.wait_ge(csem[en], v)
                        ins = o.fn(e)
                        if o.dma:
                            ins.then_inc(dsem[o.q][o.dj % NSLOT], o.inc)
                        elif o.sig is not None:
                            ins.then_inc(csem[ename], 1)
                    tail = list(mk.dma_ops[ename][-NSLOT:])
                    if ename == "pool":
                        tail += mk.dma_ops["cc"][-NSLOT:]
                    for o in tail:
                        if (o.q, o.dj) not in known_d:
                            e.wait_ge(dsem[o.q][o.dj % NSLOT], o.inc * (o.dj // NSLOT + 1))
                return body

            if self.ops["sp"]:
                block.sync(run("sp"))
            if self.ops["pe"]:
                block.tensor(run("pe"))
            if self.ops["act"]:
                block.scalar(run("act"))
            if self.ops["dve"]:
                block.vector(run("dve"))
            if self.ops["pool"]:
                block.gpsimd(run("pool"))


ARENA = 51 * 1024


class Ctx:
    def __init__(self, nc, st):
        self.nc = nc
        self.mk = MK(nc)
        self.sb_base = 16512
        self.ntens = 0
        self.psum = [st.enter_context(nc.psum_tensor("psb%d" % i, [128, 512], F32)) for i in range(8)]
        self.off = 0
        self.base = 0
        self.dram = {}

    def din(self, name, shape, dt=F32):
        if name in self.dram:
            return self.dram[name].ap()
        t = self.nc.dram_tensor(name, list(shape), dt, kind="ExternalInput")
        self.dram[name] = t
        return t.ap()

    def dscratch(self, name, shape, dt=F32):
        t = self.nc.dram_tensor(name, list(shape), dt)
        return t.ap()

    def sb(self, cols, dt=F32, parts=128):
        self.off = (self.off + 7) // 8 * 8
        self.last_off = self.off
        self.ntens += 1
        t = self.nc.alloc_sbuf_tensor_at("t%d" % self.ntens, [parts, cols], dt, offset=self.sb_base + 4 * self.off)
        self.off += cols
        assert self.off <= ARENA, ("arena overflow", self.off)
        return t[:, :]

    def sb_at(self, off, cols, dt=F32, parts=128):
        self.ntens += 1
        t = self.nc.alloc_sbuf_tensor_at("t%d" % self.ntens, [parts, cols], dt, offset=self.sb_base + 4 * off)
        return t[:, :]

    def new_phase(self):
        self.mk.barrier()
        self.off = self.base

    def op(self, *a, **k):
        return self.mk.op(*a, **k)


def load_consts(c):
    c.ident = c.sb(128)
    c.t_const = T()
    ident_d = c.din("ident", [128, 128])
    c.op("sp", lambda e: e.dma_start(out=c.ident, in_=ident_d), writes=[c.t_const], dma=True)
    c.epsc = c.sb(1)
    c.op("dve", lambda e: e.memset(c.epsc, EPS), writes=[c.t_const])
    c.base = c.off


class Front:
    def __init__(self, c, want_hT=True, xn_dt=F32):
        self.c = c
        self.xin = [c.sb(D) for _ in range(4)]
        self.t_xin = [T() for _ in range(4)]
        self.xn = []
        for i in range(4):
            self.xn.append(c.sb(D, xn_dt))
            if i == 0:
                self.xn_off = c.last_off
        self.t_xn = [T() for _ in range(4)]
        self.ss = c.sb(4)
        self.t_ss = [T() for _ in range(4)]
        self.rstd = c.sb(4)
        self.t_rstd = [T() for _ in range(4)]
        if want_hT:
            self.hT = [c.sb(TT, F32R) for _ in range(8)]
            self.t_hT = [T() for _ in range(8)]
        self.tcount = 0

    def load_norm(self, src, r0, blocks=(0, 1, 2, 3), rows=None):
        c = self.c
        if rows is None:
            rows = [src[r0 + tb * 128:r0 + (tb + 1) * 128, :] for tb in range(4)]
        for tb in blocks:
            c.op("sp", lambda e, tb=tb, ap=rows[tb]: e.dma_start(out=self.xin[tb], in_=ap),
                 writes=[self.t_xin[tb]], dma=True)
        for tb in blocks:
            c.op("dve", lambda e, tb=tb: e.scalar_tensor_tensor(out=self.xn[tb], in0=self.xin[tb], scalar=1.0, in1=self.xin[tb], op0=ALU.mult, op1=ALU.mult, accum_out=self.ss[:, tb:tb + 1]),
                 reads=[self.t_xin[tb]], writes=[self.t_xn[tb], self.t_ss[tb]])
        for tb in blocks:
            c.op("act", lambda e, tb=tb: e.activation(out=self.rstd[:, tb:tb + 1], in_=self.ss[:, tb:tb + 1], func=AF.Sqrt, scale=1.0 / D, bias=c.epsc),
                 reads=[self.t_ss[tb], c.t_const], writes=[self.t_rstd[tb]])
        for tb in blocks:
            c.op("dve", lambda e, tb=tb: e.reciprocal(out=self.rstd[:, tb:tb + 1], in_=self.rstd[:, tb:tb + 1]),
                 reads=[self.t_rstd[tb]], writes=[self.t_rstd[tb]])
            if getattr(self, "scale_on_act", False):
                c.op("act", lambda e, tb=tb: e.activation(out=self.xn[tb], in_=self.xin[tb], func=AF.Copy, scale=self.rstd[:, tb:tb + 1]),
                     reads=[self.t_xin[tb], self.t_rstd[tb]], writes=[self.t_xn[tb]])
            else:
                c.op("dve", lambda e, tb=tb: e.tensor_scalar(out=self.xn[tb], in0=self.xin[tb], scalar1=self.rstd[:, tb:tb + 1], scalar2=0.0, op0=ALU.mult, op1=ALU.add),
                     reads=[self.t_xin[tb], self.t_rstd[tb]], writes=[self.t_xn[tb]])

    def transpose(self, gcol, t_g, pbanks, t_pb):
        c = self.c
        for k in range(8):
            b = self.tcount % len(pbanks)
            self.tcount += 1
            for tb in range(4):
                c.op("pe", lambda e, tb=tb, k=k, b=b: e.transpose(out=pbanks[b][:, tb * 128:(tb + 1) * 128], in_=self.xn[tb][:, k * 128:(k + 1) * 128], identity=c.ident),
                     reads=[self.t_xn[tb], c.t_const], writes=[t_pb[b]])
            c.op("act", lambda e, k=k, b=b: e.activation(out=self.hT[k], in_=pbanks[b][:, :], func=AF.Copy, scale=gcol[:, k:k + 1]),
                 reads=[t_pb[b], t_g], writes=[self.t_hT[k]])


def ffn_phase(c, src, dst, li, ntok=L):
    c.new_phase()
    g_d = c.din("ffn_g%d" % li, [128, 8])
    wgu_d = c.din("ffn_wgu%d" % li, [NF, 128, 2048])
    wd_d = c.din("ffn_wd%d" % li, [NF, 128, D])
    gcol = c.sb(8)
    t_g = T()
    c.op("sp", lambda e: e.dma_start(out=gcol, in_=g_d), writes=[t_g], dma=True)
    fr = Front(c)
    aT = [c.sb(TT, F32R) for _ in range(NF)]
    t_aT = [T() for _ in range(NF)]
    wd = [c.sb(D, F32R) for _ in range(NF)]
    t_wd = [T() for _ in range(NF)]
    wgu = [c.sb(2048, F32R) for _ in range(2)] + [c.sb_at(fr.xn_off, 2048, F32R), c.sb_at(fr.xn_off + 2048, 2048, F32R)]
    t_wgu = [T() for _ in range(4)]
    al = {0: [], 1: [], 2: [fr.t_xn[0], fr.t_xn[1]], 3: [fr.t_xn[2], fr.t_xn[3]]}
    NWB = 4
    sg = [c.sb(TT) for _ in range(2)]
    t_sg = [T() for _ in range(2)]
    P = c.psum
    ps_t, ps_g, ps_u, ps_d = P[0:2], P[2:4], P[4:6], P[6:8]
    t_pst = [T(), T()]
    t_psg = [T(), T()]
    t_psu = [T(), T()]
    t_psd = [T(), T()]
    cnt = {"gu": 0, "d": 0}
    for it in range(ntok // TT):
        r0 = it * TT
        fr.load_norm(src, r0)
        fr.transpose(gcol, t_g, ps_t, t_pst)
        hT, t_hT = fr.hT, fr.t_hT
        for f in range(NF):
            wb = f % NWB
            c.op("pool", lambda e, f=f, wb=wb: e.dma_start(out=wgu[wb], in_=wgu_d[f]), writes=[t_wgu[wb]] + al[wb], dma=True)
            c.op("pool", lambda e, f=f: e.dma_start(out=wd[f], in_=wd_d[f]), writes=[t_wd[f]], dma=True)
            b = cnt["gu"] % 2
            cnt["gu"] += 1
            for k in range(8):
                c.op("pe", lambda e, k=k, wb=wb, b=b: e.matmul(ps_g[b][:, :], wgu[wb][:, k * 256:k * 256 + 128], hT[k], start=(k == 0), stop=(k == 7)),
                     reads=[t_wgu[wb], t_hT[k]] + al[wb], writes=[t_psg[b]])
            for k in range(8):
                c.op("pe", lambda e, k=k, wb=wb, b=b: e.matmul(ps_u[b][:, :], wgu[wb][:, k * 256 + 128:k * 256 + 256], hT[k], start=(k == 0), stop=(k == 7)),
                     reads=[t_wgu[wb], t_hT[k]] + al[wb], writes=[t_psu[b]])
            c.op("act", lambda e, b=b: e.activation(out=sg[b], in_=ps_g[b][:, :], func=AF.Silu), reads=[t_psg[b]], writes=[t_sg[b]])
            c.op("dve", lambda e, b=b, f=f: e.tensor_tensor(out=aT[f], in0=sg[b], in1=ps_u[b][:, :], op=ALU.mult),
                 reads=[t_sg[b], t_psu[b]], writes=[t_aT[f]])
        for tb in range(4):
            for dh in range(2):
                b = cnt["d"] % 2
                cnt["d"] += 1
                for f in range(NF):
                    c.op("pe", lambda e, f=f, tb=tb, dh=dh, b=b: e.matmul(ps_d[b][:, :], aT[f][:, tb * 128:(tb + 1) * 128], wd[f][:, dh * 512:(dh + 1) * 512], start=(f == 0), stop=(f == NF - 1)),
                         reads=[t_aT[f], t_wd[f]], writes=[t_psd[b]])
                c.op("dve", lambda e, tb=tb, dh=dh, b=b: e.tensor_tensor(out=fr.xin[tb][:, dh * 512:(dh + 1) * 512], in0=ps_d[b][:, :], in1=fr.xin[tb][:, dh * 512:(dh + 1) * 512], op=ALU.add),
                     reads=[t_psd[b], fr.t_xin[tb]], writes=[fr.t_xin[tb]])
            c.op("sp", lambda e, tb=tb, r0=r0: e.dma_start(out=dst[r0 + tb * 128:r0 + (tb + 1) * 128, :], in_=fr.xin[tb]), reads=[fr.t_xin[tb]], dma=True)


def ffn_host(inputs, li):
    wg = np.asarray(inputs["ff_w_gate"][li], np.float32)
    wu = np.asarray(inputs["ff_w_up"][li], np.float32)
    wdn = np.asarray(inputs["ff_w_down"][li], np.float32)
    g = np.asarray(inputs["norm_ffn"][li], np.float32)
    wgu = np.stack([wg.reshape(8, 128, NF, 128), wu.reshape(8, 128, NF, 128)], axis=0)
    wgu = np.ascontiguousarray(wgu.transpose(3, 2, 1, 0, 4)).reshape(NF, 128, 2048)
    return {"ffn_g%d" % li: np.ascontiguousarray(g.reshape(8, 128).T),
            "ffn_wgu%d" % li: wgu,
            "ffn_wd%d" % li: np.ascontiguousarray(wdn.reshape(NF, 128, D))}


POOL_WINDOWS = (2, 4, 8, 16)


NTL = L // 2
NBL = NTL // 128


def pool_consts(h):
    mats = np.zeros((4, 9, 128, 128), np.float32)
    t = np.arange(L)
    for g, w in enumerate(POOL_WINDOWS):
        r = w // 2
        lo = np.clip(t - r, 0, L)
        hi = np.clip(t + r + 1, 0, L)
        inv = (1.0 / (hi - lo)).astype(np.float32)

        def blk(bi, bj):
            if bi < 0 or bi >= NBLK:
                return np.zeros((128, 128), np.float32)
            tp = np.arange(bi * 128, (bi + 1) * 128)[:, None]
            tt = np.arange(bj * 128, (bj + 1) * 128)[None, :]
            m = ((tp >= lo[tt]) & (tp < hi[tt])).astype(np.float32) * inv[tt]
            return m - (tp == tt).astype(np.float32)
        first = h * NBL
        last = h * NBL + NBL - 1
        for j, bj in enumerate((5, first, last)):
            for k in range(3):
                mats[g, j * 3 + k] = blk(bj - 1 + k, bj)
    return np.ascontiguousarray(mats.transpose(2, 0, 1, 3)).reshape(128, 4 * 9 * 128)


def halo_exchange(c, src):
    c.new_phase()
    nc = c.nc
    c.nhalo = getattr(c, "nhalo", 0) + 1
    hin = nc.dram_tensor("halo_in%d" % c.nhalo, [256, D], F32)
    hg = nc.dram_tensor("halo_g%d" % c.nhalo, [512, D], F32)
    t_h = T()
    c.op("sp", lambda e: e.dma_start(out=hin.ap()[0:128, :], in_=src[0:128, :]), writes=[t_h], dma=True)
    t_h2 = T()
    c.op("sp", lambda e: e.dma_start(out=hin.ap()[128:256, :], in_=src[NTL - 128:NTL, :]), writes=[t_h2], dma=True)
    c.op("pool", lambda e: e.collective_compute("AllGather", ALU.bypass, replica_groups=PAIRS, ins=[hin.ap()], outs=[hg.ap()]),
         reads=[t_h, t_h2], writes=[T()], cc=True)
    return hg.ap()


PAIRS = [[0, 1], [2, 3], [4, 5], [6, 7]]


def pool_phase(c, src, dst):
    hg = halo_exchange(c, src)
    c.new_phase()
    pm_d = c.din("pl_mats", [128, 36 * 128])
    g_d = c.din("pl_g", [128, 8])
    w_d = c.din("pl_wt", [128, 8, 256])
    b_d = c.din("pl_b", [1, D])
    s_d = c.din("pl_scale", [1, D])
    pm = c.sb(36 * 128, F32R)
    gcol = c.sb(8)
    wg = c.sb(8 * 256, F32R)
    brow = c.sb(D)
    srow = c.sb(D)
    t_k = T()
    c.op("pool", lambda e: e.dma_start(out=pm, in_=pm_d), writes=[t_k], dma=True)
    t_k2 = T()
    c.op("pool", lambda e: e.dma_start(out=wg, in_=w_d.rearrange("p a b -> p (a b)")), writes=[t_k2], dma=True)
    t_k3 = T()
    c.op("sp", lambda e: e.dma_start(out=gcol, in_=g_d), writes=[t_k3], dma=True)
    t_k4 = T()
    c.op("sp", lambda e: e.dma_start(out=brow, in_=b_d.broadcast_to([128, D])), writes=[t_k4], dma=True)
    t_k5 = T()
    c.op("sp", lambda e: e.dma_start(out=srow, in_=s_d.broadcast_to([128, D])), writes=[t_k5], dma=True)
    RING = 4
    xin = [c.sb(D) for _ in range(RING)]
    t_xin = [T() for _ in range(RING)]
    xn = [c.sb(D, F32R) for _ in range(RING)]
    t_xn = [T() for _ in range(RING)]
    junk = c.sb(D)
    t_junk = T()
    ss = c.sb(RING)
    rstd = c.sb(RING)
    t_ss = [T() for _ in range(RING)]
    t_rstd = [T() for _ in range(RING)]
    dT = [c.sb(128, F32R) for _ in range(8)]
    t_dT = [T() for _ in range(8)]
    yt = [c.sb(D) for _ in range(2)]
    t_yt = [T(), T()]
    P = c.psum
    ps_p = P[0:4]
    t_psp = [T() for _ in range(4)]
    ps_y = [P[4:6], P[6:8]]
    t_psy = [T(), T()]

    def rows(i):
        if i == 0:
            return hg[128:256, :]
        if i == NBL + 1:
            return hg[256:384, :]
        return src[(i - 1) * 128:i * 128, :]

    def prep(i):
        s = i % RING
        c.op("sp", lambda e, s=s, ap=rows(i): e.dma_start(out=xin[s], in_=ap), writes=[t_xin[s]], dma=True)
        c.op("act", lambda e, s=s: e.activation(out=junk, in_=xin[s], func=AF.Square, accum_out=ss[:, s:s + 1]),
             reads=[t_xin[s]], writes=[t_junk, t_ss[s]])
        c.op("act", lambda e, s=s: e.activation(out=rstd[:, s:s + 1], in_=ss[:, s:s + 1], func=AF.Sqrt, scale=1.0 / D, bias=c.epsc),
             reads=[t_ss[s], c.t_const], writes=[t_rstd[s]])
        c.op("dve", lambda e, s=s: e.reciprocal(out=rstd[:, s:s + 1], in_=rstd[:, s:s + 1]), reads=[t_rstd[s]], writes=[t_rstd[s]])
        c.op("act", lambda e, s=s: e.activation(out=xn[s], in_=xin[s], func=AF.Copy, scale=rstd[:, s:s + 1]),
             reads=[t_xin[s], t_rstd[s]], writes=[t_xn[s]])

    prep(0)
    prep(1)
    for i in range(1, NBL + 1):
        prep(i + 1)
        mbase = 3 if i == 1 else (6 if i == NBL else 0)
        terms = [(-1, mbase), (0, mbase + 1), (1, mbase + 2)]
        for g in range(4):
            pb = g
            for j in range(2):
                cc = 2 * g + j
                for ti, (rel, mi) in enumerate(terms):
                    s = (i + rel) % RING
                    c.op("pe", lambda e, s=s, cc=cc, g=g, mi=mi, j=j, pb=pb, ti=ti, nt=len(terms):
                         e.matmul(ps_p[pb][:, j * 128:(j + 1) * 128], xn[s][:, cc * 128:(cc + 1) * 128], pm[:, (g * 9 + mi) * 128:(g * 9 + mi + 1) * 128], start=(ti == 0), stop=(ti == nt - 1)),
                         reads=[t_xn[s], t_k], writes=[t_psp[pb]])
            for j in range(2):
                cc = 2 * g + j
                c.op("act", lambda e, cc=cc, j=j, pb=pb: e.activation(out=dT[cc], in_=ps_p[pb][:, j * 128:(j + 1) * 128], func=AF.Copy, scale=gcol[:, cc:cc + 1]),
                     reads=[t_psp[pb], t_k3], writes=[t_dT[cc]])
        yb = i % 2
        for g in range(4):
            for j in range(2):
                cc = 2 * g + j
                c.op("pe", lambda e, cc=cc, g=g, j=j, yb=yb: e.matmul(ps_y[yb][g // 2][:, (g % 2) * 256:(g % 2 + 1) * 256], dT[cc], wg[:, cc * 256:(cc + 1) * 256], start=(j == 0), stop=(j == 1)),
                     reads=[t_dT[cc], t_k2], writes=[t_psy[yb]])
        s = i % RING
        for hh in range(2):
            sl = slice(hh * 512, (hh + 1) * 512)
            c.op("dve", lambda e, yb=yb, hh=hh, sl=sl: e.tensor_tensor(out=yt[yb][:, sl], in0=ps_y[yb][hh][:, :], in1=brow[:, sl], op=ALU.add),
                 reads=[t_psy[yb], t_k4], writes=[t_yt[yb]])
        c.op("pool", lambda e, yb=yb: e.tensor_tensor(out=yt[yb], in0=yt[yb], in1=srow, op=ALU.mult), reads=[t_yt[yb], t_k5], writes=[t_yt[yb]])
        c.op("pool", lambda e, yb=yb, s=s: e.tensor_tensor(out=yt[yb], in0=yt[yb], in1=xin[s], op=ALU.add), reads=[t_yt[yb], t_xin[s]], writes=[t_yt[yb]])
        c.op("sp", lambda e, yb=yb, i=i: e.dma_start(out=dst[(i - 1) * 128:i * 128, :], in_=yt[yb]), reads=[t_yt[yb]], dma=True)


def pool_host(inputs, h):
    w = np.asarray(inputs["pl_w"][0], np.float32)
    wt = w.reshape(4, 2, 128, 256).transpose(2, 0, 1, 3).reshape(128, 8, 256)
    g = np.asarray(inputs["norm_mix"][2], np.float32)
    return {"pl_mats": pool_consts(h),
            "pl_g": np.ascontiguousarray(g.reshape(8, 128).T),
            "pl_wt": np.ascontiguousarray(wt),
            "pl_b": np.asarray(inputs["pl_b"][0], np.float32).reshape(1, D),
            "pl_scale": np.asarray(inputs["pl_scale"][0], np.float32).reshape(1, D)}


NH = 16
NKV = 4
HD = 64
NEG = -30000.0
_T5_THR = (8, 12, 16, 23, 32, 46, 64, 91)


def _t5_bucket(rel):
    n = abs(rel)
    if n < 8:
        b = n
    else:
        b = 7 + sum(1 for t in _T5_THR if n >= t)
    return (16 if rel > 0 else 0) + b


def attn_onehot():
    oh = np.zeros((33, 3, 128, 128), np.float32)
    for kb in range(3):
        for a in range(128):
            for j in range(128):
                rel = 128 * (kb - 1) + j - a
                if abs(rel) <= 128:
                    oh[_t5_bucket(rel), kb, a, j] = 1.0
                else:
                    oh[32, kb, a, j] = 1.0
    return oh.reshape(33, 3 * 128 * 128)


def const_r(c, cols, val, parts):
    tmp = c.sb(cols, parts=parts)
    out = c.sb(cols, F32R, parts=parts)
    t = T()
    c.op("dve", lambda e: e.memset(tmp, val), writes=[t])
    c.op("act", lambda e: e.activation(out=out, in_=tmp, func=AF.Copy), reads=[t], writes=[t])
    return out, t


def attn_qkv_phase(c, src, S, hg):
    c.new_phase()
    g_d = c.din("at_g", [128, 8])
    w_d = c.din("at_wqkv", [8, 128, 1536])
    qg_d = c.din("at_qg", [64, 1])
    kg_d = c.din("at_kg", [64, 1])
    gcol = c.sb(8)
    qg = c.sb(1, parts=64)
    kg = c.sb(1, parts=64)
    t_g = T()
    c.op("sp", lambda e: e.dma_start(out=gcol, in_=g_d), writes=[t_g], dma=True)
    c.op("sp", lambda e: e.dma_start(out=qg, in_=qg_d), writes=[t_g], dma=True)
    c.op("sp", lambda e: e.dma_start(out=kg, in_=kg_d), writes=[t_g], dma=True)
    wq = [c.sb(1536, F32R) for _ in range(8)]
    t_wq = [T() for _ in range(8)]
    for k in range(8):
        c.op("pool", lambda e, k=k: e.dma_start(out=wq[k], in_=w_d[k]), writes=[t_wq[k]], dma=True)
    ones64, t_ones = const_r(c, 64, 1.0 / 64, 64)
    frs = [Front(c), Front(c)]
    for f_ in frs:
        f_.scale_on_act = True
    sq = [c.sb(TT, F32R, parts=64) for _ in range(2)]
    t_sq = [T(), T()]
    rs = [c.sb(TT, parts=64) for _ in range(2)]
    t_rs = [T(), T()]
    qn = [c.sb(TT, parts=64) for _ in range(2)]
    t_qn = [T(), T()]
    vt = [c.sb(256) for _ in range(2)]
    t_vt = [T(), T()]
    P = c.psum
    ps_t, ps_q, ps_m, ps_v = P[0:2], P[2:4], P[4:6], P[6:8]
    t_pst, t_psq, t_psm, t_psv = [T(), T()], [T(), T()], [T(), T()], [T(), T()]
    cnt = 0
    cv = 0
    NT = NTL // TT + 1

    def prep(it):
        fr = frs[it % 2]
        if it < NTL // TT:
            fr.load_norm(src, it * TT)
        else:
            fr.load_norm(None, 0, rows=[hg[128:256, :], hg[256:384, :], hg[128:256, :], hg[256:384, :]])
        fr.transpose(gcol, t_g, ps_t, t_pst)

    prep(0)
    for it in range(NT):
        fr = frs[it % 2]
        if it + 1 < NT:
            prep(it + 1)
        if it < NTL // TT:
            stores = [((1 + 4 * it) * 128, 0, TT)]
            store_q = True
        else:
            stores = [(0, 0, 128), ((NBL + 1) * 128, 128, 128)]
            store_q = False
        hT, t_hT = fr.hT, fr.t_hT
        for h in range(NH + NKV):
            isq = h < NH
            if isq and not store_q:
                continue
            b = cnt % 2
            cnt += 1
            col0 = h * 64 if isq else 1024 + (h - NH) * 64
            gain = qg if isq else kg
            dstT = S["qT"][h] if isq else S["kT"][h - NH]
            for k in range(8):
                c.op("pe", lambda e, k=k, b=b, col0=col0, hT=hT: e.matmul(ps_q[b][0:64, :], wq[k][:, col0:col0 + 64], hT[k], start=(k == 0), stop=(k == 7)),
                     reads=[t_wq[k], t_hT[k]], writes=[t_psq[b]])
            c.op("act", lambda e, b=b: e.activation(out=sq[b], in_=ps_q[b][0:64, :], func=AF.Square), reads=[t_psq[b]], writes=[t_sq[b]])
            c.op("pe", lambda e, b=b: e.matmul(ps_m[b][0:64, :], ones64, sq[b], start=True, stop=True), reads=[t_ones, t_sq[b]], writes=[t_psm[b]])
            c.op("act", lambda e, b=b: e.activation(out=rs[b], in_=ps_m[b][0:64, :], func=AF.Sqrt, bias=c.epsc[0:64, :]), reads=[t_psm[b], c.t_const], writes=[t_rs[b]])
            c.op("dve", lambda e, b=b: e.reciprocal(out=rs[b], in_=rs[b]), reads=[t_rs[b]], writes=[t_rs[b]])
            c.op("dve", lambda e, b=b, gain=gain: e.scalar_tensor_tensor(out=qn[b], in0=ps_q[b][0:64, :], scalar=gain, in1=rs[b], op0=ALU.mult, op1=ALU.mult),
                 reads=[t_psq[b], t_rs[b], t_g], writes=[t_qn[b]])
            for (dc, sc, wd_) in stores:
                c.op("sp", lambda e, b=b, dstT=dstT, dc=dc, sc=sc, wd_=wd_: e.dma_start(out=dstT[:, dc:dc + wd_], in_=qn[b][:, sc:sc + wd_]), reads=[t_qn[b]], dma=True)
        for tb in range(4 if store_q else 2):
            b = cv % 2
            cv += 1
            for k in range(8):
                c.op("pe", lambda e, k=k, b=b, tb=tb, hT=hT: e.matmul(ps_v[b][:, 0:256], hT[k][:, tb * 128:(tb + 1) * 128], wq[k][:, 1280:1536], start=(k == 0), stop=(k == 7)),
                     reads=[t_wq[k], t_hT[k]], writes=[t_psv[b]])
            c.op("act", lambda e, b=b: e.activation(out=vt[b], in_=ps_v[b][:, 0:256], func=AF.Copy), reads=[t_psv[b]], writes=[t_vt[b]])
            if store_q:
                vrow = (1 + 4 * it + tb) * 128
            else:
                vrow = 0 if tb == 0 else (NBL + 1) * 128
            c.op("sp", lambda e, b=b, vrow=vrow: e.dma_start(out=S["v"][vrow:vrow + 128, :], in_=vt[b]), reads=[t_vt[b]], dma=True)


def attn_core_phase(c, src, dst, S):
    c.new_phase()
    oh_d = c.din("at_oh", [33, 3 * 128 * 128])
    rt_d = c.din("at_rel", [32, 16])
    sink_d = c.din("at_sink", [1, 16])
    edge_d = c.din("at_edge", [128, 2])
    edge = c.sb(2)
    t_edge = T()
    c.op("sp", lambda e: e.dma_start(out=edge, in_=edge_d), writes=[t_edge], dma=True)
    wo_d = c.din("at_wo", [64, 16, D])
    bias = c.sb(3 * 16 * 128)
    t_bias = T()
    table = c.sb(16, parts=33)
    t_tab = T()
    c.op("dve", lambda e: e.memset(table[32:33, :], NEG), writes=[t_tab])
    c.op("sp", lambda e: e.dma_start(out=table[0:32, :], in_=rt_d), writes=[t_tab], dma=True)
    ohb = [c.sb(32 * 128, parts=33) for _ in range(2)]
    t_ohb = [T(), T()]
    P = c.psum
    ps_s, ps_o, ps_den, ps_y = P[0:2], P[2:4], P[4:6], P[6:8]
    t_pss, t_pso, t_psden, t_psy = [T(), T()], [T(), T()], [T(), T()], [T(), T()]
    nb = 0
    for kb in range(3):
        for q4 in range(4):
            ob = nb % 2
            pb = nb % 2
            nb += 1
            a0 = q4 * 32
            c.op("sp", lambda e, kb=kb, a0=a0, ob=ob: e.dma_start(out=ohb[ob], in_=oh_d[:, (kb * 128 + a0) * 128:(kb * 128 + a0 + 32) * 128]),
                 writes=[t_ohb[ob]], dma=True)
            for al in range(32):
                c.op("pe", lambda e, ob=ob, al=al, pb=pb: e.matmul(ps_s[pb][:, al * 16:(al + 1) * 16], ohb[ob][:, al * 128:(al + 1) * 128], table, start=True, stop=True),
                     reads=[t_ohb[ob], t_tab], writes=[t_pss[pb]])
            c.op("dve", lambda e, kb=kb, a0=a0, pb=pb: e.tensor_copy(
                out=bias[:, kb * 2048:(kb + 1) * 2048].rearrange("p (h a) -> p h a", h=16)[:, :, a0:a0 + 32],
                in_=ps_s[pb][:, :].rearrange("p (a h) -> p h a", h=16)),
                reads=[t_pss[pb]], writes=[t_bias])
    es16 = c.sb(16, parts=64)
    esink = c.sb(16 * 128, parts=64)
    t_es = T()
    c.op("sp", lambda e: e.dma_start(out=es16, in_=sink_d.broadcast_to([64, 16])), writes=[t_es], dma=True)
    c.op("act", lambda e: e.activation(out=es16, in_=es16, func=AF.Exp), reads=[t_es], writes=[t_es])
    c.op("dve", lambda e: e.tensor_copy(out=esink.rearrange("p (h a) -> p h a", h=16), in_=es16.unsqueeze(2).broadcast_to([64, 16, 128])), reads=[t_es], writes=[t_es])
    wo = c.sb(16 * D, F32R, parts=64)
    t_wo = T()
    c.op("pool", lambda e: e.dma_start(out=wo, in_=wo_d.rearrange("p h n -> p (h n)")), writes=[t_wo], dma=True)
    oneskv, t_ones = const_r(c, 64, 1.0, 128)
    q_sb = [c.sb(16 * 128, F32R, parts=64) for _ in range(2)]
    t_q = [T(), T()]
    RING = 4
    k_r = [c.sb(4 * 128, F32R, parts=64) for _ in range(RING)]
    t_kr = [T() for _ in range(RING)]
    v_r = [c.sb(256, F32R) for _ in range(RING)]
    t_vr = [T() for _ in range(RING)]
    xin = [c.sb(D) for _ in range(2)]
    t_xin = [T(), T()]
    tt = [c.sb(TT) for _ in range(2)]
    t_tt = [T(), T()]
    pT = [c.sb(TT, F32R) for _ in range(2)]
    t_pT = [T(), T()]
    den = [c.sb(TT, parts=64) for _ in range(2)]
    t_den = [T(), T()]
    oT = [[c.sb(TT, F32R, parts=64) for _ in range(4)] for _ in range(2)]
    t_oT = [[T() for _ in range(4)] for _ in range(2)]

    def prep_kv(i):
        s = i % RING
        c.op("pool", lambda e, s=s, i=i: e.dma_start(out=k_r[s].rearrange("p (g t) -> p g t", g=4), in_=S["kT3"][:, :, i * 128:(i + 1) * 128]), writes=[t_kr[s]], dma=True)
        c.op("pool", lambda e, s=s, i=i: e.dma_start(out=v_r[s], in_=S["v"][i * 128:(i + 1) * 128, :]), writes=[t_vr[s]], dma=True)

    prep_kv(0)
    prep_kv(1)
    steps = [(n, g, kb) for n in range(1, NBL + 1) for g in range(4) for kb in range(3)]
    deferred = []

    def block_start(n):
        prep_kv(n + 1)
        qb = n % 2
        c.op("pool", lambda e, qb=qb, n=n: e.dma_start(out=q_sb[qb].rearrange("p (h t) -> p h t", h=16), in_=S["qT3"][:, :, n * 128:(n + 1) * 128]), writes=[t_q[qb]], dma=True)
        c.op("sp", lambda e, qb=qb, n=n: e.dma_start(out=xin[qb], in_=src[(n - 1) * 128:n * 128, :]), writes=[t_xin[qb]], dma=True)

    def emit_s(i):
        n, g, kb = steps[i]
        if g == 0 and kb == 0:
            block_start(n)
        qb = n % 2
        s_ = (n + kb - 1) % RING
        b = i % 2
        c.op("pe", lambda e, s_=s_, g=g, qb=qb, b=b: e.matmul(ps_s[b][:, :], k_r[s_][:, g * 128:(g + 1) * 128], q_sb[qb][:, g * 512:(g + 1) * 512], start=True, stop=True),
             reads=[t_kr[s_], t_q[qb]], writes=[t_pss[b]])

    def emit_rest(i):
        n, g, kb = steps[i]
        qb = n % 2
        s_ = (n + kb - 1) % RING
        b = i % 2
        ob = (i // 3) % 2
        c.op("dve", lambda e, b=b, kb=kb, g=g: e.scalar_tensor_tensor(out=tt[b], in0=ps_s[b][:, :], scalar=HD ** -0.5, in1=bias[:, kb * 2048 + g * 512:kb * 2048 + (g + 1) * 512], op0=ALU.mult, op1=ALU.add),
             reads=[t_pss[b], t_bias], writes=[t_tt[b]])
        if (n == 1 and kb == 0) or (n == NBL and kb == 2):
            ecol = 0 if kb == 0 else 1
            c.op("dve", lambda e, b=b, ecol=ecol: e.tensor_scalar(out=tt[b], in0=tt[b], scalar1=edge[:, ecol:ecol + 1], scalar2=0.0, op0=ALU.add, op1=ALU.add),
                 reads=[t_tt[b], t_edge], writes=[t_tt[b]])
        c.op("act", lambda e, b=b: e.activation(out=pT[b], in_=tt[b], func=AF.Exp), reads=[t_tt[b]], writes=[t_pT[b]])
        c.op("pe", lambda e, s_=s_, g=g, b=b, ob=ob, kb=kb: e.matmul(ps_o[ob][0:64, :], v_r[s_][:, g * 64:(g + 1) * 64], pT[b], start=(kb == 0), stop=(kb == 2)),
             reads=[t_vr[s_], t_pT[b]], writes=[t_pso[ob]])
        c.op("pe", lambda e, b=b, ob=ob, kb=kb: e.matmul(ps_den[ob][0:64, :], oneskv, pT[b], start=(kb == 0), stop=(kb == 2)),
             reads=[t_ones, t_pT[b]], writes=[t_psden[ob]])
        if kb == 2:
            c.op("dve", lambda e, ob=ob, g=g: e.tensor_tensor(out=den[ob], in0=ps_den[ob][0:64, :], in1=esink[:, g * 512:(g + 1) * 512], op=ALU.add),
                 reads=[t_psden[ob], t_es], writes=[t_den[ob]])
            c.op("dve", lambda e, ob=ob: e.reciprocal(out=den[ob], in_=den[ob]), reads=[t_den[ob]], writes=[t_den[ob]])
            c.op("dve", lambda e, ob=ob, g=g, qb=qb: e.tensor_tensor(out=oT[qb][g], in0=ps_o[ob][0:64, :], in1=den[ob], op=ALU.mult),
                 reads=[t_pso[ob], t_den[ob]], writes=[t_oT[qb][g]])
            if g == 3:
                deferred.append((i + 3, lambda n=n: block_end(n)))

    def block_end(n):
        qb = n % 2
        for dh in range(2):
            yb = dh
            for h in range(NH):
                c.op("pe", lambda e, h=h, dh=dh, yb=yb, qb=qb: e.matmul(ps_y[yb][:, :], oT[qb][h // 4][:, (h % 4) * 128:(h % 4 + 1) * 128], wo[:, h * D + dh * 512:h * D + (dh + 1) * 512], start=(h == 0), stop=(h == NH - 1)),
                     reads=[t_oT[qb][h // 4], t_wo], writes=[t_psy[yb]])
            c.op("dve", lambda e, dh=dh, yb=yb, qb=qb: e.tensor_tensor(out=xin[qb][:, dh * 512:(dh + 1) * 512], in0=ps_y[yb][:, :], in1=xin[qb][:, dh * 512:(dh + 1) * 512], op=ALU.add),
                 reads=[t_psy[yb], t_xin[qb]], writes=[t_xin[qb]])
        c.op("sp", lambda e, qb=qb, n=n: e.dma_start(out=dst[(n - 1) * 128:n * 128, :], in_=xin[qb]), reads=[t_xin[qb]], dma=True)

    ns = len(steps)
    for i in range(ns + 4):
        if i < ns:
            emit_s(i)
        if 1 <= i <= ns:
            emit_rest(i - 1)
        for (at, fn) in [d for d in deferred if d[0] <= i]:
            fn()
        deferred[:] = [d for d in deferred if d[0] > i]
    assert not deferred


def attn_scratch(c):
    qT = c.nc.dram_tensor("qT_s", [NH, 64, (NBL + 2) * 128], F32)
    kT = c.nc.dram_tensor("kT_s", [NKV, 64, (NBL + 2) * 128], F32)
    v = c.nc.dram_tensor("v_s", [(NBL + 2) * 128, 256], F32)
    return {"qT": [qT.ap()[h] for h in range(NH)], "kT": [kT.ap()[h] for h in range(NKV)], "v": v.ap(),
            "qT3": qT.ap().rearrange("h p t -> p h t"), "kT3": kT.ap().rearrange("h p t -> p h t")}


def attn_host(inputs, h):
    g = np.asarray(inputs["norm_mix"][1], np.float32)
    edge = np.zeros((128, 2), np.float32)
    edge[:, h] = NEG
    wo = np.asarray(inputs["at_w_o"][0], np.float32).reshape(16, 64, D).transpose(1, 0, 2)
    return {"at_g": np.ascontiguousarray(g.reshape(8, 128).T),
            "at_wqkv": np.ascontiguousarray(np.asarray(inputs["at_w_qkv"][0], np.float32).reshape(8, 128, 1536)),
            "at_qg": np.asarray(inputs["at_q_gain"][0], np.float32).reshape(64, 1),
            "at_kg": np.asarray(inputs["at_k_gain"][0], np.float32).reshape(64, 1),
            "at_oh": attn_onehot(),
            "at_rel": np.asarray(inputs["rel_table"], np.float32),
            "at_sink": np.asarray(inputs["at_sink"][0], np.float32).reshape(1, 16),
            "at_edge": edge,
            "at_wo": np.ascontiguousarray(wo)}


NFFT = 2 * L
HY_MIN_DECAY = math.log(1e-2) / 1.5
HY_MAX_DECAY = math.log(1e-2) / 0.3
_FA, _FC, _FS, _FSN, _GCS, _GSNC, _HAC, _HASN, _NCR = 0, 256, 384, 512, 640, 896, 1152, 1216, 1280


def hy_consts():
    i = np.arange(128, dtype=np.float64)
    th = 2 * np.pi * np.outer(i, i) / 128.0
    cr = np.zeros((128, _NCR), np.float64)
    cr[:, _FA:_FA + 128] = np.cos(th)
    cr[:, _FA + 128:_FA + 256] = -np.sin(th)
    cr[:, _FC:_FC + 128] = np.cos(th)
    cr[:, _FS:_FS + 128] = np.sin(th)
    cr[:, _FSN:_FSN + 128] = -np.sin(th)
    cr[:, _GCS:_GCS + 128] = np.cos(th)
    cr[:, _GCS + 128:_GCS + 256] = np.sin(th)
    cr[:, _GSNC:_GSNC + 128] = -np.sin(th)
    cr[:, _GSNC + 128:_GSNC + 256] = np.cos(th)
    cr[:, _HAC:_HAC + 64] = np.cos(th[:, :64])
    cr[:, _HASN:_HASN + 64] = -np.sin(th[:, :64])
    tw = 2 * np.pi * np.outer(i, i) / NFFT
    cf = np.concatenate([np.tile(np.cos(tw), (1, 4)), np.tile(np.sin(tw), (1, 4))], axis=1)
    n = np.arange(NFFT)
    pos = np.where(n < L, n, L - (n - L)).astype(np.float64)
    pos[L] = 0.0
    t = pos / (L - 1)
    f = np.linspace(1e-4, 15.0, 16)
    ang = (2 * np.pi / L) * pos[None, :] * f[:, None]
    z = np.concatenate([t[None, :], np.cos(ang), -np.sin(ang)], axis=0)
    tdec = t.copy()
    tdec[L] = 1.0e4
    deltas = np.abs(np.linspace(HY_MIN_DECAY, HY_MAX_DECAY, D))
    return {"hy_cr": cr.astype(np.float32), "hy_cf": cf.astype(np.float32), "hy_z": z.astype(np.float32),
            "hy_tdec": tdec.astype(np.float32).reshape(1, NFFT),
            "hy_negdelta_full": (-deltas).astype(np.float32)}


class HyFFT:
    def __init__(self, c, inverse):
        self.c = c
        self.inverse = inverse
        cr_d = c.din("hy_cr", [128, _NCR])
        cf_d = c.din("hy_cf", [128, 1024])
        self.cr = c.sb(_NCR, F32R)
        self.cf = c.sb(1024)
        self.t_k = T()
        c.op("pool", lambda e: e.dma_start(out=self.cr, in_=cr_d), writes=[self.t_k], dma=True)
        self.t_k2 = T()
        c.op("sp", lambda e: e.dma_start(out=self.cf, in_=cf_d), writes=[self.t_k2], dma=True)
        self.C2 = self.cf[:, 0:512]
        self.S2 = self.cf[:, 512:1024]
        P = c.psum
        mk2 = lambda dt=F32: [c.sb(512, dt) for _ in range(2)]
        self.t1, self.t2 = mk2(), mk2()
        self.t_t1, self.t_t2 = [T(), T()], [T(), T()]
        self.Bre, self.Bim = mk2(F32R), mk2(F32R)
        self.t_B = [[T(), T()], [T(), T()]]
        self.ps_a, self.t_psa = P[0:2], [T(), T()]
        self.ps_x, self.t_psx = P[2:4], [T(), T()]
        if inverse:
            self.u1, self.u2 = mk2(), mk2()
            self.t_u1, self.t_u2 = [T(), T()], [T(), T()]
            self.m = [c.sb(512) for _ in range(4)]
            self.t_m = [T() for _ in range(4)]
            self.Zre, self.Zim = mk2(F32R), mk2(F32R)
            self.t_Z = [[T(), T()], [T(), T()]]
            self.Vre, self.Vim = mk2(F32R), mk2(F32R)
            self.t_V = [[T(), T()], [T(), T()]]
            self.ps_v, self.t_psv = P[4:6], [T(), T()]
            self.ps_y, self.t_psy = P[6], T()

    def _tw_mul(self, ps, t_ps, a1, a2, t_a1, t_a2):
        c = self.c
        for b in range(2):
            c.op("dve", lambda e, b=b: e.tensor_tensor(out=a1[b], in0=ps[b][:, :], in1=self.C2, op=ALU.mult), reads=[t_ps[b], self.t_k2], writes=[t_a1[b]])
            c.op("dve", lambda e, b=b: e.tensor_tensor(out=a2[b], in0=ps[b][:, :], in1=self.S2, op=ALU.mult), reads=[t_ps[b], self.t_k2], writes=[t_a2[b]])

    def _tw_comb(self, a1, a2, t_a1, t_a2, outre, outim, t_out, forward):
        c = self.c
        for b in range(2):
            v1 = a1[b].rearrange("p (s c k) -> p s c k", s=2, c=2)
            v2 = a2[b].rearrange("p (s c k) -> p s c k", s=2, c=2)
            ore = outre[:, b * 256:(b + 1) * 256].rearrange("p (s k) -> p s k", s=2)
            oim = outim[:, b * 256:(b + 1) * 256].rearrange("p (s k) -> p s k", s=2)
            op_re, op_im = (ALU.add, ALU.subtract) if forward else (ALU.subtract, ALU.add)
            c.op("pool", lambda e, v1=v1, v2=v2, ore=ore, op_re=op_re: e.tensor_tensor(out=ore, in0=v1[:, :, 0, :], in1=v2[:, :, 1, :], op=op_re),
                 reads=[t_a1[b], t_a2[b]], writes=[t_out[0]])
            c.op("pool", lambda e, v1=v1, v2=v2, oim=oim, op_im=op_im: e.tensor_tensor(out=oim, in0=v1[:, :, 1, :], in1=v2[:, :, 0, :], op=op_im),
                 reads=[t_a1[b], t_a2[b]], writes=[t_out[1]])

    def stage(self, st, g, J):
        c = self.c
        cr = self.cr
        par = g % 2
        c0 = g * 4
        if st == 0:
            ut, ka = J["ut"], J["ka"]
            for s in range(4):
                b = s // 2
                c.op("pe", lambda e, s=s, b=b, ut=ut, ka=ka, c0=c0: e.matmul(self.ps_a[b][:, (s % 2) * 256:(s % 2 + 1) * 256], ut[0:ka, (c0 + s) * 128:(c0 + s + 1) * 128], cr[0:ka, _FA:_FA + 256], start=True, stop=True),
                     reads=[J["t_ut"][g], self.t_k], writes=[self.t_psa[b]])
        elif st == 1:
            self._tw_mul(self.ps_a, self.t_psa, self.t1, self.t2, self.t_t1, self.t_t2)
        elif st == 2:
            self._tw_comb(self.t1, self.t2, self.t_t1, self.t_t2, self.Bre[par], self.Bim[par], self.t_B[par], True)
        elif st == 3:
            Bre, Bim = self.Bre[par], self.Bim[par]
            rB = [self.t_B[par][0], self.t_B[par][1], self.t_k]
            c.op("pe", lambda e, Bre=Bre: e.matmul(self.ps_x[0][:, :], cr[:, _FC:_FC + 128], Bre, start=True, stop=False), reads=rB, writes=[self.t_psx[0]])
            c.op("pe", lambda e, Bim=Bim: e.matmul(self.ps_x[0][:, :], cr[:, _FS:_FS + 128], Bim, start=False, stop=True), reads=rB, writes=[self.t_psx[0]])
            c.op("pe", lambda e, Bim=Bim: e.matmul(self.ps_x[1][:, :], cr[:, _FC:_FC + 128], Bim, start=True, stop=False), reads=rB, writes=[self.t_psx[1]])
            c.op("pe", lambda e, Bre=Bre: e.matmul(self.ps_x[1][:, :], cr[:, _FSN:_FSN + 128], Bre, start=False, stop=True), reads=rB, writes=[self.t_psx[1]])
        elif not self.inverse:
            if st == 4:
                J["spec_out"](g, self.ps_x, self.t_psx)
        elif st == 4:
            kt, t_kt = J["kt"](g)
            m, t_m = self.m, self.t_m
            xr, xi = self.ps_x[0], self.ps_x[1]
            c.op("dve", lambda e, kt=kt: e.tensor_tensor(out=m[0], in0=xr[:, :], in1=kt[:, 0:512], op=ALU.mult), reads=[self.t_psx[0], t_kt], writes=[t_m[0]])
            c.op("dve", lambda e, kt=kt: e.tensor_tensor(out=m[1], in0=xi[:, :], in1=kt[:, 512:1024], op=ALU.mult), reads=[self.t_psx[1], t_kt], writes=[t_m[1]])
            c.op("dve", lambda e, kt=kt: e.tensor_tensor(out=m[2], in0=xr[:, :], in1=kt[:, 512:1024], op=ALU.mult), reads=[self.t_psx[0], t_kt], writes=[t_m[2]])
            c.op("dve", lambda e, kt=kt: e.tensor_tensor(out=m[3], in0=xi[:, :], in1=kt[:, 0:512], op=ALU.mult), reads=[self.t_psx[1], t_kt], writes=[t_m[3]])
        elif st == 5:
            m, t_m = self.m, self.t_m
            Zre, Zim = self.Zre[par], self.Zim[par]
            c.op("pool", lambda e, Zre=Zre: e.tensor_tensor(out=Zre, in0=m[0], in1=m[1], op=ALU.subtract), reads=[t_m[0], t_m[1]], writes=[self.t_Z[par][0]])
            c.op("pool", lambda e, Zim=Zim: e.tensor_tensor(out=Zim, in0=m[2], in1=m[3], op=ALU.add), reads=[t_m[2], t_m[3]], writes=[self.t_Z[par][1]])
        elif st == 6:
            Zre, Zim = self.Zre[par], self.Zim[par]
            for s in range(4):
                b = s // 2
                reg = self.ps_v[b][:, (s % 2) * 256:(s % 2 + 1) * 256]
                c.op("pe", lambda e, s=s, reg=reg, Zre=Zre: e.matmul(reg, Zre[:, s * 128:(s + 1) * 128], cr[:, _GCS:_GCS + 256], start=True, stop=False),
                     reads=[self.t_Z[par][0], self.t_k], writes=[self.t_psv[b]])
                c.op("pe", lambda e, s=s, reg=reg, Zim=Zim: e.matmul(reg, Zim[:, s * 128:(s + 1) * 128], cr[:, _GSNC:_GSNC + 256], start=False, stop=True),
                     reads=[self.t_Z[par][1], self.t_k], writes=[self.t_psv[b]])
        elif st == 7:
            self._tw_mul(self.ps_v, self.t_psv, self.u1, self.u2, self.t_u1, self.t_u2)
        elif st == 8:
            self._tw_comb(self.u1, self.u2, self.t_u1, self.t_u2, self.Vre[par], self.Vim[par], self.t_V[par], False)
        elif st == 9:
            Vre, Vim = self.Vre[par], self.Vim[par]
            rV = [self.t_V[par][0], self.t_V[par][1], self.t_k]
            c.op("pe", lambda e, Vre=Vre: e.matmul(self.ps_y[0:64, :], cr[:, _HAC:_HAC + 64], Vre, start=True, stop=False), reads=rV, writes=[self.t_psy])
            c.op("pe", lambda e, Vim=Vim: e.matmul(self.ps_y[0:64, :], cr[:, _HASN:_HASN + 64], Vim, start=False, stop=True), reads=rV, writes=[self.t_psy])
        elif st == 10:
            yt = J["yt"]
            c.op("act", lambda e, c0=c0, yt=yt: e.activation(out=yt[0:64, c0 * 128:(c0 + 4) * 128], in_=self.ps_y[0:64, :], func=AF.Copy), reads=[self.t_psy], writes=[J["t_ut"][g]])

    def run(self, J, ngroups=32):
        nst = 11 if self.inverse else 5
        for t in range(ngroups + nst - 1):
            if "pre" in J:
                J["pre"](t)
            for st in range(nst - 1, -1, -1):
                g = t - st
                if 0 <= g < ngroups:
                    self.stage(st, g, J)


def to_time_major(c, src_ct, t_src, ut, t_ut, na, ps, t_ps):
    v = src_ct.rearrange("p (a r) -> p r a", r=128)
    u3 = ut[0:na, :].rearrange("p (c r) -> p c r", r=128)
    for r0 in range(0, 128, 4):
        b = (r0 // 4) % 2
        for j in range(4):
            c.op("pe", lambda e, r0=r0, j=j, b=b: e.transpose(out=ps[b][0:na, j * 128:(j + 1) * 128], in_=v[:, r0 + j, :], identity=c.ident),
                 reads=[t_src, c.t_const], writes=[t_ps[b]])
        c.op("act", lambda e, r0=r0, b=b: e.activation(out=u3[:, :, r0:r0 + 4], in_=ps[b][0:na, :].rearrange("p (r c) -> p c r", r=4), func=AF.Copy),
             reads=[t_ps[b]], writes=(t_ut if isinstance(t_ut, list) else [t_ut]))


def to_feature_major(c, yt, t_yt, dst_ct, t_dst, ps, t_ps):
    y3 = yt[0:64, :].rearrange("p (c r) -> p r c", r=128)
    d3 = dst_ct.rearrange("p (a r) -> p r a", r=128)
    for r0 in range(0, 128, 8):
        b = (r0 // 8) % 2
        for j in range(8):
            c.op("pe", lambda e, r0=r0, j=j, b=b: e.transpose(out=ps[b][:, j * 64:(j + 1) * 64], in_=y3[:, r0 + j, :], identity=c.ident[0:64, 0:64]),
                 reads=(t_yt if isinstance(t_yt, list) else [t_yt]) + [c.t_const], writes=[t_ps[b]])
        c.op("dve", lambda e, r0=r0, b=b: e.tensor_copy(out=d3[:, r0:r0 + 8, :], in_=ps[b][:, :].rearrange("p (r a) -> p r a", r=8)),
             reads=[t_ps[b]], writes=[t_dst])


NCC = 4


def hy_scratch(c, tag):
    nc = c.nc
    zin = [[nc.dram_tensor("hy_zin%d_%d" % (hf, cc), [128, NTL], F32) for cc in range(NCC)] for hf in range(2)]
    zg = [[nc.dram_tensor("hy_zg%d_%d" % (hf, cc), [256, NTL], F32) for cc in range(NCC)] for hf in range(2)]
    return {"p": nc.dram_tensor("hy_p" + tag, [3 * NCC, 128, L], F32).ap(),
            "zin": zin, "zg": zg,
            "h3": nc.dram_tensor("hy_h3" + tag, [64, NFFT], F32).ap(),
            "ks": nc.dram_tensor("hy_ks" + tag, [2, NCC, 32, 128, 1024], F32).ap()}


def hy_inproj_phase(c, rows_fn, S, s):
    c.new_phase()
    NO = 3 * NCC * 128
    g_d = c.din("hy_g%d" % s, [128, 8])
    w_d = c.din("hy_win%d" % s, [8, 128, NO])
    gcol = c.sb(8)
    t_g = T()
    c.op("sp", lambda e: e.dma_start(out=gcol, in_=g_d), writes=[t_g], dma=True)
    win = [c.sb(NO, F32R) for _ in range(8)]
    t_w = [T() for _ in range(8)]
    for k in range(8):
        c.op("pool", lambda e, k=k: e.dma_start(out=win[k], in_=w_d[k]), writes=[t_w[k]], dma=True)
    frs = [Front(c), Front(c)]
    stage = [c.sb(TT) for _ in range(4)]
    t_st = [T() for _ in range(4)]
    P = c.psum
    ps_t, t_pst = P[0:2], [T(), T()]
    ps_o, t_pso = P[2:6], [T() for _ in range(4)]
    cnt = 0
    ntile = L // TT

    def prep(it):
        fr = frs[it % 2]
        r0 = it * TT
        fr.load_norm(None, r0, rows=[rows_fn(r0 + tb * 128) for tb in range(4)])
        fr.transpose(gcol, t_g, ps_t, t_pst)

    prep(0)
    for it in range(ntile):
        r0 = it * TT
        fr = frs[it % 2]
        if it + 1 < ntile:
            prep(it + 1)
        for oc in range(3 * NCC):
            b = cnt % 4
            cnt += 1
            for k in range(8):
                c.op("pe", lambda e, k=k, oc=oc, b=b, fr=fr: e.matmul(ps_o[b][:, :], win[k][:, oc * 128:(oc + 1) * 128], fr.hT[k], start=(k == 0), stop=(k == 7)),
                     reads=[t_w[k], fr.t_hT[k]], writes=[t_pso[b]])
            if oc % 2 == 0:
                c.op("act", lambda e, b=b: e.activation(out=stage[b], in_=ps_o[b][:, :], func=AF.Copy), reads=[t_pso[b]], writes=[t_st[b]])
            else:
                c.op("dve", lambda e, b=b: e.tensor_copy(out=stage[b], in_=ps_o[b][:, :]), reads=[t_pso[b]], writes=[t_st[b]])
            c.op("sp", lambda e, b=b, oc=oc, r0=r0: e.dma_start(out=S["p"][oc][:, r0:r0 + TT], in_=stage[b]), reads=[t_st[b]], dma=True)


def hy_conv3_phase(c, S, s):
    c.new_phase()
    cols_d = c.din("hy_c3cols%d" % s, [128, 3 * NCC * 5])
    cols = c.sb(3 * NCC * 5)
    t_c = T()
    c.op("sp", lambda e: e.dma_start(out=cols, in_=cols_d), writes=[t_c], dma=True)
    raw = [c.sb(L + 2) for _ in range(2)]
    t_raw = [T(), T()]
    out = [c.sb(L) for _ in range(2)]
    t_out = [T(), T()]
    for oc in range(3 * NCC):
        b = oc % 2
        eng = "dve"
        k0 = oc * 5
        c.op("sp", lambda e, b=b, oc=oc: e.dma_start(out=raw[b][:, 1:L + 1], in_=S["p"][oc]), writes=[t_raw[b]], dma=True)
        c.op("act", lambda e, b=b, k0=k0: e.activation(out=raw[b][:, 1:L + 1], in_=raw[b][:, 1:L + 1], func=AF.Identity, bias=cols[:, k0:k0 + 1]),
             reads=[t_raw[b], t_c], writes=[t_raw[b]])
        c.op(eng, lambda e, b=b: e.memset(raw[b][:, 0:1], 0.0), reads=[t_raw[b]], writes=[t_raw[b]])
        c.op(eng, lambda e, b=b: e.memset(raw[b][:, L + 1:L + 2], 0.0), reads=[t_raw[b]], writes=[t_raw[b]])
        c.op("act", lambda e, b=b, k0=k0: e.activation(out=out[b], in_=raw[b][:, 0:L], func=AF.Identity, scale=cols[:, k0 + 1:k0 + 2], bias=cols[:, k0 + 4:k0 + 5]),
             reads=[t_raw[b], t_c], writes=[t_out[b]])
        c.op(eng, lambda e, b=b, k0=k0: e.scalar_tensor_tensor(out=out[b], in0=raw[b][:, 1:L + 1], scalar=cols[:, k0 + 2:k0 + 3], in1=out[b], op0=ALU.mult, op1=ALU.add),
             reads=[t_raw[b], t_c, t_out[b]], writes=[t_out[b]])
        c.op(eng, lambda e, b=b, k0=k0: e.scalar_tensor_tensor(out=out[b], in0=raw[b][:, 2:L + 2], scalar=cols[:, k0 + 3:k0 + 4], in1=out[b], op0=ALU.mult, op1=ALU.add),
             reads=[t_raw[b], t_c, t_out[b]], writes=[t_out[b]])
        c.op("sp", lambda e, b=b, oc=oc: e.dma_start(out=S["p"][oc], in_=out[b]), reads=[t_out[b]], dma=True)


def hy_mlp_phase(c, S, s):
    c.new_phase()
    z_d = c.din("hy_z", [33, NFFT])
    w1_d = c.din("hy_w1_%d" % s, [33, 64])
    wi_d = c.din("hy_wi_%d" % s, [64, 128])
    cols_d = c.din("hy_mlpcols%d" % s, [64, 4])
    w1 = c.sb(64, F32R, parts=33)
    wi = c.sb(128, F32R, parts=64)
    cols = c.sb(4, parts=64)
    negpi = c.sb(1, parts=64)
    t_k = T()
    c.op("pool", lambda e: e.dma_start(out=w1, in_=w1_d), writes=[t_k], dma=True)
    t_k1 = T()
    c.op("pool", lambda e: e.dma_start(out=wi, in_=wi_d), writes=[t_k1], dma=True)
    t_k2 = T()
    c.op("sp", lambda e: e.dma_start(out=cols, in_=cols_d), writes=[t_k2], dma=True)
    c.op("dve", lambda e: e.memset(negpi, -math.pi), writes=[t_k2])
    NI = 4
    zt = [c.sb(TT, F32R, parts=33) for _ in range(NI)]
    t_zt = [T() for _ in range(NI)]
    arg = [c.sb(TT, parts=64) for _ in range(NI)]
    t_arg = [T() for _ in range(NI)]
    kk = [c.sb(TT, parts=64) for _ in range(NI)]
    t_kk = [T() for _ in range(NI)]
    MAGIC = 12582912.0
    hh = [[c.sb(TT, F32R, parts=64) for _ in range(2)] for _ in range(NI)]
    t_hh = [[T(), T()] for _ in range(NI)]
    h3 = [c.sb(TT, parts=64) for _ in range(NI)]
    t_h3 = [T() for _ in range(NI)]
    P = c.psum
    ps, t_ps = P[0:NI], [T() for _ in range(NI)]
    for it0 in range(0, NFFT // TT, NI):
        for p in range(NI):
            n0 = (it0 + p) * TT
            c.op("pool", lambda e, p=p, n0=n0: e.dma_start(out=zt[p], in_=z_d[:, n0:n0 + TT]), writes=[t_zt[p]], dma=True)
        for layer in range(3):
            for p in range(NI):
                n0 = (it0 + p) * TT
                if layer == 0:
                    c.op("pe", lambda e, p=p: e.matmul(ps[p][0:64, :], w1, zt[p], start=True, stop=True), reads=[t_k, t_zt[p]], writes=[t_ps[p]])
                else:
                    c.op("pe", lambda e, p=p, layer=layer: e.matmul(ps[p][0:64, :], wi[:, (layer - 1) * 64:layer * 64], hh[p][(layer - 1) % 2], start=True, stop=True),
                         reads=[t_k1, t_hh[p][(layer - 1) % 2]], writes=[t_ps[p]])
            for p in range(NI):
                c.op("dve", lambda e, p=p, layer=layer: e.tensor_scalar(out=arg[p], in0=ps[p][0:64, :], scalar1=cols[:, layer:layer + 1], scalar2=cols[:, 3:4], op0=ALU.add, op1=ALU.mult),
                     reads=[t_ps[p], t_k2], writes=[t_arg[p]])
            for p in range(NI):
                c.op("dve", lambda e, p=p: e.tensor_scalar(out=kk[p], in0=arg[p], scalar1=1.0 / (2 * math.pi), scalar2=MAGIC, op0=ALU.mult, op1=ALU.add),
                     reads=[t_arg[p]], writes=[t_kk[p]])
            for p in range(NI):
                c.op("dve", lambda e, p=p: e.tensor_scalar(out=kk[p], in0=kk[p], scalar1=-MAGIC, scalar2=2 * math.pi, op0=ALU.add, op1=ALU.mult),
                     reads=[t_kk[p]], writes=[t_kk[p]])
            for p in range(NI):
                c.op("dve", lambda e, p=p: e.tensor_tensor(out=arg[p], in0=arg[p], in1=kk[p], op=ALU.subtract),
                     reads=[t_arg[p], t_kk[p]], writes=[t_arg[p]])
            for p in range(NI):
                n0 = (it0 + p) * TT
                if layer < 2:
                    c.op("act", lambda e, p=p, layer=layer: e.activation(out=hh[p][layer % 2], in_=arg[p], func=AF.Sin), reads=[t_arg[p]], writes=[t_hh[p][layer % 2]])
                else:
                    c.op("act", lambda e, p=p: e.activation(out=h3[p], in_=arg[p], func=AF.Sin), reads=[t_arg[p]], writes=[t_h3[p]])
                    c.op("sp", lambda e, p=p, n0=n0: e.dma_start(out=S["h3"][:, n0:n0 + TT], in_=h3[p]), reads=[t_h3[p]], dma=True)


def hy_filter_phase(c, S, s):
    c.new_phase()
    w3_d = c.din("hy_w3_%d" % s, [64, 4 * NCC * 128])
    nd_d = c.din("hy_negdelta", [128, NCC])
    td_d = c.din("hy_tdec", [1, NFFT])
    ff = HyFFT(c, inverse=False)
    w3 = c.sb(4 * NCC * 128, F32R, parts=64)
    t_w3 = T()
    c.op("pool", lambda e: e.dma_start(out=w3, in_=w3_d), writes=[t_w3], dma=True)
    nd = c.sb(NCC)
    t_nd = T()
    c.op("sp", lambda e: e.dma_start(out=nd, in_=nd_d), writes=[t_nd], dma=True)
    kT = c.sb(NFFT)
    t_kT = T()
    ut = c.sb(NFFT, F32R)
    t_ut = T()
    NGB = 4
    h3t = [c.sb(TT, F32R, parts=64) for _ in range(NGB)]
    t_h3t = [T() for _ in range(NGB)]
    tdb = [c.sb(TT) for _ in range(NGB)]
    t_tdb = [T() for _ in range(NGB)]
    dec = [c.sb(TT) for _ in range(NGB)]
    t_dec = [T() for _ in range(NGB)]
    junk = c.sb(TT)
    t_junk = T()
    asum = c.sb(40)
    t_as = T()
    kst = [c.sb(1024) for _ in range(2)]
    t_kst = [T(), T()]
    P = c.psum
    ps_k, t_psk = P[4:8], [T() for _ in range(4)]
    ps_tr, t_pstr = P[6:8], [t_psk[2], t_psk[3]]
    cnt = 0
    for cc in range(NCC):
        for o in range(2):
            c.op("dve", lambda e: e.memset(asum[:, 0:40], 0.0), writes=[t_as])
            for it in range(NFFT // TT):
                n0 = it * TT
                b = cnt % NGB
                cnt += 1
                dirn = 0 if n0 < L else 1
                col = (dirn * 2 + o) * NCC * 128 + cc * 128
                c.op("pool", lambda e, b=b, n0=n0: e.dma_start(out=h3t[b], in_=S["h3"][:, n0:n0 + TT]), writes=[t_h3t[b]], dma=True)
                c.op("sp", lambda e, b=b, n0=n0: e.dma_start(out=tdb[b], in_=td_d[:, n0:n0 + TT].broadcast_to([128, TT])), writes=[t_tdb[b]], dma=True)
                c.op("pe", lambda e, b=b, col=col: e.matmul(ps_k[b][:, :], w3[:, col:col + 128], h3t[b], start=True, stop=True), reads=[t_w3, t_h3t[b]], writes=[t_psk[b]])
                c.op("act", lambda e, b=b, cc=cc: e.activation(out=dec[b], in_=tdb[b], func=AF.Exp, scale=nd[:, cc:cc + 1]), reads=[t_tdb[b], t_nd], writes=[t_dec[b]])
                c.op("dve", lambda e, b=b, n0=n0: e.tensor_tensor(out=kT[:, n0:n0 + TT], in0=ps_k[b][:, :], in1=dec[b], op=ALU.mult), reads=[t_psk[b], t_dec[b]], writes=[t_kT])
                c.op("act", lambda e, n0=n0, it=it: e.activation(out=junk, in_=kT[:, n0:n0 + TT], func=AF.Abs, accum_out=asum[:, it:it + 1]), reads=[t_kT], writes=[t_junk, t_as])
            c.op("dve", lambda e: e.reduce_sum(out=asum[:, 32:33], in_=asum[:, 0:32], axis=AX.X), reads=[t_as], writes=[t_as])
            c.op("dve", lambda e: e.reciprocal(out=asum[:, 33:34], in_=asum[:, 32:33]), reads=[t_as], writes=[t_as])
            c.op("dve", lambda e: e.tensor_scalar(out=asum[:, 33:34], in0=asum[:, 33:34], scalar1=1.0 / NFFT, scalar2=0.0, op0=ALU.mult, op1=ALU.add), reads=[t_as], writes=[t_as])
            for q in range(4):
                eng = "dve" if q % 2 == 0 else "pool"
                c.op(eng, lambda e, q=q: e.tensor_scalar(out=kT[:, q * 4096:(q + 1) * 4096], in0=kT[:, q * 4096:(q + 1) * 4096], scalar1=asum[:, 33:34], scalar2=0.0, op0=ALU.mult, op1=ALU.add),
                     reads=[t_kT, t_as], writes=[t_kT])
            to_time_major(c, kT, t_kT, ut, t_ut, 128, ps_tr, t_pstr)
            def spec_out(g, ps_x, t_psx, o=o, cc=cc):
                kb = g % 2
                c.op("act", lambda e, kb=kb: e.activation(out=kst[kb][:, 0:512], in_=ps_x[0][:, :], func=AF.Copy), reads=[t_psx[0]], writes=[t_kst[kb]])
                c.op("act", lambda e, kb=kb: e.activation(out=kst[kb][:, 512:1024], in_=ps_x[1][:, :], func=AF.Copy), reads=[t_psx[1]], writes=[t_kst[kb]])
                c.op("sp", lambda e, kb=kb, g=g: e.dma_start(out=S["ks"][o, cc, g], in_=kst[kb]), reads=[t_kst[kb]], dma=True)
            ff.run({"ut": ut, "t_ut": [t_ut] * 32, "ka": 128, "spec_out": spec_out})


def hy_conv_phase(c, S, s):
    c.new_phase()
    fb_d = c.din("hy_fbias%d" % s, [128, 2 * NCC])
    ff = HyFFT(c, inverse=True)
    fb = c.sb(2 * NCC)
    t_fb = T()
    c.op("sp", lambda e: e.dma_start(out=fb, in_=fb_d), writes=[t_fb], dma=True)
    bufA = c.sb(L)
    bufB = c.sb(L)
    t_A, t_B = T(), T()
    off_ut = (c.off + 7) // 8 * 8
    ut = c.sb(64 * 256, F32R)
    yt = c.sb_at(off_ut, 64 * 256, F32)
    t_utg = [T() for _ in range(32)]
    PC = 512
    xp = [c.sb(PC) for _ in range(2)]
    t_xp = [T(), T()]
    kt = [c.sb(1024) for _ in range(3)]
    t_kt = [T(), T(), T()]
    P = c.psum
    ps_tr, t_pstr = [P[6], P[7]], [T(), T()]
    t_pstr[0] = ff.t_psy
    npc = 0
    nk = 0
    for cc in range(NCC):
        c.op("sp", lambda e, cc=cc: e.dma_start(out=bufA, in_=S["p"][2 * NCC + cc]), writes=[t_A], dma=True)
        src, t_src, dstb, t_dst = bufA, t_A, bufB, t_B
        for o in range(2):
            to_time_major(c, src, t_src, ut, t_utg, 64, ps_tr, t_pstr)
            ktmap = {}

            def pre(t, o=o, cc=cc, ktmap=ktmap):
                g = t - 2
                if 0 <= g < 32:
                    kb = g % 3
                    c.op("sp", lambda e, kb=kb, g=g: e.dma_start(out=kt[kb], in_=S["ks"][o, cc, g]), writes=[t_kt[kb]], dma=True)
                    ktmap[g] = (kt[kb], t_kt[kb])
            ff.run({"ut": ut, "t_ut": t_utg, "ka": 64, "yt": yt, "kt": lambda g, ktmap=ktmap: ktmap[g], "pre": pre})
            to_feature_major(c, yt, t_utg, dstb, t_dst, ps_tr, t_pstr)
            gate_chunk = (0 if o == 0 else NCC) + cc
            for pc in range(L // PC):
                pb = npc % 2
                npc += 1
                sl = slice(pc * PC, (pc + 1) * PC)
                c.op("sp", lambda e, pb=pb, gate_chunk=gate_chunk, sl=sl: e.dma_start(out=xp[pb], in_=S["p"][gate_chunk][:, sl]), writes=[t_xp[pb]], dma=True)
                eng = "pool"
                c.op("dve", lambda e, sl=sl, o=o, cc=cc, src=src, dstb=dstb: e.scalar_tensor_tensor(out=dstb[:, sl], in0=src[:, sl], scalar=fb[:, o * NCC + cc:o * NCC + cc + 1], in1=dstb[:, sl], op0=ALU.mult, op1=ALU.add),
                     reads=[t_src, t_dst, t_fb], writes=[t_dst])
                c.op(eng, lambda e, sl=sl, pb=pb, dstb=dstb: e.tensor_tensor(out=dstb[:, sl], in0=dstb[:, sl], in1=xp[pb], op=ALU.mult),
                     reads=[t_dst, t_xp[pb]], writes=[t_dst])
            src, t_src, dstb, t_dst = dstb, t_dst, src, t_src
        for hf in range(2):
            t_z = T()
            c.op("sp", lambda e, cc=cc, src=src, hf=hf: e.dma_start(out=S["zin"][hf][cc].ap(), in_=src[:, hf * NTL:(hf + 1) * NTL]), reads=[t_src], writes=[t_z], dma=True)
            c.op("pool", lambda e, cc=cc, hf=hf: e.collective_compute("AllGather", ALU.bypass, replica_groups=PAIRS, ins=[S["zin"][hf][cc].ap()], outs=[S["zg"][hf][cc].ap()]),
                 reads=[t_z], writes=[T()], cc=True)


def hy_outproj_phase(c, src, dst, S, s):
    c.new_phase()
    w_d = c.din("hy_wout%d" % s, [8, 128, D])
    b_d = c.din("hy_bout%d" % s, [1, D])
    sel_d = c.din("hy_sel", [128, 2])
    wo = [c.sb(D, F32R) for _ in range(8)]
    t_wo = [T() for _ in range(8)]
    for k in range(8):
        c.op("pool", lambda e, k=k: e.dma_start(out=wo[k], in_=w_d[k]), writes=[t_wo[k]], dma=True)
    brow = c.sb(D)
    t_b = T()
    c.op("sp", lambda e: e.dma_start(out=brow, in_=b_d.broadcast_to([128, D])), writes=[t_b], dma=True)
    sel = c.sb(2)
    t_sel = T()
    c.op("sp", lambda e: e.dma_start(out=sel, in_=sel_d), writes=[t_sel], dma=True)
    zT = [[c.sb(TT, F32R) for _ in range(8)] for _ in range(2)]
    t_zT = [[T() for _ in range(8)] for _ in range(2)]
    NZ = 4
    zA = [c.sb(TT) for _ in range(NZ)]
    zB = [c.sb(TT) for _ in range(NZ)]
    t_zA = [T() for _ in range(NZ)]
    t_zB = [T() for _ in range(NZ)]
    xin = [c.sb(D) for _ in range(4)]
    t_xin = [T() for _ in range(4)]
    P = c.psum
    ps, t_ps = P[0:4], [T() for _ in range(4)]
    cnt = 0
    nz = 0
    for it in range(NTL // TT):
        r0 = it * TT
        zb = it % 2
        for k in range(8):
            rank, cc = k // NCC, k % NCC
            b = nz % NZ
            nz += 1
            c.op("sp", lambda e, b=b, rank=rank, cc=cc, r0=r0: e.dma_start(out=zA[b], in_=S["zg"][0][cc].ap()[rank * 128:(rank + 1) * 128, r0:r0 + TT]), writes=[t_zA[b]], dma=True)
            c.op("sp", lambda e, b=b, rank=rank, cc=cc, r0=r0: e.dma_start(out=zB[b], in_=S["zg"][1][cc].ap()[rank * 128:(rank + 1) * 128, r0:r0 + TT]), writes=[t_zB[b]], dma=True)
            c.op("dve", lambda e, b=b: e.tensor_scalar(out=zA[b], in0=zA[b], scalar1=sel[:, 0:1], scalar2=0.0, op0=ALU.mult, op1=ALU.add), reads=[t_zA[b], t_sel], writes=[t_zA[b]])
            c.op("dve", lambda e, b=b, k=k, zb=zb: e.scalar_tensor_tensor(out=zT[zb][k], in0=zB[b], scalar=sel[:, 1:2], in1=zA[b], op0=ALU.mult, op1=ALU.add),
                 reads=[t_zA[b], t_zB[b], t_sel], writes=[t_zT[zb][k]])
        for tb in range(4):
            xb = tb
            c.op("sp", lambda e, xb=xb, tb=tb, r0=r0: e.dma_start(out=xin[xb], in_=src[r0 + tb * 128:r0 + (tb + 1) * 128, :]), writes=[t_xin[xb]], dma=True)
            for dh in range(2):
                b = cnt % 4
                cnt += 1
                sl = slice(dh * 512, (dh + 1) * 512)
                for k in range(8):
                    c.op("pe", lambda e, k=k, zb=zb, tb=tb, sl=sl, b=b: e.matmul(ps[b][:, :], zT[zb][k][:, tb * 128:(tb + 1) * 128], wo[k][:, sl], start=(k == 0), stop=(k == 7)),
                         reads=[t_zT[zb][k], t_wo[k]], writes=[t_ps[b]])
                c.op("dve", lambda e, xb=xb, sl=sl, b=b: e.tensor_tensor(out=xin[xb][:, sl], in0=ps[b][:, :], in1=xin[xb][:, sl], op=ALU.add),
                     reads=[t_ps[b], t_xin[xb]], writes=[t_xin[xb]])
                c.op("pool", lambda e, xb=xb, sl=sl: e.tensor_tensor(out=xin[xb][:, sl], in0=xin[xb][:, sl], in1=brow[:, sl], op=ALU.add),
                     reads=[t_xin[xb], t_b], writes=[t_xin[xb]])
            c.op("sp", lambda e, xb=xb, tb=tb, r0=r0: e.dma_start(out=dst[r0 + tb * 128:r0 + (tb + 1) * 128, :], in_=xin[xb]), reads=[t_xin[xb]], dma=True)


def gather_x(c, src):
    c.new_phase()
    nc = c.nc
    outs = []
    for j in range(NTL // 512):
        cin = nc.dram_tensor("xg_in%d" % j, [512, D], F32)
        cg = nc.dram_tensor("xg_out%d" % j, [1024, D], F32)
        t_c = T()
        c.op("sp", lambda e, j=j, cin=cin: e.dma_start(out=cin.ap(), in_=src[j * 512:(j + 1) * 512, :]), writes=[t_c], dma=True)
        c.op("pool", lambda e, cin=cin, cg=cg: e.collective_compute("AllGather", ALU.bypass, replica_groups=PAIRS, ins=[cin.ap()], outs=[cg.ap()]),
             reads=[t_c], writes=[T()], cc=True)
        outs.append(cg)

    def rows_fn(R):
        rank, rr = R // NTL, R % NTL
        j, i = rr // 512, rr % 512
        return outs[j].ap()[rank * 512 + i:rank * 512 + i + 128, :]
    return rows_fn


def hyena_layer(c, rows_fn, src, dst, s):
    if not hasattr(c, "hyS"):
        c.hyS = hy_scratch(c, "")
    S = c.hyS
    hy_inproj_phase(c, rows_fn, S, s)
    hy_conv3_phase(c, S, s)
    hy_mlp_phase(c, S, s)
    hy_filter_phase(c, S, s)
    hy_conv_phase(c, S, s)
    hy_outproj_phase(c, src, dst, S, s)
    return S


def hyena_host(inputs, s, li, h):
    g = np.asarray(inputs["norm_mix"][li], np.float32)
    CH = NCC * 128
    csl = np.concatenate([np.arange(t * D + h * CH, t * D + (h + 1) * CH) for t in range(3)])
    b_in = np.asarray(inputs["hy_b_in"][s], np.float32)[csl]
    cw = np.asarray(inputs["hy_conv_w"][s], np.float32)[:, csl]
    cb = np.asarray(inputs["hy_conv_b"][s], np.float32)[csl]
    c3 = np.stack([b_in, cw[0], cw[1], cw[2], cb], axis=-1).reshape(3 * NCC, 128, 5).transpose(1, 0, 2).reshape(128, 3 * NCC * 5)
    mlpc = np.stack([np.asarray(inputs["hy_f_b1"][s], np.float32), np.asarray(inputs["hy_f_bi"][s][0], np.float32),
                     np.asarray(inputs["hy_f_bi"][s][1], np.float32), np.asarray(inputs["hy_f_freq"][s], np.float32)], axis=-1)
    wi = np.asarray(inputs["hy_f_wi"][s], np.float32)
    fbias = np.asarray(inputs["hy_f_bias"][s], np.float32)[:, h * CH:(h + 1) * CH].reshape(2, NCC, 128).transpose(2, 0, 1).reshape(128, 2 * NCC)
    w3 = np.asarray(inputs["hy_f_w3"][s], np.float32).reshape(64, 4, D)[:, :, h * CH:(h + 1) * CH].reshape(64, 4 * CH)
    sel = np.zeros((128, 2), np.float32)
    sel[:, h] = 1.0
    m = {"hy_g%d" % s: np.ascontiguousarray(g.reshape(8, 128).T),
         "hy_win%d" % s: np.ascontiguousarray(np.asarray(inputs["hy_w_in"][s], np.float32)[:, csl].reshape(8, 128, 3 * CH)),
         "hy_c3cols%d" % s: np.ascontiguousarray(c3),
         "hy_w1_%d" % s: np.asarray(inputs["hy_f_w1"][s], np.float32),
         "hy_wi_%d" % s: np.ascontiguousarray(np.concatenate([wi[0], wi[1]], axis=1)),
         "hy_mlpcols%d" % s: np.ascontiguousarray(mlpc),
         "hy_w3_%d" % s: np.ascontiguousarray(w3),
         "hy_fbias%d" % s: np.ascontiguousarray(fbias),
         "hy_wout%d" % s: np.ascontiguousarray(np.asarray(inputs["hy_w_out"][s], np.float32).reshape(8, 128, D)),
         "hy_bout%d" % s: np.asarray(inputs["hy_b_out"][s], np.float32).reshape(1, D),
         "hy_sel": sel}
    hc = hy_consts()
    hc["hy_negdelta"] = np.ascontiguousarray(hc["hy_negdelta_full"][h * CH:(h + 1) * CH].reshape(NCC, 128).T)
    del hc["hy_negdelta_full"]
    m.update(hc)
    return m


def build_program(plan):
    nc = bass.Bass("TRN2", target_bir_lowering=False)
    st = contextlib.ExitStack()
    with st:
        c = Ctx(nc, st)
        x_full = c.din("x", [L, D])
        x_loc = c.din("xloc", [NTL, D])
        y_out = nc.dram_tensor("y", [NTL, D], F32, kind="ExternalOutput").ap()
        bufs = [c.dscratch("xa", [NTL, D]), c.dscratch("xb", [NTL, D])]
        load_consts(c)
        cur = x_loc
        for pi, ph in enumerate(plan):
            last = pi == len(plan) - 1
            dst = y_out if last else bufs[pi % 2]
            if ph.startswith("ffn"):
                ffn_phase(c, cur, dst, int(ph[3:]), ntok=NTL)
            elif ph == "pool2":
                pool_phase(c, cur, dst)
            elif ph == "hyena0":
                hyena_layer(c, (lambda R: x_full[R:R + 128, :]) if pi == 0 else gather_x(c, cur), cur, dst, 0)
            elif ph == "hyena3":
                hyena_layer(c, gather_x(c, cur), cur, dst, 1)
            elif ph == "attn1":
                S = attn_scratch(c)
                hg = halo_exchange(c, cur)
                attn_qkv_phase(c, cur, S, hg)
                attn_core_phase(c, cur, dst, S)
            else:
                raise ValueError(ph)
            cur = dst
        c.mk.emit()
    return nc


def host_inputs(inputs, plan, h):
    m = {"ident": np.eye(128, dtype=np.float32)}
    for ph in plan:
        if ph.startswith("ffn"):
            m.update(ffn_host(inputs, int(ph[3:])))
        elif ph == "pool2":
            m.update(pool_host(inputs, h))
        elif ph == "attn1":
            m.update(attn_host(inputs, h))
        elif ph == "hyena0":
            m.update(hyena_host(inputs, 0, 0, h))
        elif ph == "hyena3":
            m.update(hyena_host(inputs, 1, 3, h))
    return m


DEBUG_HY = False
FULL_PLAN = ["hyena0", "ffn0", "attn1", "ffn1", "pool2", "ffn2", "hyena3", "ffn3"]


def run_plan(inputs, plan, x_override=None, trace=False):
    nc = build_program(plan)
    shared = [host_inputs(inputs, plan, h) for h in range(2)]
    x = np.asarray(inputs["x"], np.float32) if x_override is None else x_override
    nb = x.shape[0]
    in_maps = []
    for core in range(8):
        b, h = (core // 2) % nb, core % 2
        m = dict(shared[h])
        m["x"] = np.ascontiguousarray(x[b])
        m["xloc"] = np.ascontiguousarray(x[b, h * NTL:(h + 1) * NTL])
        in_maps.append(m)
    res = run_bass_kernel_spmd(nc, in_maps, core_ids=list(range(8)), trace=trace)
    out = np.stack([np.concatenate([res.results[2 * b]["y"], res.results[2 * b + 1]["y"]], axis=0) for b in range(nb)], axis=0)
    return out, res


def kernel(**inputs):
    out, _ = run_plan(inputs, FULL_PLAN)
    return out.astype(np.float32)
```
